# Optimizing a Trainium2 kernel written in Bass

```python
import math
import jax, jax.numpy as jnp
from jax import lax
import numpy as np

D_MODEL = 1024
BATCH = 4
SEQ = 8192
DEPTH = 1

D_MIX = D_MODEL
D_RWKV = D_MIX // 2
D_S5 = D_MIX - D_RWKV
RWKV_HEAD = 64
N_RWKV_HEADS = D_RWKV // RWKV_HEAD
LORA_W = 64
LORA_A = 64
LORA_G = 128
S5_CH = 16
N_S5_GROUPS = D_S5 // S5_CH
S5_STATE = 64
D_FF = ((8 * D_MODEL // 3 + 255) // 256) * 256
D_SHIFT = 3 * D_RWKV + LORA_W + LORA_A + LORA_G
D_IN = D_SHIFT + D_S5
NORM_EPS = 1e-6
LNX_EPS = 64e-5
DT_MIN = 1e-3
DT_MAX = 1e-1

kernel_name = "hybrid_rwkv7_s5_adaln_block"


def rms_norm(x, eps=NORM_EPS):
    xf = x.astype(jnp.float32)
    y = xf * lax.rsqrt(jnp.mean(xf * xf, axis=-1, keepdims=True) + eps)
    return y.astype(x.dtype)


def modulate(h, shift, scale):
    return h * (1.0 + scale[:, None, :]) + shift[:, None, :]


def rwkv7_time_mix(z, w0, w2, a0, a2, g2, k_k, k_a, r_k, lnx_w, lnx_b):
    out_dtype = z.dtype
    z = z.astype(jnp.float32)
    bsz, seq = z.shape[0], z.shape[1]
    H, N = N_RWKV_HEADS, RWKV_HEAD
    o = 0
    r = z[..., o:o + D_RWKV]; o += D_RWKV
    k = z[..., o:o + D_RWKV]; o += D_RWKV
    v = z[..., o:o + D_RWKV]; o += D_RWKV
    w_lo = z[..., o:o + LORA_W]; o += LORA_W
    a_lo = z[..., o:o + LORA_A]; o += LORA_A
    g_lo = z[..., o:o + LORA_G]

    w = -jax.nn.softplus(-(w0 + jnp.tanh(w_lo) @ w2)) - 0.5
    decay = jnp.exp(-jnp.exp(w))
    a = jax.nn.sigmoid(a0 + a_lo @ a2)
    g = jax.nn.sigmoid(g_lo) @ g2
    kk = (k * k_k).reshape(bsz, seq, H, N)
    kk = kk / jnp.maximum(jnp.linalg.norm(kk, axis=-1, keepdims=True), 1e-12)
    k = k * (1.0 + (a - 1.0) * k_a)

    heads = lambda t: t.reshape(bsz, seq, H, N)
    r_h, k_h, v_h, w_h, a_h = heads(r), heads(k), heads(v), heads(decay), heads(a)
    tm = lambda t: jnp.transpose(t, (1, 0, 2, 3))
    xs = (tm(r_h), tm(w_h), tm(k_h), tm(v_h), tm(-kk), tm(kk * a_h))

    def step(S, inp):
        r_t, w_t, k_t, v_t, a_t, b_t = inp
        sa = jnp.einsum("bhvk,bhk->bhv", S, a_t)
        S = S * w_t[:, :, None, :] + sa[..., None] * b_t[:, :, None, :] + v_t[..., None] * k_t[:, :, None, :]
        y = jnp.einsum("bhvk,bhk->bhv", S, r_t)
        return S, y

    S0 = jnp.zeros((bsz, H, N, N), jnp.float32)
    _, y = lax.scan(step, S0, xs)
    y = jnp.transpose(y, (1, 0, 2, 3))

    mu = jnp.mean(y, axis=-1, keepdims=True)
    var = jnp.mean(jnp.square(y - mu), axis=-1, keepdims=True)
    y = (y - mu) * lax.rsqrt(var + LNX_EPS) * lnx_w.reshape(H, N) + lnx_b.reshape(H, N)
    y = y + jnp.sum(r_h * k_h * r_k, axis=-1, keepdims=True) * v_h
    return (y.reshape(bsz, seq, D_RWKV) * g).astype(out_dtype)


def _ssm_combine(e1, e2):
    ar1, ai1, br1, bi1 = e1
    ar2, ai2, br2, bi2 = e2
    ar = ar2 * ar1 - ai2 * ai1
    ai = ar2 * ai1 + ai2 * ar1
    br = ar2 * br1 - ai2 * bi1 + br2
    bi = ar2 * bi1 + ai2 * br1 + bi2
    return ar, ai, br, bi


def s5_mix(u, a_re, a_im, log_dt, b_re, b_im, c_re, c_im, d_skip, w_glu, b_glu, gain):
    out_dtype = u.dtype
    u = u.astype(jnp.float32)
    bsz, seq = u.shape[0], u.shape[1]
    G, P = N_S5_GROUPS, S5_STATE
    ug = u.reshape(bsz, seq, G, S5_CH)

    dt = jnp.exp(log_dt)[:, None]
    mag = jnp.exp(dt * a_re)
    abar_re = mag * jnp.cos(dt * a_im)
    abar_im = mag * jnp.sin(dt * a_im)
    den = a_re * a_re + a_im * a_im
    p, q = abar_re - 1.0, abar_im
    coef_re = (p * a_re + q * a_im) / den
    coef_im = (q * a_re - p * a_im) / den
    bbar_re = coef_re[..., None] * b_re - coef_im[..., None] * b_im
    bbar_im = coef_re[..., None] * b_im + coef_im[..., None] * b_re

    bu_re = jnp.einsum("gpc,btgc->tbgp", bbar_re, ug)
    bu_im = jnp.einsum("gpc,btgc->tbgp", bbar_im, ug)
    a_seq_re = jnp.broadcast_to(abar_re[None, None], (seq, 1, G, P))
    a_seq_im = jnp.broadcast_to(abar_im[None, None], (seq, 1, G, P))
    _, _, s_re, s_im = lax.associative_scan(_ssm_combine, (a_seq_re, a_seq_im, bu_re, bu_im), axis=0)

    y = (jnp.einsum("gcp,tbgp->btgc", c_re, s_re) - jnp.einsum("gcp,tbgp->btgc", c_im, s_im)
         + d_skip * ug).reshape(bsz, seq, D_S5)
    zz = jax.nn.gelu(y)
    out = zz * jax.nn.sigmoid(zz @ w_glu + b_glu)
    return (rms_norm(out) * gain).astype(out_dtype)


def setup_inputs(seed: int = 0) -> dict:
    key = jax.random.key(seed)
    ks = jax.random.split(key, 40)
    L = DEPTH
    nrm = lambda k, shape, s: jax.random.normal(k, shape, jnp.float32) * s
    uni = lambda k, shape, lo, hi: jax.random.uniform(k, shape, jnp.float32, lo, hi)
    H, N, G, P = N_RWKV_HEADS, RWKV_HEAD, N_S5_GROUPS, S5_STATE
    return {
        "x": nrm(ks[0], (BATCH, SEQ, D_MODEL), 1.0),
        "c": nrm(ks[1], (BATCH, D_MODEL), 1.0),
        "w_ada": nrm(ks[2], (L, D_MODEL, 6 * D_MODEL), D_MODEL ** -0.5),
        "b_ada": nrm(ks[3], (L, 6 * D_MODEL), 0.01),
        "w_in": nrm(ks[4], (L, D_MODEL, D_IN), D_MODEL ** -0.5),
        "mu_shift": uni(ks[5], (L, D_SHIFT), 0.0, 1.0),
        "rw_w0": uni(ks[6], (L, D_RWKV), -6.0, 1.0),
        "rw_w2": nrm(ks[7], (L, LORA_W, D_RWKV), 0.5 * LORA_W ** -0.5),
        "rw_a0": nrm(ks[8], (L, D_RWKV), 0.1),
        "rw_a2": nrm(ks[9], (L, LORA_A, D_RWKV), 0.5 * LORA_A ** -0.5),
        "rw_g2": nrm(ks[10], (L, LORA_G, D_RWKV), LORA_G ** -0.5),
        "rw_k_k": 0.85 + nrm(ks[11], (L, D_RWKV), 0.05),
        "rw_k_a": 1.0 + nrm(ks[12], (L, D_RWKV), 0.05),
        "rw_r_k": nrm(ks[13], (L, H, N), 0.1),
        "rw_lnx_w": 1.0 + nrm(ks[14], (L, D_RWKV), 0.05),
        "rw_lnx_b": nrm(ks[15], (L, D_RWKV), 0.01),
        "s5_a_re": -0.5 + nrm(ks[16], (L, G, P), 0.01),
        "s5_a_im": math.pi * jnp.arange(P, dtype=jnp.float32)[None, None, :] + nrm(ks[17], (L, G, P), 0.01),
        "s5_log_dt": uni(ks[18], (L, G), math.log(DT_MIN), math.log(DT_MAX)),
        "s5_b_re": nrm(ks[19], (L, G, P, S5_CH), (2 * S5_CH) ** -0.5),
        "s5_b_im": nrm(ks[20], (L, G, P, S5_CH), (2 * S5_CH) ** -0.5),
        "s5_c_re": nrm(ks[21], (L, G, S5_CH, P), (2 * P) ** -0.5),
        "s5_c_im": nrm(ks[22], (L, G, S5_CH, P), (2 * P) ** -0.5),
        "s5_d": nrm(ks[23], (L, G, S5_CH), 1.0),
        "s5_w_glu": nrm(ks[24], (L, D_S5, D_S5), D_S5 ** -0.5),
        "s5_b_glu": nrm(ks[25], (L, D_S5), 0.01),
        "s5_gain": 1.0 + nrm(ks[26], (L, D_S5), 0.05),
        "w_out": nrm(ks[27], (L, D_MIX, D_MODEL), D_MIX ** -0.5),
        "ffn_w_gate": nrm(ks[28], (L, D_MODEL, D_FF), D_MODEL ** -0.5),
        "ffn_w_up": nrm(ks[29], (L, D_MODEL, D_FF), D_MODEL ** -0.5),
        "ffn_w_down": nrm(ks[30], (L, D_FF, D_MODEL), D_FF ** -0.5),
        "final_gain": 1.0 + nrm(ks[31], (D_MODEL,), 0.05),
    }


def reference(x, c, w_ada, b_ada, w_in, mu_shift, rw_w0, rw_w2, rw_a0, rw_a2, rw_g2,
              rw_k_k, rw_k_a, rw_r_k, rw_lnx_w, rw_lnx_b, s5_a_re, s5_a_im, s5_log_dt,
              s5_b_re, s5_b_im, s5_c_re, s5_c_im, s5_d, s5_w_glu, s5_b_glu, s5_gain,
              w_out, ffn_w_gate, ffn_w_up, ffn_w_down, final_gain):
    c_act = jax.nn.silu(c)
    for l in range(DEPTH):
        ada = c_act @ w_ada[l] + b_ada[l]
        sh_m, sc_m, g_m, sh_f, sc_f, g_f = jnp.split(ada, 6, axis=-1)

        h = modulate(rms_norm(x), sh_m, sc_m)
        proj = h @ w_in[l]
        z = proj[..., :D_SHIFT]
        u = proj[..., D_SHIFT:]
        z_prev = jnp.pad(z[:, :-1], ((0, 0), (1, 0), (0, 0)))
        z = z + mu_shift[l] * (z_prev - z)
        y_rwkv = rwkv7_time_mix(z, rw_w0[l], rw_w2[l], rw_a0[l], rw_a2[l], rw_g2[l],
                                rw_k_k[l], rw_k_a[l], rw_r_k[l], rw_lnx_w[l], rw_lnx_b[l])
        y_s5 = s5_mix(u, s5_a_re[l], s5_a_im[l], s5_log_dt[l], s5_b_re[l], s5_b_im[l],
                      s5_c_re[l], s5_c_im[l], s5_d[l], s5_w_glu[l], s5_b_glu[l], s5_gain[l])
        mix = jnp.concatenate([y_rwkv, y_s5], axis=-1) @ w_out[l]
        x = x + g_m[:, None, :] * mix

        h = modulate(rms_norm(x), sh_f, sc_f)
        ffn = (jax.nn.silu(h @ ffn_w_gate[l]) * (h @ ffn_w_up[l])) @ ffn_w_down[l]
        x = x + g_f[:, None, :] * ffn
    return rms_norm(x) * final_gain
```

```python
import math
from contextlib import ExitStack

import numpy as np
import concourse.bass as bass
import concourse.mybir as mybir
from concourse.bass_utils import run_bass_kernel_spmd

F32 = mybir.dt.float32
BF16 = mybir.dt.bfloat16
I32 = mybir.dt.int32
AF = mybir.ActivationFunctionType
ALU = mybir.AluOpType
AX = mybir.AxisListType

D = 1024
TOK = 4096
W = 256
L = 64
NWIN = TOK // W
DFF = 2816
NFF = DFF // 128
TWO_PI = 2.0 * math.pi
PI_SAFE = 3.1415925


class Buf:
    __slots__ = ("name", "w", "r", "excl")

    def __init__(self, name, excl=False):
        self.name = name
        self.w = None
        self.r = {}
        self.excl = excl


class DSem:
    __slots__ = ("handle", "count", "id")


class Sched:
    def __init__(self, nc, es):
        self.nc = nc
        self.es = es
        self.eng = {"pe": nc.tensor, "act": nc.scalar, "dve": nc.vector, "pool": nc.gpsimd, "sp": nc.sync}
        self.sem = {k: es.enter_context(nc.semaphore("sem_" + k)) for k in self.eng}
        self.cnt = {k: 0 for k in self.eng}
        self.seen = {k: {} for k in self.eng}
        self.dsems = {}
        self.nds = 0
        self.const_bufs = []
        self.dead = False

    def _need(self, eng, R, W):
        need = {}

        def add(ev):
            if ev is None:
                return
            k = ev[0]
            if k not in need or need[k][2] < ev[2]:
                need[k] = ev

        for b in R:
            add(b.w)
        for b in W:
            add(b.w)
            for ev in b.r.values():
                add(ev)
        E = self.eng[eng]
        for k, ev in need.items():
            if k == ("e", eng) and eng in ("pe", "sp"):
                continue
            if self.seen[eng].get(k, 0) >= ev[2]:
                continue
            E.wait_ge(ev[1], ev[2])
            self.seen[eng][k] = ev[2]

    def op(self, eng, fn, R=(), W=(), inc=True):
        if self.dead:
            return None
        if any(b.excl for b in R):
            W = list(W) + [b for b in R if b.excl]
            R = [b for b in R if not b.excl]
        self._need(eng, R, W)
        inst = fn(self.eng[eng])
        val = self.cnt[eng] + 1
        if inc:
            inst.then_inc(self.sem[eng], 1)
            self.cnt[eng] = val
        ev = (("e", eng), self.sem[eng], val)
        for b in R:
            b.r[ev[0]] = ev
        for b in W:
            b.w = ev
            b.r = {}
        return inst

    def _dsem(self, key):
        if key not in self.dsems:
            d = DSem()
            d.handle = self.es.enter_context(self.nc.semaphore("dsem%d" % self.nds))
            d.count = 0
            d.id = self.nds
            self.nds += 1
            self.dsems[key] = d
        return self.dsems[key]

    def dma(self, out, in_, key, R=(), W=(), const=False):
        if self.dead:
            return
        self._need("sp", R, W)
        d = self._dsem(key)
        d.count += 16
        self.nc.sync.dma_start(out=out, in_=in_).then_inc(d.handle, 16)
        ev = (("d", d.id), d.handle, d.count)
        for b in R:
            b.r[ev[0]] = ev
        for b in W:
            b.w = ev
            b.r = {}
            if const:
                self.const_bufs.append(b)

    def finalize_consts(self, key):
        d = self._dsem(key)
        ev = (("d", d.id), d.handle, d.count)
        for b in self.const_bufs:
            b.w = ev
        self.const_bufs = []

    def barrier(self):
        for e, E in self.eng.items():
            for f in self.eng:
                if f == e:
                    continue
                k = ("e", f)
                if self.cnt[f] > self.seen[e].get(k, 0):
                    E.wait_ge(self.sem[f], self.cnt[f])
                    self.seen[e][k] = self.cnt[f]
            for d in self.dsems.values():
                k = ("d", d.id)
                if d.count > self.seen[e].get(k, 0):
                    E.wait_ge(d.handle, d.count)
                    self.seen[e][k] = d.count


class _StopBuild(Exception):
    pass


_DBG = {"stop": None, "dumps": [], "meta": []}


def build_program():
    nc = bass.Bass("TRN2", target_bir_lowering=False)
    dbg_on = _DBG["stop"] is not None
    if dbg_on:
        dbg_d = nc.dram_tensor("dbg", [128, 65536], F32, kind="ExternalOutput").ap()
        _DBG["meta"] = []

    def din(name, shape):
        return nc.dram_tensor(name, list(shape), F32, kind="ExternalInput").ap()

    xcat = din("xcat", [2 * TOK, D])
    maskv_d = din("maskv", [128, 1])
    c_d = din("c_l", [128, 8])
    wada_d = din("w_ada", [D, 6 * D])
    bada_d = din("b_ada", [1, 6 * D])
    win_d = din("w_in", [D, 2304])
    mu_d = din("mu_l", [128, 14])
    rwp_d = din("rwp", [128, 28])
    lo2w_d = din("lo2w", [128, 512])
    lo2a_d = din("lo2a", [128, 512])
    g2_d = din("g2", [128, 512])
    s5la_d = din("s5la", [128, 48])
    bpre_d = din("bpre", [128, 2048])
    bpim_d = din("bpim", [128, 2048])
    lbare_d = din("lbare", [128, 2048])
    lbaim_d = din("lbaim", [128, 2048])
    lbldt_d = din("lbldt", [128, 2048])
    cpre_d = din("cpre", [128, 2048])
    cpim_d = din("cpim", [128, 2048])
    s5v_d = din("s5v", [128, 12])
    wglu_d = din("w_glu", [512, 512])
    wout_d = din("w_out", [D, D])
    wg_d = din("wg", [D, DFF])
    wu_d = din("wu", [D, DFF])
    wd_d = din("wd", [DFF, D])
    fgain_d = din("fgain", [1, D])
    ident_d = din("ident", [128, 128])
    bones_d = din("bones", [128, 128])
    maskar_d = din("maskar", [128, 128])
    masknt_d = din("masknt", [128, 64])
    identp_d = din("identp", [128, 64])
    idx1_d = din("idx1", [128, 64])
    rst_d = din("rst01", [128, 256])
    out_d = nc.dram_tensor("out", [TOK, D], F32, kind="ExternalOutput").ap()

    es = ExitStack()
    with es:
        S = Sched(nc, es)
        uid = [0]

        def sbt(stack, shape, dt, nm="t"):
            uid[0] += 1
            name = "%s_%d" % (nm, uid[0])
            t = stack.enter_context(nc.sbuf_tensor(name, list(shape), dt))
            return t, Buf(name)

        def pst(nm, dt, n):
            t = es.enter_context(nc.psum_tensor(nm, [128, n], dt))
            return t

        PT = pst("PT", BF16, 1024)
        bPT = Buf("PT", True)
        PJ = [pst("PJ0", F32, 512), pst("PJ1", F32, 512)]
        bPJ = [Buf("PJ0", True), Buf("PJ1", True)]
        PM = pst("PM", F32, 512)
        _bpm = Buf("PM", True)
        bPM = [_bpm, _bpm]
        PA = [pst("PA0", F32, 512), pst("PA1", F32, 512)]
        bPA = [Buf("PA0", True), Buf("PA1", True)]
        PI = pst("PI", F32, 512)
        _bpi = Buf("PI", True)
        bPI = [_bpi, _bpi]
        PC = pst("PC", F32, 512)
        _bpc = Buf("PC", True)
        bPC = [_bpc, _bpc]

        def tt(eng, out, in0, in1, op, R, Wb):
            return S.op(eng, lambda e: e.tensor_tensor(out=out, in0=in0, in1=in1, op=op), R, Wb)

        def ts(eng, out, in0, s1, s2, op0, op1, R, Wb):
            if op1 is None and eng == "pool" and op0 == ALU.mult:
                return S.op(eng, lambda e: e.tensor_scalar(out=out, in0=in0, scalar1=s1, scalar2=0.0, op0=op0, op1=ALU.add), R, Wb)
            if op1 is None:
                return S.op(eng, lambda e: e.tensor_scalar(out=out, in0=in0, scalar1=s1, scalar2=None, op0=op0), R, Wb)
            return S.op(eng, lambda e: e.tensor_scalar(out=out, in0=in0, scalar1=s1, scalar2=s2, op0=op0, op1=op1), R, Wb)

        def stt(out, in0, scalar, in1, op0, op1, R, Wb):
            return S.op("dve", lambda e: e.scalar_tensor_tensor(out=out, in0=in0, scalar=scalar, in1=in1, op0=op0, op1=op1), R, Wb)

        def act(out, in_, func, R, Wb, bias=None, scale=None, accum=None):
            kw = {}
            if bias is not None:
                kw["bias"] = bias
            if scale is not None:
                kw["scale"] = scale
            if accum is not None:
                kw["accum_out"] = accum
            return S.op("act", lambda e: e.activation(out=out, in_=in_, func=func, **kw), R, Wb)

        def cp(eng, out, in_, R, Wb):
            if eng == "act":
                return act(out, in_, AF.Identity, R, Wb)
            return S.op(eng, lambda e: e.tensor_copy(out=out, in_=in_), R, Wb)

        def mm(out, lhsT, rhs, R, Wb, start=True, stop=True, inc=True):
            return S.op("pe", lambda e: e.matmul(out, lhsT=lhsT, rhs=rhs, start=start, stop=stop), R, Wb, inc=inc)

        def tr(out, in_, ident, R, Wb, inc=True):
            return S.op("pe", lambda e: e.transpose(out, in_, ident), R, Wb, inc=inc)

        def bc1(ap, n):
            return ap.unsqueeze(1).to_broadcast([ap.shape[0], n, ap.shape[1]])

        def bc2(ap, n):
            return ap.unsqueeze(2).to_broadcast([ap.shape[0], ap.shape[1], n])

        dbg_off = [0]
        if dbg_on:
            dstage = [sbt(es, [128, 2048], F32, "dstage%d" % i) for i in range(1)]
        dcount = [0]

        def dump(name, ap, buf):
            shape = list(ap.shape)
            n = 1
            for d_ in shape[1:]:
                n *= d_
            P_ = shape[0]
            st_t, st_b = dstage[0]
            dcount[0] += 1
            dst = st_t[0:P_, 0:n]
            if len(shape) == 3:
                dst = dst.rearrange("p (a b) -> p a b", b=shape[2])
            elif len(shape) == 4:
                dst = dst.rearrange("p (a b c) -> p a b c", b=shape[2], c=shape[3])
            cp("dve", dst, ap, [buf], [st_b])
            S.dma(dbg_d[0:P_, dbg_off[0]:dbg_off[0] + n], st_t[0:P_, 0:n], "dstage0", R=[st_b])
            _DBG["meta"].append((name, dbg_off[0], shape))
            dbg_off[0] += n

        def sub(name):
            if dbg_on and _DBG.get("sub") == name:
                S.dead = True

        def stage(name, fn=None):
            if dbg_on and _DBG["stop"] == name:
                S.dead = False
                if fn is not None:
                    fn()
                S.barrier()
                return True
            return False

        ident_f, b_ident_f = sbt(es, [128, 128], F32, "identf")
        ident_b, b_ident_b = sbt(es, [128, 128], BF16, "identb")
        maskv, b_maskv = sbt(es, [128, 1], F32, "maskv")
        modp, b_modp = sbt(es, [128, 4, 8], F32, "modp")
        gf_bc, b_gf = sbt(es, [128, D], F32, "gfbc")

        S.dma(ident_f[:], ident_d, "const", W=[b_ident_f], const=True)
        S.dma(maskv[:], maskv_d, "const", W=[b_maskv], const=True)

        esA = ExitStack()
        with esA:
            W1, b_W1 = sbt(esA, [128, 8, 2304], BF16, "W1")
            Wo, b_Wo = sbt(esA, [128, 8, D], BF16, "Wo")
            wglu, b_wglu = sbt(esA, [128, 4, 512], BF16, "wglu")
            lo2w, b_lo2w = sbt(esA, [128, 512], BF16, "lo2w")
            lo2a, b_lo2a = sbt(esA, [128, 512], BF16, "lo2a")
            g2, b_g2 = sbt(esA, [128, 512], BF16, "g2")
            bones_b, b_bones_b = sbt(esA, [128, 128], BF16, "bonesb")
            bones_f, b_bones_f = sbt(esA, [128, 128], F32, "bonesf")
            ones_b, b_ones_b = sbt(esA, [128, 128], BF16, "onesb")
            maskar, b_maskar = sbt(esA, [128, 128], F32, "maskar")
            masknt, b_masknt = sbt(esA, [128, 64], F32, "masknt")
            identp, b_identp = sbt(esA, [128, 64], F32, "identp")
            rst01, b_rst = sbt(esA, [128, 256], F32, "rst01")
            mu, b_mu = sbt(esA, [128, 14], F32, "mu")
            omu, b_omu = sbt(esA, [128, 14], F32, "omu")
            rwp, b_rwp = sbt(esA, [128, 8, 4], F32, "rwp")
            s5v, b_s5v = sbt(esA, [128, 3, 4], F32, "s5v")
            Bb_re, b_Bbre = sbt(esA, [128, 16, 128], BF16, "Bbre")
            Bb_im, b_Bbim = sbt(esA, [128, 16, 128], BF16, "Bbim")
            Cp_re, b_Cpre = sbt(esA, [128, 16, 128], BF16, "Cpre")
            Cp_imn, b_Cpim = sbt(esA, [128, 16, 128], BF16, "Cpimn")
            tcos, b_tcos = sbt(esA, [128, 16, 64], F32, "tcos")
            tsin, b_tsin = sbt(esA, [128, 16, 64], F32, "tsin")
            rho0, b_rho0 = sbt(esA, [128, 16, 64], F32, "rho0")
            rho1, b_rho1 = sbt(esA, [128, 16], F32, "rho1")
            T63, b_T63 = sbt(esA, [128, 2, 2, 16], F32, "T63")

            S.dma(maskar[:], maskar_d, "const", W=[b_maskar], const=True)
            S.dma(masknt[:], masknt_d, "const", W=[b_masknt], const=True)
            S.dma(identp[:], identp_d, "const", W=[b_identp], const=True)
            S.dma(rst01[:], rst_d, "const", W=[b_rst], const=True)
            S.dma(mu[:], mu_d, "const", W=[b_mu], const=True)
            S.dma(rwp[:, 0:7, :].rearrange("p a b -> p (a b)"), rwp_d, "const", W=[b_rwp], const=True)
            S.dma(s5v[:].rearrange("p a b -> p (a b)"), s5v_d, "const", W=[b_s5v], const=True)
            S.dma(bones_f[:], bones_d, "const", W=[b_bones_f], const=True)

            esS = ExitStack()
            with esS:
                c_l, b_c = sbt(esS, [128, 8], F32, "c")
                c_act, b_cact = sbt(esS, [128, 8], F32, "cact")
                c_rep, b_crep = sbt(esS, [128, 8, 128], F32, "crep")
                adaR, b_adaR = sbt(esS, [128, 6 * D], F32, "adaR")
                badab, b_badab = sbt(esS, [128, 6 * D], F32, "badab")
                ada_fm, b_adafm = sbt(esS, [128, 48], F32, "adafm")
                stg = [sbt(esS, [128, 8, 512], F32, "stg%d" % i) for i in range(2)]
                lst = [sbt(esS, [128, 512], F32, "lst%d" % i) for i in range(3)]
                S.dma(lst[0][0][:], lo2w_d, "const", W=[lst[0][1]], const=True)
                S.dma(lst[1][0][:], lo2a_d, "const", W=[lst[1][1]], const=True)
                S.dma(lst[2][0][:], g2_d, "const", W=[lst[2][1]], const=True)
                S.dma(c_l[:], c_d, "const", W=[b_c], const=True)
                S.dma(badab[:], bada_d.partition_broadcast(128), "const", W=[b_badab], const=True)
                S.finalize_consts("const")

                cp("dve", ident_b[:], ident_f[:], [b_ident_f], [b_ident_b])
                cp("dve", bones_b[:], bones_f[:], [b_bones_f], [b_bones_b])
                S.op("pool", lambda e: e.memset(ones_b[:], 1.0 / 512.0), [], [b_ones_b])
                ts("dve", bones_f[:], bones_f[:], 1.0 / 64.0, None, ALU.mult, None, [b_bones_f], [b_bones_f])
                ts("dve", omu[:], mu[:], -1.0, 1.0, ALU.mult, ALU.add, [b_mu], [b_omu])
                ts("dve", rwp[:, 7, :], rwp[:, 3, :], -1.0, 1.0, ALU.mult, ALU.add, [b_rwp], [b_rwp])
                cp("act", lo2w[:], lst[0][0][:], [lst[0][1]], [b_lo2w])
                cp("act", lo2a[:], lst[1][0][:], [lst[1][1]], [b_lo2a])
                cp("act", g2[:], lst[2][0][:], [lst[2][1]], [b_g2])

                act(c_act[:], c_l[:], AF.Silu, [b_c], [b_cact])
                cp("dve", c_rep[:], bc2(c_act[:], 128), [b_cact], [b_crep])
                wada_v = wada_d.rearrange("(k p) n -> p k n", p=128)
                for blk in range(12):
                    st_t, st_b = stg[blk % 2]
                    S.dma(st_t[:], wada_v[:, :, blk * 512:(blk + 1) * 512], "stg%d" % (blk % 2), W=[st_b])
                    pj, bpj = PJ[blk % 2], bPJ[blk % 2]
                    for k in range(8):
                        mm(pj[:, :], c_rep[:, k, :], st_t[:, k, :], [b_crep, st_b], [bpj],
                           start=(k == 0), stop=(k == 7), inc=(k == 7))
                    tt("dve", adaR[:, blk * 512:(blk + 1) * 512], pj[:, :], badab[:, blk * 512:(blk + 1) * 512],
                       ALU.add, [bpj, b_badab], [b_adaR])
                tt("dve", badab[:].rearrange("p (j q) -> p j q", q=128), adaR[:].rearrange("p (j q) -> p j q", q=128),
                   bc1(ident_f[:], 48), ALU.mult, [b_adaR, b_ident_f, b_badab], [b_badab])
                S.op("dve", lambda e: e.tensor_reduce(out=ada_fm[:], in_=badab[:].rearrange("p (j q) -> p j q", q=128),
                                                       axis=AX.X, op=ALU.add), [b_badab], [b_adafm])
                cp("dve", modp[:, 0, :], ada_fm[:, 0:8], [b_adafm], [b_modp])
                ts("dve", modp[:, 1, :], ada_fm[:, 8:16], 1.0, None, ALU.add, None, [b_adafm], [b_modp])
                cp("dve", modp[:, 2, :], ada_fm[:, 24:32], [b_adafm], [b_modp])
                ts("dve", modp[:, 3, :], ada_fm[:, 32:40], 1.0, None, ALU.add, None, [b_adafm], [b_modp])
                cp("dve", gf_bc[:], adaR[:, 5 * D:6 * D], [b_adaR], [b_gf])

                ci = 0
                for k in range(8):
                    st_t, st_b = stg[k % 2]
                    stv = st_t[:].rearrange("p a b -> p (a b)")
                    S.dma(stv[:, 0:2304], win_d[k * 128:(k + 1) * 128, :], "stg%d" % (k % 2), W=[st_b])
                    cp(("act", "dve", "pool")[ci % 3], W1[:, k, :], stv[:, 0:2304], [st_b], [b_W1])
                    ci += 1
                for k in range(8):
                    st_t, st_b = stg[k % 2]
                    stv = st_t[:].rearrange("p a b -> p (a b)")
                    S.dma(stv[:, 0:D], wout_d[k * 128:(k + 1) * 128, :], "stg%d" % (k % 2), W=[st_b])
                    tt("dve", Wo[:, k, :], stv[:, 0:D], adaR[:, 2 * D:3 * D], ALU.mult, [st_b, b_adaR], [b_Wo])
                for k in range(4):
                    st_t, st_b = stg[k % 2]
                    stv = st_t[:].rearrange("p a b -> p (a b)")
                    S.dma(stv[:, 0:512], wglu_d[k * 128:(k + 1) * 128, :], "stg%d" % (k % 2), W=[st_b])
                    cp("act", wglu[:, k, :], stv[:, 0:512], [st_b], [b_wglu])

                if stage('setup1', lambda: (dump('modp', modp[:], b_modp), dump('gf', gf_bc[:], b_gf), dump('W1', W1[:, 3, 0:2048], b_W1), dump('Wo', Wo[:, 2, :], b_Wo))):
                    return nc
                S.barrier()
            esS2 = ExitStack()
            with esS2:
                s5la, b_s5la = sbt(esS2, [128, 3, 16], F32, "s5la")
                idx1, b_idx1 = sbt(esS2, [128, 64], F32, "idx1")
                lb = {}
                for nm, dd in (("bpre", bpre_d), ("bpim", bpim_d), ("are", lbare_d), ("aim", lbaim_d), ("ldt", lbldt_d)):
                    lb[nm] = sbt(esS2, [128, 2048], F32, "lb" + nm)
                    S.dma(lb[nm][0][:], dd, "const2", W=[lb[nm][1]], const=True)
                cst = [sbt(esS2, [128, 2048], F32, "cst%d" % i) for i in range(1)]
                S.dma(s5la[:].rearrange("p a b -> p (a b)"), s5la_d, "const2", W=[b_s5la], const=True)
                S.dma(idx1[:], idx1_d, "const2", W=[b_idx1], const=True)
                S.finalize_consts("const2")
                tmp = [sbt(esS2, [128, 1024], F32, "s5t%d" % i) for i in range(8)]
                tmpi, b_tmpi = sbt(esS2, [128, 1024], I32, "s5ti")

                def sincos(ang, b_ang, n, o_sin, b_osin, o_cos, b_ocos, ta, tb):
                    for shift, o, bo in ((0.0, o_sin, b_osin), (0.5 * math.pi, o_cos, b_ocos)):
                        ts("dve", ta[0][:, 0:n], ang, shift, 1.0 / TWO_PI, ALU.add, ALU.mult, [b_ang], [ta[1]])
                        cp("dve", tmpi[:, 0:n], ta[0][:, 0:n], [ta[1]], [b_tmpi])
                        cp("dve", tb[0][:, 0:n], tmpi[:, 0:n], [b_tmpi], [tb[1]])
                        ts("dve", ta[0][:, 0:n], ang, shift, None, ALU.add, None, [b_ang], [ta[1]])
                        stt(ta[0][:, 0:n], tb[0][:, 0:n], -TWO_PI, ta[0][:, 0:n], ALU.mult, ALU.add, [tb[1], ta[1]], [ta[1]])
                        ts("dve", ta[0][:, 0:n], ta[0][:, 0:n], -PI_SAFE, PI_SAFE, ALU.max, ALU.min, [ta[1]], [ta[1]])
                        act(o, ta[0][:, 0:n], AF.Sin, [ta[1]], [bo])

                t_dt, t_xr, t_xi, t_mag, t_sin, t_cos, t_a, t_b = tmp
                Bre_flat = Bb_re[:].rearrange("p a b -> p (a b)")
                Bim_flat = Bb_im[:].rearrange("p a b -> p (a b)")
                for hh in range(2):
                    c2 = slice(hh * 1024, (hh + 1) * 1024)
                    are_t, are_b = lb["are"]
                    aim_t, aim_b = lb["aim"]
                    ldt_t, ldt_b = lb["ldt"]
                    bre_t, bre_b = lb["bpre"]
                    bim_t, bim_b = lb["bpim"]
                    A_re, A_im = are_t[:, c2], aim_t[:, c2]
                    act(t_dt[0][:], ldt_t[:, c2], AF.Exp, [ldt_b], [t_dt[1]])
                    tt("dve", t_xr[0][:], t_dt[0][:], A_re, ALU.mult, [t_dt[1], are_b], [t_xr[1]])
                    tt("dve", t_xi[0][:], t_dt[0][:], A_im, ALU.mult, [t_dt[1], aim_b], [t_xi[1]])
                    act(t_mag[0][:], t_xr[0][:], AF.Exp, [t_xr[1]], [t_mag[1]])
                    sincos(t_xi[0][:], t_xi[1], 1024, t_sin[0][:], t_sin[1], t_cos[0][:], t_cos[1], t_a, t_b)
                    tt("dve", t_cos[0][:], t_cos[0][:], t_mag[0][:], ALU.mult, [t_cos[1], t_mag[1]], [t_cos[1]])
                    ts("dve", t_cos[0][:], t_cos[0][:], -1.0, None, ALU.add, None, [t_cos[1]], [t_cos[1]])
                    tt("dve", t_sin[0][:], t_sin[0][:], t_mag[0][:], ALU.mult, [t_sin[1], t_mag[1]], [t_sin[1]])
                    tt("dve", t_dt[0][:], A_re, A_re, ALU.mult, [are_b], [t_dt[1]])
                    tt("dve", t_xr[0][:], A_im, A_im, ALU.mult, [aim_b], [t_xr[1]])
                    tt("dve", t_dt[0][:], t_dt[0][:], t_xr[0][:], ALU.add, [t_dt[1], t_xr[1]], [t_dt[1]])
                    S.op("dve", lambda e: e.reciprocal(out=t_dt[0][:], in_=t_dt[0][:]), [t_dt[1]], [t_dt[1]])
                    tt("dve", t_xr[0][:], t_cos[0][:], A_re, ALU.mult, [t_cos[1], are_b], [t_xr[1]])
                    tt("dve", t_a[0][:], t_sin[0][:], A_im, ALU.mult, [t_sin[1], aim_b], [t_a[1]])
                    tt("dve", t_xr[0][:], t_xr[0][:], t_a[0][:], ALU.add, [t_xr[1], t_a[1]], [t_xr[1]])
                    tt("dve", t_xr[0][:], t_xr[0][:], t_dt[0][:], ALU.mult, [t_xr[1], t_dt[1]], [t_xr[1]])
                    tt("dve", t_xi[0][:], t_sin[0][:], A_re, ALU.mult, [t_sin[1], are_b], [t_xi[1]])
                    tt("dve", t_a[0][:], t_cos[0][:], A_im, ALU.mult, [t_cos[1], aim_b], [t_a[1]])
                    tt("dve", t_xi[0][:], t_xi[0][:], t_a[0][:], ALU.subtract, [t_xi[1], t_a[1]], [t_xi[1]])
                    tt("dve", t_xi[0][:], t_xi[0][:], t_dt[0][:], ALU.mult, [t_xi[1], t_dt[1]], [t_xi[1]])
                    tt("dve", t_a[0][:], t_xr[0][:], bre_t[:, c2], ALU.mult, [t_xr[1], bre_b], [t_a[1]])
                    tt("dve", t_b[0][:], t_xi[0][:], bim_t[:, c2], ALU.mult, [t_xi[1], bim_b], [t_b[1]])
                    tt("dve", Bre_flat[:, c2], t_a[0][:], t_b[0][:], ALU.subtract, [t_a[1], t_b[1]], [b_Bbre])
                    tt("dve", t_a[0][:], t_xr[0][:], bim_t[:, c2], ALU.mult, [t_xr[1], bim_b], [t_a[1]])
                    tt("dve", t_b[0][:], t_xi[0][:], bre_t[:, c2], ALU.mult, [t_xi[1], bre_b], [t_b[1]])
                    tt("dve", Bim_flat[:, c2], t_a[0][:], t_b[0][:], ALU.add, [t_a[1], t_b[1]], [b_Bbim])
                S.dma(cst[0][0][:], cpre_d, "cst", W=[cst[0][1]])
                cp("act", Cp_re[:].rearrange("p a b -> p (a b)"), cst[0][0][:], [cst[0][1]], [b_Cpre])
                S.dma(cst[0][0][:], cpim_d, "cst", W=[cst[0][1]])
                S.op("act", lambda e: e.mul(Cp_imn[:].rearrange("p a b -> p (a b)"), cst[0][0][:], -1.0), [cst[0][1]], [b_Cpim])
                la_dt, la_th = t_mag, t_dt
                act(la_dt[0][:, 0:16], s5la[:, 2, :], AF.Exp, [b_s5la], [la_dt[1]])
                tt("dve", la_th[0][:, 0:16], la_dt[0][:, 0:16], s5la[:, 1, :], ALU.mult, [la_dt[1], b_s5la], [la_th[1]])
                tt("dve", la_dt[0][:, 0:16], la_dt[0][:, 0:16], s5la[:, 0, :], ALU.mult, [la_dt[1], b_s5la], [la_dt[1]])
                act(rho1[:], la_dt[0][:, 0:16], AF.Exp, [la_dt[1]], [b_rho1])
                tt("dve", t_xr[0][:, 0:1024].rearrange("p (a b) -> p a b", b=64), bc2(la_th[0][:, 0:16], 64), bc1(idx1[:], 16),
                   ALU.mult, [la_th[1], b_idx1], [t_xr[1]])
                sincos(t_xr[0][:, 0:1024], t_xr[1], 1024, tsin[:].rearrange("p a b -> p (a b)"), b_tsin,
                       tcos[:].rearrange("p a b -> p (a b)"), b_tcos, t_a, t_b)
                cp("dve", rho0[:], bc2(rho1[:], 64), [b_rho1], [b_rho0])
                S.op("dve", lambda e: e.memset(rho0[:, :, 0:1], 0.0), [], [b_rho0])
                tt("dve", T63[:, 0, 0, :], tcos[:, :, 63], rho1[:], ALU.mult, [b_tcos, b_rho1], [b_T63])
                tt("dve", T63[:, 0, 1, :], tsin[:, :, 63], rho1[:], ALU.mult, [b_tsin, b_rho1], [b_T63])
                tt("dve", T63[:, 1, 1, :], tcos[:, :, 63], rho1[:], ALU.mult, [b_tcos, b_rho1], [b_T63])
                stt(T63[:, 1, 0, :], tsin[:, :, 63], -1.0, rho1[:], ALU.mult, ALU.mult, [b_tsin, b_rho1], [b_T63])
                if stage('setup2', lambda: (dump('Bbre', Bb_re[:, 5, :], b_Bbre), dump('Bbim', Bb_im[:, 5, :], b_Bbim), dump('tcos', tcos[:], b_tcos), dump('tsin', tsin[:], b_tsin), dump('rho1', rho1[:], b_rho1), dump('rho0', rho0[:, 3, :], b_rho0), dump('Cpimn', Cp_imn[:, 9, :], b_Cpim))):
                    return nc
                S.barrier()

            xs = [sbt(esA, [128, D], F32, "xs%d" % i) for i in range(2)]
            xnb = [sbt(esA, [128, D], BF16, "xnb%d" % i) for i in range(1)]
            stat, b_stat = sbt(esA, [128, 8], F32, "stat")
            hT = [sbt(esA, [128, 8, W], BF16, "hT%d" % i) for i in range(1)]
            Pprev, b_Pprev = sbt(esA, [128, 14], F32, "Pprev")
            Pb = [sbt(esA, [128, W + 1], F32, "Pb%d" % i) for i in range(2)]
            zt = [sbt(esA, [128, W], F32, "zt%d" % i) for i in range(4)]
            LA, b_LA = sbt(esA, [128, W], BF16, "LA")
            gs, b_gs = sbt(esA, [128, W], BF16, "gs")
            ARs = [sbt(esA, [128, 4, 4, 128], BF16, "AR%d" % i) for i in range(2)]
            KHs = [sbt(esA, [128, 4, W], BF16, "KH%d" % i) for i in range(2)]
            BHs = [sbt(esA, [128, 4, W], BF16, "BH%d" % i) for i in range(2)]
            VBs = [sbt(esA, [128, 4, W], BF16, "VB%d" % i) for i in range(2)]
            kT, b_kT = sbt(esA, [128, 4, 4, 64], BF16, "kT")
            bT, b_bT = sbt(esA, [128, 4, 4, 64], BF16, "bT")
            vT, b_vT = sbt(esA, [128, 4, 4, 64], BF16, "vT")
            GLs = [sbt(esA, [128, 4, 4], F32, "GL%d" % i) for i in range(2)]
            gfms = [sbt(esA, [128, 4, W], BF16, "gfm%d" % i) for i in range(2)]
            bvs = [sbt(esA, [128, 4, W], BF16, "bv%d" % i) for i in range(2)]
            Yw, b_Yw = sbt(esA, [128, 4, W], F32, "Yw")
            ufms = [sbt(esA, [128, 4, W], BF16, "ufm%d" % i) for i in range(2)]
            Y5w, b_Y5w = sbt(esA, [128, 4, W], F32, "Y5w")
            ycat, b_ycat = sbt(esA, [128, 8, W], BF16, "ycat")
            et = [sbt(esA, [128, W], F32, "et%d" % i) for i in range(12)]
            etb = [sbt(esA, [128, W], BF16, "etb%d" % i) for i in range(2)]
            AbmS = [sbt(esA, [128, 4, 128], BF16, "Abm%d" % i) for i in range(2)]
            AkmS = [sbt(esA, [128, 4, 128], BF16, "Akm%d" % i) for i in range(2)]
            TfinS = [sbt(esA, [128, 4, 64], BF16, "Tfin%d" % i) for i in range(2)]
            ImT, b_ImT = sbt(esA, [128, 4, 64], F32, "ImT")
            Xb = [sbt(esA, [128, 4, 64], BF16, "Xb%d" % i) for i in range(2)]
            XTb = [sbt(esA, [128, 4, 64], BF16, "XTb%d" % i) for i in range(2)]
            Pm = [sbt(esA, [128, 4, 64], BF16, "Pm%d" % i) for i in range(2)]
            NTb, b_NTb = sbt(esA, [128, 4, 64], BF16, "NTb")
            Rb, b_Rb = Xb[0]
            TTb, b_TTb = Xb[1]
            WT, b_WT = sbt(esA, [128, 4, 64], BF16, "WT")
            UT, b_UT = sbt(esA, [128, 4, 64], BF16, "UT")
            Sf, b_Sf = sbt(esA, [128, 4, 64], F32, "Sf")
            Sb = [sbt(esA, [128, 4, 64], BF16, "Sb%d" % i) for i in range(2)]
            stmp, b_stmp = sbt(esA, [128, 4, 64], F32, "stmp")
            s5a = [sbt(esA, [128, 512], F32, "s5a%d" % i) for i in range(4)]
            s_tr, b_str = sbt(esA, [128, 512], F32, "str")
            s_ti, b_sti = sbt(esA, [128, 512], F32, "sti")
            srb, b_srb = sbt(esA, [128, 512], BF16, "srb")
            sib, b_sib = sbt(esA, [128, 512], BF16, "sib")
            s5s, b_s5s = sbt(esA, [128, 2, 16], F32, "s5s")
            s5q, b_s5q = sbt(esA, [128, 6, 8], F32, "s5q")

            S.op("dve", lambda e: e.memset(Pprev[:], 0.0), [], [b_Pprev])
            S.op("dve", lambda e: e.memset(Sf[:], 0.0), [], [b_Sf])
            S.op("dve", lambda e: e.memset(Sb[0][0][:], 0.0), [], [Sb[0][1]])
            S.op("dve", lambda e: e.memset(s5s[:], 0.0), [], [b_s5s])
            S.op("dve", lambda e: e.memset(ARs[0][0][:], 0.0), [], [ARs[0][1]])
            S.op("dve", lambda e: e.memset(ARs[1][0][:], 0.0), [], [ARs[1][1]])
            M = {}

            cur_s = [0]
            lastT = [None]
            xslot = [0]

            def load_x(row0):
                i = xslot[0] % 2
                xslot[0] += 1
                t, b = xs[i]
                S.dma(t[:], xcat[row0:row0 + 128, :], "xs%d" % i, W=[b])
                return t, b

            def norm_transpose(xt, bx, st, hT_t, hT_b, sh_i, sc_i, width):
                xn_t, xn_b = xnb[0]
                col = st % 4
                act(xn_t[:], xt[:], AF.Square, [bx], [xn_b, b_stat], accum=stat[:, col:col + 1])
                act(stat[:, 4 + col:5 + col], stat[:, col:col + 1], AF.Sqrt, [b_stat], [b_stat], bias=1e-6, scale=1.0 / D)
                S.op("dve", lambda e: e.reciprocal(out=stat[:, 4 + col:5 + col], in_=stat[:, 4 + col:5 + col]), [b_stat], [b_stat])
                act(xn_t[:], xt[:], AF.Identity, [bx, b_stat], [xn_b], scale=stat[:, 4 + col:5 + col])
                for k in range(8):
                    tr(PT[:, k * 128:(k + 1) * 128], xn_t[:, k * 128:(k + 1) * 128], ident_b[:], [xn_b, b_ident_b], [bPT], inc=(k == 7))
                for k in range(8):
                    o = hT_t[:, k, st * 128:(st + 1) * 128]
                    i_ = PT[:, k * 128:(k + 1) * 128]
                    if True:
                        act(o, i_, AF.Identity, [bPT, b_modp], [hT_b], scale=modp[:, sc_i, k:k + 1], bias=modp[:, sh_i, k:k + 1])
                    else:
                        ts("dve", o, i_, modp[:, sc_i, k:k + 1], modp[:, sh_i, k:k + 1], ALU.mult, ALU.add, [bPT, b_modp], [hT_b])

            pbi = [0]
            pji = [0]

            def project(hT_t, hT_b, ct):
                pj, bpj = PA[1], bPA[1]
                for k in range(8):
                    mm(pj[:, 0:W], W1[:, k, ct * 128:(ct + 1) * 128], hT_t[:, k, :], [b_W1, hT_b], [bpj],
                       start=(k == 0), stop=(k == 7), inc=(k == 7))
                return pj, bpj

            def shifted(hT_t, hT_b, ct, zo, b_zo):
                pj, bpj = project(hT_t, hT_b, ct)
                p_t, p_b = Pb[pbi[0] % 2]
                pbi[0] += 1
                cp("pool", p_t[:, 0:1], Pprev[:, ct:ct + 1], [b_Pprev], [p_b])
                act(p_t[:, 1:W + 1], pj[:, 0:W], AF.Identity, [bpj], [p_b])
                cp("pool", Pprev[:, ct:ct + 1], p_t[:, W:W + 1], [p_b], [b_Pprev])
                act(zo, p_t[:, 0:W], AF.Identity, [p_b, b_mu], [b_zo], scale=mu[:, ct:ct + 1])
                stt(zo, p_t[:, 1:W + 1], omu[:, ct:ct + 1], zo, ALU.mult, ALU.add, [p_b, b_omu, b_zo], [b_zo])

            def rp(i, f):
                return rwp[:, i, f:f + 1]

            def drive(gens, weights=None, background=()):
                gens = list(gens)
                bg = list(background)
                wts = {id(g_): 1 for g_ in gens}
                if weights:
                    for g_, w_ in zip(gens, weights):
                        wts[id(g_)] = w_
                while gens:
                    for g_ in list(gens):
                        for _ in range(wts[id(g_)]):
                            try:
                                next(g_)
                            except StopIteration:
                                gens.remove(g_)
                                break
                    for g_ in list(bg):
                        for _ in range(2):
                            try:
                                next(g_)
                            except StopIteration:
                                bg.remove(g_)
                                break

            def pre(idx):
                own = idx >= NWIN
                win = idx % NWIN
                par = idx % 2
                AR, b_AR = ARs[par]
                KH, b_KH = KHs[par]
                BH, b_BH = BHs[par]
                GL, b_GL = GLs[par]
                gfm, b_gfm = gfms[par]
                bv, b_bv = bvs[par]
                ufm, b_ufm = ufms[par]
                row_base = (TOK if own else 0) + win * W
                hT_t, hT_b = hT[0]
                for st in range(W // 128):
                    xt, bx = load_x(row_base + st * 128)
                    norm_transpose(xt, bx, st, hT_t, hT_b, 0, 1, W)
                    yield
                z12, b_z12 = zt[3]
                shifted(hT_t, hT_b, 12, z12[:], b_z12)
                act(LA[0:64, :], z12[0:64, :], AF.Tanh, [b_z12], [b_LA])
                cp("pool", LA[64:128, :], z12[64:128, :], [b_z12], [b_LA])
                yield
                need_r = own or (win == NWIN - 1)
                if need_r:
                    shifted(hT_t, hT_b, 13, z12[:], b_z12)
                if own:
                    act(gs[:], z12[:], AF.Sigmoid, [b_z12], [b_gs])
                yield
                for f in range(4):
                    (zr, b_zr), (zk, b_zk), (zv, b_zv) = zt[0], zt[1], zt[2]
                    if need_r:
                        shifted(hT_t, hT_b, f, zr[:], b_zr)
                        yield
                    shifted(hT_t, hT_b, 4 + f, zk[:], b_zk)
                    yield
                    shifted(hT_t, hT_b, 8 + f, zv[:], b_zv)
                    yield
                    (sw, b_sw), (asg, b_asg), (kkr, b_kkr), (nrm, b_nrm), (kk, b_kk), (t1, b_t1) = et[0:6]
                    (kmod, b_kmod), (bvec, b_bvec), (lw, b_lw), (cc, b_cc), (cm, b_cm), (ex, b_ex) = et[6:12]
                    (sq, b_sq), (rk, b_rk) = etb
                    mm(PA[1][:, 0:W], lo2w[:, f * 128:(f + 1) * 128], LA[:], [b_lo2w, b_LA], [bPA[1]])
                    mm(PA[1][:, W:2 * W], lo2a[:, f * 128:(f + 1) * 128], LA[:], [b_lo2a, b_LA], [bPA[1]])
                    act(sw[:], PA[1][:, 0:W], AF.Sigmoid, [bPA[1], b_rwp], [b_sw], bias=rp(0, f))
                    act(asg[:], PA[1][:, W:2 * W], AF.Sigmoid, [bPA[1], b_rwp], [b_asg], bias=rp(1, f))
                    if own:
                        mm(PA[1][:, 0:W], g2[:, f * 128:(f + 1) * 128], gs[:], [b_g2, b_gs], [bPA[1]])
                        act(gfm[:, f, :], PA[1][:, 0:W], AF.Identity, [bPA[1]], [b_gfm])
                    act(kkr[:], zk[:], AF.Identity, [b_zk, b_rwp], [b_kkr], scale=rp(2, f))
                    act(sq[:], kkr[:], AF.Square, [b_kkr], [b_sq])
                    yield
                    mm(PA[1][:, 0:W], bones_b[:], sq[:], [b_bones_b, b_sq], [bPA[1]])
                    act(nrm[:], PA[1][:, 0:W], AF.Sqrt, [bPA[1]], [b_nrm])
                    ts("dve", nrm[:], nrm[:], 1e-12, None, ALU.max, None, [b_nrm], [b_nrm])
                    S.op("dve", lambda e: e.reciprocal(out=nrm[:], in_=nrm[:]), [b_nrm], [b_nrm])
                    tt("pool", kk[:], kkr[:], nrm[:], ALU.mult, [b_kkr, b_nrm], [b_kk])
                    act(t1[:], asg[:], AF.Identity, [b_asg, b_rwp], [b_t1], scale=rp(3, f), bias=rp(7, f))
                    yield
                    tt("pool", kmod[:], zk[:], t1[:], ALU.mult, [b_zk, b_t1], [b_kmod])
                    tt("pool", bvec[:], kk[:], asg[:], ALU.mult, [b_kk, b_asg], [b_bvec])
                    S.op("act", lambda e: e.mul(lw[:], sw[:], -math.exp(-0.5)), [b_sw], [b_lw])
                    S.op("dve", lambda e: e.tensor_tensor_scan(out=cc[:], data0=rst01[:], data1=lw[:], initial=0.0,
                                                                op0=ALU.mult, op1=ALU.add), [b_rst, b_lw], [b_cc])
                    yield
                    tt("pool", cm[:], cc[:], lw[:], ALU.subtract, [b_cc, b_lw], [b_cm])
                    act(sw[:], cc[:], AF.Exp, [b_cc], [b_sw])
                    act(ex[:], cc[:], AF.Exp, [b_cc], [b_ex], scale=-1.0)
                    act(cm[:], cm[:], AF.Exp, [b_cm], [b_cm])
                    yield
                    if own:
                        tt("dve", AR[:, f, :, 64:128], zr[:].rearrange("p (c t) -> p c t", t=64),
                           sw[:].rearrange("p (c t) -> p c t", t=64), ALU.mult, [b_zr, b_sw], [b_AR])
                    stt(AR[:, f, :, 0:64], kk[:].rearrange("p (c t) -> p c t", t=64), -1.0,
                        cm[:].rearrange("p (c t) -> p c t", t=64), ALU.mult, ALU.mult, [b_kk, b_cm], [b_AR])
                    tt("dve", KH[:, f, :], kmod[:], ex[:], ALU.mult, [b_kmod, b_ex], [b_KH])
                    tt("pool", BH[:, f, :], bvec[:], ex[:], ALU.mult, [b_bvec, b_ex], [b_BH])
                    cp("pool", GL[:, f, :], sw[:].rearrange("p (c t) -> p c t", t=64)[:, :, 63], [b_sw], [b_GL])
                    yield
                    if own:
                        stt(rk[:], zr[:], rp(4, f), kmod[:], ALU.mult, ALU.mult, [b_zr, b_rwp, b_kmod], [b_rk])
                        mm(PA[1][:, 0:W], bones_b[:], rk[:], [b_bones_b, b_rk], [bPA[1]])
                        tt("dve", bv[:, f, :], PA[1][:, 0:W], zv[:], ALU.mult, [bPA[1], b_zv], [b_bv])
                        yield
                    cp("act", VBs[par][0][:, f, :], zv[:], [b_zv], [VBs[par][1]])
                    yield
                for t in range(4):
                    pj, bpj = project(hT_t, hT_b, 14 + t)
                    act(ufm[:, t, :], pj[:, 0:W], AF.Identity, [bpj], [b_ufm])
                    yield

            def main(idx, nxt):
                own = idx >= NWIN
                win = idx % NWIN
                par = idx % 2
                M['AR'] = ARs[par]
                M['KH'] = KHs[par]
                M['BH'] = BHs[par]
                M['GL'] = GLs[par]
                M['gfm'] = gfms[par]
                M['bv'] = bvs[par]
                M['ufm'] = ufms[par]
                KH, b_KH = KHs[par]
                BH, b_BH = BHs[par]
                VB, b_VB = VBs[par]
                for (src, b_src, dst, b_dst) in ((KH, b_KH, kT, b_kT), (BH, b_BH, bT, b_bT), (VB, b_VB, vT, b_vT)):
                    n = 0
                    for c in range(4):
                        for f in range(4):
                            for hp in range(2):
                                rs = slice(hp * 64, hp * 64 + 64)
                                n += 1
                                o0 = (c * 4 + f) * 64
                                tr(PT[rs, o0:o0 + 64], src[rs, f, c * 64:(c + 1) * 64], ident_b[rs, rs],
                                   [b_src, b_ident_b], [bPT], inc=(n == 32))
                    cp("act", dst[:].rearrange("p a b c -> p (a b c)"), PT[:, 0:1024], [bPT], [b_dst])
                if M.get('inv0_done') != idx:
                    drive([chunk_inv(0, own, 0, par)])
                extra = [nxt] if nxt is not None else []
                for c in range(4):
                    grp = [chunk_chain(c, own, c % 2), s5_chunk(c, own)]
                    wts = [2, 1]
                    if c < 3:
                        grp = [chunk_inv(c + 1, own, (c + 1) % 2, par)] + grp
                        wts = [2, 2, 1]
                    elif nxt is not None:
                        drive([nxt])
                        grp = [chunk_inv(0, (idx + 1) >= NWIN, 0, (idx + 1) % 2)] + grp
                        wts = [2, 2, 1]
                        M['inv0_done'] = idx + 1
                    drive(grp, wts, background=extra)
                    extra = []
                    if nxt is not None:
                        extra = [nxt]
                if nxt is not None:
                    drive([nxt])
                if own:
                    outputs(win, None, None)

            HEADS = [(f, hp) for f in range(4) for hp in range(2)]

            def chunk_inv(c, own, slot, par):
                AR, b_AR = ARs[par]
                KH, b_KH = KHs[par]
                BH, b_BH = BHs[par]
                cs = slice(c * 64, (c + 1) * 64)
                na = 128 if own else 64
                heads = HEADS
                Abm, b_Abm = AbmS[slot]
                Akm, b_Akm = AkmS[slot]
                for i, (f, hp) in enumerate(heads):
                    rs = slice(hp * 64, hp * 64 + 64)
                    mm(PA[0][rs, f * 128:f * 128 + na], BH[rs, f, cs], AR[rs, f, c, 0:na], [b_BH, b_AR], [bPA[0]], inc=(i == 7))
                for i, (f, hp) in enumerate(heads):
                    rs = slice(hp * 64, hp * 64 + 64)
                    mm(PA[1][rs, f * 128:f * 128 + na], KH[rs, f, cs], AR[rs, f, c, 0:na], [b_KH, b_AR], [bPA[1]], inc=(i == 7))
                for i, (f, hp) in enumerate(heads):
                    rs = slice(hp * 64, hp * 64 + 64)
                    mm(PI[rs, f * 64:(f + 1) * 64], AR[rs, f, c, 0:64], BH[rs, f, cs], [b_AR, b_BH], [bPI[0]], inc=(i == 7))
                yield
                pa0 = PA[0][:, :].rearrange("p (f n) -> p f n", n=128)
                pa1 = PA[1][:, :].rearrange("p (f n) -> p f n", n=128)
                mk = maskar[:, 0:na]
                tt("dve", Abm[:, :, 0:na], pa0[:, :, 0:na], bc1(mk, 4), ALU.mult, [bPA[0], b_maskar], [b_Abm])
                xt_t, xt_b = NTb, b_NTb
                tt("dve", xt_t[:], PI[:, 0:256].rearrange("p (f n) -> p f n", n=64), bc1(masknt[:], 4), ALU.mult,
                   [bPI[0], b_masknt], [xt_b])
                tt("dve", Akm[:, :, 0:na], pa1[:, :, 0:na], bc1(mk, 4), ALU.mult, [bPA[1], b_maskar], [b_Akm])
                p_t, p_b = Pm[0]
                tt("pool", p_t[:], Abm[:, :, 0:64], bc1(identp[:], 4), ALU.add, [b_Abm, b_identp], [p_b])
                yield
                x_ap = lambda f, rs: Abm[rs, f, 0:64]
                x_b = b_Abm
                for lvl in range(1, 6):
                    xtn_t, xtn_b = XTb[lvl % 2]
                    for i, (f, hp) in enumerate(heads):
                        rs = slice(hp * 64, hp * 64 + 64)
                        mm(PI[rs, f * 64:(f + 1) * 64], x_ap(f, rs), xt_t[rs, f, :], [x_b, xt_b], [bPI[0]], inc=(i == 7))
                    if lvl <= 4:
                        xn_t, xn_b = Xb[lvl % 2]
                        for i, (f, hp) in enumerate(heads):
                            rs = slice(hp * 64, hp * 64 + 64)
                            mm(PA[0][rs, f * 64:(f + 1) * 64], xt_t[rs, f, :], x_ap(f, rs), [x_b, xt_b], [bPA[0]], inc=(i == 7))
                    yield
                    cp("act", xtn_t[:], PI[:, 0:256].rearrange("p (f n) -> p f n", n=64), [bPI[0]], [xtn_b])
                    if lvl <= 4:
                        cp("act", xn_t[:], PA[0][:, 0:256].rearrange("p (f n) -> p f n", n=64), [bPA[0]], [xn_b])
                    yield
                    pn_t, pn_b = Pm[lvl % 2]
                    for i, (f, hp) in enumerate(heads):
                        rs = slice(hp * 64, hp * 64 + 64)
                        mm(PC[rs, f * 64:(f + 1) * 64], xtn_t[rs, f, :], p_t[rs, f, :], [xtn_b, p_b], [bPC[0]], inc=(i == 7))
                    yield
                    tt("dve", pn_t[:], PC[:, 0:256].rearrange("p (f n) -> p f n", n=64), p_t[:], ALU.add, [bPC[0], p_b], [pn_b])
                    yield
                    p_t, p_b = pn_t, pn_b
                    xt_t, xt_b = xtn_t, xtn_b
                    if lvl <= 4:
                        x_ap = (lambda t_: (lambda f, rs: t_[rs, f, :]))(xn_t)
                        x_b = xn_b
                T0_t, T0_b = p_t, p_b
                Rb, b_Rb = Xb[0]
                TTb, b_TTb = Xb[1]
                for i, (f, hp) in enumerate(heads):
                    rs = slice(hp * 64, hp * 64 + 64)
                    mm(PI[rs, f * 64:(f + 1) * 64], NTb[rs, f, :], T0_t[rs, f, :], [b_NTb, T0_b], [bPI[0]], inc=(i == 7))
                for i, (f, hp) in enumerate(heads):
                    rs = slice(hp * 64, hp * 64 + 64)
                    tr(PT[rs, f * 64:(f + 1) * 64], T0_t[rs, f, :], ident_b[rs, rs], [T0_b, b_ident_b], [bPT], inc=(i == 7))
                tt("pool", ImT[:], bc1(identp[:], 4), T0_t[:], ALU.subtract, [b_identp, T0_b], [b_ImT])
                yield
                tt("dve", Rb[:], PI[:, 0:256].rearrange("p (f n) -> p f n", n=64), ImT[:], ALU.add, [bPI[0], b_ImT], [b_Rb])
                cp("act", TTb[:], PT[:, 0:256].rearrange("p (f n) -> p f n", n=64), [bPT], [b_TTb])
                yield
                for i, (f, hp) in enumerate(heads):
                    rs = slice(hp * 64, hp * 64 + 64)
                    mm(PC[rs, f * 64:(f + 1) * 64], TTb[rs, f, :], Rb[rs, f, :], [b_TTb, b_Rb], [bPC[0]], inc=(i == 7))
                yield
                T_t, T_b = TfinS[slot]
                tt("dve", T_t[:], PC[:, 0:256].rearrange("p (f n) -> p f n", n=64), T0_t[:], ALU.add, [bPC[0], T0_b], [T_b])
                lastT[0] = (T_t, T_b)
                yield

            def chunk_chain(c, own, slot):
                AR, b_AR = M['AR']
                KH, b_KH = M['KH']
                BH, b_BH = M['BH']
                GL, b_GL = M['GL']
                gfm, b_gfm = M['gfm']
                bv, b_bv = M['bv']
                ufm, b_ufm = M['ufm']
                cs = slice(c * 64, (c + 1) * 64)
                heads = HEADS
                Abm, b_Abm = AbmS[slot]
                Akm, b_Akm = AkmS[slot]
                T_t, T_b = TfinS[slot]
                s0_t, s0_b = Sb[cur_s[0] % 2]
                s1_t, s1_b = Sb[(cur_s[0] + 1) % 2]
                cur_s[0] += 1
                pm3 = PM[:, 0:256].rearrange("p (f n) -> p f n", n=64)
                for i, (f, hp) in enumerate(heads):
                    rs = slice(hp * 64, hp * 64 + 64)
                    mm(PM[rs, f * 64:(f + 1) * 64], AR[rs, f, c, 0:64], s0_t[rs, f, :], [b_AR, s0_b], [bPM[0]], start=True, stop=False, inc=False)
                    mm(PM[rs, f * 64:(f + 1) * 64], Akm[rs, f, 0:64], vT[rs, c, f, :], [b_Akm, b_vT], [bPM[0]], start=False, stop=True, inc=(i == 7))
                yield
                cp("act", WT[:], pm3, [bPM[0]], [b_WT])
                yield
                for i, (f, hp) in enumerate(heads):
                    rs = slice(hp * 64, hp * 64 + 64)
                    mm(PM[rs, f * 64:(f + 1) * 64], T_t[rs, f, :], WT[rs, f, :], [T_b, b_WT], [bPM[0]], inc=(i == 7))
                yield
                cp("act", UT[:], pm3, [bPM[0]], [b_UT])
                yield
                for i, (f, hp) in enumerate(heads):
                    rs = slice(hp * 64, hp * 64 + 64)
                    mm(PM[rs, f * 64:(f + 1) * 64], bT[rs, c, f, :], UT[rs, f, :], [b_bT, b_UT], [bPM[0]], start=True, stop=False, inc=False)
                    mm(PM[rs, f * 64:(f + 1) * 64], kT[rs, c, f, :], vT[rs, c, f, :], [b_kT, b_vT], [bPM[0]], start=False, stop=True, inc=(i == 7))
                yield
                tt("dve", stmp[:], pm3, Sf[:], ALU.add, [bPM[0], b_Sf], [b_stmp])
                yield
                if own:
                    for i, (f, hp) in enumerate(heads):
                        rs = slice(hp * 64, hp * 64 + 64)
                        o = PM[rs, f * 64:(f + 1) * 64]
                        mm(o, s0_t[rs, f, :], AR[rs, f, c, 64:128], [s0_b, b_AR], [bPM[0]], start=True, stop=False, inc=False)
                        mm(o, UT[rs, f, :], Abm[rs, f, 64:128], [b_UT, b_Abm], [bPM[0]], start=False, stop=False, inc=False)
                        mm(o, vT[rs, c, f, :], Akm[rs, f, 64:128], [b_vT, b_Akm], [bPM[0]], start=False, stop=True, inc=(i == 7))
                    yield
                    cp("act", Yw[:, :, cs], pm3, [bPM[0]], [b_Yw])
                tt("dve", Sf[:], stmp[:], bc2(GL[:, :, c], 64), ALU.mult, [b_stmp, b_GL], [b_Sf])
                cp("act", s1_t[:], Sf[:], [b_Sf], [s1_b])
                yield

            def s5_chunk(c, own):
                AR, b_AR = M['AR']
                KH, b_KH = M['KH']
                BH, b_BH = M['BH']
                GL, b_GL = M['GL']
                gfm, b_gfm = M['gfm']
                bv, b_bv = M['bv']
                ufm, b_ufm = M['ufm']
                cs = slice(c * 64, (c + 1) * 64)
                for hf in range(2):
                    ps_ = slice(hf * 8, hf * 8 + 8)
                    for pl in range(8):
                        pr = hf * 8 + pl
                        t = pr // 4
                        mm(PJ[0][:, pl * 64:(pl + 1) * 64], Bb_re[:, pr, :], ufm[:, t, cs], [b_Bbre, b_ufm], [bPJ[0]], inc=(pl == 7))
                    for pl in range(8):
                        pr = hf * 8 + pl
                        t = pr // 4
                        mm(PJ[1][:, pl * 64:(pl + 1) * 64], Bb_im[:, pr, :], ufm[:, t, cs], [b_Bbim, b_ufm], [bPJ[1]], inc=(pl == 7))
                    yield
                    cosv = tcos[:, ps_, :].rearrange("p a b -> p (a b)")
                    sinv = tsin[:, ps_, :].rearrange("p a b -> p (a b)")
                    (a0, ba0), (a1, ba1), (a2, ba2), (a3, ba3) = s5a
                    tt("dve", a0[:], PJ[0][:, :], cosv, ALU.mult, [bPJ[0], b_tcos], [ba0])
                    tt("dve", a1[:], PJ[1][:, :], sinv, ALU.mult, [bPJ[1], b_tsin], [ba1])
                    tt("dve", a2[:], PJ[1][:, :], cosv, ALU.mult, [bPJ[1], b_tcos], [ba2])
                    tt("dve", a3[:], PJ[0][:, :], sinv, ALU.mult, [bPJ[0], b_tsin], [ba3])
                    btr, b_btr, bti, b_bti = a0, ba0, a2, ba2
                    yield
                    tt("pool", btr[:], a0[:], a1[:], ALU.add, [ba0, ba1], [b_btr])
                    tt("pool", bti[:], a2[:], a3[:], ALU.subtract, [ba2, ba3], [b_bti])
                    b3r = btr[:].rearrange("p (a b) -> p a b", b=64)
                    b3i = bti[:].rearrange("p (a b) -> p a b", b=64)
                    tt("pool", b3r[:, :, 0], b3r[:, :, 0], s5s[:, 0, ps_], ALU.add, [b_btr, b_s5s], [b_btr])
                    tt("pool", b3i[:, :, 0], b3i[:, :, 0], s5s[:, 1, ps_], ALU.add, [b_bti, b_s5s], [b_bti])
                    yield
                    rv = rho0[:, ps_, :].rearrange("p a b -> p (a b)")
                    S.op("dve", lambda e: e.tensor_tensor_scan(out=s_tr[:], data0=rv, data1=btr[:], initial=0.0,
                                                                op0=ALU.mult, op1=ALU.add), [b_rho0, b_btr], [b_str])
                    S.op("dve", lambda e: e.tensor_tensor_scan(out=s_ti[:], data0=rv, data1=bti[:], initial=0.0,
                                                                op0=ALU.mult, op1=ALU.add), [b_rho0, b_bti], [b_sti])
                    yield
                    s3r = s_tr[:].rearrange("p (a b) -> p a b", b=64)
                    s3i = s_ti[:].rearrange("p (a b) -> p a b", b=64)
                    qa = s5q[:, 0:2, :]
                    qb = s5q[:, 2:4, :]
                    tt("pool", qa, s3r[:, :, 63].unsqueeze(1).to_broadcast([128, 2, 8]), T63[:, 0, :, ps_], ALU.mult, [b_str, b_T63], [b_s5q])
                    tt("pool", qb, s3i[:, :, 63].unsqueeze(1).to_broadcast([128, 2, 8]), T63[:, 1, :, ps_], ALU.mult, [b_sti, b_T63], [b_s5q])
                    tt("pool", s5s[:, :, ps_], qa, qb, ALU.add, [b_s5q], [b_s5s])
                    yield
                    if own:
                        tt("dve", a0[:], s_tr[:], cosv, ALU.mult, [b_str, b_tcos], [ba0])
                        tt("pool", a1[:], s_ti[:], sinv, ALU.mult, [b_sti, b_tsin], [ba1])
                        tt("dve", a2[:], s_tr[:], sinv, ALU.mult, [b_str, b_tsin], [ba2])
                        tt("pool", a3[:], s_ti[:], cosv, ALU.mult, [b_sti, b_tcos], [ba3])
                        tt("dve", srb[:], a0[:], a1[:], ALU.subtract, [ba0, ba1], [b_srb])
                        tt("pool", sib[:], a2[:], a3[:], ALU.add, [ba2, ba3], [b_sib])
                        yield
                        for pl in range(8):
                            pr = hf * 8 + pl
                            t = pr // 4
                            tl = t % 2
                            o = PJ[0][:, tl * 64:(tl + 1) * 64]
                            mm(o, Cp_re[:, pr, :], srb[:, pl * 64:(pl + 1) * 64], [b_Cpre, b_srb], [bPJ[0]],
                               start=(pr % 4 == 0), stop=False, inc=False)
                            mm(o, Cp_imn[:, pr, :], sib[:, pl * 64:(pl + 1) * 64], [b_Cpim, b_sib], [bPJ[0]],
                               start=False, stop=(pr % 4 == 3), inc=(pl == 7))
                        yield
                        for tl in range(2):
                            t = hf * 2 + tl
                            stt(Y5w[:, t, cs], ufm[:, t, cs], s5v[:, 0, t:t + 1], PJ[0][:, tl * 64:(tl + 1) * 64], ALU.mult, ALU.add,
                                [b_ufm, b_s5v, bPJ[0]], [b_Y5w])
                    yield

            def outputs(win, hT_t, hT_b):
                AR, b_AR = M['AR']
                KH, b_KH = M['KH']
                BH, b_BH = M['BH']
                GL, b_GL = M['GL']
                gfm, b_gfm = M['gfm']
                bv, b_bv = M['bv']
                ufm, b_ufm = M['ufm']
                xres = [load_x(TOK + win * W + st * 128) + (xslot[0] - 1,) for st in range(W // 128)]
                for f in range(4):
                    (ysq, b_ysq), (mu2, b_mu2), (var, b_var), (dd, b_dd) = et[(f % 2) * 4:(f % 2) * 4 + 4]
                    act(ysq[:], Yw[:, f, :], AF.Square, [b_Yw], [b_ysq])
                    mm(PM[:, 0:W], bones_f[:], Yw[:, f, :], [b_bones_f, b_Yw], [bPM[0]])
                    mm(PA[0][:, 0:W], bones_f[:], ysq[:], [b_bones_f, b_ysq], [bPA[0]])
                    act(mu2[:], PM[:, 0:W], AF.Square, [bPM[0]], [b_mu2])
                    tt("dve", var[:], PA[0][:, 0:W], mu2[:], ALU.subtract, [bPA[0], b_mu2], [b_var])
                    act(var[:], var[:], AF.Sqrt, [b_var], [b_var], bias=64e-5, scale=1.0)
                    S.op("dve", lambda e: e.reciprocal(out=var[:], in_=var[:]), [b_var], [b_var])
                    tt("dve", dd[:], Yw[:, f, :], PM[:, 0:W], ALU.subtract, [b_Yw, bPM[0]], [b_dd])
                    tt("pool", dd[:], dd[:], var[:], ALU.mult, [b_dd, b_var], [b_dd])
                    ts("pool", dd[:], dd[:], rp(5, f), rp(6, f), ALU.mult, ALU.add, [b_dd, b_rwp], [b_dd])
                    tt("pool", dd[:], dd[:], bv[:, f, :], ALU.add, [b_dd, b_bv], [b_dd])
                    tt("dve", ycat[:, f, :], dd[:], gfm[:, f, :], ALU.mult, [b_dd, b_gfm], [b_ycat])
                zzbv = lambda t: (srb if t < 2 else sib)[:, (t % 2) * W:(t % 2 + 1) * W]
                zzbb = lambda t: (b_srb if t < 2 else b_sib)
                zzv = lambda t: (s_tr if t < 2 else s_ti)[:, (t % 2) * W:(t % 2 + 1) * W]
                zzb_ = lambda t: (b_str if t < 2 else b_sti)
                oo, b_oo = sbt_oo
                (x2, b_x2), (pq, b_pq), (sg, b_sg) = et[8:11]
                (osq, b_osq), _ = etb
                for t in range(4):
                    act(x2[:], Y5w[:, t, :], AF.Square, [b_Y5w], [b_x2])
                    ts("pool", pq[:], x2[:], 0.044715, 1.0, ALU.mult, ALU.add, [b_x2], [b_pq])
                    tt("pool", pq[:], pq[:], Y5w[:, t, :], ALU.mult, [b_pq, b_Y5w], [b_pq])
                    act(sg[:], pq[:], AF.Sigmoid, [b_pq], [b_sg], scale=2.0 * math.sqrt(2.0 / math.pi))
                    tt("dve", zzv(t), Y5w[:, t, :], sg[:], ALU.mult, [b_Y5w, b_sg], [zzb_(t)])
                    cp("act", zzbv(t), zzv(t), [zzb_(t)], [zzbb(t)])
                for t2 in range(4):
                    for t in range(4):
                        mm(PM[:, 0:W], wglu[:, t, t2 * 128:(t2 + 1) * 128], zzbv(t), [b_wglu, zzbb(t)], [bPM[0]],
                           start=(t == 0), stop=(t == 3), inc=(t == 3))
                    act(sg[:], PM[:, 0:W], AF.Sigmoid, [bPM[0], b_s5v], [b_sg], bias=s5v[:, 1, t2:t2 + 1])
                    tt("dve", oo[:, t2, :], zzv(t2), sg[:], ALU.mult, [zzb_(t2), b_sg], [b_oo])
                for t in range(4):
                    act(osq[:], oo[:, t, :], AF.Square, [b_oo], [b_osq])
                    mm(PA[0][:, 0:W], ones_b[:], osq[:], [b_ones_b, b_osq], [bPA[0]], start=(t == 0), stop=(t == 3))
                act(sg[:], PA[0][:, 0:W], AF.Sqrt, [bPA[0]], [b_sg], bias=1e-6, scale=1.0)
                S.op("dve", lambda e: e.reciprocal(out=sg[:], in_=sg[:]), [b_sg], [b_sg])
                for t in range(4):
                    stt(ycat[:, 4 + t, :], oo[:, t, :], s5v[:, 2, t:t + 1], sg[:], ALU.mult, ALU.mult, [b_oo, b_s5v, b_sg], [b_ycat])
                for st in range(W // 128):
                    xt, bx, xsl_i = xres[st]
                    for hf in range(2):
                        pj, bpj = PJ[hf], bPJ[hf]
                        for k in range(8):
                            mm(pj[:, :], ycat[:, k, st * 128:(st + 1) * 128], Wo[:, k, hf * 512:(hf + 1) * 512], [b_ycat, b_Wo], [bpj],
                               start=(k == 0), stop=(k == 7), inc=(k == 7))
                        tt("dve", xt[:, hf * 512:(hf + 1) * 512], pj[:, :], xt[:, hf * 512:(hf + 1) * 512], ALU.add, [bpj, bx], [bx])
                    orow = win * W + st * 128
                    S.dma(out_d[orow:orow + 128, :], xt[:], "xst%d" % (xsl_i % 2), R=[bx], W=[b_outrows[orow // 128]])

            sbt_oo = (Yw, b_Yw)
            b_outrows = [Buf("orow%d" % i) for i in range(TOK // 128)]

            def dump_win():
                dump('hT', hT[0][0][:, 2, :], hT[0][1])
                dump('Pprev', Pprev[:], b_Pprev)
                dump('AR', AR[:, 1, :, :], b_AR)
                dump('KH', KH[:, 1, :], b_KH)
                dump('BH', BH[:, 1, :], b_BH)
                dump('kT', kT[:, 3, 1, :], b_kT)
                dump('vT', vT[:, 3, 1, :], b_vT)
                dump('GL', GL[:], b_GL)
                dump('Abm', AbmS[1][0][:], AbmS[1][1])
                dump('Akm', AkmS[1][0][:], AkmS[1][1])
                dump('T', lastT[0][0][:], lastT[0][1])
                dump('WT', WT[:], b_WT)
                dump('UT', UT[:], b_UT)
                dump('Sf', Sf[:], b_Sf)
                dump('s5s', s5s[:], b_s5s)
                dump('ufm', ufm[:, 2, :], b_ufm)
                dump('Yw', Yw[:], b_Yw)
                dump('Y5w', Y5w[:], b_Y5w)
                dump('ycat', ycat[:], b_ycat)

            gens = {0: pre(0)}
            drive([gens[0]])
            for idx in range(2 * NWIN):
                nxt = None
                if idx + 1 < 2 * NWIN:
                    if idx + 1 == NWIN:
                        ts("dve", Pprev[:], Pprev[:], maskv[:, 0:1], None, ALU.mult, None, [b_Pprev, b_maskv], [b_Pprev])
                    nxt = pre(idx + 1)
                main(idx, nxt)
                if idx == NWIN - 1:
                    ts("dve", Sf[:].rearrange("p a b -> p (a b)"), Sf[:].rearrange("p a b -> p (a b)"), maskv[:, 0:1], None, ALU.mult, None,
                       [b_Sf, b_maskv], [b_Sf])
                    sbc_t, sbc_b = Sb[cur_s[0] % 2]
                    cp("act", sbc_t[:], Sf[:], [b_Sf], [sbc_b])
                    ts("dve", s5s[:].rearrange("p a b -> p (a b)"), s5s[:].rearrange("p a b -> p (a b)"), maskv[:, 0:1], None, ALU.mult, None,
                       [b_s5s, b_maskv], [b_s5s])
                if stage(('own:%d' % (idx - NWIN)) if idx >= NWIN else ('pre:%d' % idx), dump_win):
                    return nc
            S.barrier()
        esB = ExitStack()
        with esB:
            HS = DFF // 2
            wgb, b_wgb = sbt(esB, [128, 8, DFF], BF16, "wgb")
            wub, b_wub = sbt(esB, [128, 8, DFF], BF16, "wub")
            wdb, b_wdb = sbt(esB, [128, NFF, D], BF16, "wdb")
            stB = [sbt(esB, [128, HS], F32, "stB%d" % i) for i in range(2)]
            fg_bc, b_fg = sbt(esB, [128, D], F32, "fgbc")
            S.dma(fg_bc[:], fgain_d.partition_broadcast(128), "fgbc", W=[b_fg])
            TB = 512
            NSB = TB // 128
            xsB = [sbt(esB, [128, D], F32, "xsB%d" % i) for i in range(NSB + 1)]
            xnB, b_xnB = sbt(esB, [128, D], BF16, "xnB")
            statB, b_statB = sbt(esB, [128, 4], F32, "statB")
            h2T = [sbt(esB, [128, 8, TB], BF16, "h2T%d" % i) for i in range(1)]
            actT, b_actT = sbt(esB, [128, NFF, TB], BF16, "actT")
            sil = [sbt(esB, [128, TB], BF16, "sil%d" % i) for i in range(2)]
            b_wgh = [Buf("wg_h0"), Buf("wg_h1")]
            b_wuh = [Buf("wu_h0"), Buf("wu_h1")]
            b_wdj = [Buf("wd_%d" % j) for j in range(NFF)]
            wprog = [0]

            def wload():
                ci = 0
                for hh in range(2):
                    for (src_d, dst, bl) in ((wg_d, wgb, b_wgh), (wu_d, wub, b_wuh)):
                        for k in range(8):
                            st_t, st_b = stB[ci % 2]
                            S.dma(st_t[:], src_d[k * 128:(k + 1) * 128, hh * HS:(hh + 1) * HS], "stB%d" % (ci % 2), W=[st_b])
                            cp(("act", "dve", "pool")[ci % 3], dst[:, k, hh * HS:(hh + 1) * HS], st_t[:], [st_b], [bl[hh]])
                            ci += 1
                            wprog[0] += 1
                            yield
                for j in range(NFF):
                    st_t, st_b = stB[ci % 2]
                    S.dma(st_t[:, 0:D], wd_d[j * 128:(j + 1) * 128, :], "stB%d" % (ci % 2), W=[st_b])
                    tt(("dve", "pool")[ci % 2], wdb[:, j, :], st_t[:, 0:D], gf_bc[:], ALU.mult, [st_b, b_gf], [b_wdj[j]])
                    ci += 1
                    wprog[0] += 1
                    yield

            wgen = wload()

            def need(n):
                while wprog[0] < n:
                    next(wgen)

            def bgstep():
                try:
                    next(wgen)
                except StopIteration:
                    pass

            xq = [0]
            need(4)
            for wi in range(TOK // TB):
                hT_t, hT_b = h2T[0]
                tiles = []
                for st in range(NSB):
                    ti = wi * NSB + st
                    i = xq[0] % (NSB + 1)
                    xq[0] += 1
                    xt, bx = xsB[i]
                    orow = ti * 128
                    S.dma(xt[:], out_d[orow:orow + 128, :], "xsB%d" % i, R=[b_outrows[ti]], W=[bx])
                    tiles.append((xt, bx, i, orow, ti))
                for st in range(NSB):
                    xt, bx, i, orow, ti = tiles[st]
                    bgstep()
                    bgstep()
                    act(xnB[:], xt[:], AF.Square, [bx], [b_xnB, b_statB], accum=statB[:, 0:1])
                    act(statB[:, 1:2], statB[:, 0:1], AF.Sqrt, [b_statB], [b_statB], bias=1e-6, scale=1.0 / D)
                    S.op("dve", lambda e: e.reciprocal(out=statB[:, 1:2], in_=statB[:, 1:2]), [b_statB], [b_statB])
                    act(xnB[:], xt[:], AF.Identity, [bx, b_statB], [b_xnB], scale=statB[:, 1:2])
                    for k in range(8):
                        tr(PT[:, k * 128:(k + 1) * 128], xnB[:, k * 128:(k + 1) * 128], ident_b[:], [b_xnB, b_ident_b], [bPT], inc=(k == 7))
                    for k in range(8):
                        o = hT_t[:, k, st * 128:(st + 1) * 128]
                        i_ = PT[:, k * 128:(k + 1) * 128]
                        if st % 2 == 0:
                            act(o, i_, AF.Identity, [bPT, b_modp], [hT_b], scale=modp[:, 3, k:k + 1], bias=modp[:, 2, k:k + 1])
                        else:
                            ts("dve", o, i_, modp[:, 3, k:k + 1], modp[:, 2, k:k + 1], ALU.mult, ALU.add, [bPT, b_modp], [hT_b])
                for j in range(NFF):
                    wh = 0 if j < 11 else 1
                    need(16 if wh == 0 else 32)
                    bgstep()
                    pg, bpg = (PJ[0], bPJ[0]) if j % 2 == 0 else (PA[0], bPA[0])
                    pu, bpu = (PJ[1], bPJ[1]) if j % 2 == 0 else (PA[1], bPA[1])
                    for k in range(8):
                        mm(pg[:, 0:TB], wgb[:, k, j * 128:(j + 1) * 128], hT_t[:, k, :], [b_wgh[wh], hT_b], [bpg],
                           start=(k == 0), stop=(k == 7), inc=(k == 7))
                    for k in range(8):
                        mm(pu[:, 0:TB], wub[:, k, j * 128:(j + 1) * 128], hT_t[:, k, :], [b_wuh[wh], hT_b], [bpu],
                           start=(k == 0), stop=(k == 7), inc=(k == 7))
                    s_t, s_b = sil[j % 2]
                    act(s_t[:], pg[:, 0:TB], AF.Silu, [bpg], [s_b])
                    tt("dve", actT[:, j, :], s_t[:], pu[:, 0:TB], ALU.mult, [s_b, bpu], [b_actT])
                for st in range(NSB):
                    xt, bx, i, orow, ti = tiles[st]
                    for hf in range(2):
                        pd, bpd = (PI, bPI[0]) if hf == 0 else (PC, bPC[0])
                        need(32 + NFF)
                        for j in range(NFF):
                            mm(pd[:, :], actT[:, j, st * 128:(st + 1) * 128], wdb[:, j, hf * 512:(hf + 1) * 512], [b_actT, b_wdj[j]], [bpd],
                               start=(j == 0), stop=(j == NFF - 1), inc=(j == NFF - 1))
                        tt("dve", xt[:, hf * 512:(hf + 1) * 512], pd[:, :], xt[:, hf * 512:(hf + 1) * 512], ALU.add, [bpd, bx], [bx])
                    act(xnB[:], xt[:], AF.Square, [bx], [b_xnB, b_statB], accum=statB[:, 2:3])
                    act(statB[:, 3:4], statB[:, 2:3], AF.Sqrt, [b_statB], [b_statB], bias=1e-6, scale=1.0 / D)
                    S.op("dve", lambda e: e.reciprocal(out=statB[:, 3:4], in_=statB[:, 3:4]), [b_statB], [b_statB])
                    stt(xt[:], xt[:], statB[:, 3:4], fg_bc[:], ALU.mult, ALU.mult, [bx, b_statB, b_fg], [bx])
                    S.dma(out_d[orow:orow + 128, :], xt[:], "xstB%d" % i, R=[bx], W=[b_outrows[ti]])
            S.barrier()
    return nc


_NC = None


def _layout_inputs(inp):
    f32 = np.float32
    g = lambda k: np.asarray(inp[k], dtype=f32)
    x = g("x")
    c = g("c")
    shared = {}
    shared["w_ada"] = np.ascontiguousarray(g("w_ada")[0])
    shared["b_ada"] = np.ascontiguousarray(g("b_ada")[0][None, :])
    shared["w_in"] = np.ascontiguousarray(g("w_in")[0])
    shared["mu_l"] = np.ascontiguousarray(g("mu_shift")[0].reshape(14, 128).T)
    v512 = lambda a: a.reshape(4, 128).T
    rw = [g("rw_w0")[0], g("rw_a0")[0], g("rw_k_k")[0], g("rw_k_a")[0], g("rw_r_k")[0].reshape(512),
          g("rw_lnx_w")[0], g("rw_lnx_b")[0]]
    shared["rwp"] = np.ascontiguousarray(np.stack([v512(a) for a in rw], axis=1).reshape(128, 28))
    lo2w = np.zeros((128, 512), f32)
    lo2w[0:64] = g("rw_w2")[0]
    lo2a = np.zeros((128, 512), f32)
    lo2a[64:128] = g("rw_a2")[0]
    shared["lo2w"] = lo2w
    shared["lo2a"] = lo2a
    shared["g2"] = np.ascontiguousarray(g("rw_g2")[0])
    a_re = g("s5_a_re")[0]
    a_im = g("s5_a_im")[0]
    ldt = g("s5_log_dt")[0]
    b_re = g("s5_b_re")[0]
    b_im = g("s5_b_im")[0]
    c_re = g("s5_c_re")[0]
    c_im = g("s5_c_im")[0]
    la = np.zeros((128, 3, 16), f32)
    for pr in range(16):
        for gp in range(2):
            gg = 2 * pr + gp
            la[gp * 64:(gp + 1) * 64, 0, pr] = a_re[gg]
            la[gp * 64:(gp + 1) * 64, 1, pr] = a_im[gg]
            la[gp * 64:(gp + 1) * 64, 2, pr] = ldt[gg]
    shared["s5la"] = la.reshape(128, 48)
    bpre = np.zeros((128, 16, 128), f32)
    bpim = np.zeros((128, 16, 128), f32)
    lbare = np.zeros((128, 16, 128), f32)
    lbaim = np.zeros((128, 16, 128), f32)
    lbldt = np.zeros((128, 16, 128), f32)
    cpre = np.zeros((128, 16, 128), f32)
    cpim = np.zeros((128, 16, 128), f32)
    for pr in range(16):
        tile = pr // 4
        for g8 in range(8):
            gg = tile * 8 + g8
            rows = slice(g8 * 16, g8 * 16 + 16)
            for gp in range(2):
                cols = slice(gp * 64, gp * 64 + 64)
                lbare[rows, pr, cols] = a_re[gg][None, :]
                lbaim[rows, pr, cols] = a_im[gg][None, :]
                lbldt[rows, pr, cols] = ldt[gg]
            if g8 // 2 == pr % 4:
                gp = g8 % 2
                cols = slice(gp * 64, gp * 64 + 64)
                bpre[rows, pr, cols] = b_re[gg].T
                bpim[rows, pr, cols] = b_im[gg].T
        for gp in range(2):
            gg = 2 * pr + gp
            g8 = gg % 8
            cpre[gp * 64:(gp + 1) * 64, pr, g8 * 16:(g8 + 1) * 16] = c_re[gg].T
            cpim[gp * 64:(gp + 1) * 64, pr, g8 * 16:(g8 + 1) * 16] = c_im[gg].T
    shared["bpre"] = bpre.reshape(128, 2048)
    shared["bpim"] = bpim.reshape(128, 2048)
    shared["lbare"] = lbare.reshape(128, 2048)
    shared["lbaim"] = lbaim.reshape(128, 2048)
    shared["lbldt"] = lbldt.reshape(128, 2048)
    shared["cpre"] = cpre.reshape(128, 2048)
    shared["cpim"] = cpim.reshape(128, 2048)
    s5v = [g("s5_d")[0].reshape(512), g("s5_b_glu")[0], g("s5_gain")[0]]
    shared["s5v"] = np.ascontiguousarray(np.stack([v512(a) for a in s5v], axis=1).reshape(128, 12))
    shared["w_glu"] = np.ascontiguousarray(g("s5_w_glu")[0])
    shared["w_out"] = np.ascontiguousarray(g("w_out")[0])
    shared["wg"] = np.ascontiguousarray(g("ffn_w_gate")[0])
    shared["wu"] = np.ascontiguousarray(g("ffn_w_up")[0])
    shared["wd"] = np.ascontiguousarray(g("ffn_w_down")[0])
    shared["fgain"] = np.ascontiguousarray(g("final_gain")[None, :])
    shared["ident"] = np.eye(128, dtype=f32)
    bo = np.zeros((128, 128), f32)
    bo[0:64, 0:64] = 1.0
    bo[64:128, 64:128] = 1.0
    shared["bones"] = bo
    j = np.arange(64)
    strict = (j[:, None] < j[None, :]).astype(f32)
    incl = (j[:, None] <= j[None, :]).astype(f32)
    mar = np.concatenate([strict, incl], axis=1)
    shared["maskar"] = np.concatenate([mar, mar], axis=0)
    low = (j[None, :] < j[:, None]).astype(f32)
    shared["masknt"] = np.concatenate([low, low], axis=0)
    shared["identp"] = np.concatenate([np.eye(64, dtype=f32)] * 2, axis=0)
    shared["idx1"] = np.tile((np.arange(64, dtype=f32) + 1.0)[None, :], (128, 1))
    rst = np.ones((128, 256), f32)
    rst[:, ::64] = 0.0
    shared["rst01"] = rst
    maps = []
    for core in range(8):
        b, s = core // 2, core % 2
        m = dict(shared)
        m["xcat"] = np.ascontiguousarray(np.concatenate([x[b, 0:TOK], x[b, s * TOK:(s + 1) * TOK]], axis=0))
        m["maskv"] = np.full((128, 1), float(s), f32)
        m["c_l"] = np.ascontiguousarray(c[b].reshape(8, 128).T)
        maps.append(m)
    return maps


def kernel(**inputs):
    global _NC
    if _NC is None:
        _NC = build_program()
    maps = _layout_inputs(inputs)
    res = run_bass_kernel_spmd(_NC, maps, core_ids=list(range(8)))
    out = np.zeros((4, 2 * TOK, D), np.float32)
    for core in range(8):
        b, s = core // 2, core % 2
        out[b, s * TOK:(s + 1) * TOK] = res.results[core]["out"]
    return out
```

```python
import math
from contextlib import ExitStack

import numpy as np
import concourse.bass as bass
import concourse.mybir as mybir
from concourse.bass_utils import run_bass_kernel_spmd

F32 = mybir.dt.float32
BF16 = mybir.dt.bfloat16
I32 = mybir.dt.int32
AF = mybir.ActivationFunctionType
ALU = mybir.AluOpType
AX = mybir.AxisListType

D = 1024
TOK = 4096
W = 256
L = 64
NWIN = TOK // W
DFF = 2816
NFF = DFF // 128
TWO_PI = 2.0 * math.pi
PI_SAFE = 3.1415925


class Buf:
    __slots__ = ("name", "w", "r", "excl")

    def __init__(self, name, excl=False):
        self.name = name
        self.w = None
        self.r = {}
        self.excl = excl


class DSem:
    __slots__ = ("handle", "count", "id")


class Sched:
    def __init__(self, nc, es):
        self.nc = nc
        self.es = es
        self.eng = {"pe": nc.tensor, "act": nc.scalar, "dve": nc.vector, "pool": nc.gpsimd, "sp": nc.sync}
        self.sem = {k: es.enter_context(nc.semaphore("sem_" + k)) for k in self.eng}
        self.cnt = {k: 0 for k in self.eng}
        self.seen = {k: {} for k in self.eng}
        self.dsems = {}
        self.nds = 0
        self.const_bufs = []
        self.dead = False

    def _need(self, eng, R, W):
        need = {}

        def add(ev):
            if ev is None:
                return
            k = ev[0]
            if k not in need or need[k][2] < ev[2]:
                need[k] = ev

        for b in R:
            add(b.w)
        for b in W:
            add(b.w)
            for ev in b.r.values():
                add(ev)
        E = self.eng[eng]
        for k, ev in need.items():
            if k == ("e", eng) and eng in ("pe", "sp"):
                continue
            if self.seen[eng].get(k, 0) >= ev[2]:
                continue
            E.wait_ge(ev[1], ev[2])
            self.seen[eng][k] = ev[2]

    def op(self, eng, fn, R=(), W=(), inc=True):
        if self.dead:
            return None
        if any(b.excl for b in R):
            W = list(W) + [b for b in R if b.excl]
            R = [b for b in R if not b.excl]
        self._need(eng, R, W)
        inst = fn(self.eng[eng])
        val = self.cnt[eng] + 1
        if inc:
            inst.then_inc(self.sem[eng], 1)
            self.cnt[eng] = val
        ev = (("e", eng), self.sem[eng], val)
        for b in R:
            b.r[ev[0]] = ev
        for b in W:
            b.w = ev
            b.r = {}
        return inst

    def _dsem(self, key):
        if key not in self.dsems:
            d = DSem()
            d.handle = self.es.enter_context(self.nc.semaphore("dsem%d" % self.nds))
            d.count = 0
            d.id = self.nds
            self.nds += 1
            self.dsems[key] = d
        return self.dsems[key]

    def dma(self, out, in_, key, R=(), W=(), const=False):
        if self.dead:
            return
        self._need("sp", R, W)
        d = self._dsem(key)
        d.count += 16
        self.nc.sync.dma_start(out=out, in_=in_).then_inc(d.handle, 16)
        ev = (("d", d.id), d.handle, d.count)
        for b in R:
            b.r[ev[0]] = ev
        for b in W:
            b.w = ev
            b.r = {}
            if const:
                self.const_bufs.append(b)

    def finalize_consts(self, key):
        d = self._dsem(key)
        ev = (("d", d.id), d.handle, d.count)
        for b in self.const_bufs:
            b.w = ev
        self.const_bufs = []

    def barrier(self):
        for e, E in self.eng.items():
            for f in self.eng:
                if f == e:
                    continue
                k = ("e", f)
                if self.cnt[f] > self.seen[e].get(k, 0):
                    E.wait_ge(self.sem[f], self.cnt[f])
                    self.seen[e][k] = self.cnt[f]
            for d in self.dsems.values():
                k = ("d", d.id)
                if d.count > self.seen[e].get(k, 0):
                    E.wait_ge(d.handle, d.count)
                    self.seen[e][k] = d.count


class _StopBuild(Exception):
    pass


_DBG = {"stop": None, "dumps": [], "meta": []}


def build_program():
    nc = bass.Bass("TRN2", target_bir_lowering=False)
    dbg_on = _DBG["stop"] is not None
    if dbg_on:
        dbg_d = nc.dram_tensor("dbg", [128, 65536], F32, kind="ExternalOutput").ap()
        _DBG["meta"] = []

    def din(name, shape):
        return nc.dram_tensor(name, list(shape), F32, kind="ExternalInput").ap()

    xcat = din("xcat", [2 * TOK, D])
    maskv_d = din("maskv", [128, 1])
    c_d = din("c_l", [128, 8])
    wada_d = din("w_ada", [D, 6 * D])
    bada_d = din("b_ada", [1, 6 * D])
    win_d = din("w_in", [D, 2304])
    mu_d = din("mu_l", [128, 14])
    rwp_d = din("rwp", [128, 28])
    lo2w_d = din("lo2w", [128, 512])
    lo2a_d = din("lo2a", [128, 512])
    g2_d = din("g2", [128, 512])
    s5la_d = din("s5la", [128, 48])
    bpre_d = din("bpre", [128, 2048])
    bpim_d = din("bpim", [128, 2048])
    lbare_d = din("lbare", [128, 2048])
    lbaim_d = din("lbaim", [128, 2048])
    lbldt_d = din("lbldt", [128, 2048])
    cpre_d = din("cpre", [128, 2048])
    cpim_d = din("cpim", [128, 2048])
    s5v_d = din("s5v", [128, 12])
    wglu_d = din("w_glu", [512, 512])
    wout_d = din("w_out", [D, D])
    wg_d = din("wg", [D, DFF])
    wu_d = din("wu", [D, DFF])
    wd_d = din("wd", [DFF, D])
    fgain_d = din("fgain", [1, D])
    ident_d = din("ident", [128, 128])
    bones_d = din("bones", [128, 128])
    maskar_d = din("maskar", [128, 128])
    masknt_d = din("masknt", [128, 64])
    identp_d = din("identp", [128, 64])
    idx1_d = din("idx1", [128, 64])
    rst_d = din("rst01", [128, 256])
    out_d = nc.dram_tensor("out", [TOK, D], F32, kind="ExternalOutput").ap()

    es = ExitStack()
    with es:
        S = Sched(nc, es)
        uid = [0]

        def sbt(stack, shape, dt, nm="t"):
            uid[0] += 1
            name = "%s_%d" % (nm, uid[0])
            t = stack.enter_context(nc.sbuf_tensor(name, list(shape), dt))
            return t, Buf(name)

        def pst(nm, dt, n):
            t = es.enter_context(nc.psum_tensor(nm, [128, n], dt))
            return t

        PT = pst("PT", BF16, 1024)
        bPT = Buf("PT", True)
        PJ = [pst("PJ0", F32, 512), pst("PJ1", F32, 512)]
        bPJ = [Buf("PJ0", True), Buf("PJ1", True)]
        PM = pst("PM", F32, 512)
        _bpm = Buf("PM", True)
        bPM = [_bpm, _bpm]
        PA = [pst("PA0", F32, 512), pst("PA1", F32, 512)]
        bPA = [Buf("PA0", True), Buf("PA1", True)]
        PI = pst("PI", F32, 512)
        _bpi = Buf("PI", True)
        bPI = [_bpi, _bpi]
        PC = pst("PC", F32, 512)
        _bpc = Buf("PC", True)
        bPC = [_bpc, _bpc]

        def tt(eng, out, in0, in1, op, R, Wb):
            return S.op(eng, lambda e: e.tensor_tensor(out=out, in0=in0, in1=in1, op=op), R, Wb)

        def ts(eng, out, in0, s1, s2, op0, op1, R, Wb):
            if op1 is None and eng == "pool" and op0 == ALU.mult:
                return S.op(eng, lambda e: e.tensor_scalar(out=out, in0=in0, scalar1=s1, scalar2=0.0, op0=op0, op1=ALU.add), R, Wb)
            if op1 is None:
                return S.op(eng, lambda e: e.tensor_scalar(out=out, in0=in0, scalar1=s1, scalar2=None, op0=op0), R, Wb)
            return S.op(eng, lambda e: e.tensor_scalar(out=out, in0=in0, scalar1=s1, scalar2=s2, op0=op0, op1=op1), R, Wb)

        def stt(out, in0, scalar, in1, op0, op1, R, Wb):
            return S.op("dve", lambda e: e.scalar_tensor_tensor(out=out, in0=in0, scalar=scalar, in1=in1, op0=op0, op1=op1), R, Wb)

        def act(out, in_, func, R, Wb, bias=None, scale=None, accum=None):
            kw = {}
            if bias is not None:
                kw["bias"] = bias
            if scale is not None:
                kw["scale"] = scale
            if accum is not None:
                kw["accum_out"] = accum
            return S.op("act", lambda e: e.activation(out=out, in_=in_, func=func, **kw), R, Wb)

        def cp(eng, out, in_, R, Wb):
            if eng == "act":
                return act(out, in_, AF.Identity, R, Wb)
            return S.op(eng, lambda e: e.tensor_copy(out=out, in_=in_), R, Wb)

        def mm(out, lhsT, rhs, R, Wb, start=True, stop=True, inc=True):
            return S.op("pe", lambda e: e.matmul(out, lhsT=lhsT, rhs=rhs, start=start, stop=stop), R, Wb, inc=inc)

        def tr(out, in_, ident, R, Wb, inc=True):
            return S.op("pe", lambda e: e.transpose(out, in_, ident), R, Wb, inc=inc)

        def bc1(ap, n):
            return ap.unsqueeze(1).to_broadcast([ap.shape[0], n, ap.shape[1]])

        def bc2(ap, n):
            return ap.unsqueeze(2).to_broadcast([ap.shape[0], ap.shape[1], n])

        dbg_off = [0]
        if dbg_on:
            dstage = [sbt(es, [128, 2048], F32, "dstage%d" % i) for i in range(1)]
        dcount = [0]

        def dump(name, ap, buf):
            shape = list(ap.shape)
            n = 1
            for d_ in shape[1:]:
                n *= d_
            P_ = shape[0]
            st_t, st_b = dstage[0]
            dcount[0] += 1
            dst = st_t[0:P_, 0:n]
            if len(shape) == 3:
                dst = dst.rearrange("p (a b) -> p a b", b=shape[2])
            elif len(shape) == 4:
                dst = dst.rearrange("p (a b c) -> p a b c", b=shape[2], c=shape[3])
            cp("dve", dst, ap, [buf], [st_b])
            S.dma(dbg_d[0:P_, dbg_off[0]:dbg_off[0] + n], st_t[0:P_, 0:n], "dstage0", R=[st_b])
            _DBG["meta"].append((name, dbg_off[0], shape))
            dbg_off[0] += n

        def sub(name):
            if dbg_on and _DBG.get("sub") == name:
                S.dead = True

        def stage(name, fn=None):
            if dbg_on and _DBG["stop"] == name:
                S.dead = False
                if fn is not None:
                    fn()
                S.barrier()
                return True
            return False

        ident_f, b_ident_f = sbt(es, [128, 128], F32, "identf")
        ident_b, b_ident_b = sbt(es, [128, 128], BF16, "identb")
        maskv, b_maskv = sbt(es, [128, 1], F32, "maskv")
        modp, b_modp = sbt(es, [128, 4, 8], F32, "modp")
        gf_bc, b_gf = sbt(es, [128, D], F32, "gfbc")

        S.dma(ident_f[:], ident_d, "const", W=[b_ident_f], const=True)
        S.dma(maskv[:], maskv_d, "const", W=[b_maskv], const=True)

        esA = ExitStack()
        with esA:
            W1, b_W1 = sbt(esA, [128, 8, 2304], BF16, "W1")
            Wo, b_Wo = sbt(esA, [128, 8, D], BF16, "Wo")
            wglu, b_wglu = sbt(esA, [128, 4, 512], BF16, "wglu")
            lo2w, b_lo2w = sbt(esA, [128, 512], BF16, "lo2w")
            lo2a, b_lo2a = sbt(esA, [128, 512], BF16, "lo2a")
            g2, b_g2 = sbt(esA, [128, 512], BF16, "g2")
            bones_b, b_bones_b = sbt(esA, [128, 128], BF16, "bonesb")
            bones_f, b_bones_f = sbt(esA, [128, 128], F32, "bonesf")
            ones_b, b_ones_b = sbt(esA, [128, 128], BF16, "onesb")
            maskar, b_maskar = sbt(esA, [128, 128], F32, "maskar")
            masknt, b_masknt = sbt(esA, [128, 64], F32, "masknt")
            identp, b_identp = sbt(esA, [128, 64], F32, "identp")
            rst01, b_rst = sbt(esA, [128, 256], F32, "rst01")
            mu, b_mu = sbt(esA, [128, 14], F32, "mu")
            omu, b_omu = sbt(esA, [128, 14], F32, "omu")
            rwp, b_rwp = sbt(esA, [128, 8, 4], F32, "rwp")
            s5v, b_s5v = sbt(esA, [128, 3, 4], F32, "s5v")
            Bb_re, b_Bbre = sbt(esA, [128, 16, 128], BF16, "Bbre")
            Bb_im, b_Bbim = sbt(esA, [128, 16, 128], BF16, "Bbim")
            Cp_re, b_Cpre = sbt(esA, [128, 16, 128], BF16, "Cpre")
            Cp_imn, b_Cpim = sbt(esA, [128, 16, 128], BF16, "Cpimn")
            tcos, b_tcos = sbt(esA, [128, 16, 64], F32, "tcos")
            tsin, b_tsin = sbt(esA, [128, 16, 64], F32, "tsin")
            rho0, b_rho0 = sbt(esA, [128, 16, 64], F32, "rho0")
            rho1, b_rho1 = sbt(esA, [128, 16], F32, "rho1")
            T63, b_T63 = sbt(esA, [128, 2, 2, 16], F32, "T63")

            S.dma(maskar[:], maskar_d, "const", W=[b_maskar], const=True)
            S.dma(masknt[:], masknt_d, "const", W=[b_masknt], const=True)
            S.dma(identp[:], identp_d, "const", W=[b_identp], const=True)
            S.dma(rst01[:], rst_d, "const", W=[b_rst], const=True)
            S.dma(mu[:], mu_d, "const", W=[b_mu], const=True)
            S.dma(rwp[:, 0:7, :].rearrange("p a b -> p (a b)"), rwp_d, "const", W=[b_rwp], const=True)
            S.dma(s5v[:].rearrange("p a b -> p (a b)"), s5v_d, "const", W=[b_s5v], const=True)
            S.dma(bones_f[:], bones_d, "const", W=[b_bones_f], const=True)

            esS = ExitStack()
            with esS:
                c_l, b_c = sbt(esS, [128, 8], F32, "c")
                c_act, b_cact = sbt(esS, [128, 8], F32, "cact")
                c_rep, b_crep = sbt(esS, [128, 8, 128], F32, "crep")
                adaR, b_adaR = sbt(esS, [128, 6 * D], F32, "adaR")
                badab, b_badab = sbt(esS, [128, 6 * D], F32, "badab")
                ada_fm, b_adafm = sbt(esS, [128, 48], F32, "adafm")
                stg = [sbt(esS, [128, 8, 512], F32, "stg%d" % i) for i in range(2)]
                lst = [sbt(esS, [128, 512], F32, "lst%d" % i) for i in range(3)]
                S.dma(lst[0][0][:], lo2w_d, "const", W=[lst[0][1]], const=True)
                S.dma(lst[1][0][:], lo2a_d, "const", W=[lst[1][1]], const=True)
                S.dma(lst[2][0][:], g2_d, "const", W=[lst[2][1]], const=True)
                S.dma(c_l[:], c_d, "const", W=[b_c], const=True)
                S.dma(badab[:], bada_d.partition_broadcast(128), "const", W=[b_badab], const=True)
                S.finalize_consts("const")

                cp("dve", ident_b[:], ident_f[:], [b_ident_f], [b_ident_b])
                cp("dve", bones_b[:], bones_f[:], [b_bones_f], [b_bones_b])
                S.op("pool", lambda e: e.memset(ones_b[:], 1.0 / 512.0), [], [b_ones_b])
                ts("dve", bones_f[:], bones_f[:], 1.0 / 64.0, None, ALU.mult, None, [b_bones_f], [b_bones_f])
                ts("dve", omu[:], mu[:], -1.0, 1.0, ALU.mult, ALU.add, [b_mu], [b_omu])
                ts("dve", rwp[:, 7, :], rwp[:, 3, :], -1.0, 1.0, ALU.mult, ALU.add, [b_rwp], [b_rwp])
                cp("act", lo2w[:], lst[0][0][:], [lst[0][1]], [b_lo2w])
                cp("act", lo2a[:], lst[1][0][:], [lst[1][1]], [b_lo2a])
                cp("act", g2[:], lst[2][0][:], [lst[2][1]], [b_g2])

                act(c_act[:], c_l[:], AF.Silu, [b_c], [b_cact])
                cp("dve", c_rep[:], bc2(c_act[:], 128), [b_cact], [b_crep])
                wada_v = wada_d.rearrange("(k p) n -> p k n", p=128)
                for blk in range(12):
                    st_t, st_b = stg[blk % 2]
                    S.dma(st_t[:], wada_v[:, :, blk * 512:(blk + 1) * 512], "stg%d" % (blk % 2), W=[st_b])
                    pj, bpj = PJ[blk % 2], bPJ[blk % 2]
                    for k in range(8):
                        mm(pj[:, :], c_rep[:, k, :], st_t[:, k, :], [b_crep, st_b], [bpj],
                           start=(k == 0), stop=(k == 7), inc=(k == 7))
                    tt("dve", adaR[:, blk * 512:(blk + 1) * 512], pj[:, :], badab[:, blk * 512:(blk + 1) * 512],
                       ALU.add, [bpj, b_badab], [b_adaR])
                tt("dve", badab[:].rearrange("p (j q) -> p j q", q=128), adaR[:].rearrange("p (j q) -> p j q", q=128),
                   bc1(ident_f[:], 48), ALU.mult, [b_adaR, b_ident_f, b_badab], [b_badab])
                S.op("dve", lambda e: e.tensor_reduce(out=ada_fm[:], in_=badab[:].rearrange("p (j q) -> p j q", q=128),
                                                       axis=AX.X, op=ALU.add), [b_badab], [b_adafm])
                cp("dve", modp[:, 0, :], ada_fm[:, 0:8], [b_adafm], [b_modp])
                ts("dve", modp[:, 1, :], ada_fm[:, 8:16], 1.0, None, ALU.add, None, [b_adafm], [b_modp])
                cp("dve", modp[:, 2, :], ada_fm[:, 24:32], [b_adafm], [b_modp])
                ts("dve", modp[:, 3, :], ada_fm[:, 32:40], 1.0, None, ALU.add, None, [b_adafm], [b_modp])
                cp("dve", gf_bc[:], adaR[:, 5 * D:6 * D], [b_adaR], [b_gf])

                ci = 0
                for k in range(8):
                    st_t, st_b = stg[k % 2]
                    stv = st_t[:].rearrange("p a b -> p (a b)")
                    S.dma(stv[:, 0:2304], win_d[k * 128:(k + 1) * 128, :], "stg%d" % (k % 2), W=[st_b])
                    cp(("act", "dve", "pool")[ci % 3], W1[:, k, :], stv[:, 0:2304], [st_b], [b_W1])
                    ci += 1
                for k in range(8):
                    st_t, st_b = stg[k % 2]
                    stv = st_t[:].rearrange("p a b -> p (a b)")
                    S.dma(stv[:, 0:D], wout_d[k * 128:(k + 1) * 128, :], "stg%d" % (k % 2), W=[st_b])
                    tt("dve", Wo[:, k, :], stv[:, 0:D], adaR[:, 2 * D:3 * D], ALU.mult, [st_b, b_adaR], [b_Wo])
                for k in range(4):
                    st_t, st_b = stg[k % 2]
                    stv = st_t[:].rearrange("p a b -> p (a b)")
                    S.dma(stv[:, 0:512], wglu_d[k * 128:(k + 1) * 128, :], "stg%d" % (k % 2), W=[st_b])
                    cp("act", wglu[:, k, :], stv[:, 0:512], [st_b], [b_wglu])

                if stage('setup1', lambda: (dump('modp', modp[:], b_modp), dump('gf', gf_bc[:], b_gf), dump('W1', W1[:, 3, 0:2048], b_W1), dump('Wo', Wo[:, 2, :], b_Wo))):
                    return nc
                S.barrier()
            esS2 = ExitStack()
            with esS2:
                s5la, b_s5la = sbt(esS2, [128, 3, 16], F32, "s5la")
                idx1, b_idx1 = sbt(esS2, [128, 64], F32, "idx1")
                lb = {}
                for nm, dd in (("bpre", bpre_d), ("bpim", bpim_d), ("are", lbare_d), ("aim", lbaim_d), ("ldt", lbldt_d)):
                    lb[nm] = sbt(esS2, [128, 2048], F32, "lb" + nm)
                    S.dma(lb[nm][0][:], dd, "const2", W=[lb[nm][1]], const=True)
                cst = [sbt(esS2, [128, 2048], F32, "cst%d" % i) for i in range(1)]
                S.dma(s5la[:].rearrange("p a b -> p (a b)"), s5la_d, "const2", W=[b_s5la], const=True)
                S.dma(idx1[:], idx1_d, "const2", W=[b_idx1], const=True)
                S.finalize_consts("const2")
                tmp = [sbt(esS2, [128, 1024], F32, "s5t%d" % i) for i in range(8)]
                tmpi, b_tmpi = sbt(esS2, [128, 1024], I32, "s5ti")

                def sincos(ang, b_ang, n, o_sin, b_osin, o_cos, b_ocos, ta, tb):
                    for shift, o, bo in ((0.0, o_sin, b_osin), (0.5 * math.pi, o_cos, b_ocos)):
                        ts("dve", ta[0][:, 0:n], ang, shift, 1.0 / TWO_PI, ALU.add, ALU.mult, [b_ang], [ta[1]])
                        cp("dve", tmpi[:, 0:n], ta[0][:, 0:n], [ta[1]], [b_tmpi])
                        cp("dve", tb[0][:, 0:n], tmpi[:, 0:n], [b_tmpi], [tb[1]])
                        ts("dve", ta[0][:, 0:n], ang, shift, None, ALU.add, None, [b_ang], [ta[1]])
                        stt(ta[0][:, 0:n], tb[0][:, 0:n], -TWO_PI, ta[0][:, 0:n], ALU.mult, ALU.add, [tb[1], ta[1]], [ta[1]])
                        ts("dve", ta[0][:, 0:n], ta[0][:, 0:n], -PI_SAFE, PI_SAFE, ALU.max, ALU.min, [ta[1]], [ta[1]])
                        act(o, ta[0][:, 0:n], AF.Sin, [ta[1]], [bo])

                t_dt, t_xr, t_xi, t_mag, t_sin, t_cos, t_a, t_b = tmp
                Bre_flat = Bb_re[:].rearrange("p a b -> p (a b)")
                Bim_flat = Bb_im[:].rearrange("p a b -> p (a b)")
                for hh in range(2):
                    c2 = slice(hh * 1024, (hh + 1) * 1024)
                    are_t, are_b = lb["are"]
                    aim_t, aim_b = lb["aim"]
                    ldt_t, ldt_b = lb["ldt"]
                    bre_t, bre_b = lb["bpre"]
                    bim_t, bim_b = lb["bpim"]
                    A_re, A_im = are_t[:, c2], aim_t[:, c2]
                    act(t_dt[0][:], ldt_t[:, c2], AF.Exp, [ldt_b], [t_dt[1]])
                    tt("dve", t_xr[0][:], t_dt[0][:], A_re, ALU.mult, [t_dt[1], are_b], [t_xr[1]])
                    tt("dve", t_xi[0][:], t_dt[0][:], A_im, ALU.mult, [t_dt[1], aim_b], [t_xi[1]])
                    act(t_mag[0][:], t_xr[0][:], AF.Exp, [t_xr[1]], [t_mag[1]])
                    sincos(t_xi[0][:], t_xi[1], 1024, t_sin[0][:], t_sin[1], t_cos[0][:], t_cos[1], t_a, t_b)
                    tt("dve", t_cos[0][:], t_cos[0][:], t_mag[0][:], ALU.mult, [t_cos[1], t_mag[1]], [t_cos[1]])
                    ts("dve", t_cos[0][:], t_cos[0][:], -1.0, None, ALU.add, None, [t_cos[1]], [t_cos[1]])
                    tt("dve", t_sin[0][:], t_sin[0][:], t_mag[0][:], ALU.mult, [t_sin[1], t_mag[1]], [t_sin[1]])
                    tt("dve", t_dt[0][:], A_re, A_re, ALU.mult, [are_b], [t_dt[1]])
                    tt("dve", t_xr[0][:], A_im, A_im, ALU.mult, [aim_b], [t_xr[1]])
                    tt("dve", t_dt[0][:], t_dt[0][:], t_xr[0][:], ALU.add, [t_dt[1], t_xr[1]], [t_dt[1]])
                    S.op("dve", lambda e: e.reciprocal(out=t_dt[0][:], in_=t_dt[0][:]), [t_dt[1]], [t_dt[1]])
                    tt("dve", t_xr[0][:], t_cos[0][:], A_re, ALU.mult, [t_cos[1], are_b], [t_xr[1]])
                    tt("dve", t_a[0][:], t_sin[0][:], A_im, ALU.mult, [t_sin[1], aim_b], [t_a[1]])
                    tt("dve", t_xr[0][:], t_xr[0][:], t_a[0][:], ALU.add, [t_xr[1], t_a[1]], [t_xr[1]])
                    tt("dve", t_xr[0][:], t_xr[0][:], t_dt[0][:], ALU.mult, [t_xr[1], t_dt[1]], [t_xr[1]])
                    tt("dve", t_xi[0][:], t_sin[0][:], A_re, ALU.mult, [t_sin[1], are_b], [t_xi[1]])
                    tt("dve", t_a[0][:], t_cos[0][:], A_im, ALU.mult, [t_cos[1], aim_b], [t_a[1]])
                    tt("dve", t_xi[0][:], t_xi[0][:], t_a[0][:], ALU.subtract, [t_xi[1], t_a[1]], [t_xi[1]])
                    tt("dve", t_xi[0][:], t_xi[0][:], t_dt[0][:], ALU.mult, [t_xi[1], t_dt[1]], [t_xi[1]])
                    tt("dve", t_a[0][:], t_xr[0][:], bre_t[:, c2], ALU.mult, [t_xr[1], bre_b], [t_a[1]])
                    tt("dve", t_b[0][:], t_xi[0][:], bim_t[:, c2], ALU.mult, [t_xi[1], bim_b], [t_b[1]])
                    tt("dve", Bre_flat[:, c2], t_a[0][:], t_b[0][:], ALU.subtract, [t_a[1], t_b[1]], [b_Bbre])
                    tt("dve", t_a[0][:], t_xr[0][:], bim_t[:, c2], ALU.mult, [t_xr[1], bim_b], [t_a[1]])
                    tt("dve", t_b[0][:], t_xi[0][:], bre_t[:, c2], ALU.mult, [t_xi[1], bre_b], [t_b[1]])
                    tt("dve", Bim_flat[:, c2], t_a[0][:], t_b[0][:], ALU.add, [t_a[1], t_b[1]], [b_Bbim])
                S.dma(cst[0][0][:], cpre_d, "cst", W=[cst[0][1]])
                cp("act", Cp_re[:].rearrange("p a b -> p (a b)"), cst[0][0][:], [cst[0][1]], [b_Cpre])
                S.dma(cst[0][0][:], cpim_d, "cst", W=[cst[0][1]])
                S.op("act", lambda e: e.mul(Cp_imn[:].rearrange("p a b -> p (a b)"), cst[0][0][:], -1.0), [cst[0][1]], [b_Cpim])
                la_dt, la_th = t_mag, t_dt
                act(la_dt[0][:, 0:16], s5la[:, 2, :], AF.Exp, [b_s5la], [la_dt[1]])
                tt("dve", la_th[0][:, 0:16], la_dt[0][:, 0:16], s5la[:, 1, :], ALU.mult, [la_dt[1], b_s5la], [la_th[1]])
                tt("dve", la_dt[0][:, 0:16], la_dt[0][:, 0:16], s5la[:, 0, :], ALU.mult, [la_dt[1], b_s5la], [la_dt[1]])
                act(rho1[:], la_dt[0][:, 0:16], AF.Exp, [la_dt[1]], [b_rho1])
                tt("dve", t_xr[0][:, 0:1024].rearrange("p (a b) -> p a b", b=64), bc2(la_th[0][:, 0:16], 64), bc1(idx1[:], 16),
                   ALU.mult, [la_th[1], b_idx1], [t_xr[1]])
                sincos(t_xr[0][:, 0:1024], t_xr[1], 1024, tsin[:].rearrange("p a b -> p (a b)"), b_tsin,
                       tcos[:].rearrange("p a b -> p (a b)"), b_tcos, t_a, t_b)
                cp("dve", rho0[:], bc2(rho1[:], 64), [b_rho1], [b_rho0])
                S.op("dve", lambda e: e.memset(rho0[:, :, 0:1], 0.0), [], [b_rho0])
                tt("dve", T63[:, 0, 0, :], tcos[:, :, 63], rho1[:], ALU.mult, [b_tcos, b_rho1], [b_T63])
                tt("dve", T63[:, 0, 1, :], tsin[:, :, 63], rho1[:], ALU.mult, [b_tsin, b_rho1], [b_T63])
                tt("dve", T63[:, 1, 1, :], tcos[:, :, 63], rho1[:], ALU.mult, [b_tcos, b_rho1], [b_T63])
                stt(T63[:, 1, 0, :], tsin[:, :, 63], -1.0, rho1[:], ALU.mult, ALU.mult, [b_tsin, b_rho1], [b_T63])
                if stage('setup2', lambda: (dump('Bbre', Bb_re[:, 5, :], b_Bbre), dump('Bbim', Bb_im[:, 5, :], b_Bbim), dump('tcos', tcos[:], b_tcos), dump('tsin', tsin[:], b_tsin), dump('rho1', rho1[:], b_rho1), dump('rho0', rho0[:, 3, :], b_rho0), dump('Cpimn', Cp_imn[:, 9, :], b_Cpim))):
                    return nc
                S.barrier()

            xs = [sbt(esA, [128, D], F32, "xs%d" % i) for i in range(2)]
            xnb = [sbt(esA, [128, D], BF16, "xnb%d" % i) for i in range(1)]
            stat, b_stat = sbt(esA, [128, 8], F32, "stat")
            hT = [sbt(esA, [128, 8, W], BF16, "hT%d" % i) for i in range(1)]
            Pprev, b_Pprev = sbt(esA, [128, 14], F32, "Pprev")
            Pb = [sbt(esA, [128, W + 1], F32, "Pb%d" % i) for i in range(2)]
            zt = [sbt(esA, [128, W], F32, "zt%d" % i) for i in range(4)]
            LA, b_LA = sbt(esA, [128, W], BF16, "LA")
            gs, b_gs = sbt(esA, [128, W], BF16, "gs")
            ARs = [sbt(esA, [128, 4, 4, 128], BF16, "AR%d" % i) for i in range(2)]
            KHs = [sbt(esA, [128, 4, W], BF16, "KH%d" % i) for i in range(2)]
            BHs = [sbt(esA, [128, 4, W], BF16, "BH%d" % i) for i in range(2)]
            VBs = [sbt(esA, [128, 4, W], BF16, "VB%d" % i) for i in range(2)]
            kT, b_kT = sbt(esA, [128, 4, 4, 64], BF16, "kT")
            bT, b_bT = sbt(esA, [128, 4, 4, 64], BF16, "bT")
            vT, b_vT = sbt(esA, [128, 4, 4, 64], BF16, "vT")
            GLs = [sbt(esA, [128, 4, 4], F32, "GL%d" % i) for i in range(2)]
            gfms = [sbt(esA, [128, 4, W], BF16, "gfm%d" % i) for i in range(2)]
            bvs = [sbt(esA, [128, 4, W], BF16, "bv%d" % i) for i in range(2)]
            Yw, b_Yw = sbt(esA, [128, 4, W], F32, "Yw")
            ufms = [sbt(esA, [128, 4, W], BF16, "ufm%d" % i) for i in range(2)]
            Y5w, b_Y5w = sbt(esA, [128, 4, W], F32, "Y5w")
            ycat, b_ycat = sbt(esA, [128, 8, W], BF16, "ycat")
            et = [sbt(esA, [128, W], F32, "et%d" % i) for i in range(12)]
            etb = [sbt(esA, [128, W], BF16, "etb%d" % i) for i in range(2)]
            AbmS = [sbt(esA, [128, 4, 128], BF16, "Abm%d" % i) for i in range(2)]
            AkmS = [sbt(esA, [128, 4, 128], BF16, "Akm%d" % i) for i in range(2)]
            TfinS = [sbt(esA, [128, 4, 64], BF16, "Tfin%d" % i) for i in range(2)]
            ImT, b_ImT = sbt(esA, [128, 4, 64], F32, "ImT")
            Xb = [sbt(esA, [128, 4, 64], BF16, "Xb%d" % i) for i in range(2)]
            XTb = [sbt(esA, [128, 4, 64], BF16, "XTb%d" % i) for i in range(2)]
            Pm = [sbt(esA, [128, 4, 64], BF16, "Pm%d" % i) for i in range(2)]
            NTb, b_NTb = sbt(esA, [128, 4, 64], BF16, "NTb")
            Rb, b_Rb = Xb[0]
            TTb, b_TTb = Xb[1]
            WT, b_WT = sbt(esA, [128, 4, 64], BF16, "WT")
            UT, b_UT = sbt(esA, [128, 4, 64], BF16, "UT")
            Sf, b_Sf = sbt(esA, [128, 4, 64], F32, "Sf")
            Sb = [sbt(esA, [128, 4, 64], BF16, "Sb%d" % i) for i in range(2)]
            stmp, b_stmp = sbt(esA, [128, 4, 64], F32, "stmp")
            s5a = [sbt(esA, [128, 512], F32, "s5a%d" % i) for i in range(4)]
            s_tr, b_str = sbt(esA, [128, 512], F32, "str")
            s_ti, b_sti = sbt(esA, [128, 512], F32, "sti")
            srb, b_srb = sbt(esA, [128, 512], BF16, "srb")
            sib, b_sib = sbt(esA, [128, 512], BF16, "sib")
            s5s, b_s5s = sbt(esA, [128, 2, 16], F32, "s5s")
            s5q, b_s5q = sbt(esA, [128, 6, 8], F32, "s5q")

            S.op("dve", lambda e: e.memset(Pprev[:], 0.0), [], [b_Pprev])
            S.op("dve", lambda e: e.memset(Sf[:], 0.0), [], [b_Sf])
            S.op("dve", lambda e: e.memset(Sb[0][0][:], 0.0), [], [Sb[0][1]])
            S.op("dve", lambda e: e.memset(s5s[:], 0.0), [], [b_s5s])
            S.op("dve", lambda e: e.memset(ARs[0][0][:], 0.0), [], [ARs[0][1]])
            S.op("dve", lambda e: e.memset(ARs[1][0][:], 0.0), [], [ARs[1][1]])
            M = {}

            cur_s = [0]
            lastT = [None]
            xslot = [0]

            def load_x(row0):
                i = xslot[0] % 2
                xslot[0] += 1
                t, b = xs[i]
                S.dma(t[:], xcat[row0:row0 + 128, :], "xs%d" % i, W=[b])
                return t, b

            def norm_transpose(xt, bx, st, hT_t, hT_b, sh_i, sc_i, width):
                xn_t, xn_b = xnb[0]
                col = st % 4
                act(xn_t[:], xt[:], AF.Square, [bx], [xn_b, b_stat], accum=stat[:, col:col + 1])
                act(stat[:, 4 + col:5 + col], stat[:, col:col + 1], AF.Sqrt, [b_stat], [b_stat], bias=1e-6, scale=1.0 / D)
                S.op("dve", lambda e: e.reciprocal(out=stat[:, 4 + col:5 + col], in_=stat[:, 4 + col:5 + col]), [b_stat], [b_stat])
                act(xn_t[:], xt[:], AF.Identity, [bx, b_stat], [xn_b], scale=stat[:, 4 + col:5 + col])
                for k in range(8):
                    tr(PT[:, k * 128:(k + 1) * 128], xn_t[:, k * 128:(k + 1) * 128], ident_b[:], [xn_b, b_ident_b], [bPT], inc=(k == 7))
                for k in range(8):
                    o = hT_t[:, k, st * 128:(st + 1) * 128]
                    i_ = PT[:, k * 128:(k + 1) * 128]
                    if True:
                        act(o, i_, AF.Identity, [bPT, b_modp], [hT_b], scale=modp[:, sc_i, k:k + 1], bias=modp[:, sh_i, k:k + 1])
                    else:
                        ts("dve", o, i_, modp[:, sc_i, k:k + 1], modp[:, sh_i, k:k + 1], ALU.mult, ALU.add, [bPT, b_modp], [hT_b])

            pbi = [0]
            pji = [0]

            def project(hT_t, hT_b, ct):
                pj, bpj = PA[1], bPA[1]
                for k in range(8):
                    mm(pj[:, 0:W], W1[:, k, ct * 128:(ct + 1) * 128], hT_t[:, k, :], [b_W1, hT_b], [bpj],
                       start=(k == 0), stop=(k == 7), inc=(k == 7))
                return pj, bpj

            def shifted(hT_t, hT_b, ct, zo, b_zo):
                pj, bpj = project(hT_t, hT_b, ct)
                p_t, p_b = Pb[pbi[0] % 2]
                pbi[0] += 1
                cp("pool", p_t[:, 0:1], Pprev[:, ct:ct + 1], [b_Pprev], [p_b])
                act(p_t[:, 1:W + 1], pj[:, 0:W], AF.Identity, [bpj], [p_b])
                cp("pool", Pprev[:, ct:ct + 1], p_t[:, W:W + 1], [p_b], [b_Pprev])
                act(zo, p_t[:, 0:W], AF.Identity, [p_b, b_mu], [b_zo], scale=mu[:, ct:ct + 1])
                stt(zo, p_t[:, 1:W + 1], omu[:, ct:ct + 1], zo, ALU.mult, ALU.add, [p_b, b_omu, b_zo], [b_zo])

            def rp(i, f):
                return rwp[:, i, f:f + 1]

            def drive(gens, weights=None, background=()):
                gens = list(gens)
                bg = list(background)
                wts = {id(g_): 1 for g_ in gens}
                if weights:
                    for g_, w_ in zip(gens, weights):
                        wts[id(g_)] = w_
                while gens:
                    for g_ in list(gens):
                        for _ in range(wts[id(g_)]):
                            try:
                                next(g_)
                            except StopIteration:
                                gens.remove(g_)
                                break
                    for g_ in list(bg):
                        try:
                            next(g_)
                        except StopIteration:
                            bg.remove(g_)

            def pre(idx):
                own = idx >= NWIN
                win = idx % NWIN
                par = idx % 2
                AR, b_AR = ARs[par]
                KH, b_KH = KHs[par]
                BH, b_BH = BHs[par]
                GL, b_GL = GLs[par]
                gfm, b_gfm = gfms[par]
                bv, b_bv = bvs[par]
                ufm, b_ufm = ufms[par]
                row_base = (TOK if own else 0) + win * W
                hT_t, hT_b = hT[0]
                for st in range(W // 128):
                    xt, bx = load_x(row_base + st * 128)
                    norm_transpose(xt, bx, st, hT_t, hT_b, 0, 1, W)
                    yield
                z12, b_z12 = zt[3]
                shifted(hT_t, hT_b, 12, z12[:], b_z12)
                act(LA[0:64, :], z12[0:64, :], AF.Tanh, [b_z12], [b_LA])
                cp("pool", LA[64:128, :], z12[64:128, :], [b_z12], [b_LA])
                yield
                need_r = own or (win == NWIN - 1)
                if need_r:
                    shifted(hT_t, hT_b, 13, z12[:], b_z12)
                if own:
                    act(gs[:], z12[:], AF.Sigmoid, [b_z12], [b_gs])
                yield
                for f in range(4):
                    (zr, b_zr), (zk, b_zk), (zv, b_zv) = zt[0], zt[1], zt[2]
                    if need_r:
                        shifted(hT_t, hT_b, f, zr[:], b_zr)
                        yield
                    shifted(hT_t, hT_b, 4 + f, zk[:], b_zk)
                    yield
                    shifted(hT_t, hT_b, 8 + f, zv[:], b_zv)
                    yield
                    (sw, b_sw), (asg, b_asg), (kkr, b_kkr), (nrm, b_nrm), (kk, b_kk), (t1, b_t1) = et[0:6]
                    (kmod, b_kmod), (bvec, b_bvec), (lw, b_lw), (cc, b_cc), (cm, b_cm), (ex, b_ex) = et[6:12]
                    (sq, b_sq), (rk, b_rk) = etb
                    mm(PA[1][:, 0:W], lo2w[:, f * 128:(f + 1) * 128], LA[:], [b_lo2w, b_LA], [bPA[1]])
                    mm(PA[1][:, W:2 * W], lo2a[:, f * 128:(f + 1) * 128], LA[:], [b_lo2a, b_LA], [bPA[1]])
                    act(sw[:], PA[1][:, 0:W], AF.Sigmoid, [bPA[1], b_rwp], [b_sw], bias=rp(0, f))
                    act(asg[:], PA[1][:, W:2 * W], AF.Sigmoid, [bPA[1], b_rwp], [b_asg], bias=rp(1, f))
                    if own:
                        mm(PA[1][:, 0:W], g2[:, f * 128:(f + 1) * 128], gs[:], [b_g2, b_gs], [bPA[1]])
                        act(gfm[:, f, :], PA[1][:, 0:W], AF.Identity, [bPA[1]], [b_gfm])
                    act(kkr[:], zk[:], AF.Identity, [b_zk, b_rwp], [b_kkr], scale=rp(2, f))
                    act(sq[:], kkr[:], AF.Square, [b_kkr], [b_sq])
                    yield
                    mm(PA[1][:, 0:W], bones_b[:], sq[:], [b_bones_b, b_sq], [bPA[1]])
                    act(nrm[:], PA[1][:, 0:W], AF.Sqrt, [bPA[1]], [b_nrm])
                    ts("dve", nrm[:], nrm[:], 1e-12, None, ALU.max, None, [b_nrm], [b_nrm])
                    S.op("dve", lambda e: e.reciprocal(out=nrm[:], in_=nrm[:]), [b_nrm], [b_nrm])
                    tt("pool", kk[:], kkr[:], nrm[:], ALU.mult, [b_kkr, b_nrm], [b_kk])
                    act(t1[:], asg[:], AF.Identity, [b_asg, b_rwp], [b_t1], scale=rp(3, f), bias=rp(7, f))
                    yield
                    tt("pool", kmod[:], zk[:], t1[:], ALU.mult, [b_zk, b_t1], [b_kmod])
                    tt("pool", bvec[:], kk[:], asg[:], ALU.mult, [b_kk, b_asg], [b_bvec])
                    S.op("act", lambda e: e.mul(lw[:], sw[:], -math.exp(-0.5)), [b_sw], [b_lw])
                    S.op("dve", lambda e: e.tensor_tensor_scan(out=cc[:], data0=rst01[:], data1=lw[:], initial=0.0,
                                                                op0=ALU.mult, op1=ALU.add), [b_rst, b_lw], [b_cc])
                    yield
                    tt("pool", cm[:], cc[:], lw[:], ALU.subtract, [b_cc, b_lw], [b_cm])
                    act(sw[:], cc[:], AF.Exp, [b_cc], [b_sw])
                    act(ex[:], cc[:], AF.Exp, [b_cc], [b_ex], scale=-1.0)
                    act(cm[:], cm[:], AF.Exp, [b_cm], [b_cm])
                    yield
                    if own:
                        tt("dve", AR[:, f, :, 64:128], zr[:].rearrange("p (c t) -> p c t", t=64),
                           sw[:].rearrange("p (c t) -> p c t", t=64), ALU.mult, [b_zr, b_sw], [b_AR])
                    stt(AR[:, f, :, 0:64], kk[:].rearrange("p (c t) -> p c t", t=64), -1.0,
                        cm[:].rearrange("p (c t) -> p c t", t=64), ALU.mult, ALU.mult, [b_kk, b_cm], [b_AR])
                    tt("dve", KH[:, f, :], kmod[:], ex[:], ALU.mult, [b_kmod, b_ex], [b_KH])
                    tt("pool", BH[:, f, :], bvec[:], ex[:], ALU.mult, [b_bvec, b_ex], [b_BH])
                    cp("pool", GL[:, f, :], sw[:].rearrange("p (c t) -> p c t", t=64)[:, :, 63], [b_sw], [b_GL])
                    yield
                    if own:
                        stt(rk[:], zr[:], rp(4, f), kmod[:], ALU.mult, ALU.mult, [b_zr, b_rwp, b_kmod], [b_rk])
                        mm(PA[1][:, 0:W], bones_b[:], rk[:], [b_bones_b, b_rk], [bPA[1]])
                        tt("dve", bv[:, f, :], PA[1][:, 0:W], zv[:], ALU.mult, [bPA[1], b_zv], [b_bv])
                        yield
                    cp("act", VBs[par][0][:, f, :], zv[:], [b_zv], [VBs[par][1]])
                    yield
                for t in range(4):
                    pj, bpj = project(hT_t, hT_b, 14 + t)
                    act(ufm[:, t, :], pj[:, 0:W], AF.Identity, [bpj], [b_ufm])
                    yield

            def main(idx, nxt):
                own = idx >= NWIN
                win = idx % NWIN
                par = idx % 2
                M['AR'] = ARs[par]
                M['KH'] = KHs[par]
                M['BH'] = BHs[par]
                M['GL'] = GLs[par]
                M['gfm'] = gfms[par]
                M['bv'] = bvs[par]
                M['ufm'] = ufms[par]
                KH, b_KH = KHs[par]
                BH, b_BH = BHs[par]
                VB, b_VB = VBs[par]
                for (src, b_src, dst, b_dst) in ((KH, b_KH, kT, b_kT), (BH, b_BH, bT, b_bT), (VB, b_VB, vT, b_vT)):
                    n = 0
                    for c in range(4):
                        for f in range(4):
                            for hp in range(2):
                                rs = slice(hp * 64, hp * 64 + 64)
                                n += 1
                                o0 = (c * 4 + f) * 64
                                tr(PT[rs, o0:o0 + 64], src[rs, f, c * 64:(c + 1) * 64], ident_b[rs, rs],
                                   [b_src, b_ident_b], [bPT], inc=(n == 32))
                    cp("act", dst[:].rearrange("p a b c -> p (a b c)"), PT[:, 0:1024], [bPT], [b_dst])
                if M.get('inv0_done') != idx:
                    drive([chunk_inv(0, own, 0, par)])
                extra = [nxt] if nxt is not None else []
                for c in range(4):
                    grp = [chunk_chain(c, own, c % 2), s5_chunk(c, own)]
                    wts = [2, 1]
                    if c < 3:
                        grp = [chunk_inv(c + 1, own, (c + 1) % 2, par)] + grp
                        wts = [3, 2, 1]
                    elif nxt is not None:
                        drive([nxt])
                        grp = [chunk_inv(0, (idx + 1) >= NWIN, 0, (idx + 1) % 2)] + grp
                        wts = [3, 2, 1]
                        M['inv0_done'] = idx + 1
                    drive(grp, wts, background=extra)
                    extra = []
                    if nxt is not None:
                        extra = [nxt]
                if nxt is not None:
                    drive([nxt])
                if own:
                    outputs(win, None, None)

            HEADS = [(f, hp) for f in range(4) for hp in range(2)]

            def chunk_inv(c, own, slot, par):
                AR, b_AR = ARs[par]
                KH, b_KH = KHs[par]
                BH, b_BH = BHs[par]
                cs = slice(c * 64, (c + 1) * 64)
                na = 128 if own else 64
                heads = HEADS
                Abm, b_Abm = AbmS[slot]
                Akm, b_Akm = AkmS[slot]
                for i, (f, hp) in enumerate(heads):
                    rs = slice(hp * 64, hp * 64 + 64)
                    mm(PA[0][rs, f * 128:f * 128 + na], BH[rs, f, cs], AR[rs, f, c, 0:na], [b_BH, b_AR], [bPA[0]], inc=(i == 7))
                for i, (f, hp) in enumerate(heads):
                    rs = slice(hp * 64, hp * 64 + 64)
                    mm(PA[1][rs, f * 128:f * 128 + na], KH[rs, f, cs], AR[rs, f, c, 0:na], [b_KH, b_AR], [bPA[1]], inc=(i == 7))
                for i, (f, hp) in enumerate(heads):
                    rs = slice(hp * 64, hp * 64 + 64)
                    mm(PI[rs, f * 64:(f + 1) * 64], AR[rs, f, c, 0:64], BH[rs, f, cs], [b_AR, b_BH], [bPI[0]], inc=(i == 7))
                yield
                pa0 = PA[0][:, :].rearrange("p (f n) -> p f n", n=128)
                pa1 = PA[1][:, :].rearrange("p (f n) -> p f n", n=128)
                mk = maskar[:, 0:na]
                tt("dve", Abm[:, :, 0:na], pa0[:, :, 0:na], bc1(mk, 4), ALU.mult, [bPA[0], b_maskar], [b_Abm])
                xt_t, xt_b = NTb, b_NTb
                tt("dve", xt_t[:], PI[:, 0:256].rearrange("p (f n) -> p f n", n=64), bc1(masknt[:], 4), ALU.mult,
                   [bPI[0], b_masknt], [xt_b])
                tt("dve", Akm[:, :, 0:na], pa1[:, :, 0:na], bc1(mk, 4), ALU.mult, [bPA[1], b_maskar], [b_Akm])
                p_t, p_b = Pm[0]
                tt("pool", p_t[:], Abm[:, :, 0:64], bc1(identp[:], 4), ALU.add, [b_Abm, b_identp], [p_b])
                yield
                x_ap = lambda f, rs: Abm[rs, f, 0:64]
                x_b = b_Abm
                for lvl in range(1, 6):
                    xtn_t, xtn_b = XTb[lvl % 2]
                    for i, (f, hp) in enumerate(heads):
                        rs = slice(hp * 64, hp * 64 + 64)
                        mm(PI[rs, f * 64:(f + 1) * 64], x_ap(f, rs), xt_t[rs, f, :], [x_b, xt_b], [bPI[0]], inc=(i == 7))
                    if lvl <= 4:
                        xn_t, xn_b = Xb[lvl % 2]
                        for i, (f, hp) in enumerate(heads):
                            rs = slice(hp * 64, hp * 64 + 64)
                            mm(PA[0][rs, f * 64:(f + 1) * 64], xt_t[rs, f, :], x_ap(f, rs), [x_b, xt_b], [bPA[0]], inc=(i == 7))
                    yield
                    cp("act", xtn_t[:], PI[:, 0:256].rearrange("p (f n) -> p f n", n=64), [bPI[0]], [xtn_b])
                    if lvl <= 4:
                        cp("act", xn_t[:], PA[0][:, 0:256].rearrange("p (f n) -> p f n", n=64), [bPA[0]], [xn_b])
                    yield
                    pn_t, pn_b = Pm[lvl % 2]
                    for i, (f, hp) in enumerate(heads):
                        rs = slice(hp * 64, hp * 64 + 64)
                        mm(PC[rs, f * 64:(f + 1) * 64], xtn_t[rs, f, :], p_t[rs, f, :], [xtn_b, p_b], [bPC[0]], inc=(i == 7))
                    yield
                    tt("dve", pn_t[:], PC[:, 0:256].rearrange("p (f n) -> p f n", n=64), p_t[:], ALU.add, [bPC[0], p_b], [pn_b])
                    yield
                    p_t, p_b = pn_t, pn_b
                    xt_t, xt_b = xtn_t, xtn_b
                    if lvl <= 4:
                        x_ap = (lambda t_: (lambda f, rs: t_[rs, f, :]))(xn_t)
                        x_b = xn_b
                T0_t, T0_b = p_t, p_b
                Rb, b_Rb = Xb[0]
                TTb, b_TTb = Xb[1]
                for i, (f, hp) in enumerate(heads):
                    rs = slice(hp * 64, hp * 64 + 64)
                    mm(PI[rs, f * 64:(f + 1) * 64], NTb[rs, f, :], T0_t[rs, f, :], [b_NTb, T0_b], [bPI[0]], inc=(i == 7))
                for i, (f, hp) in enumerate(heads):
                    rs = slice(hp * 64, hp * 64 + 64)
                    tr(PT[rs, f * 64:(f + 1) * 64], T0_t[rs, f, :], ident_b[rs, rs], [T0_b, b_ident_b], [bPT], inc=(i == 7))
                tt("pool", ImT[:], bc1(identp[:], 4), T0_t[:], ALU.subtract, [b_identp, T0_b], [b_ImT])
                yield
                tt("dve", Rb[:], PI[:, 0:256].rearrange("p (f n) -> p f n", n=64), ImT[:], ALU.add, [bPI[0], b_ImT], [b_Rb])
                cp("act", TTb[:], PT[:, 0:256].rearrange("p (f n) -> p f n", n=64), [bPT], [b_TTb])
                yield
                for i, (f, hp) in enumerate(heads):
                    rs = slice(hp * 64, hp * 64 + 64)
                    mm(PC[rs, f * 64:(f + 1) * 64], TTb[rs, f, :], Rb[rs, f, :], [b_TTb, b_Rb], [bPC[0]], inc=(i == 7))
                yield
                T_t, T_b = TfinS[slot]
                tt("dve", T_t[:], PC[:, 0:256].rearrange("p (f n) -> p f n", n=64), T0_t[:], ALU.add, [bPC[0], T0_b], [T_b])
                lastT[0] = (T_t, T_b)
                yield

            def chunk_chain(c, own, slot):
                AR, b_AR = M['AR']
                KH, b_KH = M['KH']
                BH, b_BH = M['BH']
                GL, b_GL = M['GL']
                gfm, b_gfm = M['gfm']
                bv, b_bv = M['bv']
                ufm, b_ufm = M['ufm']
                cs = slice(c * 64, (c + 1) * 64)
                heads = HEADS
                Abm, b_Abm = AbmS[slot]
                Akm, b_Akm = AkmS[slot]
                T_t, T_b = TfinS[slot]
                s0_t, s0_b = Sb[cur_s[0] % 2]
                s1_t, s1_b = Sb[(cur_s[0] + 1) % 2]
                cur_s[0] += 1
                pm3 = PM[:, 0:256].rearrange("p (f n) -> p f n", n=64)
                for i, (f, hp) in enumerate(heads):
                    rs = slice(hp * 64, hp * 64 + 64)
                    mm(PM[rs, f * 64:(f + 1) * 64], AR[rs, f, c, 0:64], s0_t[rs, f, :], [b_AR, s0_b], [bPM[0]], start=True, stop=False, inc=False)
                    mm(PM[rs, f * 64:(f + 1) * 64], Akm[rs, f, 0:64], vT[rs, c, f, :], [b_Akm, b_vT], [bPM[0]], start=False, stop=True, inc=(i == 7))
                yield
                cp("act", WT[:], pm3, [bPM[0]], [b_WT])
                yield
                for i, (f, hp) in enumerate(heads):
                    rs = slice(hp * 64, hp * 64 + 64)
                    mm(PM[rs, f * 64:(f + 1) * 64], T_t[rs, f, :], WT[rs, f, :], [T_b, b_WT], [bPM[0]], inc=(i == 7))
                yield
                cp("act", UT[:], pm3, [bPM[0]], [b_UT])
                yield
                for i, (f, hp) in enumerate(heads):
                    rs = slice(hp * 64, hp * 64 + 64)
                    mm(PM[rs, f * 64:(f + 1) * 64], bT[rs, c, f, :], UT[rs, f, :], [b_bT, b_UT], [bPM[0]], start=True, stop=False, inc=False)
                    mm(PM[rs, f * 64:(f + 1) * 64], kT[rs, c, f, :], vT[rs, c, f, :], [b_kT, b_vT], [bPM[0]], start=False, stop=True, inc=(i == 7))
                yield
                tt("dve", stmp[:], pm3, Sf[:], ALU.add, [bPM[0], b_Sf], [b_stmp])
                yield
                if own:
                    for i, (f, hp) in enumerate(heads):
                        rs = slice(hp * 64, hp * 64 + 64)
                        o = PM[rs, f * 64:(f + 1) * 64]
                        mm(o, s0_t[rs, f, :], AR[rs, f, c, 64:128], [s0_b, b_AR], [bPM[0]], start=True, stop=False, inc=False)
                        mm(o, UT[rs, f, :], Abm[rs, f, 64:128], [b_UT, b_Abm], [bPM[0]], start=False, stop=False, inc=False)
                        mm(o, vT[rs, c, f, :], Akm[rs, f, 64:128], [b_vT, b_Akm], [bPM[0]], start=False, stop=True, inc=(i == 7))
                    yield
                    cp("act", Yw[:, :, cs], pm3, [bPM[0]], [b_Yw])
                tt("dve", Sf[:], stmp[:], bc2(GL[:, :, c], 64), ALU.mult, [b_stmp, b_GL], [b_Sf])
                cp("act", s1_t[:], Sf[:], [b_Sf], [s1_b])
                yield

            def s5_chunk(c, own):
                AR, b_AR = M['AR']
                KH, b_KH = M['KH']
                BH, b_BH = M['BH']
                GL, b_GL = M['GL']
                gfm, b_gfm = M['gfm']
                bv, b_bv = M['bv']
                ufm, b_ufm = M['ufm']
                cs = slice(c * 64, (c + 1) * 64)
                for hf in range(2):
                    ps_ = slice(hf * 8, hf * 8 + 8)
                    for pl in range(8):
                        pr = hf * 8 + pl
                        t = pr // 4
                        mm(PJ[0][:, pl * 64:(pl + 1) * 64], Bb_re[:, pr, :], ufm[:, t, cs], [b_Bbre, b_ufm], [bPJ[0]], inc=(pl == 7))
                    for pl in range(8):
                        pr = hf * 8 + pl
                        t = pr // 4
                        mm(PJ[1][:, pl * 64:(pl + 1) * 64], Bb_im[:, pr, :], ufm[:, t, cs], [b_Bbim, b_ufm], [bPJ[1]], inc=(pl == 7))
                    yield
                    cosv = tcos[:, ps_, :].rearrange("p a b -> p (a b)")
                    sinv = tsin[:, ps_, :].rearrange("p a b -> p (a b)")
                    (a0, ba0), (a1, ba1), (a2, ba2), (a3, ba3) = s5a
                    tt("dve", a0[:], PJ[0][:, :], cosv, ALU.mult, [bPJ[0], b_tcos], [ba0])
                    tt("dve", a1[:], PJ[1][:, :], sinv, ALU.mult, [bPJ[1], b_tsin], [ba1])
                    tt("dve", a2[:], PJ[1][:, :], cosv, ALU.mult, [bPJ[1], b_tcos], [ba2])
                    tt("dve", a3[:], PJ[0][:, :], sinv, ALU.mult, [bPJ[0], b_tsin], [ba3])
                    btr, b_btr, bti, b_bti = a0, ba0, a2, ba2
                    yield
                    tt("pool", btr[:], a0[:], a1[:], ALU.add, [ba0, ba1], [b_btr])
                    tt("pool", bti[:], a2[:], a3[:], ALU.subtract, [ba2, ba3], [b_bti])
                    b3r = btr[:].rearrange("p (a b) -> p a b", b=64)
                    b3i = bti[:].rearrange("p (a b) -> p a b", b=64)
                    tt("pool", b3r[:, :, 0], b3r[:, :, 0], s5s[:, 0, ps_], ALU.add, [b_btr, b_s5s], [b_btr])
                    tt("pool", b3i[:, :, 0], b3i[:, :, 0], s5s[:, 1, ps_], ALU.add, [b_bti, b_s5s], [b_bti])
                    yield
                    rv = rho0[:, ps_, :].rearrange("p a b -> p (a b)")
                    S.op("dve", lambda e: e.tensor_tensor_scan(out=s_tr[:], data0=rv, data1=btr[:], initial=0.0,
                                                                op0=ALU.mult, op1=ALU.add), [b_rho0, b_btr], [b_str])
                    S.op("dve", lambda e: e.tensor_tensor_scan(out=s_ti[:], data0=rv, data1=bti[:], initial=0.0,
                                                                op0=ALU.mult, op1=ALU.add), [b_rho0, b_bti], [b_sti])
                    yield
                    s3r = s_tr[:].rearrange("p (a b) -> p a b", b=64)
                    s3i = s_ti[:].rearrange("p (a b) -> p a b", b=64)
                    qa = s5q[:, 0:2, :]
                    qb = s5q[:, 2:4, :]
                    tt("pool", qa, s3r[:, :, 63].unsqueeze(1).to_broadcast([128, 2, 8]), T63[:, 0, :, ps_], ALU.mult, [b_str, b_T63], [b_s5q])
                    tt("pool", qb, s3i[:, :, 63].unsqueeze(1).to_broadcast([128, 2, 8]), T63[:, 1, :, ps_], ALU.mult, [b_sti, b_T63], [b_s5q])
                    tt("pool", s5s[:, :, ps_], qa, qb, ALU.add, [b_s5q], [b_s5s])
                    yield
                    if own:
                        tt("dve", a0[:], s_tr[:], cosv, ALU.mult, [b_str, b_tcos], [ba0])
                        tt("pool", a1[:], s_ti[:], sinv, ALU.mult, [b_sti, b_tsin], [ba1])
                        tt("dve", a2[:], s_tr[:], sinv, ALU.mult, [b_str, b_tsin], [ba2])
                        tt("pool", a3[:], s_ti[:], cosv, ALU.mult, [b_sti, b_tcos], [ba3])
                        tt("dve", srb[:], a0[:], a1[:], ALU.subtract, [ba0, ba1], [b_srb])
                        tt("pool", sib[:], a2[:], a3[:], ALU.add, [ba2, ba3], [b_sib])
                        yield
                        for pl in range(8):
                            pr = hf * 8 + pl
                            t = pr // 4
                            tl = t % 2
                            o = PJ[0][:, tl * 64:(tl + 1) * 64]
                            mm(o, Cp_re[:, pr, :], srb[:, pl * 64:(pl + 1) * 64], [b_Cpre, b_srb], [bPJ[0]],
                               start=(pr % 4 == 0), stop=False, inc=False)
                            mm(o, Cp_imn[:, pr, :], sib[:, pl * 64:(pl + 1) * 64], [b_Cpim, b_sib], [bPJ[0]],
                               start=False, stop=(pr % 4 == 3), inc=(pl == 7))
                        yield
                        for tl in range(2):
                            t = hf * 2 + tl
                            stt(Y5w[:, t, cs], ufm[:, t, cs], s5v[:, 0, t:t + 1], PJ[0][:, tl * 64:(tl + 1) * 64], ALU.mult, ALU.add,
                                [b_ufm, b_s5v, bPJ[0]], [b_Y5w])
                    yield

            def outputs(win, hT_t, hT_b):
                AR, b_AR = M['AR']
                KH, b_KH = M['KH']
                BH, b_BH = M['BH']
                GL, b_GL = M['GL']
                gfm, b_gfm = M['gfm']
                bv, b_bv = M['bv']
                ufm, b_ufm = M['ufm']
                xres = [load_x(TOK + win * W + st * 128) + (xslot[0] - 1,) for st in range(W // 128)]
                for f in range(4):
                    (ysq, b_ysq), (mu2, b_mu2), (var, b_var), (dd, b_dd) = et[(f % 2) * 4:(f % 2) * 4 + 4]
                    act(ysq[:], Yw[:, f, :], AF.Square, [b_Yw], [b_ysq])
                    mm(PM[:, 0:W], bones_f[:], Yw[:, f, :], [b_bones_f, b_Yw], [bPM[0]])
                    mm(PA[0][:, 0:W], bones_f[:], ysq[:], [b_bones_f, b_ysq], [bPA[0]])
                    act(mu2[:], PM[:, 0:W], AF.Square, [bPM[0]], [b_mu2])
                    tt("dve", var[:], PA[0][:, 0:W], mu2[:], ALU.subtract, [bPA[0], b_mu2], [b_var])
                    act(var[:], var[:], AF.Sqrt, [b_var], [b_var], bias=64e-5, scale=1.0)
                    S.op("dve", lambda e: e.reciprocal(out=var[:], in_=var[:]), [b_var], [b_var])
                    tt("dve", dd[:], Yw[:, f, :], PM[:, 0:W], ALU.subtract, [b_Yw, bPM[0]], [b_dd])
                    tt("pool", dd[:], dd[:], var[:], ALU.mult, [b_dd, b_var], [b_dd])
                    ts("pool", dd[:], dd[:], rp(5, f), rp(6, f), ALU.mult, ALU.add, [b_dd, b_rwp], [b_dd])
                    tt("pool", dd[:], dd[:], bv[:, f, :], ALU.add, [b_dd, b_bv], [b_dd])
                    tt("dve", ycat[:, f, :], dd[:], gfm[:, f, :], ALU.mult, [b_dd, b_gfm], [b_ycat])
                zzbv = lambda t: (srb if t < 2 else sib)[:, (t % 2) * W:(t % 2 + 1) * W]
                zzbb = lambda t: (b_srb if t < 2 else b_sib)
                zzv = lambda t: (s_tr if t < 2 else s_ti)[:, (t % 2) * W:(t % 2 + 1) * W]
                zzb_ = lambda t: (b_str if t < 2 else b_sti)
                oo, b_oo = sbt_oo
                (x2, b_x2), (pq, b_pq), (sg, b_sg) = et[8:11]
                (osq, b_osq), _ = etb
                for t in range(4):
                    act(x2[:], Y5w[:, t, :], AF.Square, [b_Y5w], [b_x2])
                    ts("pool", pq[:], x2[:], 0.044715, 1.0, ALU.mult, ALU.add, [b_x2], [b_pq])
                    tt("pool", pq[:], pq[:], Y5w[:, t, :], ALU.mult, [b_pq, b_Y5w], [b_pq])
                    act(sg[:], pq[:], AF.Sigmoid, [b_pq], [b_sg], scale=2.0 * math.sqrt(2.0 / math.pi))
                    tt("dve", zzv(t), Y5w[:, t, :], sg[:], ALU.mult, [b_Y5w, b_sg], [zzb_(t)])
                    cp("act", zzbv(t), zzv(t), [zzb_(t)], [zzbb(t)])
                for t2 in range(4):
                    for t in range(4):
                        mm(PM[:, 0:W], wglu[:, t, t2 * 128:(t2 + 1) * 128], zzbv(t), [b_wglu, zzbb(t)], [bPM[0]],
                           start=(t == 0), stop=(t == 3), inc=(t == 3))
                    act(sg[:], PM[:, 0:W], AF.Sigmoid, [bPM[0], b_s5v], [b_sg], bias=s5v[:, 1, t2:t2 + 1])
                    tt("dve", oo[:, t2, :], zzv(t2), sg[:], ALU.mult, [zzb_(t2), b_sg], [b_oo])
                for t in range(4):
                    act(osq[:], oo[:, t, :], AF.Square, [b_oo], [b_osq])
                    mm(PA[0][:, 0:W], ones_b[:], osq[:], [b_ones_b, b_osq], [bPA[0]], start=(t == 0), stop=(t == 3))
                act(sg[:], PA[0][:, 0:W], AF.Sqrt, [bPA[0]], [b_sg], bias=1e-6, scale=1.0)
                S.op("dve", lambda e: e.reciprocal(out=sg[:], in_=sg[:]), [b_sg], [b_sg])
                for t in range(4):
                    stt(ycat[:, 4 + t, :], oo[:, t, :], s5v[:, 2, t:t + 1], sg[:], ALU.mult, ALU.mult, [b_oo, b_s5v, b_sg], [b_ycat])
                for st in range(W // 128):
                    xt, bx, xsl_i = xres[st]
                    for hf in range(2):
                        pj, bpj = PJ[hf], bPJ[hf]
                        for k in range(8):
                            mm(pj[:, :], ycat[:, k, st * 128:(st + 1) * 128], Wo[:, k, hf * 512:(hf + 1) * 512], [b_ycat, b_Wo], [bpj],
                               start=(k == 0), stop=(k == 7), inc=(k == 7))
                        tt("dve", xt[:, hf * 512:(hf + 1) * 512], pj[:, :], xt[:, hf * 512:(hf + 1) * 512], ALU.add, [bpj, bx], [bx])
                    orow = win * W + st * 128
                    S.dma(out_d[orow:orow + 128, :], xt[:], "xst%d" % (xsl_i % 2), R=[bx], W=[b_outrows[orow // 128]])

            sbt_oo = (Yw, b_Yw)
            b_outrows = [Buf("orow%d" % i) for i in range(TOK // 128)]

            def dump_win():
                dump('hT', hT[0][0][:, 2, :], hT[0][1])
                dump('Pprev', Pprev[:], b_Pprev)
                dump('AR', AR[:, 1, :, :], b_AR)
                dump('KH', KH[:, 1, :], b_KH)
                dump('BH', BH[:, 1, :], b_BH)
                dump('kT', kT[:, 3, 1, :], b_kT)
                dump('vT', vT[:, 3, 1, :], b_vT)
                dump('GL', GL[:], b_GL)
                dump('Abm', AbmS[1][0][:], AbmS[1][1])
                dump('Akm', AkmS[1][0][:], AkmS[1][1])
                dump('T', lastT[0][0][:], lastT[0][1])
                dump('WT', WT[:], b_WT)
                dump('UT', UT[:], b_UT)
                dump('Sf', Sf[:], b_Sf)
                dump('s5s', s5s[:], b_s5s)
                dump('ufm', ufm[:, 2, :], b_ufm)
                dump('Yw', Yw[:], b_Yw)
                dump('Y5w', Y5w[:], b_Y5w)
                dump('ycat', ycat[:], b_ycat)

            gens = {0: pre(0)}
            drive([gens[0]])
            for idx in range(2 * NWIN):
                nxt = None
                if idx + 1 < 2 * NWIN:
                    if idx + 1 == NWIN:
                        ts("dve", Pprev[:], Pprev[:], maskv[:, 0:1], None, ALU.mult, None, [b_Pprev, b_maskv], [b_Pprev])
                    nxt = pre(idx + 1)
                main(idx, nxt)
                if idx == NWIN - 1:
                    ts("dve", Sf[:].rearrange("p a b -> p (a b)"), Sf[:].rearrange("p a b -> p (a b)"), maskv[:, 0:1], None, ALU.mult, None,
                       [b_Sf, b_maskv], [b_Sf])
                    sbc_t, sbc_b = Sb[cur_s[0] % 2]
                    cp("act", sbc_t[:], Sf[:], [b_Sf], [sbc_b])
                    ts("dve", s5s[:].rearrange("p a b -> p (a b)"), s5s[:].rearrange("p a b -> p (a b)"), maskv[:, 0:1], None, ALU.mult, None,
                       [b_s5s, b_maskv], [b_s5s])
                if stage(('own:%d' % (idx - NWIN)) if idx >= NWIN else ('pre:%d' % idx), dump_win):
                    return nc
            S.barrier()
        esB = ExitStack()
        with esB:
            HS = DFF // 2
            wgb, b_wgb = sbt(esB, [128, 8, DFF], BF16, "wgb")
            wub, b_wub = sbt(esB, [128, 8, DFF], BF16, "wub")
            wdb, b_wdb = sbt(esB, [128, NFF, D], BF16, "wdb")
            stB = [sbt(esB, [128, HS], F32, "stB%d" % i) for i in range(2)]
            fg_bc, b_fg = sbt(esB, [128, D], F32, "fgbc")
            S.dma(fg_bc[:], fgain_d.partition_broadcast(128), "fgbc", W=[b_fg])
            TB = 512
            NSB = TB // 128
            xsB = [sbt(esB, [128, D], F32, "xsB%d" % i) for i in range(NSB + 1)]
            xnB, b_xnB = sbt(esB, [128, D], BF16, "xnB")
            statB, b_statB = sbt(esB, [128, 4], F32, "statB")
            h2T = [sbt(esB, [128, 8, TB], BF16, "h2T%d" % i) for i in range(1)]
            actT, b_actT = sbt(esB, [128, NFF, TB], BF16, "actT")
            sil = [sbt(esB, [128, TB], BF16, "sil%d" % i) for i in range(2)]
            b_wgh = [Buf("wg_h0"), Buf("wg_h1")]
            b_wuh = [Buf("wu_h0"), Buf("wu_h1")]
            b_wdj = [Buf("wd_%d" % j) for j in range(NFF)]
            wprog = [0]

            def wload():
                ci = 0
                for hh in range(2):
                    for (src_d, dst, bl) in ((wg_d, wgb, b_wgh), (wu_d, wub, b_wuh)):
                        for k in range(8):
                            st_t, st_b = stB[ci % 2]
                            S.dma(st_t[:], src_d[k * 128:(k + 1) * 128, hh * HS:(hh + 1) * HS], "stB%d" % (ci % 2), W=[st_b])
                            cp(("act", "dve", "pool")[ci % 3], dst[:, k, hh * HS:(hh + 1) * HS], st_t[:], [st_b], [bl[hh]])
                            ci += 1
                            wprog[0] += 1
                            yield
                for j in range(NFF):
                    st_t, st_b = stB[ci % 2]
                    S.dma(st_t[:, 0:D], wd_d[j * 128:(j + 1) * 128, :], "stB%d" % (ci % 2), W=[st_b])
                    tt(("dve", "pool")[ci % 2], wdb[:, j, :], st_t[:, 0:D], gf_bc[:], ALU.mult, [st_b, b_gf], [b_wdj[j]])
                    ci += 1
                    wprog[0] += 1
                    yield

            wgen = wload()

            def need(n):
                while wprog[0] < n:
                    next(wgen)

            def bgstep():
                try:
                    next(wgen)
                except StopIteration:
                    pass

            xq = [0]
            need(4)
            for wi in range(TOK // TB):
                hT_t, hT_b = h2T[0]
                tiles = []
                for st in range(NSB):
                    ti = wi * NSB + st
                    i = xq[0] % (NSB + 1)
                    xq[0] += 1
                    xt, bx = xsB[i]
                    orow = ti * 128
                    S.dma(xt[:], out_d[orow:orow + 128, :], "xsB%d" % i, R=[b_outrows[ti]], W=[bx])
                    tiles.append((xt, bx, i, orow, ti))
                for st in range(NSB):
                    xt, bx, i, orow, ti = tiles[st]
                    bgstep()
                    bgstep()
                    act(xnB[:], xt[:], AF.Square, [bx], [b_xnB, b_statB], accum=statB[:, 0:1])
                    act(statB[:, 1:2], statB[:, 0:1], AF.Sqrt, [b_statB], [b_statB], bias=1e-6, scale=1.0 / D)
                    S.op("dve", lambda e: e.reciprocal(out=statB[:, 1:2], in_=statB[:, 1:2]), [b_statB], [b_statB])
                    act(xnB[:], xt[:], AF.Identity, [bx, b_statB], [b_xnB], scale=statB[:, 1:2])
                    for k in range(8):
                        tr(PT[:, k * 128:(k + 1) * 128], xnB[:, k * 128:(k + 1) * 128], ident_b[:], [b_xnB, b_ident_b], [bPT], inc=(k == 7))
                    for k in range(8):
                        o = hT_t[:, k, st * 128:(st + 1) * 128]
                        i_ = PT[:, k * 128:(k + 1) * 128]
                        if st % 2 == 0:
                            act(o, i_, AF.Identity, [bPT, b_modp], [hT_b], scale=modp[:, 3, k:k + 1], bias=modp[:, 2, k:k + 1])
                        else:
                            ts("dve", o, i_, modp[:, 3, k:k + 1], modp[:, 2, k:k + 1], ALU.mult, ALU.add, [bPT, b_modp], [hT_b])
                for j in range(NFF):
                    wh = 0 if j < 11 else 1
                    need(16 if wh == 0 else 32)
                    bgstep()
                    pg, bpg = (PJ[0], bPJ[0]) if j % 2 == 0 else (PA[0], bPA[0])
                    pu, bpu = (PJ[1], bPJ[1]) if j % 2 == 0 else (PA[1], bPA[1])
                    for k in range(8):
                        mm(pg[:, 0:TB], wgb[:, k, j * 128:(j + 1) * 128], hT_t[:, k, :], [b_wgh[wh], hT_b], [bpg],
                           start=(k == 0), stop=(k == 7), inc=(k == 7))
                    for k in range(8):
                        mm(pu[:, 0:TB], wub[:, k, j * 128:(j + 1) * 128], hT_t[:, k, :], [b_wuh[wh], hT_b], [bpu],
                           start=(k == 0), stop=(k == 7), inc=(k == 7))
                    s_t, s_b = sil[j % 2]
                    act(s_t[:], pg[:, 0:TB], AF.Silu, [bpg], [s_b])
                    tt("dve", actT[:, j, :], s_t[:], pu[:, 0:TB], ALU.mult, [s_b, bpu], [b_actT])
                for st in range(NSB):
                    xt, bx, i, orow, ti = tiles[st]
                    for hf in range(2):
                        pd, bpd = (PI, bPI[0]) if hf == 0 else (PC, bPC[0])
                        need(32 + NFF)
                        for j in range(NFF):
                            mm(pd[:, :], actT[:, j, st * 128:(st + 1) * 128], wdb[:, j, hf * 512:(hf + 1) * 512], [b_actT, b_wdj[j]], [bpd],
                               start=(j == 0), stop=(j == NFF - 1), inc=(j == NFF - 1))
                        tt("dve", xt[:, hf * 512:(hf + 1) * 512], pd[:, :], xt[:, hf * 512:(hf + 1) * 512], ALU.add, [bpd, bx], [bx])
                    act(xnB[:], xt[:], AF.Square, [bx], [b_xnB, b_statB], accum=statB[:, 2:3])
                    act(statB[:, 3:4], statB[:, 2:3], AF.Sqrt, [b_statB], [b_statB], bias=1e-6, scale=1.0 / D)
                    S.op("dve", lambda e: e.reciprocal(out=statB[:, 3:4], in_=statB[:, 3:4]), [b_statB], [b_statB])
                    stt(xt[:], xt[:], statB[:, 3:4], fg_bc[:], ALU.mult, ALU.mult, [bx, b_statB, b_fg], [bx])
                    S.dma(out_d[orow:orow + 128, :], xt[:], "xstB%d" % i, R=[bx], W=[b_outrows[ti]])
            S.barrier()
    return nc


_NC = None


def _layout_inputs(inp):
    f32 = np.float32
    g = lambda k: np.asarray(inp[k], dtype=f32)
    x = g("x")
    c = g("c")
    shared = {}
    shared["w_ada"] = np.ascontiguousarray(g("w_ada")[0])
    shared["b_ada"] = np.ascontiguousarray(g("b_ada")[0][None, :])
    shared["w_in"] = np.ascontiguousarray(g("w_in")[0])
    shared["mu_l"] = np.ascontiguousarray(g("mu_shift")[0].reshape(14, 128).T)
    v512 = lambda a: a.reshape(4, 128).T
    rw = [g("rw_w0")[0], g("rw_a0")[0], g("rw_k_k")[0], g("rw_k_a")[0], g("rw_r_k")[0].reshape(512),
          g("rw_lnx_w")[0], g("rw_lnx_b")[0]]
    shared["rwp"] = np.ascontiguousarray(np.stack([v512(a) for a in rw], axis=1).reshape(128, 28))
    lo2w = np.zeros((128, 512), f32)
    lo2w[0:64] = g("rw_w2")[0]
    lo2a = np.zeros((128, 512), f32)
    lo2a[64:128] = g("rw_a2")[0]
    shared["lo2w"] = lo2w
    shared["lo2a"] = lo2a
    shared["g2"] = np.ascontiguousarray(g("rw_g2")[0])
    a_re = g("s5_a_re")[0]
    a_im = g("s5_a_im")[0]
    ldt = g("s5_log_dt")[0]
    b_re = g("s5_b_re")[0]
    b_im = g("s5_b_im")[0]
    c_re = g("s5_c_re")[0]
    c_im = g("s5_c_im")[0]
    la = np.zeros((128, 3, 16), f32)
    for pr in range(16):
        for gp in range(2):
            gg = 2 * pr + gp
            la[gp * 64:(gp + 1) * 64, 0, pr] = a_re[gg]
            la[gp * 64:(gp + 1) * 64, 1, pr] = a_im[gg]
            la[gp * 64:(gp + 1) * 64, 2, pr] = ldt[gg]
    shared["s5la"] = la.reshape(128, 48)
    bpre = np.zeros((128, 16, 128), f32)
    bpim = np.zeros((128, 16, 128), f32)
    lbare = np.zeros((128, 16, 128), f32)
    lbaim = np.zeros((128, 16, 128), f32)
    lbldt = np.zeros((128, 16, 128), f32)
    cpre = np.zeros((128, 16, 128), f32)
    cpim = np.zeros((128, 16, 128), f32)
    for pr in range(16):
        tile = pr // 4
        for g8 in range(8):
            gg = tile * 8 + g8
            rows = slice(g8 * 16, g8 * 16 + 16)
            for gp in range(2):
                cols = slice(gp * 64, gp * 64 + 64)
                lbare[rows, pr, cols] = a_re[gg][None, :]
                lbaim[rows, pr, cols] = a_im[gg][None, :]
                lbldt[rows, pr, cols] = ldt[gg]
            if g8 // 2 == pr % 4:
                gp = g8 % 2
                cols = slice(gp * 64, gp * 64 + 64)
                bpre[rows, pr, cols] = b_re[gg].T
                bpim[rows, pr, cols] = b_im[gg].T
        for gp in range(2):
            gg = 2 * pr + gp
            g8 = gg % 8
            cpre[gp * 64:(gp + 1) * 64, pr, g8 * 16:(g8 + 1) * 16] = c_re[gg].T
            cpim[gp * 64:(gp + 1) * 64, pr, g8 * 16:(g8 + 1) * 16] = c_im[gg].T
    shared["bpre"] = bpre.reshape(128, 2048)
    shared["bpim"] = bpim.reshape(128, 2048)
    shared["lbare"] = lbare.reshape(128, 2048)
    shared["lbaim"] = lbaim.reshape(128, 2048)
    shared["lbldt"] = lbldt.reshape(128, 2048)
    shared["cpre"] = cpre.reshape(128, 2048)
    shared["cpim"] = cpim.reshape(128, 2048)
    s5v = [g("s5_d")[0].reshape(512), g("s5_b_glu")[0], g("s5_gain")[0]]
    shared["s5v"] = np.ascontiguousarray(np.stack([v512(a) for a in s5v], axis=1).reshape(128, 12))
    shared["w_glu"] = np.ascontiguousarray(g("s5_w_glu")[0])
    shared["w_out"] = np.ascontiguousarray(g("w_out")[0])
    shared["wg"] = np.ascontiguousarray(g("ffn_w_gate")[0])
    shared["wu"] = np.ascontiguousarray(g("ffn_w_up")[0])
    shared["wd"] = np.ascontiguousarray(g("ffn_w_down")[0])
    shared["fgain"] = np.ascontiguousarray(g("final_gain")[None, :])
    shared["ident"] = np.eye(128, dtype=f32)
    bo = np.zeros((128, 128), f32)
    bo[0:64, 0:64] = 1.0
    bo[64:128, 64:128] = 1.0
    shared["bones"] = bo
    j = np.arange(64)
    strict = (j[:, None] < j[None, :]).astype(f32)
    incl = (j[:, None] <= j[None, :]).astype(f32)
    mar = np.concatenate([strict, incl], axis=1)
    shared["maskar"] = np.concatenate([mar, mar], axis=0)
    low = (j[None, :] < j[:, None]).astype(f32)
    shared["masknt"] = np.concatenate([low, low], axis=0)
    shared["identp"] = np.concatenate([np.eye(64, dtype=f32)] * 2, axis=0)
    shared["idx1"] = np.tile((np.arange(64, dtype=f32) + 1.0)[None, :], (128, 1))
    rst = np.ones((128, 256), f32)
    rst[:, ::64] = 0.0
    shared["rst01"] = rst
    maps = []
    for core in range(8):
        b, s = core // 2, core % 2
        m = dict(shared)
        m["xcat"] = np.ascontiguousarray(np.concatenate([x[b, 0:TOK], x[b, s * TOK:(s + 1) * TOK]], axis=0))
        m["maskv"] = np.full((128, 1), float(s), f32)
        m["c_l"] = np.ascontiguousarray(c[b].reshape(8, 128).T)
        maps.append(m)
    return maps


def kernel(**inputs):
    global _NC
    if _NC is None:
        _NC = build_program()
    maps = _layout_inputs(inputs)
    res = run_bass_kernel_spmd(_NC, maps, core_ids=list(range(8)))
    out = np.zeros((4, 2 * TOK, D), np.float32)
    for core in range(8):
        b, s = core // 2, core % 2
        out[b, s * TOK:(s + 1) * TOK] = res.results[core]["out"]
    return out
```

```python
import math
from contextlib import ExitStack

import numpy as np
import concourse.bass as bass
import concourse.mybir as mybir
from concourse.bass_utils import run_bass_kernel_spmd

F32 = mybir.dt.float32
BF16 = mybir.dt.bfloat16
I32 = mybir.dt.int32
AF = mybir.ActivationFunctionType
ALU = mybir.AluOpType
AX = mybir.AxisListType

D = 1024
TOK = 4096
W = 256
L = 64
NWIN = TOK // W
DFF = 2816
NFF = DFF // 128
TWO_PI = 2.0 * math.pi
PI_SAFE = 3.1415925


class Buf:
    __slots__ = ("name", "w", "r", "excl")

    def __init__(self, name, excl=False):
        self.name = name
        self.w = None
        self.r = {}
        self.excl = excl


class DSem:
    __slots__ = ("handle", "count", "id")


class Sched:
    def __init__(self, nc, es):
        self.nc = nc
        self.es = es
        self.eng = {"pe": nc.tensor, "act": nc.scalar, "dve": nc.vector, "pool": nc.gpsimd, "sp": nc.sync}
        self.sem = {k: es.enter_context(nc.semaphore("sem_" + k)) for k in self.eng}
        self.cnt = {k: 0 for k in self.eng}
        self.seen = {k: {} for k in self.eng}
        self.dsems = {}
        self.nds = 0
        self.const_bufs = []
        self.dead = False

    def _need(self, eng, R, W):
        need = {}

        def add(ev):
            if ev is None:
                return
            k = ev[0]
            if k not in need or need[k][2] < ev[2]:
                need[k] = ev

        for b in R:
            add(b.w)
        for b in W:
            add(b.w)
            for ev in b.r.values():
                add(ev)
        E = self.eng[eng]
        for k, ev in need.items():
            if k == ("e", eng) and eng in ("pe", "sp"):
                continue
            if self.seen[eng].get(k, 0) >= ev[2]:
                continue
            E.wait_ge(ev[1], ev[2])
            self.seen[eng][k] = ev[2]

    def op(self, eng, fn, R=(), W=(), inc=True):
        if self.dead:
            return None
        if any(b.excl for b in R):
            W = list(W) + [b for b in R if b.excl]
            R = [b for b in R if not b.excl]
        self._need(eng, R, W)
        inst = fn(self.eng[eng])
        val = self.cnt[eng] + 1
        if inc:
            inst.then_inc(self.sem[eng], 1)
            self.cnt[eng] = val
        ev = (("e", eng), self.sem[eng], val)
        for b in R:
            b.r[ev[0]] = ev
        for b in W:
            b.w = ev
            b.r = {}
        return inst

    def _dsem(self, key):
        if key not in self.dsems:
            d = DSem()
            d.handle = self.es.enter_context(self.nc.semaphore("dsem%d" % self.nds))
            d.count = 0
            d.id = self.nds
            self.nds += 1
            self.dsems[key] = d
        return self.dsems[key]

    def dma(self, out, in_, key, R=(), W=(), const=False):
        if self.dead:
            return
        self._need("sp", R, W)
        d = self._dsem(key)
        d.count += 16
        self.nc.sync.dma_start(out=out, in_=in_).then_inc(d.handle, 16)
        ev = (("d", d.id), d.handle, d.count)
        for b in R:
            b.r[ev[0]] = ev
        for b in W:
            b.w = ev
            b.r = {}
            if const:
                self.const_bufs.append(b)

    def finalize_consts(self, key):
        d = self._dsem(key)
        ev = (("d", d.id), d.handle, d.count)
        for b in self.const_bufs:
            b.w = ev
        self.const_bufs = []

    def barrier(self):
        for e, E in self.eng.items():
            for f in self.eng:
                if f == e:
                    continue
                k = ("e", f)
                if self.cnt[f] > self.seen[e].get(k, 0):
                    E.wait_ge(self.sem[f], self.cnt[f])
                    self.seen[e][k] = self.cnt[f]
            for d in self.dsems.values():
                k = ("d", d.id)
                if d.count > self.seen[e].get(k, 0):
                    E.wait_ge(d.handle, d.count)
                    self.seen[e][k] = d.count


class _StopBuild(Exception):
    pass


_DBG = {"stop": None, "dumps": [], "meta": []}


def build_program():
    nc = bass.Bass("TRN2", target_bir_lowering=False)
    dbg_on = _DBG["stop"] is not None
    if dbg_on:
        dbg_d = nc.dram_tensor("dbg", [128, 65536], F32, kind="ExternalOutput").ap()
        _DBG["meta"] = []

    def din(name, shape):
        return nc.dram_tensor(name, list(shape), F32, kind="ExternalInput").ap()

    xcat = din("xcat", [2 * TOK, D])
    maskv_d = din("maskv", [128, 1])
    c_d = din("c_l", [128, 8])
    wada_d = din("w_ada", [D, 6 * D])
    bada_d = din("b_ada", [1, 6 * D])
    win_d = din("w_in", [D, 2304])
    mu_d = din("mu_l", [128, 14])
    rwp_d = din("rwp", [128, 28])
    lo2w_d = din("lo2w", [128, 512])
    lo2a_d = din("lo2a", [128, 512])
    g2_d = din("g2", [128, 512])
    s5la_d = din("s5la", [128, 48])
    bpre_d = din("bpre", [128, 2048])
    bpim_d = din("bpim", [128, 2048])
    lbare_d = din("lbare", [128, 2048])
    lbaim_d = din("lbaim", [128, 2048])
    lbldt_d = din("lbldt", [128, 2048])
    cpre_d = din("cpre", [128, 2048])
    cpim_d = din("cpim", [128, 2048])
    s5v_d = din("s5v", [128, 12])
    wglu_d = din("w_glu", [512, 512])
    wout_d = din("w_out", [D, D])
    wg_d = din("wg", [D, DFF])
    wu_d = din("wu", [D, DFF])
    wd_d = din("wd", [DFF, D])
    fgain_d = din("fgain", [1, D])
    ident_d = din("ident", [128, 128])
    bones_d = din("bones", [128, 128])
    maskar_d = din("maskar", [128, 128])
    masknt_d = din("masknt", [128, 64])
    identp_d = din("identp", [128, 64])
    idx1_d = din("idx1", [128, 64])
    rst_d = din("rst01", [128, 256])
    out_d = nc.dram_tensor("out", [TOK, D], F32, kind="ExternalOutput").ap()

    es = ExitStack()
    with es:
        S = Sched(nc, es)
        uid = [0]

        def sbt(stack, shape, dt, nm="t"):
            uid[0] += 1
            name = "%s_%d" % (nm, uid[0])
            t = stack.enter_context(nc.sbuf_tensor(name, list(shape), dt))
            return t, Buf(name)

        def pst(nm, dt, n):
            t = es.enter_context(nc.psum_tensor(nm, [128, n], dt))
            return t

        PT = pst("PT", BF16, 1024)
        bPT = Buf("PT", True)
        PJ = [pst("PJ0", F32, 512), pst("PJ1", F32, 512)]
        bPJ = [Buf("PJ0", True), Buf("PJ1", True)]
        PM = pst("PM", F32, 512)
        _bpm = Buf("PM", True)
        bPM = [_bpm, _bpm]
        PA = [pst("PA0", F32, 512), pst("PA1", F32, 512)]
        bPA = [Buf("PA0", True), Buf("PA1", True)]
        PI = pst("PI", F32, 512)
        _bpi = Buf("PI", True)
        bPI = [_bpi, _bpi]
        PC = pst("PC", F32, 512)
        _bpc = Buf("PC", True)
        bPC = [_bpc, _bpc]

        def tt(eng, out, in0, in1, op, R, Wb):
            return S.op(eng, lambda e: e.tensor_tensor(out=out, in0=in0, in1=in1, op=op), R, Wb)

        def ts(eng, out, in0, s1, s2, op0, op1, R, Wb):
            if op1 is None and eng == "pool" and op0 == ALU.mult:
                return S.op(eng, lambda e: e.tensor_scalar(out=out, in0=in0, scalar1=s1, scalar2=0.0, op0=op0, op1=ALU.add), R, Wb)
            if op1 is None:
                return S.op(eng, lambda e: e.tensor_scalar(out=out, in0=in0, scalar1=s1, scalar2=None, op0=op0), R, Wb)
            return S.op(eng, lambda e: e.tensor_scalar(out=out, in0=in0, scalar1=s1, scalar2=s2, op0=op0, op1=op1), R, Wb)

        def stt(out, in0, scalar, in1, op0, op1, R, Wb):
            return S.op("dve", lambda e: e.scalar_tensor_tensor(out=out, in0=in0, scalar=scalar, in1=in1, op0=op0, op1=op1), R, Wb)

        def act(out, in_, func, R, Wb, bias=None, scale=None, accum=None):
            kw = {}
            if bias is not None:
                kw["bias"] = bias
            if scale is not None:
                kw["scale"] = scale
            if accum is not None:
                kw["accum_out"] = accum
            return S.op("act", lambda e: e.activation(out=out, in_=in_, func=func, **kw), R, Wb)

        def cp(eng, out, in_, R, Wb):
            if eng == "act":
                return act(out, in_, AF.Identity, R, Wb)
            return S.op(eng, lambda e: e.tensor_copy(out=out, in_=in_), R, Wb)

        def mm(out, lhsT, rhs, R, Wb, start=True, stop=True, inc=True):
            return S.op("pe", lambda e: e.matmul(out, lhsT=lhsT, rhs=rhs, start=start, stop=stop), R, Wb, inc=inc)

        def tr(out, in_, ident, R, Wb, inc=True):
            return S.op("pe", lambda e: e.transpose(out, in_, ident), R, Wb, inc=inc)

        def bc1(ap, n):
            return ap.unsqueeze(1).to_broadcast([ap.shape[0], n, ap.shape[1]])

        def bc2(ap, n):
            return ap.unsqueeze(2).to_broadcast([ap.shape[0], ap.shape[1], n])

        dbg_off = [0]
        if dbg_on:
            dstage = [sbt(es, [128, 2048], F32, "dstage%d" % i) for i in range(1)]
        dcount = [0]

        def dump(name, ap, buf):
            shape = list(ap.shape)
            n = 1
            for d_ in shape[1:]:
                n *= d_
            P_ = shape[0]
            st_t, st_b = dstage[0]
            dcount[0] += 1
            dst = st_t[0:P_, 0:n]
            if len(shape) == 3:
                dst = dst.rearrange("p (a b) -> p a b", b=shape[2])
            elif len(shape) == 4:
                dst = dst.rearrange("p (a b c) -> p a b c", b=shape[2], c=shape[3])
            cp("dve", dst, ap, [buf], [st_b])
            S.dma(dbg_d[0:P_, dbg_off[0]:dbg_off[0] + n], st_t[0:P_, 0:n], "dstage0", R=[st_b])
            _DBG["meta"].append((name, dbg_off[0], shape))
            dbg_off[0] += n

        def sub(name):
            if dbg_on and _DBG.get("sub") == name:
                S.dead = True

        def stage(name, fn=None):
            if dbg_on and _DBG["stop"] == name:
                S.dead = False
                if fn is not None:
                    fn()
                S.barrier()
                return True
            return False

        ident_f, b_ident_f = sbt(es, [128, 128], F32, "identf")
        ident_b, b_ident_b = sbt(es, [128, 128], BF16, "identb")
        maskv, b_maskv = sbt(es, [128, 1], F32, "maskv")
        modp, b_modp = sbt(es, [128, 4, 8], F32, "modp")
        gf_bc, b_gf = sbt(es, [128, D], F32, "gfbc")

        S.dma(ident_f[:], ident_d, "const", W=[b_ident_f], const=True)
        S.dma(maskv[:], maskv_d, "const", W=[b_maskv], const=True)

        esA = ExitStack()
        with esA:
            W1, b_W1 = sbt(esA, [128, 8, 2304], BF16, "W1")
            Wo, b_Wo = sbt(esA, [128, 8, D], BF16, "Wo")
            wglu, b_wglu = sbt(esA, [128, 4, 512], BF16, "wglu")
            lo2w, b_lo2w = sbt(esA, [128, 512], BF16, "lo2w")
            lo2a, b_lo2a = sbt(esA, [128, 512], BF16, "lo2a")
            g2, b_g2 = sbt(esA, [128, 512], BF16, "g2")
            bones_b, b_bones_b = sbt(esA, [128, 128], BF16, "bonesb")
            bones_f, b_bones_f = sbt(esA, [128, 128], F32, "bonesf")
            ones_b, b_ones_b = sbt(esA, [128, 128], BF16, "onesb")
            maskar, b_maskar = sbt(esA, [128, 128], F32, "maskar")
            masknt, b_masknt = sbt(esA, [128, 64], F32, "masknt")
            identp, b_identp = sbt(esA, [128, 64], F32, "identp")
            rst01, b_rst = sbt(esA, [128, 256], F32, "rst01")
            mu, b_mu = sbt(esA, [128, 14], F32, "mu")
            omu, b_omu = sbt(esA, [128, 14], F32, "omu")
            rwp, b_rwp = sbt(esA, [128, 8, 4], F32, "rwp")
            s5v, b_s5v = sbt(esA, [128, 3, 4], F32, "s5v")
            Bb_re, b_Bbre = sbt(esA, [128, 16, 128], BF16, "Bbre")
            Bb_im, b_Bbim = sbt(esA, [128, 16, 128], BF16, "Bbim")
            Cp_re, b_Cpre = sbt(esA, [128, 16, 128], BF16, "Cpre")
            Cp_imn, b_Cpim = sbt(esA, [128, 16, 128], BF16, "Cpimn")
            tcos, b_tcos = sbt(esA, [128, 16, 64], F32, "tcos")
            tsin, b_tsin = sbt(esA, [128, 16, 64], F32, "tsin")
            rho0, b_rho0 = sbt(esA, [128, 16, 64], F32, "rho0")
            rho1, b_rho1 = sbt(esA, [128, 16], F32, "rho1")
            T63, b_T63 = sbt(esA, [128, 2, 2, 16], F32, "T63")

            S.dma(maskar[:], maskar_d, "const", W=[b_maskar], const=True)
            S.dma(masknt[:], masknt_d, "const", W=[b_masknt], const=True)
            S.dma(identp[:], identp_d, "const", W=[b_identp], const=True)
            S.dma(rst01[:], rst_d, "const", W=[b_rst], const=True)
            S.dma(mu[:], mu_d, "const", W=[b_mu], const=True)
            S.dma(rwp[:, 0:7, :].rearrange("p a b -> p (a b)"), rwp_d, "const", W=[b_rwp], const=True)
            S.dma(s5v[:].rearrange("p a b -> p (a b)"), s5v_d, "const", W=[b_s5v], const=True)
            S.dma(bones_f[:], bones_d, "const", W=[b_bones_f], const=True)

            esS = ExitStack()
            with esS:
                c_l, b_c = sbt(esS, [128, 8], F32, "c")
                c_act, b_cact = sbt(esS, [128, 8], F32, "cact")
                c_rep, b_crep = sbt(esS, [128, 8, 128], F32, "crep")
                adaR, b_adaR = sbt(esS, [128, 6 * D], F32, "adaR")
                badab, b_badab = sbt(esS, [128, 6 * D], F32, "badab")
                ada_fm, b_adafm = sbt(esS, [128, 48], F32, "adafm")
                stg = [sbt(esS, [128, 8, 512], F32, "stg%d" % i) for i in range(2)]
                lst = [sbt(esS, [128, 512], F32, "lst%d" % i) for i in range(3)]
                S.dma(lst[0][0][:], lo2w_d, "const", W=[lst[0][1]], const=True)
                S.dma(lst[1][0][:], lo2a_d, "const", W=[lst[1][1]], const=True)
                S.dma(lst[2][0][:], g2_d, "const", W=[lst[2][1]], const=True)
                S.dma(c_l[:], c_d, "const", W=[b_c], const=True)
                S.dma(badab[:], bada_d.partition_broadcast(128), "const", W=[b_badab], const=True)
                S.finalize_consts("const")

                cp("dve", ident_b[:], ident_f[:], [b_ident_f], [b_ident_b])
                cp("dve", bones_b[:], bones_f[:], [b_bones_f], [b_bones_b])
                S.op("pool", lambda e: e.memset(ones_b[:], 1.0 / 512.0), [], [b_ones_b])
                ts("dve", bones_f[:], bones_f[:], 1.0 / 64.0, None, ALU.mult, None, [b_bones_f], [b_bones_f])
                ts("dve", omu[:], mu[:], -1.0, 1.0, ALU.mult, ALU.add, [b_mu], [b_omu])
                ts("dve", rwp[:, 7, :], rwp[:, 3, :], -1.0, 1.0, ALU.mult, ALU.add, [b_rwp], [b_rwp])
                cp("act", lo2w[:], lst[0][0][:], [lst[0][1]], [b_lo2w])
                cp("act", lo2a[:], lst[1][0][:], [lst[1][1]], [b_lo2a])
                cp("act", g2[:], lst[2][0][:], [lst[2][1]], [b_g2])

                act(c_act[:], c_l[:], AF.Silu, [b_c], [b_cact])
                cp("dve", c_rep[:], bc2(c_act[:], 128), [b_cact], [b_crep])
                wada_v = wada_d.rearrange("(k p) n -> p k n", p=128)
                for blk in range(12):
                    st_t, st_b = stg[blk % 2]
                    S.dma(st_t[:], wada_v[:, :, blk * 512:(blk + 1) * 512], "stg%d" % (blk % 2), W=[st_b])
                    pj, bpj = PJ[blk % 2], bPJ[blk % 2]
                    for k in range(8):
                        mm(pj[:, :], c_rep[:, k, :], st_t[:, k, :], [b_crep, st_b], [bpj],
                           start=(k == 0), stop=(k == 7), inc=(k == 7))
                    tt("dve", adaR[:, blk * 512:(blk + 1) * 512], pj[:, :], badab[:, blk * 512:(blk + 1) * 512],
                       ALU.add, [bpj, b_badab], [b_adaR])
                tt("dve", badab[:].rearrange("p (j q) -> p j q", q=128), adaR[:].rearrange("p (j q) -> p j q", q=128),
                   bc1(ident_f[:], 48), ALU.mult, [b_adaR, b_ident_f, b_badab], [b_badab])
                S.op("dve", lambda e: e.tensor_reduce(out=ada_fm[:], in_=badab[:].rearrange("p (j q) -> p j q", q=128),
                                                       axis=AX.X, op=ALU.add), [b_badab], [b_adafm])
                cp("dve", modp[:, 0, :], ada_fm[:, 0:8], [b_adafm], [b_modp])
                ts("dve", modp[:, 1, :], ada_fm[:, 8:16], 1.0, None, ALU.add, None, [b_adafm], [b_modp])
                cp("dve", modp[:, 2, :], ada_fm[:, 24:32], [b_adafm], [b_modp])
                ts("dve", modp[:, 3, :], ada_fm[:, 32:40], 1.0, None, ALU.add, None, [b_adafm], [b_modp])
                cp("dve", gf_bc[:], adaR[:, 5 * D:6 * D], [b_adaR], [b_gf])

                ci = 0
                for k in range(8):
                    st_t, st_b = stg[k % 2]
                    stv = st_t[:].rearrange("p a b -> p (a b)")
                    S.dma(stv[:, 0:2304], win_d[k * 128:(k + 1) * 128, :], "stg%d" % (k % 2), W=[st_b])
                    cp(("act", "dve", "pool")[ci % 3], W1[:, k, :], stv[:, 0:2304], [st_b], [b_W1])
                    ci += 1
                for k in range(8):
                    st_t, st_b = stg[k % 2]
                    stv = st_t[:].rearrange("p a b -> p (a b)")
                    S.dma(stv[:, 0:D], wout_d[k * 128:(k + 1) * 128, :], "stg%d" % (k % 2), W=[st_b])
                    tt("dve", Wo[:, k, :], stv[:, 0:D], adaR[:, 2 * D:3 * D], ALU.mult, [st_b, b_adaR], [b_Wo])
                for k in range(4):
                    st_t, st_b = stg[k % 2]
                    stv = st_t[:].rearrange("p a b -> p (a b)")
                    S.dma(stv[:, 0:512], wglu_d[k * 128:(k + 1) * 128, :], "stg%d" % (k % 2), W=[st_b])
                    cp("act", wglu[:, k, :], stv[:, 0:512], [st_b], [b_wglu])

                if stage('setup1', lambda: (dump('modp', modp[:], b_modp), dump('gf', gf_bc[:], b_gf), dump('W1', W1[:, 3, 0:2048], b_W1), dump('Wo', Wo[:, 2, :], b_Wo))):
                    return nc
                S.barrier()
            esS2 = ExitStack()
            with esS2:
                s5la, b_s5la = sbt(esS2, [128, 3, 16], F32, "s5la")
                idx1, b_idx1 = sbt(esS2, [128, 64], F32, "idx1")
                lb = {}
                for nm, dd in (("bpre", bpre_d), ("bpim", bpim_d), ("are", lbare_d), ("aim", lbaim_d), ("ldt", lbldt_d)):
                    lb[nm] = sbt(esS2, [128, 2048], F32, "lb" + nm)
                    S.dma(lb[nm][0][:], dd, "const2", W=[lb[nm][1]], const=True)
                cst = [sbt(esS2, [128, 2048], F32, "cst%d" % i) for i in range(1)]
                S.dma(s5la[:].rearrange("p a b -> p (a b)"), s5la_d, "const2", W=[b_s5la], const=True)
                S.dma(idx1[:], idx1_d, "const2", W=[b_idx1], const=True)
                S.finalize_consts("const2")
                tmp = [sbt(esS2, [128, 1024], F32, "s5t%d" % i) for i in range(8)]
                tmpi, b_tmpi = sbt(esS2, [128, 1024], I32, "s5ti")

                def sincos(ang, b_ang, n, o_sin, b_osin, o_cos, b_ocos, ta, tb):
                    for shift, o, bo in ((0.0, o_sin, b_osin), (0.5 * math.pi, o_cos, b_ocos)):
                        ts("dve", ta[0][:, 0:n], ang, shift, 1.0 / TWO_PI, ALU.add, ALU.mult, [b_ang], [ta[1]])
                        cp("dve", tmpi[:, 0:n], ta[0][:, 0:n], [ta[1]], [b_tmpi])
                        cp("dve", tb[0][:, 0:n], tmpi[:, 0:n], [b_tmpi], [tb[1]])
                        ts("dve", ta[0][:, 0:n], ang, shift, None, ALU.add, None, [b_ang], [ta[1]])
                        stt(ta[0][:, 0:n], tb[0][:, 0:n], -TWO_PI, ta[0][:, 0:n], ALU.mult, ALU.add, [tb[1], ta[1]], [ta[1]])
                        ts("dve", ta[0][:, 0:n], ta[0][:, 0:n], -PI_SAFE, PI_SAFE, ALU.max, ALU.min, [ta[1]], [ta[1]])
                        act(o, ta[0][:, 0:n], AF.Sin, [ta[1]], [bo])

                t_dt, t_xr, t_xi, t_mag, t_sin, t_cos, t_a, t_b = tmp
                Bre_flat = Bb_re[:].rearrange("p a b -> p (a b)")
                Bim_flat = Bb_im[:].rearrange("p a b -> p (a b)")
                for hh in range(2):
                    c2 = slice(hh * 1024, (hh + 1) * 1024)
                    are_t, are_b = lb["are"]
                    aim_t, aim_b = lb["aim"]
                    ldt_t, ldt_b = lb["ldt"]
                    bre_t, bre_b = lb["bpre"]
                    bim_t, bim_b = lb["bpim"]
                    A_re, A_im = are_t[:, c2], aim_t[:, c2]
                    act(t_dt[0][:], ldt_t[:, c2], AF.Exp, [ldt_b], [t_dt[1]])
                    tt("dve", t_xr[0][:], t_dt[0][:], A_re, ALU.mult, [t_dt[1], are_b], [t_xr[1]])
                    tt("dve", t_xi[0][:], t_dt[0][:], A_im, ALU.mult, [t_dt[1], aim_b], [t_xi[1]])
                    act(t_mag[0][:], t_xr[0][:], AF.Exp, [t_xr[1]], [t_mag[1]])
                    sincos(t_xi[0][:], t_xi[1], 1024, t_sin[0][:], t_sin[1], t_cos[0][:], t_cos[1], t_a, t_b)
                    tt("dve", t_cos[0][:], t_cos[0][:], t_mag[0][:], ALU.mult, [t_cos[1], t_mag[1]], [t_cos[1]])
                    ts("dve", t_cos[0][:], t_cos[0][:], -1.0, None, ALU.add, None, [t_cos[1]], [t_cos[1]])
                    tt("dve", t_sin[0][:], t_sin[0][:], t_mag[0][:], ALU.mult, [t_sin[1], t_mag[1]], [t_sin[1]])
                    tt("dve", t_dt[0][:], A_re, A_re, ALU.mult, [are_b], [t_dt[1]])
                    tt("dve", t_xr[0][:], A_im, A_im, ALU.mult, [aim_b], [t_xr[1]])
                    tt("dve", t_dt[0][:], t_dt[0][:], t_xr[0][:], ALU.add, [t_dt[1], t_xr[1]], [t_dt[1]])
                    S.op("dve", lambda e: e.reciprocal(out=t_dt[0][:], in_=t_dt[0][:]), [t_dt[1]], [t_dt[1]])
                    tt("dve", t_xr[0][:], t_cos[0][:], A_re, ALU.mult, [t_cos[1], are_b], [t_xr[1]])
                    tt("dve", t_a[0][:], t_sin[0][:], A_im, ALU.mult, [t_sin[1], aim_b], [t_a[1]])
                    tt("dve", t_xr[0][:], t_xr[0][:], t_a[0][:], ALU.add, [t_xr[1], t_a[1]], [t_xr[1]])
                    tt("dve", t_xr[0][:], t_xr[0][:], t_dt[0][:], ALU.mult, [t_xr[1], t_dt[1]], [t_xr[1]])
                    tt("dve", t_xi[0][:], t_sin[0][:], A_re, ALU.mult, [t_sin[1], are_b], [t_xi[1]])
                    tt("dve", t_a[0][:], t_cos[0][:], A_im, ALU.mult, [t_cos[1], aim_b], [t_a[1]])
                    tt("dve", t_xi[0][:], t_xi[0][:], t_a[0][:], ALU.subtract, [t_xi[1], t_a[1]], [t_xi[1]])
                    tt("dve", t_xi[0][:], t_xi[0][:], t_dt[0][:], ALU.mult, [t_xi[1], t_dt[1]], [t_xi[1]])
                    tt("dve", t_a[0][:], t_xr[0][:], bre_t[:, c2], ALU.mult, [t_xr[1], bre_b], [t_a[1]])
                    tt("dve", t_b[0][:], t_xi[0][:], bim_t[:, c2], ALU.mult, [t_xi[1], bim_b], [t_b[1]])
                    tt("dve", Bre_flat[:, c2], t_a[0][:], t_b[0][:], ALU.subtract, [t_a[1], t_b[1]], [b_Bbre])
                    tt("dve", t_a[0][:], t_xr[0][:], bim_t[:, c2], ALU.mult, [t_xr[1], bim_b], [t_a[1]])
                    tt("dve", t_b[0][:], t_xi[0][:], bre_t[:, c2], ALU.mult, [t_xi[1], bre_b], [t_b[1]])
                    tt("dve", Bim_flat[:, c2], t_a[0][:], t_b[0][:], ALU.add, [t_a[1], t_b[1]], [b_Bbim])
                S.dma(cst[0][0][:], cpre_d, "cst", W=[cst[0][1]])
                cp("act", Cp_re[:].rearrange("p a b -> p (a b)"), cst[0][0][:], [cst[0][1]], [b_Cpre])
                S.dma(cst[0][0][:], cpim_d, "cst", W=[cst[0][1]])
                S.op("act", lambda e: e.mul(Cp_imn[:].rearrange("p a b -> p (a b)"), cst[0][0][:], -1.0), [cst[0][1]], [b_Cpim])
                la_dt, la_th = t_mag, t_dt
                act(la_dt[0][:, 0:16], s5la[:, 2, :], AF.Exp, [b_s5la], [la_dt[1]])
                tt("dve", la_th[0][:, 0:16], la_dt[0][:, 0:16], s5la[:, 1, :], ALU.mult, [la_dt[1], b_s5la], [la_th[1]])
                tt("dve", la_dt[0][:, 0:16], la_dt[0][:, 0:16], s5la[:, 0, :], ALU.mult, [la_dt[1], b_s5la], [la_dt[1]])
                act(rho1[:], la_dt[0][:, 0:16], AF.Exp, [la_dt[1]], [b_rho1])
                tt("dve", t_xr[0][:, 0:1024].rearrange("p (a b) -> p a b", b=64), bc2(la_th[0][:, 0:16], 64), bc1(idx1[:], 16),
                   ALU.mult, [la_th[1], b_idx1], [t_xr[1]])
                sincos(t_xr[0][:, 0:1024], t_xr[1], 1024, tsin[:].rearrange("p a b -> p (a b)"), b_tsin,
                       tcos[:].rearrange("p a b -> p (a b)"), b_tcos, t_a, t_b)
                cp("dve", rho0[:], bc2(rho1[:], 64), [b_rho1], [b_rho0])
                S.op("dve", lambda e: e.memset(rho0[:, :, 0:1], 0.0), [], [b_rho0])
                tt("dve", T63[:, 0, 0, :], tcos[:, :, 63], rho1[:], ALU.mult, [b_tcos, b_rho1], [b_T63])
                tt("dve", T63[:, 0, 1, :], tsin[:, :, 63], rho1[:], ALU.mult, [b_tsin, b_rho1], [b_T63])
                tt("dve", T63[:, 1, 1, :], tcos[:, :, 63], rho1[:], ALU.mult, [b_tcos, b_rho1], [b_T63])
                stt(T63[:, 1, 0, :], tsin[:, :, 63], -1.0, rho1[:], ALU.mult, ALU.mult, [b_tsin, b_rho1], [b_T63])
                if stage('setup2', lambda: (dump('Bbre', Bb_re[:, 5, :], b_Bbre), dump('Bbim', Bb_im[:, 5, :], b_Bbim), dump('tcos', tcos[:], b_tcos), dump('tsin', tsin[:], b_tsin), dump('rho1', rho1[:], b_rho1), dump('rho0', rho0[:, 3, :], b_rho0), dump('Cpimn', Cp_imn[:, 9, :], b_Cpim))):
                    return nc
                S.barrier()

            xs = [sbt(esA, [128, D], F32, "xs%d" % i) for i in range(2)]
            xnb = [sbt(esA, [128, D], BF16, "xnb%d" % i) for i in range(1)]
            stat, b_stat = sbt(esA, [128, 8], F32, "stat")
            hT = [sbt(esA, [128, 8, W], BF16, "hT%d" % i) for i in range(1)]
            Pprev, b_Pprev = sbt(esA, [128, 14], F32, "Pprev")
            Pb = [sbt(esA, [128, W + 1], F32, "Pb%d" % i) for i in range(2)]
            zt = [sbt(esA, [128, W], F32, "zt%d" % i) for i in range(4)]
            LA, b_LA = sbt(esA, [128, W], BF16, "LA")
            gs, b_gs = sbt(esA, [128, W], BF16, "gs")
            ARs = [sbt(esA, [128, 4, 4, 128], BF16, "AR%d" % i) for i in range(2)]
            KHs = [sbt(esA, [128, 4, W], BF16, "KH%d" % i) for i in range(2)]
            BHs = [sbt(esA, [128, 4, W], BF16, "BH%d" % i) for i in range(2)]
            VBs = [sbt(esA, [128, 4, W], BF16, "VB%d" % i) for i in range(2)]
            kT, b_kT = sbt(esA, [128, 4, 4, 64], BF16, "kT")
            bT, b_bT = sbt(esA, [128, 4, 4, 64], BF16, "bT")
            vT, b_vT = sbt(esA, [128, 4, 4, 64], BF16, "vT")
            GLs = [sbt(esA, [128, 4, 4], F32, "GL%d" % i) for i in range(2)]
            gfms = [sbt(esA, [128, 4, W], BF16, "gfm%d" % i) for i in range(2)]
            bvs = [sbt(esA, [128, 4, W], BF16, "bv%d" % i) for i in range(2)]
            Yw, b_Yw = sbt(esA, [128, 4, W], F32, "Yw")
            ufms = [sbt(esA, [128, 4, W], BF16, "ufm%d" % i) for i in range(2)]
            Y5w, b_Y5w = sbt(esA, [128, 4, W], F32, "Y5w")
            ycat, b_ycat = sbt(esA, [128, 8, W], BF16, "ycat")
            et = [sbt(esA, [128, W], F32, "et%d" % i) for i in range(12)]
            etb = [sbt(esA, [128, W], BF16, "etb%d" % i) for i in range(2)]
            AbmS = [sbt(esA, [128, 4, 128], BF16, "Abm%d" % i) for i in range(2)]
            AkmS = [sbt(esA, [128, 4, 128], BF16, "Akm%d" % i) for i in range(2)]
            TfinS = [sbt(esA, [128, 4, 64], BF16, "Tfin%d" % i) for i in range(2)]
            ImT, b_ImT = sbt(esA, [128, 4, 64], F32, "ImT")
            Xb = [sbt(esA, [128, 4, 64], BF16, "Xb%d" % i) for i in range(2)]
            XTb = [sbt(esA, [128, 4, 64], BF16, "XTb%d" % i) for i in range(2)]
            Pm = [sbt(esA, [128, 4, 64], BF16, "Pm%d" % i) for i in range(2)]
            NTb, b_NTb = sbt(esA, [128, 4, 64], BF16, "NTb")
            Rb, b_Rb = Xb[0]
            TTb, b_TTb = Xb[1]
            WT, b_WT = sbt(esA, [128, 4, 64], BF16, "WT")
            UT, b_UT = sbt(esA, [128, 4, 64], BF16, "UT")
            Sf, b_Sf = sbt(esA, [128, 4, 64], F32, "Sf")
            Sb = [sbt(esA, [128, 4, 64], BF16, "Sb%d" % i) for i in range(2)]
            stmp, b_stmp = sbt(esA, [128, 4, 64], F32, "stmp")
            s5a = [sbt(esA, [128, 512], F32, "s5a%d" % i) for i in range(4)]
            s_tr, b_str = sbt(esA, [128, 512], F32, "str")
            s_ti, b_sti = sbt(esA, [128, 512], F32, "sti")
            srb, b_srb = sbt(esA, [128, 512], BF16, "srb")
            sib, b_sib = sbt(esA, [128, 512], BF16, "sib")
            s5s, b_s5s = sbt(esA, [128, 2, 16], F32, "s5s")
            s5q, b_s5q = sbt(esA, [128, 6, 8], F32, "s5q")

            S.op("dve", lambda e: e.memset(Pprev[:], 0.0), [], [b_Pprev])
            S.op("dve", lambda e: e.memset(Sf[:], 0.0), [], [b_Sf])
            S.op("dve", lambda e: e.memset(Sb[0][0][:], 0.0), [], [Sb[0][1]])
            S.op("dve", lambda e: e.memset(s5s[:], 0.0), [], [b_s5s])
            S.op("dve", lambda e: e.memset(ARs[0][0][:], 0.0), [], [ARs[0][1]])
            S.op("dve", lambda e: e.memset(ARs[1][0][:], 0.0), [], [ARs[1][1]])
            M = {}

            cur_s = [0]
            lastT = [None]
            xslot = [0]

            def load_x(row0):
                i = xslot[0] % 2
                xslot[0] += 1
                t, b = xs[i]
                S.dma(t[:], xcat[row0:row0 + 128, :], "xs%d" % i, W=[b])
                return t, b

            def norm_transpose(xt, bx, st, hT_t, hT_b, sh_i, sc_i, width):
                xn_t, xn_b = xnb[0]
                col = st % 4
                act(xn_t[:], xt[:], AF.Square, [bx], [xn_b, b_stat], accum=stat[:, col:col + 1])
                act(stat[:, 4 + col:5 + col], stat[:, col:col + 1], AF.Sqrt, [b_stat], [b_stat], bias=1e-6, scale=1.0 / D)
                S.op("dve", lambda e: e.reciprocal(out=stat[:, 4 + col:5 + col], in_=stat[:, 4 + col:5 + col]), [b_stat], [b_stat])
                act(xn_t[:], xt[:], AF.Identity, [bx, b_stat], [xn_b], scale=stat[:, 4 + col:5 + col])
                for k in range(8):
                    tr(PT[:, k * 128:(k + 1) * 128], xn_t[:, k * 128:(k + 1) * 128], ident_b[:], [xn_b, b_ident_b], [bPT], inc=(k == 7))
                for k in range(8):
                    o = hT_t[:, k, st * 128:(st + 1) * 128]
                    i_ = PT[:, k * 128:(k + 1) * 128]
                    if True:
                        act(o, i_, AF.Identity, [bPT, b_modp], [hT_b], scale=modp[:, sc_i, k:k + 1], bias=modp[:, sh_i, k:k + 1])
                    else:
                        ts("dve", o, i_, modp[:, sc_i, k:k + 1], modp[:, sh_i, k:k + 1], ALU.mult, ALU.add, [bPT, b_modp], [hT_b])

            pbi = [0]
            pji = [0]

            def project(hT_t, hT_b, ct):
                pj, bpj = PA[1], bPA[1]
                for k in range(8):
                    mm(pj[:, 0:W], W1[:, k, ct * 128:(ct + 1) * 128], hT_t[:, k, :], [b_W1, hT_b], [bpj],
                       start=(k == 0), stop=(k == 7), inc=(k == 7))
                return pj, bpj

            def shifted(hT_t, hT_b, ct, zo, b_zo):
                pj, bpj = project(hT_t, hT_b, ct)
                p_t, p_b = Pb[pbi[0] % 2]
                pbi[0] += 1
                cp("pool", p_t[:, 0:1], Pprev[:, ct:ct + 1], [b_Pprev], [p_b])
                act(p_t[:, 1:W + 1], pj[:, 0:W], AF.Identity, [bpj], [p_b])
                cp("pool", Pprev[:, ct:ct + 1], p_t[:, W:W + 1], [p_b], [b_Pprev])
                act(zo, p_t[:, 0:W], AF.Identity, [p_b, b_mu], [b_zo], scale=mu[:, ct:ct + 1])
                stt(zo, p_t[:, 1:W + 1], omu[:, ct:ct + 1], zo, ALU.mult, ALU.add, [p_b, b_omu, b_zo], [b_zo])

            def rp(i, f):
                return rwp[:, i, f:f + 1]

            def drive(gens, weights=None, background=()):
                gens = list(gens)
                bg = list(background)
                wts = {id(g_): 1 for g_ in gens}
                if weights:
                    for g_, w_ in zip(gens, weights):
                        wts[id(g_)] = w_
                while gens:
                    for g_ in list(gens):
                        for _ in range(wts[id(g_)]):
                            try:
                                next(g_)
                            except StopIteration:
                                gens.remove(g_)
                                break
                    for g_ in list(bg):
                        try:
                            next(g_)
                        except StopIteration:
                            bg.remove(g_)

            def pre(idx):
                own = idx >= NWIN
                win = idx % NWIN
                par = idx % 2
                AR, b_AR = ARs[par]
                KH, b_KH = KHs[par]
                BH, b_BH = BHs[par]
                GL, b_GL = GLs[par]
                gfm, b_gfm = gfms[par]
                bv, b_bv = bvs[par]
                ufm, b_ufm = ufms[par]
                row_base = (TOK if own else 0) + win * W
                hT_t, hT_b = hT[0]
                for st in range(W // 128):
                    xt, bx = load_x(row_base + st * 128)
                    norm_transpose(xt, bx, st, hT_t, hT_b, 0, 1, W)
                    yield
                z12, b_z12 = zt[3]
                shifted(hT_t, hT_b, 12, z12[:], b_z12)
                act(LA[0:64, :], z12[0:64, :], AF.Tanh, [b_z12], [b_LA])
                cp("pool", LA[64:128, :], z12[64:128, :], [b_z12], [b_LA])
                yield
                need_r = own or (win == NWIN - 1)
                if need_r:
                    shifted(hT_t, hT_b, 13, z12[:], b_z12)
                if own:
                    act(gs[:], z12[:], AF.Sigmoid, [b_z12], [b_gs])
                yield
                for f in range(4):
                    (zr, b_zr), (zk, b_zk), (zv, b_zv) = zt[0], zt[1], zt[2]
                    if need_r:
                        shifted(hT_t, hT_b, f, zr[:], b_zr)
                        yield
                    shifted(hT_t, hT_b, 4 + f, zk[:], b_zk)
                    yield
                    shifted(hT_t, hT_b, 8 + f, zv[:], b_zv)
                    yield
                    (sw, b_sw), (asg, b_asg), (kkr, b_kkr), (nrm, b_nrm), (kk, b_kk), (t1, b_t1) = et[0:6]
                    (kmod, b_kmod), (bvec, b_bvec), (lw, b_lw), (cc, b_cc), (cm, b_cm), (ex, b_ex) = et[6:12]
                    (sq, b_sq), (rk, b_rk) = etb
                    mm(PA[1][:, 0:W], lo2w[:, f * 128:(f + 1) * 128], LA[:], [b_lo2w, b_LA], [bPA[1]])
                    mm(PA[1][:, W:2 * W], lo2a[:, f * 128:(f + 1) * 128], LA[:], [b_lo2a, b_LA], [bPA[1]])
                    act(sw[:], PA[1][:, 0:W], AF.Sigmoid, [bPA[1], b_rwp], [b_sw], bias=rp(0, f))
                    act(asg[:], PA[1][:, W:2 * W], AF.Sigmoid, [bPA[1], b_rwp], [b_asg], bias=rp(1, f))
                    if own:
                        mm(PA[1][:, 0:W], g2[:, f * 128:(f + 1) * 128], gs[:], [b_g2, b_gs], [bPA[1]])
                        act(gfm[:, f, :], PA[1][:, 0:W], AF.Identity, [bPA[1]], [b_gfm])
                    act(kkr[:], zk[:], AF.Identity, [b_zk, b_rwp], [b_kkr], scale=rp(2, f))
                    act(sq[:], kkr[:], AF.Square, [b_kkr], [b_sq])
                    yield
                    mm(PA[1][:, 0:W], bones_b[:], sq[:], [b_bones_b, b_sq], [bPA[1]])
                    act(nrm[:], PA[1][:, 0:W], AF.Sqrt, [bPA[1]], [b_nrm])
                    ts("dve", nrm[:], nrm[:], 1e-12, None, ALU.max, None, [b_nrm], [b_nrm])
                    S.op("dve", lambda e: e.reciprocal(out=nrm[:], in_=nrm[:]), [b_nrm], [b_nrm])
                    tt("pool", kk[:], kkr[:], nrm[:], ALU.mult, [b_kkr, b_nrm], [b_kk])
                    act(t1[:], asg[:], AF.Identity, [b_asg, b_rwp], [b_t1], scale=rp(3, f), bias=rp(7, f))
                    yield
                    tt("pool", kmod[:], zk[:], t1[:], ALU.mult, [b_zk, b_t1], [b_kmod])
                    tt("pool", bvec[:], kk[:], asg[:], ALU.mult, [b_kk, b_asg], [b_bvec])
                    S.op("act", lambda e: e.mul(lw[:], sw[:], -math.exp(-0.5)), [b_sw], [b_lw])
                    S.op("dve", lambda e: e.tensor_tensor_scan(out=cc[:], data0=rst01[:], data1=lw[:], initial=0.0,
                                                                op0=ALU.mult, op1=ALU.add), [b_rst, b_lw], [b_cc])
                    yield
                    tt("pool", cm[:], cc[:], lw[:], ALU.subtract, [b_cc, b_lw], [b_cm])
                    act(sw[:], cc[:], AF.Exp, [b_cc], [b_sw])
                    act(ex[:], cc[:], AF.Exp, [b_cc], [b_ex], scale=-1.0)
                    act(cm[:], cm[:], AF.Exp, [b_cm], [b_cm])
                    yield
                    if own:
                        tt("dve", AR[:, f, :, 64:128], zr[:].rearrange("p (c t) -> p c t", t=64),
                           sw[:].rearrange("p (c t) -> p c t", t=64), ALU.mult, [b_zr, b_sw], [b_AR])
                    stt(AR[:, f, :, 0:64], kk[:].rearrange("p (c t) -> p c t", t=64), -1.0,
                        cm[:].rearrange("p (c t) -> p c t", t=64), ALU.mult, ALU.mult, [b_kk, b_cm], [b_AR])
                    tt("dve", KH[:, f, :], kmod[:], ex[:], ALU.mult, [b_kmod, b_ex], [b_KH])
                    tt("pool", BH[:, f, :], bvec[:], ex[:], ALU.mult, [b_bvec, b_ex], [b_BH])
                    cp("pool", GL[:, f, :], sw[:].rearrange("p (c t) -> p c t", t=64)[:, :, 63], [b_sw], [b_GL])
                    yield
                    if own:
                        stt(rk[:], zr[:], rp(4, f), kmod[:], ALU.mult, ALU.mult, [b_zr, b_rwp, b_kmod], [b_rk])
                        mm(PA[1][:, 0:W], bones_b[:], rk[:], [b_bones_b, b_rk], [bPA[1]])
                        tt("dve", bv[:, f, :], PA[1][:, 0:W], zv[:], ALU.mult, [bPA[1], b_zv], [b_bv])
                        yield
                    cp("act", VBs[par][0][:, f, :], zv[:], [b_zv], [VBs[par][1]])
                    yield
                for t in range(4):
                    pj, bpj = project(hT_t, hT_b, 14 + t)
                    act(ufm[:, t, :], pj[:, 0:W], AF.Identity, [bpj], [b_ufm])
                    yield

            def main(idx, nxt):
                own = idx >= NWIN
                win = idx % NWIN
                par = idx % 2
                M['AR'] = ARs[par]
                M['KH'] = KHs[par]
                M['BH'] = BHs[par]
                M['GL'] = GLs[par]
                M['gfm'] = gfms[par]
                M['bv'] = bvs[par]
                M['ufm'] = ufms[par]
                KH, b_KH = KHs[par]
                BH, b_BH = BHs[par]
                VB, b_VB = VBs[par]
                for (src, b_src, dst, b_dst) in ((KH, b_KH, kT, b_kT), (BH, b_BH, bT, b_bT), (VB, b_VB, vT, b_vT)):
                    n = 0
                    for c in range(4):
                        for f in range(4):
                            for hp in range(2):
                                rs = slice(hp * 64, hp * 64 + 64)
                                n += 1
                                o0 = (c * 4 + f) * 64
                                tr(PT[rs, o0:o0 + 64], src[rs, f, c * 64:(c + 1) * 64], ident_b[rs, rs],
                                   [b_src, b_ident_b], [bPT], inc=(n == 32))
                    cp("act", dst[:].rearrange("p a b c -> p (a b c)"), PT[:, 0:1024], [bPT], [b_dst])
                if M.get('inv0_done') != idx:
                    drive([chunk_inv(0, own, 0, par)])
                extra = [nxt] if nxt is not None else []
                for c in range(4):
                    grp = [chunk_chain(c, own, c % 2), s5_chunk(c, own)]
                    wts = [2, 2]
                    if c < 3:
                        grp = [chunk_inv(c + 1, own, (c + 1) % 2, par)] + grp
                        wts = [2, 2, 2]
                    elif nxt is not None:
                        drive([nxt])
                        grp = [chunk_inv(0, (idx + 1) >= NWIN, 0, (idx + 1) % 2)] + grp
                        wts = [2, 2, 2]
                        M['inv0_done'] = idx + 1
                    drive(grp, wts, background=extra)
                    extra = []
                    if nxt is not None:
                        extra = [nxt]
                if nxt is not None:
                    drive([nxt])
                if own:
                    outputs(win, None, None)

            HEADS = [(f, hp) for f in range(4) for hp in range(2)]

            def chunk_inv(c, own, slot, par):
                AR, b_AR = ARs[par]
                KH, b_KH = KHs[par]
                BH, b_BH = BHs[par]
                cs = slice(c * 64, (c + 1) * 64)
                na = 128 if own else 64
                heads = HEADS
                Abm, b_Abm = AbmS[slot]
                Akm, b_Akm = AkmS[slot]
                for i, (f, hp) in enumerate(heads):
                    rs = slice(hp * 64, hp * 64 + 64)
                    mm(PA[0][rs, f * 128:f * 128 + na], BH[rs, f, cs], AR[rs, f, c, 0:na], [b_BH, b_AR], [bPA[0]], inc=(i == 7))
                for i, (f, hp) in enumerate(heads):
                    rs = slice(hp * 64, hp * 64 + 64)
                    mm(PA[1][rs, f * 128:f * 128 + na], KH[rs, f, cs], AR[rs, f, c, 0:na], [b_KH, b_AR], [bPA[1]], inc=(i == 7))
                for i, (f, hp) in enumerate(heads):
                    rs = slice(hp * 64, hp * 64 + 64)
                    mm(PI[rs, f * 64:(f + 1) * 64], AR[rs, f, c, 0:64], BH[rs, f, cs], [b_AR, b_BH], [bPI[0]], inc=(i == 7))
                yield
                pa0 = PA[0][:, :].rearrange("p (f n) -> p f n", n=128)
                pa1 = PA[1][:, :].rearrange("p (f n) -> p f n", n=128)
                mk = maskar[:, 0:na]
                tt("dve", Abm[:, :, 0:na], pa0[:, :, 0:na], bc1(mk, 4), ALU.mult, [bPA[0], b_maskar], [b_Abm])
                xt_t, xt_b = NTb, b_NTb
                tt("dve", xt_t[:], PI[:, 0:256].rearrange("p (f n) -> p f n", n=64), bc1(masknt[:], 4), ALU.mult,
                   [bPI[0], b_masknt], [xt_b])
                tt("dve", Akm[:, :, 0:na], pa1[:, :, 0:na], bc1(mk, 4), ALU.mult, [bPA[1], b_maskar], [b_Akm])
                p_t, p_b = Pm[0]
                tt("pool", p_t[:], Abm[:, :, 0:64], bc1(identp[:], 4), ALU.add, [b_Abm, b_identp], [p_b])
                yield
                x_ap = lambda f, rs: Abm[rs, f, 0:64]
                x_b = b_Abm
                for lvl in range(1, 6):
                    xtn_t, xtn_b = XTb[lvl % 2]
                    for i, (f, hp) in enumerate(heads):
                        rs = slice(hp * 64, hp * 64 + 64)
                        mm(PI[rs, f * 64:(f + 1) * 64], x_ap(f, rs), xt_t[rs, f, :], [x_b, xt_b], [bPI[0]], inc=(i == 7))
                    if lvl <= 4:
                        xn_t, xn_b = Xb[lvl % 2]
                        for i, (f, hp) in enumerate(heads):
                            rs = slice(hp * 64, hp * 64 + 64)
                            mm(PA[0][rs, f * 64:(f + 1) * 64], xt_t[rs, f, :], x_ap(f, rs), [x_b, xt_b], [bPA[0]], inc=(i == 7))
                    yield
                    cp("act", xtn_t[:], PI[:, 0:256].rearrange("p (f n) -> p f n", n=64), [bPI[0]], [xtn_b])
                    if lvl <= 4:
                        cp("act", xn_t[:], PA[0][:, 0:256].rearrange("p (f n) -> p f n", n=64), [bPA[0]], [xn_b])
                    yield
                    pn_t, pn_b = Pm[lvl % 2]
                    for i, (f, hp) in enumerate(heads):
                        rs = slice(hp * 64, hp * 64 + 64)
                        mm(PC[rs, f * 64:(f + 1) * 64], xtn_t[rs, f, :], p_t[rs, f, :], [xtn_b, p_b], [bPC[0]], inc=(i == 7))
                    yield
                    tt("dve", pn_t[:], PC[:, 0:256].rearrange("p (f n) -> p f n", n=64), p_t[:], ALU.add, [bPC[0], p_b], [pn_b])
                    yield
                    p_t, p_b = pn_t, pn_b
                    xt_t, xt_b = xtn_t, xtn_b
                    if lvl <= 4:
                        x_ap = (lambda t_: (lambda f, rs: t_[rs, f, :]))(xn_t)
                        x_b = xn_b
                T0_t, T0_b = p_t, p_b
                Rb, b_Rb = Xb[0]
                TTb, b_TTb = Xb[1]
                for i, (f, hp) in enumerate(heads):
                    rs = slice(hp * 64, hp * 64 + 64)
                    mm(PI[rs, f * 64:(f + 1) * 64], NTb[rs, f, :], T0_t[rs, f, :], [b_NTb, T0_b], [bPI[0]], inc=(i == 7))
                for i, (f, hp) in enumerate(heads):
                    rs = slice(hp * 64, hp * 64 + 64)
                    tr(PT[rs, f * 64:(f + 1) * 64], T0_t[rs, f, :], ident_b[rs, rs], [T0_b, b_ident_b], [bPT], inc=(i == 7))
                tt("pool", ImT[:], bc1(identp[:], 4), T0_t[:], ALU.subtract, [b_identp, T0_b], [b_ImT])
                yield
                tt("dve", Rb[:], PI[:, 0:256].rearrange("p (f n) -> p f n", n=64), ImT[:], ALU.add, [bPI[0], b_ImT], [b_Rb])
                cp("act", TTb[:], PT[:, 0:256].rearrange("p (f n) -> p f n", n=64), [bPT], [b_TTb])
                yield
                for i, (f, hp) in enumerate(heads):
                    rs = slice(hp * 64, hp * 64 + 64)
                    mm(PC[rs, f * 64:(f + 1) * 64], TTb[rs, f, :], Rb[rs, f, :], [b_TTb, b_Rb], [bPC[0]], inc=(i == 7))
                yield
                T_t, T_b = TfinS[slot]
                tt("dve", T_t[:], PC[:, 0:256].rearrange("p (f n) -> p f n", n=64), T0_t[:], ALU.add, [bPC[0], T0_b], [T_b])
                lastT[0] = (T_t, T_b)
                yield

            def chunk_chain(c, own, slot):
                AR, b_AR = M['AR']
                KH, b_KH = M['KH']
                BH, b_BH = M['BH']
                GL, b_GL = M['GL']
                gfm, b_gfm = M['gfm']
                bv, b_bv = M['bv']
                ufm, b_ufm = M['ufm']
                cs = slice(c * 64, (c + 1) * 64)
                heads = HEADS
                Abm, b_Abm = AbmS[slot]
                Akm, b_Akm = AkmS[slot]
                T_t, T_b = TfinS[slot]
                s0_t, s0_b = Sb[cur_s[0] % 2]
                s1_t, s1_b = Sb[(cur_s[0] + 1) % 2]
                cur_s[0] += 1
                pm3 = PM[:, 0:256].rearrange("p (f n) -> p f n", n=64)
                for i, (f, hp) in enumerate(heads):
                    rs = slice(hp * 64, hp * 64 + 64)
                    mm(PM[rs, f * 64:(f + 1) * 64], AR[rs, f, c, 0:64], s0_t[rs, f, :], [b_AR, s0_b], [bPM[0]], start=True, stop=False, inc=False)
                    mm(PM[rs, f * 64:(f + 1) * 64], Akm[rs, f, 0:64], vT[rs, c, f, :], [b_Akm, b_vT], [bPM[0]], start=False, stop=True, inc=(i == 7))
                yield
                cp("act", WT[:], pm3, [bPM[0]], [b_WT])
                yield
                for i, (f, hp) in enumerate(heads):
                    rs = slice(hp * 64, hp * 64 + 64)
                    mm(PM[rs, f * 64:(f + 1) * 64], T_t[rs, f, :], WT[rs, f, :], [T_b, b_WT], [bPM[0]], inc=(i == 7))
                yield
                cp("act", UT[:], pm3, [bPM[0]], [b_UT])
                yield
                for i, (f, hp) in enumerate(heads):
                    rs = slice(hp * 64, hp * 64 + 64)
                    mm(PM[rs, f * 64:(f + 1) * 64], bT[rs, c, f, :], UT[rs, f, :], [b_bT, b_UT], [bPM[0]], start=True, stop=False, inc=False)
                    mm(PM[rs, f * 64:(f + 1) * 64], kT[rs, c, f, :], vT[rs, c, f, :], [b_kT, b_vT], [bPM[0]], start=False, stop=True, inc=(i == 7))
                yield
                tt("dve", stmp[:], pm3, Sf[:], ALU.add, [bPM[0], b_Sf], [b_stmp])
                yield
                if own:
                    for i, (f, hp) in enumerate(heads):
                        rs = slice(hp * 64, hp * 64 + 64)
                        o = PM[rs, f * 64:(f + 1) * 64]
                        mm(o, s0_t[rs, f, :], AR[rs, f, c, 64:128], [s0_b, b_AR], [bPM[0]], start=True, stop=False, inc=False)
                        mm(o, UT[rs, f, :], Abm[rs, f, 64:128], [b_UT, b_Abm], [bPM[0]], start=False, stop=False, inc=False)
                        mm(o, vT[rs, c, f, :], Akm[rs, f, 64:128], [b_vT, b_Akm], [bPM[0]], start=False, stop=True, inc=(i == 7))
                    yield
                    cp("act", Yw[:, :, cs], pm3, [bPM[0]], [b_Yw])
                tt("dve", Sf[:], stmp[:], bc2(GL[:, :, c], 64), ALU.mult, [b_stmp, b_GL], [b_Sf])
                cp("act", s1_t[:], Sf[:], [b_Sf], [s1_b])
                yield

            def s5_chunk(c, own):
                AR, b_AR = M['AR']
                KH, b_KH = M['KH']
                BH, b_BH = M['BH']
                GL, b_GL = M['GL']
                gfm, b_gfm = M['gfm']
                bv, b_bv = M['bv']
                ufm, b_ufm = M['ufm']
                cs = slice(c * 64, (c + 1) * 64)
                for hf in range(2):
                    ps_ = slice(hf * 8, hf * 8 + 8)
                    for pl in range(8):
                        pr = hf * 8 + pl
                        t = pr // 4
                        mm(PJ[0][:, pl * 64:(pl + 1) * 64], Bb_re[:, pr, :], ufm[:, t, cs], [b_Bbre, b_ufm], [bPJ[0]], inc=(pl == 7))
                    for pl in range(8):
                        pr = hf * 8 + pl
                        t = pr // 4
                        mm(PJ[1][:, pl * 64:(pl + 1) * 64], Bb_im[:, pr, :], ufm[:, t, cs], [b_Bbim, b_ufm], [bPJ[1]], inc=(pl == 7))
                    yield
                    cosv = tcos[:, ps_, :].rearrange("p a b -> p (a b)")
                    sinv = tsin[:, ps_, :].rearrange("p a b -> p (a b)")
                    (a0, ba0), (a1, ba1), (a2, ba2), (a3, ba3) = s5a
                    tt("dve", a0[:], PJ[0][:, :], cosv, ALU.mult, [bPJ[0], b_tcos], [ba0])
                    tt("dve", a1[:], PJ[1][:, :], sinv, ALU.mult, [bPJ[1], b_tsin], [ba1])
                    tt("dve", a2[:], PJ[1][:, :], cosv, ALU.mult, [bPJ[1], b_tcos], [ba2])
                    tt("dve", a3[:], PJ[0][:, :], sinv, ALU.mult, [bPJ[0], b_tsin], [ba3])
                    btr, b_btr, bti, b_bti = a0, ba0, a2, ba2
                    yield
                    tt("pool", btr[:], a0[:], a1[:], ALU.add, [ba0, ba1], [b_btr])
                    tt("pool", bti[:], a2[:], a3[:], ALU.subtract, [ba2, ba3], [b_bti])
                    b3r = btr[:].rearrange("p (a b) -> p a b", b=64)
                    b3i = bti[:].rearrange("p (a b) -> p a b", b=64)
                    tt("pool", b3r[:, :, 0], b3r[:, :, 0], s5s[:, 0, ps_], ALU.add, [b_btr, b_s5s], [b_btr])
                    tt("pool", b3i[:, :, 0], b3i[:, :, 0], s5s[:, 1, ps_], ALU.add, [b_bti, b_s5s], [b_bti])
                    yield
                    rv = rho0[:, ps_, :].rearrange("p a b -> p (a b)")
                    S.op("dve", lambda e: e.tensor_tensor_scan(out=s_tr[:], data0=rv, data1=btr[:], initial=0.0,
                                                                op0=ALU.mult, op1=ALU.add), [b_rho0, b_btr], [b_str])
                    S.op("dve", lambda e: e.tensor_tensor_scan(out=s_ti[:], data0=rv, data1=bti[:], initial=0.0,
                                                                op0=ALU.mult, op1=ALU.add), [b_rho0, b_bti], [b_sti])
                    yield
                    s3r = s_tr[:].rearrange("p (a b) -> p a b", b=64)
                    s3i = s_ti[:].rearrange("p (a b) -> p a b", b=64)
                    qa = s5q[:, 0:2, :]
                    qb = s5q[:, 2:4, :]
                    tt("pool", qa, s3r[:, :, 63].unsqueeze(1).to_broadcast([128, 2, 8]), T63[:, 0, :, ps_], ALU.mult, [b_str, b_T63], [b_s5q])
                    tt("pool", qb, s3i[:, :, 63].unsqueeze(1).to_broadcast([128, 2, 8]), T63[:, 1, :, ps_], ALU.mult, [b_sti, b_T63], [b_s5q])
                    tt("pool", s5s[:, :, ps_], qa, qb, ALU.add, [b_s5q], [b_s5s])
                    yield
                    if own:
                        tt("dve", a0[:], s_tr[:], cosv, ALU.mult, [b_str, b_tcos], [ba0])
                        tt("pool", a1[:], s_ti[:], sinv, ALU.mult, [b_sti, b_tsin], [ba1])
                        tt("dve", a2[:], s_tr[:], sinv, ALU.mult, [b_str, b_tsin], [ba2])
                        tt("pool", a3[:], s_ti[:], cosv, ALU.mult, [b_sti, b_tcos], [ba3])
                        tt("dve", srb[:], a0[:], a1[:], ALU.subtract, [ba0, ba1], [b_srb])
                        tt("pool", sib[:], a2[:], a3[:], ALU.add, [ba2, ba3], [b_sib])
                        yield
                        for pl in range(8):
                            pr = hf * 8 + pl
                            t = pr // 4
                            tl = t % 2
                            o = PJ[0][:, tl * 64:(tl + 1) * 64]
                            mm(o, Cp_re[:, pr, :], srb[:, pl * 64:(pl + 1) * 64], [b_Cpre, b_srb], [bPJ[0]],
                               start=(pr % 4 == 0), stop=False, inc=False)
                            mm(o, Cp_imn[:, pr, :], sib[:, pl * 64:(pl + 1) * 64], [b_Cpim, b_sib], [bPJ[0]],
                               start=False, stop=(pr % 4 == 3), inc=(pl == 7))
                        yield
                        for tl in range(2):
                            t = hf * 2 + tl
                            stt(Y5w[:, t, cs], ufm[:, t, cs], s5v[:, 0, t:t + 1], PJ[0][:, tl * 64:(tl + 1) * 64], ALU.mult, ALU.add,
                                [b_ufm, b_s5v, bPJ[0]], [b_Y5w])
                    yield

            def outputs(win, hT_t, hT_b):
                AR, b_AR = M['AR']
                KH, b_KH = M['KH']
                BH, b_BH = M['BH']
                GL, b_GL = M['GL']
                gfm, b_gfm = M['gfm']
                bv, b_bv = M['bv']
                ufm, b_ufm = M['ufm']
                xres = [load_x(TOK + win * W + st * 128) + (xslot[0] - 1,) for st in range(W // 128)]
                for f in range(4):
                    (ysq, b_ysq), (mu2, b_mu2), (var, b_var), (dd, b_dd) = et[(f % 2) * 4:(f % 2) * 4 + 4]
                    act(ysq[:], Yw[:, f, :], AF.Square, [b_Yw], [b_ysq])
                    mm(PM[:, 0:W], bones_f[:], Yw[:, f, :], [b_bones_f, b_Yw], [bPM[0]])
                    mm(PA[0][:, 0:W], bones_f[:], ysq[:], [b_bones_f, b_ysq], [bPA[0]])
                    act(mu2[:], PM[:, 0:W], AF.Square, [bPM[0]], [b_mu2])
                    tt("dve", var[:], PA[0][:, 0:W], mu2[:], ALU.subtract, [bPA[0], b_mu2], [b_var])
                    act(var[:], var[:], AF.Sqrt, [b_var], [b_var], bias=64e-5, scale=1.0)
                    S.op("dve", lambda e: e.reciprocal(out=var[:], in_=var[:]), [b_var], [b_var])
                    tt("dve", dd[:], Yw[:, f, :], PM[:, 0:W], ALU.subtract, [b_Yw, bPM[0]], [b_dd])
                    tt("pool", dd[:], dd[:], var[:], ALU.mult, [b_dd, b_var], [b_dd])
                    ts("pool", dd[:], dd[:], rp(5, f), rp(6, f), ALU.mult, ALU.add, [b_dd, b_rwp], [b_dd])
                    tt("pool", dd[:], dd[:], bv[:, f, :], ALU.add, [b_dd, b_bv], [b_dd])
                    tt("dve", ycat[:, f, :], dd[:], gfm[:, f, :], ALU.mult, [b_dd, b_gfm], [b_ycat])
                zzbv = lambda t: (srb if t < 2 else sib)[:, (t % 2) * W:(t % 2 + 1) * W]
                zzbb = lambda t: (b_srb if t < 2 else b_sib)
                zzv = lambda t: (s_tr if t < 2 else s_ti)[:, (t % 2) * W:(t % 2 + 1) * W]
                zzb_ = lambda t: (b_str if t < 2 else b_sti)
                oo, b_oo = sbt_oo
                (x2, b_x2), (pq, b_pq), (sg, b_sg) = et[8:11]
                (osq, b_osq), _ = etb
                for t in range(4):
                    act(x2[:], Y5w[:, t, :], AF.Square, [b_Y5w], [b_x2])
                    ts("pool", pq[:], x2[:], 0.044715, 1.0, ALU.mult, ALU.add, [b_x2], [b_pq])
                    tt("pool", pq[:], pq[:], Y5w[:, t, :], ALU.mult, [b_pq, b_Y5w], [b_pq])
                    act(sg[:], pq[:], AF.Sigmoid, [b_pq], [b_sg], scale=2.0 * math.sqrt(2.0 / math.pi))
                    tt("dve", zzv(t), Y5w[:, t, :], sg[:], ALU.mult, [b_Y5w, b_sg], [zzb_(t)])
                    cp("act", zzbv(t), zzv(t), [zzb_(t)], [zzbb(t)])
                for t2 in range(4):
                    for t in range(4):
                        mm(PM[:, 0:W], wglu[:, t, t2 * 128:(t2 + 1) * 128], zzbv(t), [b_wglu, zzbb(t)], [bPM[0]],
                           start=(t == 0), stop=(t == 3), inc=(t == 3))
                    act(sg[:], PM[:, 0:W], AF.Sigmoid, [bPM[0], b_s5v], [b_sg], bias=s5v[:, 1, t2:t2 + 1])
                    tt("dve", oo[:, t2, :], zzv(t2), sg[:], ALU.mult, [zzb_(t2), b_sg], [b_oo])
                for t in range(4):
                    act(osq[:], oo[:, t, :], AF.Square, [b_oo], [b_osq])
                    mm(PA[0][:, 0:W], ones_b[:], osq[:], [b_ones_b, b_osq], [bPA[0]], start=(t == 0), stop=(t == 3))
                act(sg[:], PA[0][:, 0:W], AF.Sqrt, [bPA[0]], [b_sg], bias=1e-6, scale=1.0)
                S.op("dve", lambda e: e.reciprocal(out=sg[:], in_=sg[:]), [b_sg], [b_sg])
                for t in range(4):
                    stt(ycat[:, 4 + t, :], oo[:, t, :], s5v[:, 2, t:t + 1], sg[:], ALU.mult, ALU.mult, [b_oo, b_s5v, b_sg], [b_ycat])
                for st in range(W // 128):
                    xt, bx, xsl_i = xres[st]
                    for hf in range(2):
                        pj, bpj = PJ[hf], bPJ[hf]
                        for k in range(8):
                            mm(pj[:, :], ycat[:, k, st * 128:(st + 1) * 128], Wo[:, k, hf * 512:(hf + 1) * 512], [b_ycat, b_Wo], [bpj],
                               start=(k == 0), stop=(k == 7), inc=(k == 7))
                        tt("dve", xt[:, hf * 512:(hf + 1) * 512], pj[:, :], xt[:, hf * 512:(hf + 1) * 512], ALU.add, [bpj, bx], [bx])
                    orow = win * W + st * 128
                    S.dma(out_d[orow:orow + 128, :], xt[:], "xst%d" % (xsl_i % 2), R=[bx], W=[b_outrows[orow // 128]])

            sbt_oo = (Yw, b_Yw)
            b_outrows = [Buf("orow%d" % i) for i in range(TOK // 128)]

            def dump_win():
                dump('hT', hT[0][0][:, 2, :], hT[0][1])
                dump('Pprev', Pprev[:], b_Pprev)
                dump('AR', AR[:, 1, :, :], b_AR)
                dump('KH', KH[:, 1, :], b_KH)
                dump('BH', BH[:, 1, :], b_BH)
                dump('kT', kT[:, 3, 1, :], b_kT)
                dump('vT', vT[:, 3, 1, :], b_vT)
                dump('GL', GL[:], b_GL)
                dump('Abm', AbmS[1][0][:], AbmS[1][1])
                dump('Akm', AkmS[1][0][:], AkmS[1][1])
                dump('T', lastT[0][0][:], lastT[0][1])
                dump('WT', WT[:], b_WT)
                dump('UT', UT[:], b_UT)
                dump('Sf', Sf[:], b_Sf)
                dump('s5s', s5s[:], b_s5s)
                dump('ufm', ufm[:, 2, :], b_ufm)
                dump('Yw', Yw[:], b_Yw)
                dump('Y5w', Y5w[:], b_Y5w)
                dump('ycat', ycat[:], b_ycat)

            gens = {0: pre(0)}
            drive([gens[0]])
            for idx in range(2 * NWIN):
                nxt = None
                if idx + 1 < 2 * NWIN:
                    if idx + 1 == NWIN:
                        ts("dve", Pprev[:], Pprev[:], maskv[:, 0:1], None, ALU.mult, None, [b_Pprev, b_maskv], [b_Pprev])
                    nxt = pre(idx + 1)
                main(idx, nxt)
                if idx == NWIN - 1:
                    ts("dve", Sf[:].rearrange("p a b -> p (a b)"), Sf[:].rearrange("p a b -> p (a b)"), maskv[:, 0:1], None, ALU.mult, None,
                       [b_Sf, b_maskv], [b_Sf])
                    sbc_t, sbc_b = Sb[cur_s[0] % 2]
                    cp("act", sbc_t[:], Sf[:], [b_Sf], [sbc_b])
                    ts("dve", s5s[:].rearrange("p a b -> p (a b)"), s5s[:].rearrange("p a b -> p (a b)"), maskv[:, 0:1], None, ALU.mult, None,
                       [b_s5s, b_maskv], [b_s5s])
                if stage(('own:%d' % (idx - NWIN)) if idx >= NWIN else ('pre:%d' % idx), dump_win):
                    return nc
            S.barrier()
        esB = ExitStack()
        with esB:
            HS = DFF // 2
            wgb, b_wgb = sbt(esB, [128, 8, DFF], BF16, "wgb")
            wub, b_wub = sbt(esB, [128, 8, DFF], BF16, "wub")
            wdb, b_wdb = sbt(esB, [128, NFF, D], BF16, "wdb")
            stB = [sbt(esB, [128, HS], F32, "stB%d" % i) for i in range(2)]
            fg_bc, b_fg = sbt(esB, [128, D], F32, "fgbc")
            S.dma(fg_bc[:], fgain_d.partition_broadcast(128), "fgbc", W=[b_fg])
            TB = 512
            NSB = TB // 128
            xsB = [sbt(esB, [128, D], F32, "xsB%d" % i) for i in range(NSB + 1)]
            xnB, b_xnB = sbt(esB, [128, D], BF16, "xnB")
            statB, b_statB = sbt(esB, [128, 4], F32, "statB")
            h2T = [sbt(esB, [128, 8, TB], BF16, "h2T%d" % i) for i in range(1)]
            actT, b_actT = sbt(esB, [128, NFF, TB], BF16, "actT")
            sil = [sbt(esB, [128, TB], BF16, "sil%d" % i) for i in range(2)]
            b_wgh = [Buf("wg_h0"), Buf("wg_h1")]
            b_wuh = [Buf("wu_h0"), Buf("wu_h1")]
            b_wdj = [Buf("wd_%d" % j) for j in range(NFF)]
            wprog = [0]

            def wload():
                ci = 0
                for hh in range(2):
                    for (src_d, dst, bl) in ((wg_d, wgb, b_wgh), (wu_d, wub, b_wuh)):
                        for k in range(8):
                            st_t, st_b = stB[ci % 2]
                            S.dma(st_t[:], src_d[k * 128:(k + 1) * 128, hh * HS:(hh + 1) * HS], "stB%d" % (ci % 2), W=[st_b])
                            cp(("act", "dve", "pool")[ci % 3], dst[:, k, hh * HS:(hh + 1) * HS], st_t[:], [st_b], [bl[hh]])
                            ci += 1
                            wprog[0] += 1
                            yield
                for j in range(NFF):
                    st_t, st_b = stB[ci % 2]
                    S.dma(st_t[:, 0:D], wd_d[j * 128:(j + 1) * 128, :], "stB%d" % (ci % 2), W=[st_b])
                    tt(("dve", "pool")[ci % 2], wdb[:, j, :], st_t[:, 0:D], gf_bc[:], ALU.mult, [st_b, b_gf], [b_wdj[j]])
                    ci += 1
                    wprog[0] += 1
                    yield

            wgen = wload()

            def need(n):
                while wprog[0] < n:
                    next(wgen)

            def bgstep():
                try:
                    next(wgen)
                except StopIteration:
                    pass

            xq = [0]
            need(4)
            for wi in range(TOK // TB):
                hT_t, hT_b = h2T[0]
                tiles = []
                for st in range(NSB):
                    ti = wi * NSB + st
                    i = xq[0] % (NSB + 1)
                    xq[0] += 1
                    xt, bx = xsB[i]
                    orow = ti * 128
                    S.dma(xt[:], out_d[orow:orow + 128, :], "xsB%d" % i, R=[b_outrows[ti]], W=[bx])
                    tiles.append((xt, bx, i, orow, ti))
                for st in range(NSB):
                    xt, bx, i, orow, ti = tiles[st]
                    bgstep()
                    bgstep()
                    act(xnB[:], xt[:], AF.Square, [bx], [b_xnB, b_statB], accum=statB[:, 0:1])
                    act(statB[:, 1:2], statB[:, 0:1], AF.Sqrt, [b_statB], [b_statB], bias=1e-6, scale=1.0 / D)
                    S.op("dve", lambda e: e.reciprocal(out=statB[:, 1:2], in_=statB[:, 1:2]), [b_statB], [b_statB])
                    act(xnB[:], xt[:], AF.Identity, [bx, b_statB], [b_xnB], scale=statB[:, 1:2])
                    for k in range(8):
                        tr(PT[:, k * 128:(k + 1) * 128], xnB[:, k * 128:(k + 1) * 128], ident_b[:], [b_xnB, b_ident_b], [bPT], inc=(k == 7))
                    for k in range(8):
                        o = hT_t[:, k, st * 128:(st + 1) * 128]
                        i_ = PT[:, k * 128:(k + 1) * 128]
                        if st % 2 == 0:
                            act(o, i_, AF.Identity, [bPT, b_modp], [hT_b], scale=modp[:, 3, k:k + 1], bias=modp[:, 2, k:k + 1])
                        else:
                            ts("dve", o, i_, modp[:, 3, k:k + 1], modp[:, 2, k:k + 1], ALU.mult, ALU.add, [bPT, b_modp], [hT_b])
                for j in range(NFF):
                    wh = 0 if j < 11 else 1
                    need(16 if wh == 0 else 32)
                    bgstep()
                    pg, bpg = (PJ[0], bPJ[0]) if j % 2 == 0 else (PA[0], bPA[0])
                    pu, bpu = (PJ[1], bPJ[1]) if j % 2 == 0 else (PA[1], bPA[1])
                    for k in range(8):
                        mm(pg[:, 0:TB], wgb[:, k, j * 128:(j + 1) * 128], hT_t[:, k, :], [b_wgh[wh], hT_b], [bpg],
                           start=(k == 0), stop=(k == 7), inc=(k == 7))
                    for k in range(8):
                        mm(pu[:, 0:TB], wub[:, k, j * 128:(j + 1) * 128], hT_t[:, k, :], [b_wuh[wh], hT_b], [bpu],
                           start=(k == 0), stop=(k == 7), inc=(k == 7))
                    s_t, s_b = sil[j % 2]
                    act(s_t[:], pg[:, 0:TB], AF.Silu, [bpg], [s_b])
                    tt("dve", actT[:, j, :], s_t[:], pu[:, 0:TB], ALU.mult, [s_b, bpu], [b_actT])
                for st in range(NSB):
                    xt, bx, i, orow, ti = tiles[st]
                    for hf in range(2):
                        pd, bpd = (PI, bPI[0]) if hf == 0 else (PC, bPC[0])
                        need(32 + NFF)
                        for j in range(NFF):
                            mm(pd[:, :], actT[:, j, st * 128:(st + 1) * 128], wdb[:, j, hf * 512:(hf + 1) * 512], [b_actT, b_wdj[j]], [bpd],
                               start=(j == 0), stop=(j == NFF - 1), inc=(j == NFF - 1))
                        tt("dve", xt[:, hf * 512:(hf + 1) * 512], pd[:, :], xt[:, hf * 512:(hf + 1) * 512], ALU.add, [bpd, bx], [bx])
                    act(xnB[:], xt[:], AF.Square, [bx], [b_xnB, b_statB], accum=statB[:, 2:3])
                    act(statB[:, 3:4], statB[:, 2:3], AF.Sqrt, [b_statB], [b_statB], bias=1e-6, scale=1.0 / D)
                    S.op("dve", lambda e: e.reciprocal(out=statB[:, 3:4], in_=statB[:, 3:4]), [b_statB], [b_statB])
                    stt(xt[:], xt[:], statB[:, 3:4], fg_bc[:], ALU.mult, ALU.mult, [bx, b_statB, b_fg], [bx])
                    S.dma(out_d[orow:orow + 128, :], xt[:], "xstB%d" % i, R=[bx], W=[b_outrows[ti]])
            S.barrier()
    return nc


_NC = None


def _layout_inputs(inp):
    f32 = np.float32
    g = lambda k: np.asarray(inp[k], dtype=f32)
    x = g("x")
    c = g("c")
    shared = {}
    shared["w_ada"] = np.ascontiguousarray(g("w_ada")[0])
    shared["b_ada"] = np.ascontiguousarray(g("b_ada")[0][None, :])
    shared["w_in"] = np.ascontiguousarray(g("w_in")[0])
    shared["mu_l"] = np.ascontiguousarray(g("mu_shift")[0].reshape(14, 128).T)
    v512 = lambda a: a.reshape(4, 128).T
    rw = [g("rw_w0")[0], g("rw_a0")[0], g("rw_k_k")[0], g("rw_k_a")[0], g("rw_r_k")[0].reshape(512),
          g("rw_lnx_w")[0], g("rw_lnx_b")[0]]
    shared["rwp"] = np.ascontiguousarray(np.stack([v512(a) for a in rw], axis=1).reshape(128, 28))
    lo2w = np.zeros((128, 512), f32)
    lo2w[0:64] = g("rw_w2")[0]
    lo2a = np.zeros((128, 512), f32)
    lo2a[64:128] = g("rw_a2")[0]
    shared["lo2w"] = lo2w
    shared["lo2a"] = lo2a
    shared["g2"] = np.ascontiguousarray(g("rw_g2")[0])
    a_re = g("s5_a_re")[0]
    a_im = g("s5_a_im")[0]
    ldt = g("s5_log_dt")[0]
    b_re = g("s5_b_re")[0]
    b_im = g("s5_b_im")[0]
    c_re = g("s5_c_re")[0]
    c_im = g("s5_c_im")[0]
    la = np.zeros((128, 3, 16), f32)
    for pr in range(16):
        for gp in range(2):
            gg = 2 * pr + gp
            la[gp * 64:(gp + 1) * 64, 0, pr] = a_re[gg]
            la[gp * 64:(gp + 1) * 64, 1, pr] = a_im[gg]
            la[gp * 64:(gp + 1) * 64, 2, pr] = ldt[gg]
    shared["s5la"] = la.reshape(128, 48)
    bpre = np.zeros((128, 16, 128), f32)
    bpim = np.zeros((128, 16, 128), f32)
    lbare = np.zeros((128, 16, 128), f32)
    lbaim = np.zeros((128, 16, 128), f32)
    lbldt = np.zeros((128, 16, 128), f32)
    cpre = np.zeros((128, 16, 128), f32)
    cpim = np.zeros((128, 16, 128), f32)
    for pr in range(16):
        tile = pr // 4
        for g8 in range(8):
            gg = tile * 8 + g8
            rows = slice(g8 * 16, g8 * 16 + 16)
            for gp in range(2):
                cols = slice(gp * 64, gp * 64 + 64)
                lbare[rows, pr, cols] = a_re[gg][None, :]
                lbaim[rows, pr, cols] = a_im[gg][None, :]
                lbldt[rows, pr, cols] = ldt[gg]
            if g8 // 2 == pr % 4:
                gp = g8 % 2
                cols = slice(gp * 64, gp * 64 + 64)
                bpre[rows, pr, cols] = b_re[gg].T
                bpim[rows, pr, cols] = b_im[gg].T
        for gp in range(2):
            gg = 2 * pr + gp
            g8 = gg % 8
            cpre[gp * 64:(gp + 1) * 64, pr, g8 * 16:(g8 + 1) * 16] = c_re[gg].T
            cpim[gp * 64:(gp + 1) * 64, pr, g8 * 16:(g8 + 1) * 16] = c_im[gg].T
    shared["bpre"] = bpre.reshape(128, 2048)
    shared["bpim"] = bpim.reshape(128, 2048)
    shared["lbare"] = lbare.reshape(128, 2048)
    shared["lbaim"] = lbaim.reshape(128, 2048)
    shared["lbldt"] = lbldt.reshape(128, 2048)
    shared["cpre"] = cpre.reshape(128, 2048)
    shared["cpim"] = cpim.reshape(128, 2048)
    s5v = [g("s5_d")[0].reshape(512), g("s5_b_glu")[0], g("s5_gain")[0]]
    shared["s5v"] = np.ascontiguousarray(np.stack([v512(a) for a in s5v], axis=1).reshape(128, 12))
    shared["w_glu"] = np.ascontiguousarray(g("s5_w_glu")[0])
    shared["w_out"] = np.ascontiguousarray(g("w_out")[0])
    shared["wg"] = np.ascontiguousarray(g("ffn_w_gate")[0])
    shared["wu"] = np.ascontiguousarray(g("ffn_w_up")[0])
    shared["wd"] = np.ascontiguousarray(g("ffn_w_down")[0])
    shared["fgain"] = np.ascontiguousarray(g("final_gain")[None, :])
    shared["ident"] = np.eye(128, dtype=f32)
    bo = np.zeros((128, 128), f32)
    bo[0:64, 0:64] = 1.0
    bo[64:128, 64:128] = 1.0
    shared["bones"] = bo
    j = np.arange(64)
    strict = (j[:, None] < j[None, :]).astype(f32)
    incl = (j[:, None] <= j[None, :]).astype(f32)
    mar = np.concatenate([strict, incl], axis=1)
    shared["maskar"] = np.concatenate([mar, mar], axis=0)
    low = (j[None, :] < j[:, None]).astype(f32)
    shared["masknt"] = np.concatenate([low, low], axis=0)
    shared["identp"] = np.concatenate([np.eye(64, dtype=f32)] * 2, axis=0)
    shared["idx1"] = np.tile((np.arange(64, dtype=f32) + 1.0)[None, :], (128, 1))
    rst = np.ones((128, 256), f32)
    rst[:, ::64] = 0.0
    shared["rst01"] = rst
    maps = []
    for core in range(8):
        b, s = core // 2, core % 2
        m = dict(shared)
        m["xcat"] = np.ascontiguousarray(np.concatenate([x[b, 0:TOK], x[b, s * TOK:(s + 1) * TOK]], axis=0))
        m["maskv"] = np.full((128, 1), float(s), f32)
        m["c_l"] = np.ascontiguousarray(c[b].reshape(8, 128).T)
        maps.append(m)
    return maps


def kernel(**inputs):
    global _NC
    if _NC is None:
        _NC = build_program()
    maps = _layout_inputs(inputs)
    res = run_bass_kernel_spmd(_NC, maps, core_ids=list(range(8)))
    out = np.zeros((4, 2 * TOK, D), np.float32)
    for core in range(8):
        b, s = core // 2, core % 2
        out[b, s * TOK:(s + 1) * TOK] = res.results[core]["out"]
    return out
```

```python
import math
from contextlib import ExitStack

import numpy as np
import concourse.bass as bass
import concourse.mybir as mybir
from concourse.bass_utils import run_bass_kernel_spmd

F32 = mybir.dt.float32
BF16 = mybir.dt.bfloat16
I32 = mybir.dt.int32
AF = mybir.ActivationFunctionType
ALU = mybir.AluOpType
AX = mybir.AxisListType

D = 1024
TOK = 4096
W = 256
L = 64
NWIN = TOK // W
DFF = 2816
NFF = DFF // 128
TWO_PI = 2.0 * math.pi
PI_SAFE = 3.1415925


class Buf:
    __slots__ = ("name", "w", "r", "excl")

    def __init__(self, name, excl=False):
        self.name = name
        self.w = None
        self.r = {}
        self.excl = excl


class DSem:
    __slots__ = ("handle", "count", "id")


class Sched:
    def __init__(self, nc, es):
        self.nc = nc
        self.es = es
        self.eng = {"pe": nc.tensor, "act": nc.scalar, "dve": nc.vector, "pool": nc.gpsimd, "sp": nc.sync}
        self.sem = {k: es.enter_context(nc.semaphore("sem_" + k)) for k in self.eng}
        self.cnt = {k: 0 for k in self.eng}
        self.seen = {k: {} for k in self.eng}
        self.dsems = {}
        self.nds = 0
        self.const_bufs = []
        self.dead = False

    def _need(self, eng, R, W):
        need = {}

        def add(ev):
            if ev is None:
                return
            k = ev[0]
            if k not in need or need[k][2] < ev[2]:
                need[k] = ev

        for b in R:
            add(b.w)
        for b in W:
            add(b.w)
            for ev in b.r.values():
                add(ev)
        E = self.eng[eng]
        for k, ev in need.items():
            if k == ("e", eng) and eng in ("pe", "sp"):
                continue
            if self.seen[eng].get(k, 0) >= ev[2]:
                continue
            E.wait_ge(ev[1], ev[2])
            self.seen[eng][k] = ev[2]

    def op(self, eng, fn, R=(), W=(), inc=True):
        if self.dead:
            return None
        if any(b.excl for b in R):
            W = list(W) + [b for b in R if b.excl]
            R = [b for b in R if not b.excl]
        self._need(eng, R, W)
        inst = fn(self.eng[eng])
        val = self.cnt[eng] + 1
        if inc:
            inst.then_inc(self.sem[eng], 1)
            self.cnt[eng] = val
        ev = (("e", eng), self.sem[eng], val)
        for b in R:
            b.r[ev[0]] = ev
        for b in W:
            b.w = ev
            b.r = {}
        return inst

    def _dsem(self, key):
        if key not in self.dsems:
            d = DSem()
            d.handle = self.es.enter_context(self.nc.semaphore("dsem%d" % self.nds))
            d.count = 0
            d.id = self.nds
            self.nds += 1
            self.dsems[key] = d
        return self.dsems[key]

    def dma(self, out, in_, key, R=(), W=(), const=False):
        if self.dead:
            return
        self._need("sp", R, W)
        d = self._dsem(key)
        d.count += 16
        self.nc.sync.dma_start(out=out, in_=in_).then_inc(d.handle, 16)
        ev = (("d", d.id), d.handle, d.count)
        for b in R:
            b.r[ev[0]] = ev
        for b in W:
            b.w = ev
            b.r = {}
            if const:
                self.const_bufs.append(b)

    def finalize_consts(self, key):
        d = self._dsem(key)
        ev = (("d", d.id), d.handle, d.count)
        for b in self.const_bufs:
            b.w = ev
        self.const_bufs = []

    def barrier(self):
        for e, E in self.eng.items():
            for f in self.eng:
                if f == e:
                    continue
                k = ("e", f)
                if self.cnt[f] > self.seen[e].get(k, 0):
                    E.wait_ge(self.sem[f], self.cnt[f])
                    self.seen[e][k] = self.cnt[f]
            for d in self.dsems.values():
                k = ("d", d.id)
                if d.count > self.seen[e].get(k, 0):
                    E.wait_ge(d.handle, d.count)
                    self.seen[e][k] = d.count


class _StopBuild(Exception):
    pass


_DBG = {"stop": None, "dumps": [], "meta": []}


def build_program():
    nc = bass.Bass("TRN2", target_bir_lowering=False)
    dbg_on = _DBG["stop"] is not None
    if dbg_on:
        dbg_d = nc.dram_tensor("dbg", [128, 65536], F32, kind="ExternalOutput").ap()
        _DBG["meta"] = []

    def din(name, shape):
        return nc.dram_tensor(name, list(shape), F32, kind="ExternalInput").ap()

    xcat = din("xcat", [2 * TOK, D])
    maskv_d = din("maskv", [128, 1])
    c_d = din("c_l", [128, 8])
    wada_d = din("w_ada", [D, 6 * D])
    bada_d = din("b_ada", [1, 6 * D])
    win_d = din("w_in", [D, 2304])
    mu_d = din("mu_l", [128, 14])
    rwp_d = din("rwp", [128, 28])
    lo2w_d = din("lo2w", [128, 512])
    lo2a_d = din("lo2a", [128, 512])
    g2_d = din("g2", [128, 512])
    s5la_d = din("s5la", [128, 48])
    bpre_d = din("bpre", [128, 2048])
    bpim_d = din("bpim", [128, 2048])
    lbare_d = din("lbare", [128, 2048])
    lbaim_d = din("lbaim", [128, 2048])
    lbldt_d = din("lbldt", [128, 2048])
    cpre_d = din("cpre", [128, 2048])
    cpim_d = din("cpim", [128, 2048])
    s5v_d = din("s5v", [128, 12])
    wglu_d = din("w_glu", [512, 512])
    wout_d = din("w_out", [D, D])
    wg_d = din("wg", [D, DFF])
    wu_d = din("wu", [D, DFF])
    wd_d = din("wd", [DFF, D])
    fgain_d = din("fgain", [1, D])
    ident_d = din("ident", [128, 128])
    bones_d = din("bones", [128, 128])
    maskar_d = din("maskar", [128, 128])
    masknt_d = din("masknt", [128, 64])
    identp_d = din("identp", [128, 64])
    idx1_d = din("idx1", [128, 64])
    rst_d = din("rst01", [128, 256])
    out_d = nc.dram_tensor("out", [TOK, D], F32, kind="ExternalOutput").ap()

    es = ExitStack()
    with es:
        S = Sched(nc, es)
        uid = [0]

        def sbt(stack, shape, dt, nm="t"):
            uid[0] += 1
            name = "%s_%d" % (nm, uid[0])
            t = stack.enter_context(nc.sbuf_tensor(name, list(shape), dt))
            return t, Buf(name)

        def pst(nm, dt, n):
            t = es.enter_context(nc.psum_tensor(nm, [128, n], dt))
            return t

        PT = pst("PT", BF16, 1024)
        bPT = Buf("PT", True)
        PJ = [pst("PJ0", F32, 512), pst("PJ1", F32, 512)]
        bPJ = [Buf("PJ0", True), Buf("PJ1", True)]
        PM = pst("PM", F32, 512)
        _bpm = Buf("PM", True)
        bPM = [_bpm, _bpm]
        PA = [pst("PA0", F32, 512), pst("PA1", F32, 512)]
        bPA = [Buf("PA0", True), Buf("PA1", True)]
        PI = pst("PI", F32, 512)
        _bpi = Buf("PI", True)
        bPI = [_bpi, _bpi]
        PC = pst("PC", F32, 512)
        _bpc = Buf("PC", True)
        bPC = [_bpc, _bpc]

        def tt(eng, out, in0, in1, op, R, Wb):
            return S.op(eng, lambda e: e.tensor_tensor(out=out, in0=in0, in1=in1, op=op), R, Wb)

        def ts(eng, out, in0, s1, s2, op0, op1, R, Wb):
            if op1 is None and eng == "pool" and op0 == ALU.mult:
                return S.op(eng, lambda e: e.tensor_scalar(out=out, in0=in0, scalar1=s1, scalar2=0.0, op0=op0, op1=ALU.add), R, Wb)
            if op1 is None:
                return S.op(eng, lambda e: e.tensor_scalar(out=out, in0=in0, scalar1=s1, scalar2=None, op0=op0), R, Wb)
            return S.op(eng, lambda e: e.tensor_scalar(out=out, in0=in0, scalar1=s1, scalar2=s2, op0=op0, op1=op1), R, Wb)

        def stt(out, in0, scalar, in1, op0, op1, R, Wb):
            return S.op("dve", lambda e: e.scalar_tensor_tensor(out=out, in0=in0, scalar=scalar, in1=in1, op0=op0, op1=op1), R, Wb)

        def act(out, in_, func, R, Wb, bias=None, scale=None, accum=None):
            kw = {}
            if bias is not None:
                kw["bias"] = bias
            if scale is not None:
                kw["scale"] = scale
            if accum is not None:
                kw["accum_out"] = accum
            return S.op("act", lambda e: e.activation(out=out, in_=in_, func=func, **kw), R, Wb)

        def cp(eng, out, in_, R, Wb):
            if eng == "act":
                return act(out, in_, AF.Identity, R, Wb)
            return S.op(eng, lambda e: e.tensor_copy(out=out, in_=in_), R, Wb)

        def mm(out, lhsT, rhs, R, Wb, start=True, stop=True, inc=True):
            return S.op("pe", lambda e: e.matmul(out, lhsT=lhsT, rhs=rhs, start=start, stop=stop), R, Wb, inc=inc)

        def tr(out, in_, ident, R, Wb, inc=True):
            return S.op("pe", lambda e: e.transpose(out, in_, ident), R, Wb, inc=inc)

        def bc1(ap, n):
            return ap.unsqueeze(1).to_broadcast([ap.shape[0], n, ap.shape[1]])

        def bc2(ap, n):
            return ap.unsqueeze(2).to_broadcast([ap.shape[0], ap.shape[1], n])

        dbg_off = [0]
        if dbg_on:
            dstage = [sbt(es, [128, 2048], F32, "dstage%d" % i) for i in range(1)]
        dcount = [0]

        def dump(name, ap, buf):
            shape = list(ap.shape)
            n = 1
            for d_ in shape[1:]:
                n *= d_
            P_ = shape[0]
            st_t, st_b = dstage[0]
            dcount[0] += 1
            dst = st_t[0:P_, 0:n]
            if len(shape) == 3:
                dst = dst.rearrange("p (a b) -> p a b", b=shape[2])
            elif len(shape) == 4:
                dst = dst.rearrange("p (a b c) -> p a b c", b=shape[2], c=shape[3])
            cp("dve", dst, ap, [buf], [st_b])
            S.dma(dbg_d[0:P_, dbg_off[0]:dbg_off[0] + n], st_t[0:P_, 0:n], "dstage0", R=[st_b])
            _DBG["meta"].append((name, dbg_off[0], shape))
            dbg_off[0] += n

        def sub(name):
            if dbg_on and _DBG.get("sub") == name:
                S.dead = True

        def stage(name, fn=None):
            if dbg_on and _DBG["stop"] == name:
                S.dead = False
                if fn is not None:
                    fn()
                S.barrier()
                return True
            return False

        ident_f, b_ident_f = sbt(es, [128, 128], F32, "identf")
        ident_b, b_ident_b = sbt(es, [128, 128], BF16, "identb")
        maskv, b_maskv = sbt(es, [128, 1], F32, "maskv")
        modp, b_modp = sbt(es, [128, 4, 8], F32, "modp")
        gf_bc, b_gf = sbt(es, [128, D], F32, "gfbc")

        S.dma(ident_f[:], ident_d, "const", W=[b_ident_f], const=True)
        S.dma(maskv[:], maskv_d, "const", W=[b_maskv], const=True)

        esA = ExitStack()
        with esA:
            W1, b_W1 = sbt(esA, [128, 8, 2304], BF16, "W1")
            Wo, b_Wo = sbt(esA, [128, 8, D], BF16, "Wo")
            wglu, b_wglu = sbt(esA, [128, 4, 512], BF16, "wglu")
            lo2w, b_lo2w = sbt(esA, [128, 512], BF16, "lo2w")
            lo2a, b_lo2a = sbt(esA, [128, 512], BF16, "lo2a")
            g2, b_g2 = sbt(esA, [128, 512], BF16, "g2")
            bones_b, b_bones_b = sbt(esA, [128, 128], BF16, "bonesb")
            bones_f, b_bones_f = sbt(esA, [128, 128], F32, "bonesf")
            ones_b, b_ones_b = sbt(esA, [128, 128], BF16, "onesb")
            maskar, b_maskar = sbt(esA, [128, 128], F32, "maskar")
            masknt, b_masknt = sbt(esA, [128, 64], F32, "masknt")
            identp, b_identp = sbt(esA, [128, 64], F32, "identp")
            rst01, b_rst = sbt(esA, [128, 256], F32, "rst01")
            mu, b_mu = sbt(esA, [128, 14], F32, "mu")
            omu, b_omu = sbt(esA, [128, 14], F32, "omu")
            rwp, b_rwp = sbt(esA, [128, 8, 4], F32, "rwp")
            s5v, b_s5v = sbt(esA, [128, 3, 4], F32, "s5v")
            Bb_re, b_Bbre = sbt(esA, [128, 16, 128], BF16, "Bbre")
            Bb_im, b_Bbim = sbt(esA, [128, 16, 128], BF16, "Bbim")
            Cp_re, b_Cpre = sbt(esA, [128, 16, 128], BF16, "Cpre")
            Cp_imn, b_Cpim = sbt(esA, [128, 16, 128], BF16, "Cpimn")
            tcos, b_tcos = sbt(esA, [128, 16, 64], F32, "tcos")
            tsin, b_tsin = sbt(esA, [128, 16, 64], F32, "tsin")
            rho0, b_rho0 = sbt(esA, [128, 16, 64], F32, "rho0")
            rho1, b_rho1 = sbt(esA, [128, 16], F32, "rho1")
            T63, b_T63 = sbt(esA, [128, 2, 2, 16], F32, "T63")

            S.dma(maskar[:], maskar_d, "const", W=[b_maskar], const=True)
            S.dma(masknt[:], masknt_d, "const", W=[b_masknt], const=True)
            S.dma(identp[:], identp_d, "const", W=[b_identp], const=True)
            S.dma(rst01[:], rst_d, "const", W=[b_rst], const=True)
            S.dma(mu[:], mu_d, "const", W=[b_mu], const=True)
            S.dma(rwp[:, 0:7, :].rearrange("p a b -> p (a b)"), rwp_d, "const", W=[b_rwp], const=True)
            S.dma(s5v[:].rearrange("p a b -> p (a b)"), s5v_d, "const", W=[b_s5v], const=True)
            S.dma(bones_f[:], bones_d, "const", W=[b_bones_f], const=True)

            esS = ExitStack()
            with esS:
                c_l, b_c = sbt(esS, [128, 8], F32, "c")
                c_act, b_cact = sbt(esS, [128, 8], F32, "cact")
                c_rep, b_crep = sbt(esS, [128, 8, 128], F32, "crep")
                adaR, b_adaR = sbt(esS, [128, 6 * D], F32, "adaR")
                badab, b_badab = sbt(esS, [128, 6 * D], F32, "badab")
                ada_fm, b_adafm = sbt(esS, [128, 48], F32, "adafm")
                stg = [sbt(esS, [128, 8, 512], F32, "stg%d" % i) for i in range(2)]
                lst = [sbt(esS, [128, 512], F32, "lst%d" % i) for i in range(3)]
                S.dma(lst[0][0][:], lo2w_d, "const", W=[lst[0][1]], const=True)
                S.dma(lst[1][0][:], lo2a_d, "const", W=[lst[1][1]], const=True)
                S.dma(lst[2][0][:], g2_d, "const", W=[lst[2][1]], const=True)
                S.dma(c_l[:], c_d, "const", W=[b_c], const=True)
                S.dma(badab[:], bada_d.partition_broadcast(128), "const", W=[b_badab], const=True)
                S.finalize_consts("const")

                cp("dve", ident_b[:], ident_f[:], [b_ident_f], [b_ident_b])
                cp("dve", bones_b[:], bones_f[:], [b_bones_f], [b_bones_b])
                S.op("pool", lambda e: e.memset(ones_b[:], 1.0 / 512.0), [], [b_ones_b])
                ts("dve", bones_f[:], bones_f[:], 1.0 / 64.0, None, ALU.mult, None, [b_bones_f], [b_bones_f])
                ts("dve", omu[:], mu[:], -1.0, 1.0, ALU.mult, ALU.add, [b_mu], [b_omu])
                ts("dve", rwp[:, 7, :], rwp[:, 3, :], -1.0, 1.0, ALU.mult, ALU.add, [b_rwp], [b_rwp])
                cp("act", lo2w[:], lst[0][0][:], [lst[0][1]], [b_lo2w])
                cp("act", lo2a[:], lst[1][0][:], [lst[1][1]], [b_lo2a])
                cp("act", g2[:], lst[2][0][:], [lst[2][1]], [b_g2])

                act(c_act[:], c_l[:], AF.Silu, [b_c], [b_cact])
                cp("dve", c_rep[:], bc2(c_act[:], 128), [b_cact], [b_crep])
                wada_v = wada_d.rearrange("(k p) n -> p k n", p=128)
                for blk in range(12):
                    st_t, st_b = stg[blk % 2]
                    S.dma(st_t[:], wada_v[:, :, blk * 512:(blk + 1) * 512], "stg%d" % (blk % 2), W=[st_b])
                    pj, bpj = PJ[blk % 2], bPJ[blk % 2]
                    for k in range(8):
                        mm(pj[:, :], c_rep[:, k, :], st_t[:, k, :], [b_crep, st_b], [bpj],
                           start=(k == 0), stop=(k == 7), inc=(k == 7))
                    tt("dve", adaR[:, blk * 512:(blk + 1) * 512], pj[:, :], badab[:, blk * 512:(blk + 1) * 512],
                       ALU.add, [bpj, b_badab], [b_adaR])
                tt("dve", badab[:].rearrange("p (j q) -> p j q", q=128), adaR[:].rearrange("p (j q) -> p j q", q=128),
                   bc1(ident_f[:], 48), ALU.mult, [b_adaR, b_ident_f, b_badab], [b_badab])
                S.op("dve", lambda e: e.tensor_reduce(out=ada_fm[:], in_=badab[:].rearrange("p (j q) -> p j q", q=128),
                                                       axis=AX.X, op=ALU.add), [b_badab], [b_adafm])
                cp("dve", modp[:, 0, :], ada_fm[:, 0:8], [b_adafm], [b_modp])
                ts("dve", modp[:, 1, :], ada_fm[:, 8:16], 1.0, None, ALU.add, None, [b_adafm], [b_modp])
                cp("dve", modp[:, 2, :], ada_fm[:, 24:32], [b_adafm], [b_modp])
                ts("dve", modp[:, 3, :], ada_fm[:, 32:40], 1.0, None, ALU.add, None, [b_adafm], [b_modp])
                cp("dve", gf_bc[:], adaR[:, 5 * D:6 * D], [b_adaR], [b_gf])

                ci = 0
                for k in range(8):
                    st_t, st_b = stg[k % 2]
                    stv = st_t[:].rearrange("p a b -> p (a b)")
                    S.dma(stv[:, 0:2304], win_d[k * 128:(k + 1) * 128, :], "stg%d" % (k % 2), W=[st_b])
                    cp(("act", "dve", "pool")[ci % 3], W1[:, k, :], stv[:, 0:2304], [st_b], [b_W1])
                    ci += 1
                for k in range(8):
                    st_t, st_b = stg[k % 2]
                    stv = st_t[:].rearrange("p a b -> p (a b)")
                    S.dma(stv[:, 0:D], wout_d[k * 128:(k + 1) * 128, :], "stg%d" % (k % 2), W=[st_b])
                    tt("dve", Wo[:, k, :], stv[:, 0:D], adaR[:, 2 * D:3 * D], ALU.mult, [st_b, b_adaR], [b_Wo])
                for k in range(4):
                    st_t, st_b = stg[k % 2]
                    stv = st_t[:].rearrange("p a b -> p (a b)")
                    S.dma(stv[:, 0:512], wglu_d[k * 128:(k + 1) * 128, :], "stg%d" % (k % 2), W=[st_b])
                    cp("act", wglu[:, k, :], stv[:, 0:512], [st_b], [b_wglu])

                if stage('setup1', lambda: (dump('modp', modp[:], b_modp), dump('gf', gf_bc[:], b_gf), dump('W1', W1[:, 3, 0:2048], b_W1), dump('Wo', Wo[:, 2, :], b_Wo))):
                    return nc
                S.barrier()
            esS2 = ExitStack()
            with esS2:
                s5la, b_s5la = sbt(esS2, [128, 3, 16], F32, "s5la")
                idx1, b_idx1 = sbt(esS2, [128, 64], F32, "idx1")
                lb = {}
                for nm, dd in (("bpre", bpre_d), ("bpim", bpim_d), ("are", lbare_d), ("aim", lbaim_d), ("ldt", lbldt_d)):
                    lb[nm] = sbt(esS2, [128, 2048], F32, "lb" + nm)
                    S.dma(lb[nm][0][:], dd, "const2", W=[lb[nm][1]], const=True)
                cst = [sbt(esS2, [128, 2048], F32, "cst%d" % i) for i in range(1)]
                S.dma(s5la[:].rearrange("p a b -> p (a b)"), s5la_d, "const2", W=[b_s5la], const=True)
                S.dma(idx1[:], idx1_d, "const2", W=[b_idx1], const=True)
                S.finalize_consts("const2")
                tmp = [sbt(esS2, [128, 1024], F32, "s5t%d" % i) for i in range(8)]
                tmpi, b_tmpi = sbt(esS2, [128, 1024], I32, "s5ti")

                def sincos(ang, b_ang, n, o_sin, b_osin, o_cos, b_ocos, ta, tb):
                    for shift, o, bo in ((0.0, o_sin, b_osin), (0.5 * math.pi, o_cos, b_ocos)):
                        ts("dve", ta[0][:, 0:n], ang, shift, 1.0 / TWO_PI, ALU.add, ALU.mult, [b_ang], [ta[1]])
                        cp("dve", tmpi[:, 0:n], ta[0][:, 0:n], [ta[1]], [b_tmpi])
                        cp("dve", tb[0][:, 0:n], tmpi[:, 0:n], [b_tmpi], [tb[1]])
                        ts("dve", ta[0][:, 0:n], ang, shift, None, ALU.add, None, [b_ang], [ta[1]])
                        stt(ta[0][:, 0:n], tb[0][:, 0:n], -TWO_PI, ta[0][:, 0:n], ALU.mult, ALU.add, [tb[1], ta[1]], [ta[1]])
                        ts("dve", ta[0][:, 0:n], ta[0][:, 0:n], -PI_SAFE, PI_SAFE, ALU.max, ALU.min, [ta[1]], [ta[1]])
                        act(o, ta[0][:, 0:n], AF.Sin, [ta[1]], [bo])

                t_dt, t_xr, t_xi, t_mag, t_sin, t_cos, t_a, t_b = tmp
                Bre_flat = Bb_re[:].rearrange("p a b -> p (a b)")
                Bim_flat = Bb_im[:].rearrange("p a b -> p (a b)")
                for hh in range(2):
                    c2 = slice(hh * 1024, (hh + 1) * 1024)
                    are_t, are_b = lb["are"]
                    aim_t, aim_b = lb["aim"]
                    ldt_t, ldt_b = lb["ldt"]
                    bre_t, bre_b = lb["bpre"]
                    bim_t, bim_b = lb["bpim"]
                    A_re, A_im = are_t[:, c2], aim_t[:, c2]
                    act(t_dt[0][:], ldt_t[:, c2], AF.Exp, [ldt_b], [t_dt[1]])
                    tt("dve", t_xr[0][:], t_dt[0][:], A_re, ALU.mult, [t_dt[1], are_b], [t_xr[1]])
                    tt("dve", t_xi[0][:], t_dt[0][:], A_im, ALU.mult, [t_dt[1], aim_b], [t_xi[1]])
                    act(t_mag[0][:], t_xr[0][:], AF.Exp, [t_xr[1]], [t_mag[1]])
                    sincos(t_xi[0][:], t_xi[1], 1024, t_sin[0][:], t_sin[1], t_cos[0][:], t_cos[1], t_a, t_b)
                    tt("dve", t_cos[0][:], t_cos[0][:], t_mag[0][:], ALU.mult, [t_cos[1], t_mag[1]], [t_cos[1]])
                    ts("dve", t_cos[0][:], t_cos[0][:], -1.0, None, ALU.add, None, [t_cos[1]], [t_cos[1]])
                    tt("dve", t_sin[0][:], t_sin[0][:], t_mag[0][:], ALU.mult, [t_sin[1], t_mag[1]], [t_sin[1]])
                    tt("dve", t_dt[0][:], A_re, A_re, ALU.mult, [are_b], [t_dt[1]])
                    tt("dve", t_xr[0][:], A_im, A_im, ALU.mult, [aim_b], [t_xr[1]])
                    tt("dve", t_dt[0][:], t_dt[0][:], t_xr[0][:], ALU.add, [t_dt[1], t_xr[1]], [t_dt[1]])
                    S.op("dve", lambda e: e.reciprocal(out=t_dt[0][:], in_=t_dt[0][:]), [t_dt[1]], [t_dt[1]])
                    tt("dve", t_xr[0][:], t_cos[0][:], A_re, ALU.mult, [t_cos[1], are_b], [t_xr[1]])
                    tt("dve", t_a[0][:], t_sin[0][:], A_im, ALU.mult, [t_sin[1], aim_b], [t_a[1]])
                    tt("dve", t_xr[0][:], t_xr[0][:], t_a[0][:], ALU.add, [t_xr[1], t_a[1]], [t_xr[1]])
                    tt("dve", t_xr[0][:], t_xr[0][:], t_dt[0][:], ALU.mult, [t_xr[1], t_dt[1]], [t_xr[1]])
                    tt("dve", t_xi[0][:], t_sin[0][:], A_re, ALU.mult, [t_sin[1], are_b], [t_xi[1]])
                    tt("dve", t_a[0][:], t_cos[0][:], A_im, ALU.mult, [t_cos[1], aim_b], [t_a[1]])
                    tt("dve", t_xi[0][:], t_xi[0][:], t_a[0][:], ALU.subtract, [t_xi[1], t_a[1]], [t_xi[1]])
                    tt("dve", t_xi[0][:], t_xi[0][:], t_dt[0][:], ALU.mult, [t_xi[1], t_dt[1]], [t_xi[1]])
                    tt("dve", t_a[0][:], t_xr[0][:], bre_t[:, c2], ALU.mult, [t_xr[1], bre_b], [t_a[1]])
                    tt("dve", t_b[0][:], t_xi[0][:], bim_t[:, c2], ALU.mult, [t_xi[1], bim_b], [t_b[1]])
                    tt("dve", Bre_flat[:, c2], t_a[0][:], t_b[0][:], ALU.subtract, [t_a[1], t_b[1]], [b_Bbre])
                    tt("dve", t_a[0][:], t_xr[0][:], bim_t[:, c2], ALU.mult, [t_xr[1], bim_b], [t_a[1]])
                    tt("dve", t_b[0][:], t_xi[0][:], bre_t[:, c2], ALU.mult, [t_xi[1], bre_b], [t_b[1]])
                    tt("dve", Bim_flat[:, c2], t_a[0][:], t_b[0][:], ALU.add, [t_a[1], t_b[1]], [b_Bbim])
                S.dma(cst[0][0][:], cpre_d, "cst", W=[cst[0][1]])
                cp("act", Cp_re[:].rearrange("p a b -> p (a b)"), cst[0][0][:], [cst[0][1]], [b_Cpre])
                S.dma(cst[0][0][:], cpim_d, "cst", W=[cst[0][1]])
                S.op("act", lambda e: e.mul(Cp_imn[:].rearrange("p a b -> p (a b)"), cst[0][0][:], -1.0), [cst[0][1]], [b_Cpim])
                la_dt, la_th = t_mag, t_dt
                act(la_dt[0][:, 0:16], s5la[:, 2, :], AF.Exp, [b_s5la], [la_dt[1]])
                tt("dve", la_th[0][:, 0:16], la_dt[0][:, 0:16], s5la[:, 1, :], ALU.mult, [la_dt[1], b_s5la], [la_th[1]])
                tt("dve", la_dt[0][:, 0:16], la_dt[0][:, 0:16], s5la[:, 0, :], ALU.mult, [la_dt[1], b_s5la], [la_dt[1]])
                act(rho1[:], la_dt[0][:, 0:16], AF.Exp, [la_dt[1]], [b_rho1])
                tt("dve", t_xr[0][:, 0:1024].rearrange("p (a b) -> p a b", b=64), bc2(la_th[0][:, 0:16], 64), bc1(idx1[:], 16),
                   ALU.mult, [la_th[1], b_idx1], [t_xr[1]])
                sincos(t_xr[0][:, 0:1024], t_xr[1], 1024, tsin[:].rearrange("p a b -> p (a b)"), b_tsin,
                       tcos[:].rearrange("p a b -> p (a b)"), b_tcos, t_a, t_b)
                cp("dve", rho0[:], bc2(rho1[:], 64), [b_rho1], [b_rho0])
                S.op("dve", lambda e: e.memset(rho0[:, :, 0:1], 0.0), [], [b_rho0])
                tt("dve", T63[:, 0, 0, :], tcos[:, :, 63], rho1[:], ALU.mult, [b_tcos, b_rho1], [b_T63])
                tt("dve", T63[:, 0, 1, :], tsin[:, :, 63], rho1[:], ALU.mult, [b_tsin, b_rho1], [b_T63])
                tt("dve", T63[:, 1, 1, :], tcos[:, :, 63], rho1[:], ALU.mult, [b_tcos, b_rho1], [b_T63])
                stt(T63[:, 1, 0, :], tsin[:, :, 63], -1.0, rho1[:], ALU.mult, ALU.mult, [b_tsin, b_rho1], [b_T63])
                if stage('setup2', lambda: (dump('Bbre', Bb_re[:, 5, :], b_Bbre), dump('Bbim', Bb_im[:, 5, :], b_Bbim), dump('tcos', tcos[:], b_tcos), dump('tsin', tsin[:], b_tsin), dump('rho1', rho1[:], b_rho1), dump('rho0', rho0[:, 3, :], b_rho0), dump('Cpimn', Cp_imn[:, 9, :], b_Cpim))):
                    return nc
                S.barrier()

            xs = [sbt(esA, [128, D], F32, "xs%d" % i) for i in range(2)]
            xnb = [sbt(esA, [128, D], BF16, "xnb%d" % i) for i in range(1)]
            stat, b_stat = sbt(esA, [128, 8], F32, "stat")
            hT = [sbt(esA, [128, 8, W], BF16, "hT%d" % i) for i in range(1)]
            Pprev, b_Pprev = sbt(esA, [128, 14], F32, "Pprev")
            Pb = [sbt(esA, [128, W + 1], F32, "Pb%d" % i) for i in range(2)]
            zt = [sbt(esA, [128, W], F32, "zt%d" % i) for i in range(4)]
            LA, b_LA = sbt(esA, [128, W], BF16, "LA")
            gs, b_gs = sbt(esA, [128, W], BF16, "gs")
            ARs = [sbt(esA, [128, 4, 4, 128], BF16, "AR%d" % i) for i in range(2)]
            KHs = [sbt(esA, [128, 4, W], BF16, "KH%d" % i) for i in range(2)]
            BHs = [sbt(esA, [128, 4, W], BF16, "BH%d" % i) for i in range(2)]
            VBs = [sbt(esA, [128, 4, W], BF16, "VB%d" % i) for i in range(2)]
            kT, b_kT = sbt(esA, [128, 4, 4, 64], BF16, "kT")
            bT, b_bT = sbt(esA, [128, 4, 4, 64], BF16, "bT")
            vT, b_vT = sbt(esA, [128, 4, 4, 64], BF16, "vT")
            GLs = [sbt(esA, [128, 4, 4], F32, "GL%d" % i) for i in range(2)]
            gfms = [sbt(esA, [128, 4, W], BF16, "gfm%d" % i) for i in range(2)]
            bvs = [sbt(esA, [128, 4, W], BF16, "bv%d" % i) for i in range(2)]
            Yw, b_Yw = sbt(esA, [128, 4, W], F32, "Yw")
            ufms = [sbt(esA, [128, 4, W], BF16, "ufm%d" % i) for i in range(2)]
            Y5w, b_Y5w = sbt(esA, [128, 4, W], F32, "Y5w")
            ycat, b_ycat = sbt(esA, [128, 8, W], BF16, "ycat")
            et = [sbt(esA, [128, W], F32, "et%d" % i) for i in range(12)]
            etb = [sbt(esA, [128, W], BF16, "etb%d" % i) for i in range(2)]
            AbmS = [sbt(esA, [128, 4, 128], BF16, "Abm%d" % i) for i in range(2)]
            AkmS = [sbt(esA, [128, 4, 128], BF16, "Akm%d" % i) for i in range(2)]
            TfinS = [sbt(esA, [128, 4, 64], BF16, "Tfin%d" % i) for i in range(2)]
            ImT, b_ImT = sbt(esA, [128, 4, 64], F32, "ImT")
            Xb = [sbt(esA, [128, 4, 64], BF16, "Xb%d" % i) for i in range(2)]
            XTb = [sbt(esA, [128, 4, 64], BF16, "XTb%d" % i) for i in range(2)]
            Pm = [sbt(esA, [128, 4, 64], BF16, "Pm%d" % i) for i in range(2)]
            NTb, b_NTb = sbt(esA, [128, 4, 64], BF16, "NTb")
            Rb, b_Rb = Xb[0]
            TTb, b_TTb = Xb[1]
            WT, b_WT = sbt(esA, [128, 4, 64], BF16, "WT")
            UT, b_UT = sbt(esA, [128, 4, 64], BF16, "UT")
            Sf, b_Sf = sbt(esA, [128, 4, 64], F32, "Sf")
            Sb = [sbt(esA, [128, 4, 64], BF16, "Sb%d" % i) for i in range(2)]
            stmp, b_stmp = sbt(esA, [128, 4, 64], F32, "stmp")
            s5a = [sbt(esA, [128, 512], F32, "s5a%d" % i) for i in range(4)]
            s_tr, b_str = sbt(esA, [128, 512], F32, "str")
            s_ti, b_sti = sbt(esA, [128, 512], F32, "sti")
            srb, b_srb = sbt(esA, [128, 512], BF16, "srb")
            sib, b_sib = sbt(esA, [128, 512], BF16, "sib")
            s5s, b_s5s = sbt(esA, [128, 2, 16], F32, "s5s")
            s5q, b_s5q = sbt(esA, [128, 6, 8], F32, "s5q")

            S.op("dve", lambda e: e.memset(Pprev[:], 0.0), [], [b_Pprev])
            S.op("dve", lambda e: e.memset(Sf[:], 0.0), [], [b_Sf])
            S.op("dve", lambda e: e.memset(Sb[0][0][:], 0.0), [], [Sb[0][1]])
            S.op("dve", lambda e: e.memset(s5s[:], 0.0), [], [b_s5s])
            S.op("dve", lambda e: e.memset(ARs[0][0][:], 0.0), [], [ARs[0][1]])
            S.op("dve", lambda e: e.memset(ARs[1][0][:], 0.0), [], [ARs[1][1]])
            M = {}

            cur_s = [0]
            lastT = [None]
            xslot = [0]

            def load_x(row0):
                i = xslot[0] % 2
                xslot[0] += 1
                t, b = xs[i]
                S.dma(t[:], xcat[row0:row0 + 128, :], "xs%d" % i, W=[b])
                return t, b

            def norm_transpose(xt, bx, st, hT_t, hT_b, sh_i, sc_i, width):
                xn_t, xn_b = xnb[0]
                col = st % 4
                act(xn_t[:], xt[:], AF.Square, [bx], [xn_b, b_stat], accum=stat[:, col:col + 1])
                act(stat[:, 4 + col:5 + col], stat[:, col:col + 1], AF.Ln, [b_stat], [b_stat], bias=1e-6, scale=1.0 / D)
                act(stat[:, 4 + col:5 + col], stat[:, 4 + col:5 + col], AF.Exp, [b_stat], [b_stat], scale=-0.5)
                act(xn_t[:], xt[:], AF.Identity, [bx, b_stat], [xn_b], scale=stat[:, 4 + col:5 + col])
                for k in range(8):
                    tr(PT[:, k * 128:(k + 1) * 128], xn_t[:, k * 128:(k + 1) * 128], ident_b[:], [xn_b, b_ident_b], [bPT], inc=(k == 7))
                for k in range(8):
                    o = hT_t[:, k, st * 128:(st + 1) * 128]
                    i_ = PT[:, k * 128:(k + 1) * 128]
                    if True:
                        act(o, i_, AF.Identity, [bPT, b_modp], [hT_b], scale=modp[:, sc_i, k:k + 1], bias=modp[:, sh_i, k:k + 1])
                    else:
                        ts("dve", o, i_, modp[:, sc_i, k:k + 1], modp[:, sh_i, k:k + 1], ALU.mult, ALU.add, [bPT, b_modp], [hT_b])

            pbi = [0]
            pji = [0]

            def project(hT_t, hT_b, ct):
                pj, bpj = PA[1], bPA[1]
                for k in range(8):
                    mm(pj[:, 0:W], W1[:, k, ct * 128:(ct + 1) * 128], hT_t[:, k, :], [b_W1, hT_b], [bpj],
                       start=(k == 0), stop=(k == 7), inc=(k == 7))
                return pj, bpj

            def shifted(hT_t, hT_b, ct, zo, b_zo):
                pj, bpj = project(hT_t, hT_b, ct)
                p_t, p_b = Pb[pbi[0] % 2]
                pbi[0] += 1
                cp("pool", p_t[:, 0:1], Pprev[:, ct:ct + 1], [b_Pprev], [p_b])
                act(p_t[:, 1:W + 1], pj[:, 0:W], AF.Identity, [bpj], [p_b])
                cp("pool", Pprev[:, ct:ct + 1], p_t[:, W:W + 1], [p_b], [b_Pprev])
                act(zo, p_t[:, 0:W], AF.Identity, [p_b, b_mu], [b_zo], scale=mu[:, ct:ct + 1])
                stt(zo, p_t[:, 1:W + 1], omu[:, ct:ct + 1], zo, ALU.mult, ALU.add, [p_b, b_omu, b_zo], [b_zo])

            def rp(i, f):
                return rwp[:, i, f:f + 1]

            def drive(gens, weights=None, background=()):
                gens = list(gens)
                bg = list(background)
                wts = {id(g_): 1 for g_ in gens}
                if weights:
                    for g_, w_ in zip(gens, weights):
                        wts[id(g_)] = w_
                while gens:
                    for g_ in list(gens):
                        for _ in range(wts[id(g_)]):
                            try:
                                next(g_)
                            except StopIteration:
                                gens.remove(g_)
                                break
                    for g_ in list(bg):
                        try:
                            next(g_)
                        except StopIteration:
                            bg.remove(g_)

            def pre(idx):
                own = idx >= NWIN
                win = idx % NWIN
                par = idx % 2
                AR, b_AR = ARs[par]
                KH, b_KH = KHs[par]
                BH, b_BH = BHs[par]
                GL, b_GL = GLs[par]
                gfm, b_gfm = gfms[par]
                bv, b_bv = bvs[par]
                ufm, b_ufm = ufms[par]
                row_base = (TOK if own else 0) + win * W
                hT_t, hT_b = hT[0]
                for st in range(W // 128):
                    xt, bx = load_x(row_base + st * 128)
                    norm_transpose(xt, bx, st, hT_t, hT_b, 0, 1, W)
                    yield
                z12, b_z12 = zt[3]
                shifted(hT_t, hT_b, 12, z12[:], b_z12)
                act(LA[0:64, :], z12[0:64, :], AF.Tanh, [b_z12], [b_LA])
                cp("pool", LA[64:128, :], z12[64:128, :], [b_z12], [b_LA])
                yield
                need_r = own or (win == NWIN - 1)
                if need_r:
                    shifted(hT_t, hT_b, 13, z12[:], b_z12)
                if own:
                    act(gs[:], z12[:], AF.Sigmoid, [b_z12], [b_gs])
                yield
                for f in range(4):
                    (zr, b_zr), (zk, b_zk), (zv, b_zv) = zt[0], zt[1], zt[2]
                    if need_r:
                        shifted(hT_t, hT_b, f, zr[:], b_zr)
                        yield
                    shifted(hT_t, hT_b, 4 + f, zk[:], b_zk)
                    yield
                    shifted(hT_t, hT_b, 8 + f, zv[:], b_zv)
                    yield
                    (sw, b_sw), (asg, b_asg), (kkr, b_kkr), (nrm, b_nrm), (kk, b_kk), (t1, b_t1) = et[0:6]
                    (kmod, b_kmod), (bvec, b_bvec), (lw, b_lw), (cc, b_cc), (cm, b_cm), (ex, b_ex) = et[6:12]
                    (sq, b_sq), (rk, b_rk) = etb
                    mm(PA[1][:, 0:W], lo2w[:, f * 128:(f + 1) * 128], LA[:], [b_lo2w, b_LA], [bPA[1]])
                    mm(PA[1][:, W:2 * W], lo2a[:, f * 128:(f + 1) * 128], LA[:], [b_lo2a, b_LA], [bPA[1]])
                    act(sw[:], PA[1][:, 0:W], AF.Sigmoid, [bPA[1], b_rwp], [b_sw], bias=rp(0, f))
                    act(asg[:], PA[1][:, W:2 * W], AF.Sigmoid, [bPA[1], b_rwp], [b_asg], bias=rp(1, f))
                    if own:
                        mm(PA[1][:, 0:W], g2[:, f * 128:(f + 1) * 128], gs[:], [b_g2, b_gs], [bPA[1]])
                        act(gfm[:, f, :], PA[1][:, 0:W], AF.Identity, [bPA[1]], [b_gfm])
                    act(kkr[:], zk[:], AF.Identity, [b_zk, b_rwp], [b_kkr], scale=rp(2, f))
                    act(sq[:], kkr[:], AF.Square, [b_kkr], [b_sq])
                    yield
                    mm(PA[1][:, 0:W], bones_b[:], sq[:], [b_bones_b, b_sq], [bPA[1]])
                    act(nrm[:], PA[1][:, 0:W], AF.Sqrt, [bPA[1]], [b_nrm])
                    ts("dve", nrm[:], nrm[:], 1e-12, None, ALU.max, None, [b_nrm], [b_nrm])
                    S.op("dve", lambda e: e.reciprocal(out=nrm[:], in_=nrm[:]), [b_nrm], [b_nrm])
                    tt("pool", kk[:], kkr[:], nrm[:], ALU.mult, [b_kkr, b_nrm], [b_kk])
                    act(t1[:], asg[:], AF.Identity, [b_asg, b_rwp], [b_t1], scale=rp(3, f), bias=rp(7, f))
                    yield
                    tt("pool", kmod[:], zk[:], t1[:], ALU.mult, [b_zk, b_t1], [b_kmod])
                    tt("pool", bvec[:], kk[:], asg[:], ALU.mult, [b_kk, b_asg], [b_bvec])
                    S.op("act", lambda e: e.mul(lw[:], sw[:], -math.exp(-0.5)), [b_sw], [b_lw])
                    S.op("dve", lambda e: e.tensor_tensor_scan(out=cc[:], data0=rst01[:], data1=lw[:], initial=0.0,
                                                                op0=ALU.mult, op1=ALU.add), [b_rst, b_lw], [b_cc])
                    yield
                    tt("pool", cm[:], cc[:], lw[:], ALU.subtract, [b_cc, b_lw], [b_cm])
                    act(sw[:], cc[:], AF.Exp, [b_cc], [b_sw])
                    act(ex[:], cc[:], AF.Exp, [b_cc], [b_ex], scale=-1.0)
                    act(cm[:], cm[:], AF.Exp, [b_cm], [b_cm])
                    yield
                    if own:
                        tt("dve", AR[:, f, :, 64:128], zr[:].rearrange("p (c t) -> p c t", t=64),
                           sw[:].rearrange("p (c t) -> p c t", t=64), ALU.mult, [b_zr, b_sw], [b_AR])
                    stt(AR[:, f, :, 0:64], kk[:].rearrange("p (c t) -> p c t", t=64), -1.0,
                        cm[:].rearrange("p (c t) -> p c t", t=64), ALU.mult, ALU.mult, [b_kk, b_cm], [b_AR])
                    tt("dve", KH[:, f, :], kmod[:], ex[:], ALU.mult, [b_kmod, b_ex], [b_KH])
                    tt("pool", BH[:, f, :], bvec[:], ex[:], ALU.mult, [b_bvec, b_ex], [b_BH])
                    cp("pool", GL[:, f, :], sw[:].rearrange("p (c t) -> p c t", t=64)[:, :, 63], [b_sw], [b_GL])
                    yield
                    if own:
                        stt(rk[:], zr[:], rp(4, f), kmod[:], ALU.mult, ALU.mult, [b_zr, b_rwp, b_kmod], [b_rk])
                        mm(PA[1][:, 0:W], bones_b[:], rk[:], [b_bones_b, b_rk], [bPA[1]])
                        tt("dve", bv[:, f, :], PA[1][:, 0:W], zv[:], ALU.mult, [bPA[1], b_zv], [b_bv])
                        yield
                    cp("act", VBs[par][0][:, f, :], zv[:], [b_zv], [VBs[par][1]])
                    yield
                for t in range(4):
                    pj, bpj = project(hT_t, hT_b, 14 + t)
                    act(ufm[:, t, :], pj[:, 0:W], AF.Identity, [bpj], [b_ufm])
                    yield

            def main(idx, nxt):
                own = idx >= NWIN
                win = idx % NWIN
                par = idx % 2
                M['AR'] = ARs[par]
                M['KH'] = KHs[par]
                M['BH'] = BHs[par]
                M['GL'] = GLs[par]
                M['gfm'] = gfms[par]
                M['bv'] = bvs[par]
                M['ufm'] = ufms[par]
                KH, b_KH = KHs[par]
                BH, b_BH = BHs[par]
                VB, b_VB = VBs[par]
                for (src, b_src, dst, b_dst) in ((KH, b_KH, kT, b_kT), (BH, b_BH, bT, b_bT), (VB, b_VB, vT, b_vT)):
                    n = 0
                    for c in range(4):
                        for f in range(4):
                            for hp in range(2):
                                rs = slice(hp * 64, hp * 64 + 64)
                                n += 1
                                o0 = (c * 4 + f) * 64
                                tr(PT[rs, o0:o0 + 64], src[rs, f, c * 64:(c + 1) * 64], ident_b[rs, rs],
                                   [b_src, b_ident_b], [bPT], inc=(n == 32))
                    cp("act", dst[:].rearrange("p a b c -> p (a b c)"), PT[:, 0:1024], [bPT], [b_dst])
                if M.get('inv0_done') != idx:
                    drive([chunk_inv(0, own, 0, par)])
                extra = [nxt] if nxt is not None else []
                for c in range(4):
                    grp = [chunk_chain(c, own, c % 2), s5_chunk(c, own)]
                    wts = [2, 1]
                    if c < 3:
                        grp = [chunk_inv(c + 1, own, (c + 1) % 2, par)] + grp
                        wts = [2, 2, 1]
                    elif nxt is not None:
                        drive([nxt])
                        grp = [chunk_inv(0, (idx + 1) >= NWIN, 0, (idx + 1) % 2)] + grp
                        wts = [2, 2, 1]
                        M['inv0_done'] = idx + 1
                    drive(grp, wts, background=extra)
                    extra = []
                    if nxt is not None:
                        extra = [nxt]
                if nxt is not None:
                    drive([nxt])
                if own:
                    outputs(win, None, None)

            HEADS = [(f, hp) for f in range(4) for hp in range(2)]

            def chunk_inv(c, own, slot, par):
                AR, b_AR = ARs[par]
                KH, b_KH = KHs[par]
                BH, b_BH = BHs[par]
                cs = slice(c * 64, (c + 1) * 64)
                na = 128 if own else 64
                heads = HEADS
                Abm, b_Abm = AbmS[slot]
                Akm, b_Akm = AkmS[slot]
                for i, (f, hp) in enumerate(heads):
                    rs = slice(hp * 64, hp * 64 + 64)
                    mm(PA[0][rs, f * 128:f * 128 + na], BH[rs, f, cs], AR[rs, f, c, 0:na], [b_BH, b_AR], [bPA[0]], inc=(i == 7))
                for i, (f, hp) in enumerate(heads):
                    rs = slice(hp * 64, hp * 64 + 64)
                    mm(PA[1][rs, f * 128:f * 128 + na], KH[rs, f, cs], AR[rs, f, c, 0:na], [b_KH, b_AR], [bPA[1]], inc=(i == 7))
                for i, (f, hp) in enumerate(heads):
                    rs = slice(hp * 64, hp * 64 + 64)
                    mm(PI[rs, f * 64:(f + 1) * 64], AR[rs, f, c, 0:64], BH[rs, f, cs], [b_AR, b_BH], [bPI[0]], inc=(i == 7))
                yield
                pa0 = PA[0][:, :].rearrange("p (f n) -> p f n", n=128)
                pa1 = PA[1][:, :].rearrange("p (f n) -> p f n", n=128)
                mk = maskar[:, 0:na]
                tt("dve", Abm[:, :, 0:na], pa0[:, :, 0:na], bc1(mk, 4), ALU.mult, [bPA[0], b_maskar], [b_Abm])
                xt_t, xt_b = NTb, b_NTb
                tt("dve", xt_t[:], PI[:, 0:256].rearrange("p (f n) -> p f n", n=64), bc1(masknt[:], 4), ALU.mult,
                   [bPI[0], b_masknt], [xt_b])
                tt("dve", Akm[:, :, 0:na], pa1[:, :, 0:na], bc1(mk, 4), ALU.mult, [bPA[1], b_maskar], [b_Akm])
                p_t, p_b = Pm[0]
                tt("pool", p_t[:], Abm[:, :, 0:64], bc1(identp[:], 4), ALU.add, [b_Abm, b_identp], [p_b])
                yield
                x_ap = lambda f, rs: Abm[rs, f, 0:64]
                x_b = b_Abm
                for lvl in range(1, 6):
                    xtn_t, xtn_b = XTb[lvl % 2]
                    for i, (f, hp) in enumerate(heads):
                        rs = slice(hp * 64, hp * 64 + 64)
                        mm(PI[rs, f * 64:(f + 1) * 64], x_ap(f, rs), xt_t[rs, f, :], [x_b, xt_b], [bPI[0]], inc=(i == 7))
                    if lvl <= 4:
                        xn_t, xn_b = Xb[lvl % 2]
                        for i, (f, hp) in enumerate(heads):
                            rs = slice(hp * 64, hp * 64 + 64)
                            mm(PA[0][rs, f * 64:(f + 1) * 64], xt_t[rs, f, :], x_ap(f, rs), [x_b, xt_b], [bPA[0]], inc=(i == 7))
                    yield
                    cp("act", xtn_t[:], PI[:, 0:256].rearrange("p (f n) -> p f n", n=64), [bPI[0]], [xtn_b])
                    if lvl <= 4:
                        cp("act", xn_t[:], PA[0][:, 0:256].rearrange("p (f n) -> p f n", n=64), [bPA[0]], [xn_b])
                    yield
                    pn_t, pn_b = Pm[lvl % 2]
                    for i, (f, hp) in enumerate(heads):
                        rs = slice(hp * 64, hp * 64 + 64)
                        mm(PC[rs, f * 64:(f + 1) * 64], xtn_t[rs, f, :], p_t[rs, f, :], [xtn_b, p_b], [bPC[0]], inc=(i == 7))
                    yield
                    tt("dve", pn_t[:], PC[:, 0:256].rearrange("p (f n) -> p f n", n=64), p_t[:], ALU.add, [bPC[0], p_b], [pn_b])
                    yield
                    p_t, p_b = pn_t, pn_b
                    xt_t, xt_b = xtn_t, xtn_b
                    if lvl <= 4:
                        x_ap = (lambda t_: (lambda f, rs: t_[rs, f, :]))(xn_t)
                        x_b = xn_b
                T0_t, T0_b = p_t, p_b
                Rb, b_Rb = Xb[0]
                TTb, b_TTb = Xb[1]
                for i, (f, hp) in enumerate(heads):
                    rs = slice(hp * 64, hp * 64 + 64)
                    mm(PI[rs, f * 64:(f + 1) * 64], NTb[rs, f, :], T0_t[rs, f, :], [b_NTb, T0_b], [bPI[0]], inc=(i == 7))
                for i, (f, hp) in enumerate(heads):
                    rs = slice(hp * 64, hp * 64 + 64)
                    tr(PT[rs, f * 64:(f + 1) * 64], T0_t[rs, f, :], ident_b[rs, rs], [T0_b, b_ident_b], [bPT], inc=(i == 7))
                tt("pool", ImT[:], bc1(identp[:], 4), T0_t[:], ALU.subtract, [b_identp, T0_b], [b_ImT])
                yield
                tt("dve", Rb[:], PI[:, 0:256].rearrange("p (f n) -> p f n", n=64), ImT[:], ALU.add, [bPI[0], b_ImT], [b_Rb])
                cp("act", TTb[:], PT[:, 0:256].rearrange("p (f n) -> p f n", n=64), [bPT], [b_TTb])
                yield
                for i, (f, hp) in enumerate(heads):
                    rs = slice(hp * 64, hp * 64 + 64)
                    mm(PC[rs, f * 64:(f + 1) * 64], TTb[rs, f, :], Rb[rs, f, :], [b_TTb, b_Rb], [bPC[0]], inc=(i == 7))
                yield
                T_t, T_b = TfinS[slot]
                tt("dve", T_t[:], PC[:, 0:256].rearrange("p (f n) -> p f n", n=64), T0_t[:], ALU.add, [bPC[0], T0_b], [T_b])
                lastT[0] = (T_t, T_b)
                yield

            def chunk_chain(c, own, slot):
                AR, b_AR = M['AR']
                KH, b_KH = M['KH']
                BH, b_BH = M['BH']
                GL, b_GL = M['GL']
                gfm, b_gfm = M['gfm']
                bv, b_bv = M['bv']
                ufm, b_ufm = M['ufm']
                cs = slice(c * 64, (c + 1) * 64)
                heads = HEADS
                Abm, b_Abm = AbmS[slot]
                Akm, b_Akm = AkmS[slot]
                T_t, T_b = TfinS[slot]
                s0_t, s0_b = Sb[cur_s[0] % 2]
                s1_t, s1_b = Sb[(cur_s[0] + 1) % 2]
                cur_s[0] += 1
                pm3 = PM[:, 0:256].rearrange("p (f n) -> p f n", n=64)
                for i, (f, hp) in enumerate(heads):
                    rs = slice(hp * 64, hp * 64 + 64)
                    mm(PM[rs, f * 64:(f + 1) * 64], AR[rs, f, c, 0:64], s0_t[rs, f, :], [b_AR, s0_b], [bPM[0]], start=True, stop=False, inc=False)
                    mm(PM[rs, f * 64:(f + 1) * 64], Akm[rs, f, 0:64], vT[rs, c, f, :], [b_Akm, b_vT], [bPM[0]], start=False, stop=True, inc=(i == 7))
                yield
                cp("act", WT[:], pm3, [bPM[0]], [b_WT])
                yield
                for i, (f, hp) in enumerate(heads):
                    rs = slice(hp * 64, hp * 64 + 64)
                    mm(PM[rs, f * 64:(f + 1) * 64], T_t[rs, f, :], WT[rs, f, :], [T_b, b_WT], [bPM[0]], inc=(i == 7))
                yield
                cp("act", UT[:], pm3, [bPM[0]], [b_UT])
                yield
                for i, (f, hp) in enumerate(heads):
                    rs = slice(hp * 64, hp * 64 + 64)
                    mm(PM[rs, f * 64:(f + 1) * 64], bT[rs, c, f, :], UT[rs, f, :], [b_bT, b_UT], [bPM[0]], start=True, stop=False, inc=False)
                    mm(PM[rs, f * 64:(f + 1) * 64], kT[rs, c, f, :], vT[rs, c, f, :], [b_kT, b_vT], [bPM[0]], start=False, stop=True, inc=(i == 7))
                yield
                tt("dve", stmp[:], pm3, Sf[:], ALU.add, [bPM[0], b_Sf], [b_stmp])
                yield
                if own:
                    for i, (f, hp) in enumerate(heads):
                        rs = slice(hp * 64, hp * 64 + 64)
                        o = PM[rs, f * 64:(f + 1) * 64]
                        mm(o, s0_t[rs, f, :], AR[rs, f, c, 64:128], [s0_b, b_AR], [bPM[0]], start=True, stop=False, inc=False)
                        mm(o, UT[rs, f, :], Abm[rs, f, 64:128], [b_UT, b_Abm], [bPM[0]], start=False, stop=False, inc=False)
                        mm(o, vT[rs, c, f, :], Akm[rs, f, 64:128], [b_vT, b_Akm], [bPM[0]], start=False, stop=True, inc=(i == 7))
                    yield
                    cp("act", Yw[:, :, cs], pm3, [bPM[0]], [b_Yw])
                tt("dve", Sf[:], stmp[:], bc2(GL[:, :, c], 64), ALU.mult, [b_stmp, b_GL], [b_Sf])
                cp("act", s1_t[:], Sf[:], [b_Sf], [s1_b])
                yield

            def s5_chunk(c, own):
                AR, b_AR = M['AR']
                KH, b_KH = M['KH']
                BH, b_BH = M['BH']
                GL, b_GL = M['GL']
                gfm, b_gfm = M['gfm']
                bv, b_bv = M['bv']
                ufm, b_ufm = M['ufm']
                cs = slice(c * 64, (c + 1) * 64)
                for hf in range(2):
                    ps_ = slice(hf * 8, hf * 8 + 8)
                    for pl in range(8):
                        pr = hf * 8 + pl
                        t = pr // 4
                        mm(PJ[0][:, pl * 64:(pl + 1) * 64], Bb_re[:, pr, :], ufm[:, t, cs], [b_Bbre, b_ufm], [bPJ[0]], inc=(pl == 7))
                    for pl in range(8):
                        pr = hf * 8 + pl
                        t = pr // 4
                        mm(PJ[1][:, pl * 64:(pl + 1) * 64], Bb_im[:, pr, :], ufm[:, t, cs], [b_Bbim, b_ufm], [bPJ[1]], inc=(pl == 7))
                    yield
                    cosv = tcos[:, ps_, :].rearrange("p a b -> p (a b)")
                    sinv = tsin[:, ps_, :].rearrange("p a b -> p (a b)")
                    (a0, ba0), (a1, ba1), (a2, ba2), (a3, ba3) = s5a
                    tt("dve", a0[:], PJ[0][:, :], cosv, ALU.mult, [bPJ[0], b_tcos], [ba0])
                    tt("dve", a1[:], PJ[1][:, :], sinv, ALU.mult, [bPJ[1], b_tsin], [ba1])
                    tt("dve", a2[:], PJ[1][:, :], cosv, ALU.mult, [bPJ[1], b_tcos], [ba2])
                    tt("dve", a3[:], PJ[0][:, :], sinv, ALU.mult, [bPJ[0], b_tsin], [ba3])
                    btr, b_btr, bti, b_bti = a0, ba0, a2, ba2
                    yield
                    tt("pool", btr[:], a0[:], a1[:], ALU.add, [ba0, ba1], [b_btr])
                    tt("pool", bti[:], a2[:], a3[:], ALU.subtract, [ba2, ba3], [b_bti])
                    b3r = btr[:].rearrange("p (a b) -> p a b", b=64)
                    b3i = bti[:].rearrange("p (a b) -> p a b", b=64)
                    tt("pool", b3r[:, :, 0], b3r[:, :, 0], s5s[:, 0, ps_], ALU.add, [b_btr, b_s5s], [b_btr])
                    tt("pool", b3i[:, :, 0], b3i[:, :, 0], s5s[:, 1, ps_], ALU.add, [b_bti, b_s5s], [b_bti])
                    yield
                    rv = rho0[:, ps_, :].rearrange("p a b -> p (a b)")
                    S.op("dve", lambda e: e.tensor_tensor_scan(out=s_tr[:], data0=rv, data1=btr[:], initial=0.0,
                                                                op0=ALU.mult, op1=ALU.add), [b_rho0, b_btr], [b_str])
                    S.op("dve", lambda e: e.tensor_tensor_scan(out=s_ti[:], data0=rv, data1=bti[:], initial=0.0,
                                                                op0=ALU.mult, op1=ALU.add), [b_rho0, b_bti], [b_sti])
                    yield
                    s3r = s_tr[:].rearrange("p (a b) -> p a b", b=64)
                    s3i = s_ti[:].rearrange("p (a b) -> p a b", b=64)
                    qa = s5q[:, 0:2, :]
                    qb = s5q[:, 2:4, :]
                    tt("pool", qa, s3r[:, :, 63].unsqueeze(1).to_broadcast([128, 2, 8]), T63[:, 0, :, ps_], ALU.mult, [b_str, b_T63], [b_s5q])
                    tt("pool", qb, s3i[:, :, 63].unsqueeze(1).to_broadcast([128, 2, 8]), T63[:, 1, :, ps_], ALU.mult, [b_sti, b_T63], [b_s5q])
                    tt("pool", s5s[:, :, ps_], qa, qb, ALU.add, [b_s5q], [b_s5s])
                    yield
                    if own:
                        tt("dve", a0[:], s_tr[:], cosv, ALU.mult, [b_str, b_tcos], [ba0])
                        tt("pool", a1[:], s_ti[:], sinv, ALU.mult, [b_sti, b_tsin], [ba1])
                        tt("dve", a2[:], s_tr[:], sinv, ALU.mult, [b_str, b_tsin], [ba2])
                        tt("pool", a3[:], s_ti[:], cosv, ALU.mult, [b_sti, b_tcos], [ba3])
                        tt("dve", srb[:], a0[:], a1[:], ALU.subtract, [ba0, ba1], [b_srb])
                        tt("pool", sib[:], a2[:], a3[:], ALU.add, [ba2, ba3], [b_sib])
                        yield
                        for pl in range(8):
                            pr = hf * 8 + pl
                            t = pr // 4
                            tl = t % 2
                            o = PJ[0][:, tl * 64:(tl + 1) * 64]
                            mm(o, Cp_re[:, pr, :], srb[:, pl * 64:(pl + 1) * 64], [b_Cpre, b_srb], [bPJ[0]],
                               start=(pr % 4 == 0), stop=False, inc=False)
                            mm(o, Cp_imn[:, pr, :], sib[:, pl * 64:(pl + 1) * 64], [b_Cpim, b_sib], [bPJ[0]],
                               start=False, stop=(pr % 4 == 3), inc=(pl == 7))
                        yield
                        for tl in range(2):
                            t = hf * 2 + tl
                            stt(Y5w[:, t, cs], ufm[:, t, cs], s5v[:, 0, t:t + 1], PJ[0][:, tl * 64:(tl + 1) * 64], ALU.mult, ALU.add,
                                [b_ufm, b_s5v, bPJ[0]], [b_Y5w])
                    yield

            def outputs(win, hT_t, hT_b):
                AR, b_AR = M['AR']
                KH, b_KH = M['KH']
                BH, b_BH = M['BH']
                GL, b_GL = M['GL']
                gfm, b_gfm = M['gfm']
                bv, b_bv = M['bv']
                ufm, b_ufm = M['ufm']
                xres = [load_x(TOK + win * W + st * 128) + (xslot[0] - 1,) for st in range(W // 128)]
                for f in range(4):
                    (ysq, b_ysq), (mu2, b_mu2), (var, b_var), (dd, b_dd) = et[(f % 2) * 4:(f % 2) * 4 + 4]
                    act(ysq[:], Yw[:, f, :], AF.Square, [b_Yw], [b_ysq])
                    mm(PM[:, 0:W], bones_f[:], Yw[:, f, :], [b_bones_f, b_Yw], [bPM[0]])
                    mm(PA[0][:, 0:W], bones_f[:], ysq[:], [b_bones_f, b_ysq], [bPA[0]])
                    act(mu2[:], PM[:, 0:W], AF.Square, [bPM[0]], [b_mu2])
                    tt("dve", var[:], PA[0][:, 0:W], mu2[:], ALU.subtract, [bPA[0], b_mu2], [b_var])
                    act(var[:], var[:], AF.Ln, [b_var], [b_var], bias=64e-5, scale=1.0)
                    act(var[:], var[:], AF.Exp, [b_var], [b_var], scale=-0.5)
                    tt("dve", dd[:], Yw[:, f, :], PM[:, 0:W], ALU.subtract, [b_Yw, bPM[0]], [b_dd])
                    tt("pool", dd[:], dd[:], var[:], ALU.mult, [b_dd, b_var], [b_dd])
                    ts("pool", dd[:], dd[:], rp(5, f), rp(6, f), ALU.mult, ALU.add, [b_dd, b_rwp], [b_dd])
                    tt("pool", dd[:], dd[:], bv[:, f, :], ALU.add, [b_dd, b_bv], [b_dd])
                    tt("dve", ycat[:, f, :], dd[:], gfm[:, f, :], ALU.mult, [b_dd, b_gfm], [b_ycat])
                zzbv = lambda t: (srb if t < 2 else sib)[:, (t % 2) * W:(t % 2 + 1) * W]
                zzbb = lambda t: (b_srb if t < 2 else b_sib)
                zzv = lambda t: (s_tr if t < 2 else s_ti)[:, (t % 2) * W:(t % 2 + 1) * W]
                zzb_ = lambda t: (b_str if t < 2 else b_sti)
                oo, b_oo = sbt_oo
                (x2, b_x2), (pq, b_pq), (sg, b_sg) = et[8:11]
                (osq, b_osq), _ = etb
                for t in range(4):
                    act(x2[:], Y5w[:, t, :], AF.Square, [b_Y5w], [b_x2])
                    ts("pool", pq[:], x2[:], 0.044715, 1.0, ALU.mult, ALU.add, [b_x2], [b_pq])
                    tt("pool", pq[:], pq[:], Y5w[:, t, :], ALU.mult, [b_pq, b_Y5w], [b_pq])
                    act(sg[:], pq[:], AF.Sigmoid, [b_pq], [b_sg], scale=2.0 * math.sqrt(2.0 / math.pi))
                    tt("dve", zzv(t), Y5w[:, t, :], sg[:], ALU.mult, [b_Y5w, b_sg], [zzb_(t)])
                    cp("act", zzbv(t), zzv(t), [zzb_(t)], [zzbb(t)])
                for t2 in range(4):
                    for t in range(4):
                        mm(PM[:, 0:W], wglu[:, t, t2 * 128:(t2 + 1) * 128], zzbv(t), [b_wglu, zzbb(t)], [bPM[0]],
                           start=(t == 0), stop=(t == 3), inc=(t == 3))
                    act(sg[:], PM[:, 0:W], AF.Sigmoid, [bPM[0], b_s5v], [b_sg], bias=s5v[:, 1, t2:t2 + 1])
                    tt("dve", oo[:, t2, :], zzv(t2), sg[:], ALU.mult, [zzb_(t2), b_sg], [b_oo])
                for t in range(4):
                    act(osq[:], oo[:, t, :], AF.Square, [b_oo], [b_osq])
                    mm(PA[0][:, 0:W], ones_b[:], osq[:], [b_ones_b, b_osq], [bPA[0]], start=(t == 0), stop=(t == 3))
                act(sg[:], PA[0][:, 0:W], AF.Ln, [bPA[0]], [b_sg], bias=1e-6, scale=1.0)
                act(sg[:], sg[:], AF.Exp, [b_sg], [b_sg], scale=-0.5)
                for t in range(4):
                    stt(ycat[:, 4 + t, :], oo[:, t, :], s5v[:, 2, t:t + 1], sg[:], ALU.mult, ALU.mult, [b_oo, b_s5v, b_sg], [b_ycat])
                for st in range(W // 128):
                    xt, bx, xsl_i = xres[st]
                    for hf in range(2):
                        pj, bpj = PJ[hf], bPJ[hf]
                        for k in range(8):
                            mm(pj[:, :], ycat[:, k, st * 128:(st + 1) * 128], Wo[:, k, hf * 512:(hf + 1) * 512], [b_ycat, b_Wo], [bpj],
                               start=(k == 0), stop=(k == 7), inc=(k == 7))
                        tt("dve", xt[:, hf * 512:(hf + 1) * 512], pj[:, :], xt[:, hf * 512:(hf + 1) * 512], ALU.add, [bpj, bx], [bx])
                    orow = win * W + st * 128
                    S.dma(out_d[orow:orow + 128, :], xt[:], "xst%d" % (xsl_i % 2), R=[bx], W=[b_outrows[orow // 128]])

            sbt_oo = (Yw, b_Yw)
            b_outrows = [Buf("orow%d" % i) for i in range(TOK // 128)]

            def dump_win():
                dump('hT', hT[0][0][:, 2, :], hT[0][1])
                dump('Pprev', Pprev[:], b_Pprev)
                dump('AR', AR[:, 1, :, :], b_AR)
                dump('KH', KH[:, 1, :], b_KH)
                dump('BH', BH[:, 1, :], b_BH)
                dump('kT', kT[:, 3, 1, :], b_kT)
                dump('vT', vT[:, 3, 1, :], b_vT)
                dump('GL', GL[:], b_GL)
                dump('Abm', AbmS[1][0][:], AbmS[1][1])
                dump('Akm', AkmS[1][0][:], AkmS[1][1])
                dump('T', lastT[0][0][:], lastT[0][1])
                dump('WT', WT[:], b_WT)
                dump('UT', UT[:], b_UT)
                dump('Sf', Sf[:], b_Sf)
                dump('s5s', s5s[:], b_s5s)
                dump('ufm', ufm[:, 2, :], b_ufm)
                dump('Yw', Yw[:], b_Yw)
                dump('Y5w', Y5w[:], b_Y5w)
                dump('ycat', ycat[:], b_ycat)

            gens = {0: pre(0)}
            drive([gens[0]])
            for idx in range(2 * NWIN):
                nxt = None
                if idx + 1 < 2 * NWIN:
                    if idx + 1 == NWIN:
                        ts("dve", Pprev[:], Pprev[:], maskv[:, 0:1], None, ALU.mult, None, [b_Pprev, b_maskv], [b_Pprev])
                    nxt = pre(idx + 1)
                main(idx, nxt)
                if idx == NWIN - 1:
                    ts("dve", Sf[:].rearrange("p a b -> p (a b)"), Sf[:].rearrange("p a b -> p (a b)"), maskv[:, 0:1], None, ALU.mult, None,
                       [b_Sf, b_maskv], [b_Sf])
                    sbc_t, sbc_b = Sb[cur_s[0] % 2]
                    cp("act", sbc_t[:], Sf[:], [b_Sf], [sbc_b])
                    ts("dve", s5s[:].rearrange("p a b -> p (a b)"), s5s[:].rearrange("p a b -> p (a b)"), maskv[:, 0:1], None, ALU.mult, None,
                       [b_s5s, b_maskv], [b_s5s])
                if stage(('own:%d' % (idx - NWIN)) if idx >= NWIN else ('pre:%d' % idx), dump_win):
                    return nc
            S.barrier()
        esB = ExitStack()
        with esB:
            HS = DFF // 2
            wgb, b_wgb = sbt(esB, [128, 8, DFF], BF16, "wgb")
            wub, b_wub = sbt(esB, [128, 8, DFF], BF16, "wub")
            wdb, b_wdb = sbt(esB, [128, NFF, D], BF16, "wdb")
            stB = [sbt(esB, [128, HS], F32, "stB%d" % i) for i in range(2)]
            fg_bc, b_fg = sbt(esB, [128, D], F32, "fgbc")
            S.dma(fg_bc[:], fgain_d.partition_broadcast(128), "fgbc", W=[b_fg])
            TB = 512
            NSB = TB // 128
            xsB = [sbt(esB, [128, D], F32, "xsB%d" % i) for i in range(NSB + 1)]
            xnB, b_xnB = sbt(esB, [128, D], BF16, "xnB")
            statB, b_statB = sbt(esB, [128, 4], F32, "statB")
            h2T = [sbt(esB, [128, 8, TB], BF16, "h2T%d" % i) for i in range(1)]
            actT, b_actT = sbt(esB, [128, NFF, TB], BF16, "actT")
            sil = [sbt(esB, [128, TB], BF16, "sil%d" % i) for i in range(2)]
            b_wgh = [Buf("wg_h0"), Buf("wg_h1")]
            b_wuh = [Buf("wu_h0"), Buf("wu_h1")]
            b_wdj = [Buf("wd_%d" % j) for j in range(NFF)]
            wprog = [0]

            def wload():
                ci = 0
                for hh in range(2):
                    for (src_d, dst, bl) in ((wg_d, wgb, b_wgh), (wu_d, wub, b_wuh)):
                        for k in range(8):
                            st_t, st_b = stB[ci % 2]
                            S.dma(st_t[:], src_d[k * 128:(k + 1) * 128, hh * HS:(hh + 1) * HS], "stB%d" % (ci % 2), W=[st_b])
                            cp(("act", "dve", "pool")[ci % 3], dst[:, k, hh * HS:(hh + 1) * HS], st_t[:], [st_b], [bl[hh]])
                            ci += 1
                            wprog[0] += 1
                            yield
                for j in range(NFF):
                    st_t, st_b = stB[ci % 2]
                    S.dma(st_t[:, 0:D], wd_d[j * 128:(j + 1) * 128, :], "stB%d" % (ci % 2), W=[st_b])
                    tt(("dve", "pool")[ci % 2], wdb[:, j, :], st_t[:, 0:D], gf_bc[:], ALU.mult, [st_b, b_gf], [b_wdj[j]])
                    ci += 1
                    wprog[0] += 1
                    yield

            wgen = wload()

            def need(n):
                while wprog[0] < n:
                    next(wgen)

            def bgstep():
                try:
                    next(wgen)
                except StopIteration:
                    pass

            xq = [0]
            need(4)
            for wi in range(TOK // TB):
                hT_t, hT_b = h2T[0]
                tiles = []
                for st in range(NSB):
                    ti = wi * NSB + st
                    i = xq[0] % (NSB + 1)
                    xq[0] += 1
                    xt, bx = xsB[i]
                    orow = ti * 128
                    S.dma(xt[:], out_d[orow:orow + 128, :], "xsB%d" % i, R=[b_outrows[ti]], W=[bx])
                    tiles.append((xt, bx, i, orow, ti))
                for st in range(NSB):
                    xt, bx, i, orow, ti = tiles[st]
                    bgstep()
                    bgstep()
                    act(xnB[:], xt[:], AF.Square, [bx], [b_xnB, b_statB], accum=statB[:, 0:1])
                    act(statB[:, 1:2], statB[:, 0:1], AF.Ln, [b_statB], [b_statB], bias=1e-6, scale=1.0 / D)
                    act(statB[:, 1:2], statB[:, 1:2], AF.Exp, [b_statB], [b_statB], scale=-0.5)
                    act(xnB[:], xt[:], AF.Identity, [bx, b_statB], [b_xnB], scale=statB[:, 1:2])
                    for k in range(8):
                        tr(PT[:, k * 128:(k + 1) * 128], xnB[:, k * 128:(k + 1) * 128], ident_b[:], [b_xnB, b_ident_b], [bPT], inc=(k == 7))
                    for k in range(8):
                        o = hT_t[:, k, st * 128:(st + 1) * 128]
                        i_ = PT[:, k * 128:(k + 1) * 128]
                        if st % 2 == 0:
                            act(o, i_, AF.Identity, [bPT, b_modp], [hT_b], scale=modp[:, 3, k:k + 1], bias=modp[:, 2, k:k + 1])
                        else:
                            ts("dve", o, i_, modp[:, 3, k:k + 1], modp[:, 2, k:k + 1], ALU.mult, ALU.add, [bPT, b_modp], [hT_b])
                for j in range(NFF):
                    wh = 0 if j < 11 else 1
                    need(16 if wh == 0 else 32)
                    bgstep()
                    pg, bpg = (PJ[0], bPJ[0]) if j % 2 == 0 else (PA[0], bPA[0])
                    pu, bpu = (PJ[1], bPJ[1]) if j % 2 == 0 else (PA[1], bPA[1])
                    for k in range(8):
                        mm(pg[:, 0:TB], wgb[:, k, j * 128:(j + 1) * 128], hT_t[:, k, :], [b_wgh[wh], hT_b], [bpg],
                           start=(k == 0), stop=(k == 7), inc=(k == 7))
                    for k in range(8):
                        mm(pu[:, 0:TB], wub[:, k, j * 128:(j + 1) * 128], hT_t[:, k, :], [b_wuh[wh], hT_b], [bpu],
                           start=(k == 0), stop=(k == 7), inc=(k == 7))
                    s_t, s_b = sil[j % 2]
                    act(s_t[:], pg[:, 0:TB], AF.Silu, [bpg], [s_b])
                    tt("dve", actT[:, j, :], s_t[:], pu[:, 0:TB], ALU.mult, [s_b, bpu], [b_actT])
                for st in range(NSB):
                    xt, bx, i, orow, ti = tiles[st]
                    for hf in range(2):
                        pd, bpd = (PI, bPI[0]) if hf == 0 else (PC, bPC[0])
                        need(32 + NFF)
                        for j in range(NFF):
                            mm(pd[:, :], actT[:, j, st * 128:(st + 1) * 128], wdb[:, j, hf * 512:(hf + 1) * 512], [b_actT, b_wdj[j]], [bpd],
                               start=(j == 0), stop=(j == NFF - 1), inc=(j == NFF - 1))
                        tt("dve", xt[:, hf * 512:(hf + 1) * 512], pd[:, :], xt[:, hf * 512:(hf + 1) * 512], ALU.add, [bpd, bx], [bx])
                    act(xnB[:], xt[:], AF.Square, [bx], [b_xnB, b_statB], accum=statB[:, 2:3])
                    act(statB[:, 3:4], statB[:, 2:3], AF.Ln, [b_statB], [b_statB], bias=1e-6, scale=1.0 / D)
                    act(statB[:, 3:4], statB[:, 3:4], AF.Exp, [b_statB], [b_statB], scale=-0.5)
                    stt(xt[:], xt[:], statB[:, 3:4], fg_bc[:], ALU.mult, ALU.mult, [bx, b_statB, b_fg], [bx])
                    S.dma(out_d[orow:orow + 128, :], xt[:], "xstB%d" % i, R=[bx], W=[b_outrows[ti]])
            S.barrier()
    return nc


_NC = None


def _layout_inputs(inp):
    f32 = np.float32
    g = lambda k: np.asarray(inp[k], dtype=f32)
    x = g("x")
    c = g("c")
    shared = {}
    shared["w_ada"] = np.ascontiguousarray(g("w_ada")[0])
    shared["b_ada"] = np.ascontiguousarray(g("b_ada")[0][None, :])
    shared["w_in"] = np.ascontiguousarray(g("w_in")[0])
    shared["mu_l"] = np.ascontiguousarray(g("mu_shift")[0].reshape(14, 128).T)
    v512 = lambda a: a.reshape(4, 128).T
    rw = [g("rw_w0")[0], g("rw_a0")[0], g("rw_k_k")[0], g("rw_k_a")[0], g("rw_r_k")[0].reshape(512),
          g("rw_lnx_w")[0], g("rw_lnx_b")[0]]
    shared["rwp"] = np.ascontiguousarray(np.stack([v512(a) for a in rw], axis=1).reshape(128, 28))
    lo2w = np.zeros((128, 512), f32)
    lo2w[0:64] = g("rw_w2")[0]
    lo2a = np.zeros((128, 512), f32)
    lo2a[64:128] = g("rw_a2")[0]
    shared["lo2w"] = lo2w
    shared["lo2a"] = lo2a
    shared["g2"] = np.ascontiguousarray(g("rw_g2")[0])
    a_re = g("s5_a_re")[0]
    a_im = g("s5_a_im")[0]
    ldt = g("s5_log_dt")[0]
    b_re = g("s5_b_re")[0]
    b_im = g("s5_b_im")[0]
    c_re = g("s5_c_re")[0]
    c_im = g("s5_c_im")[0]
    la = np.zeros((128, 3, 16), f32)
    for pr in range(16):
        for gp in range(2):
            gg = 2 * pr + gp
            la[gp * 64:(gp + 1) * 64, 0, pr] = a_re[gg]
            la[gp * 64:(gp + 1) * 64, 1, pr] = a_im[gg]
            la[gp * 64:(gp + 1) * 64, 2, pr] = ldt[gg]
    shared["s5la"] = la.reshape(128, 48)
    bpre = np.zeros((128, 16, 128), f32)
    bpim = np.zeros((128, 16, 128), f32)
    lbare = np.zeros((128, 16, 128), f32)
    lbaim = np.zeros((128, 16, 128), f32)
    lbldt = np.zeros((128, 16, 128), f32)
    cpre = np.zeros((128, 16, 128), f32)
    cpim = np.zeros((128, 16, 128), f32)
    for pr in range(16):
        tile = pr // 4
        for g8 in range(8):
            gg = tile * 8 + g8
            rows = slice(g8 * 16, g8 * 16 + 16)
            for gp in range(2):
                cols = slice(gp * 64, gp * 64 + 64)
                lbare[rows, pr, cols] = a_re[gg][None, :]
                lbaim[rows, pr, cols] = a_im[gg][None, :]
                lbldt[rows, pr, cols] = ldt[gg]
            if g8 // 2 == pr % 4:
                gp = g8 % 2
                cols = slice(gp * 64, gp * 64 + 64)
                bpre[rows, pr, cols] = b_re[gg].T
                bpim[rows, pr, cols] = b_im[gg].T
        for gp in range(2):
            gg = 2 * pr + gp
            g8 = gg % 8
            cpre[gp * 64:(gp + 1) * 64, pr, g8 * 16:(g8 + 1) * 16] = c_re[gg].T
            cpim[gp * 64:(gp + 1) * 64, pr, g8 * 16:(g8 + 1) * 16] = c_im[gg].T
    shared["bpre"] = bpre.reshape(128, 2048)
    shared["bpim"] = bpim.reshape(128, 2048)
    shared["lbare"] = lbare.reshape(128, 2048)
    shared["lbaim"] = lbaim.reshape(128, 2048)
    shared["lbldt"] = lbldt.reshape(128, 2048)
    shared["cpre"] = cpre.reshape(128, 2048)
    shared["cpim"] = cpim.reshape(128, 2048)
    s5v = [g("s5_d")[0].reshape(512), g("s5_b_glu")[0], g("s5_gain")[0]]
    shared["s5v"] = np.ascontiguousarray(np.stack([v512(a) for a in s5v], axis=1).reshape(128, 12))
    shared["w_glu"] = np.ascontiguousarray(g("s5_w_glu")[0])
    shared["w_out"] = np.ascontiguousarray(g("w_out")[0])
    shared["wg"] = np.ascontiguousarray(g("ffn_w_gate")[0])
    shared["wu"] = np.ascontiguousarray(g("ffn_w_up")[0])
    shared["wd"] = np.ascontiguousarray(g("ffn_w_down")[0])
    shared["fgain"] = np.ascontiguousarray(g("final_gain")[None, :])
    shared["ident"] = np.eye(128, dtype=f32)
    bo = np.zeros((128, 128), f32)
    bo[0:64, 0:64] = 1.0
    bo[64:128, 64:128] = 1.0
    shared["bones"] = bo
    j = np.arange(64)
    strict = (j[:, None] < j[None, :]).astype(f32)
    incl = (j[:, None] <= j[None, :]).astype(f32)
    mar = np.concatenate([strict, incl], axis=1)
    shared["maskar"] = np.concatenate([mar, mar], axis=0)
    low = (j[None, :] < j[:, None]).astype(f32)
    shared["masknt"] = np.concatenate([low, low], axis=0)
    shared["identp"] = np.concatenate([np.eye(64, dtype=f32)] * 2, axis=0)
    shared["idx1"] = np.tile((np.arange(64, dtype=f32) + 1.0)[None, :], (128, 1))
    rst = np.ones((128, 256), f32)
    rst[:, ::64] = 0.0
    shared["rst01"] = rst
    maps = []
    for core in range(8):
        b, s = core // 2, core % 2
        m = dict(shared)
        m["xcat"] = np.ascontiguousarray(np.concatenate([x[b, 0:TOK], x[b, s * TOK:(s + 1) * TOK]], axis=0))
        m["maskv"] = np.full((128, 1), float(s), f32)
        m["c_l"] = np.ascontiguousarray(c[b].reshape(8, 128).T)
        maps.append(m)
    return maps


def kernel(**inputs):
    global _NC
    if _NC is None:
        _NC = build_program()
    maps = _layout_inputs(inputs)
    res = run_bass_kernel_spmd(_NC, maps, core_ids=list(range(8)))
    out = np.zeros((4, 2 * TOK, D), np.float32)
    for core in range(8):
        b, s = core // 2, core % 2
        out[b, s * TOK:(s + 1) * TOK] = res.results[core]["out"]
    return out
```

```python
import math
from contextlib import ExitStack

import numpy as np
import concourse.bass as bass
import concourse.mybir as mybir
from concourse.bass_utils import run_bass_kernel_spmd

F32 = mybir.dt.float32
BF16 = mybir.dt.bfloat16
I32 = mybir.dt.int32
AF = mybir.ActivationFunctionType
ALU = mybir.AluOpType
AX = mybir.AxisListType

D = 1024
TOK = 4096
W = 256
L = 64
NWIN = TOK // W
DFF = 2816
NFF = DFF // 128
TWO_PI = 2.0 * math.pi
PI_SAFE = 3.1415925


class Buf:
    __slots__ = ("name", "w", "r", "excl")

    def __init__(self, name, excl=False):
        self.name = name
        self.w = None
        self.r = {}
        self.excl = excl


class DSem:
    __slots__ = ("handle", "count", "id")


class Sched:
    def __init__(self, nc, es):
        self.nc = nc
        self.es = es
        self.eng = {"pe": nc.tensor, "act": nc.scalar, "dve": nc.vector, "pool": nc.gpsimd, "sp": nc.sync}
        self.sem = {k: es.enter_context(nc.semaphore("sem_" + k)) for k in self.eng}
        self.cnt = {k: 0 for k in self.eng}
        self.seen = {k: {} for k in self.eng}
        self.dsems = {}
        self.nds = 0
        self.const_bufs = []
        self.dead = False

    def _need(self, eng, R, W):
        need = {}

        def add(ev):
            if ev is None:
                return
            k = ev[0]
            if k not in need or need[k][2] < ev[2]:
                need[k] = ev

        for b in R:
            add(b.w)
        for b in W:
            add(b.w)
            for ev in b.r.values():
                add(ev)
        E = self.eng[eng]
        for k, ev in need.items():
            if k == ("e", eng) and eng in ("pe", "sp"):
                continue
            if self.seen[eng].get(k, 0) >= ev[2]:
                continue
            E.wait_ge(ev[1], ev[2])
            self.seen[eng][k] = ev[2]

    def op(self, eng, fn, R=(), W=(), inc=True):
        if self.dead:
            return None
        if any(b.excl for b in R):
            W = list(W) + [b for b in R if b.excl]
            R = [b for b in R if not b.excl]
        self._need(eng, R, W)
        inst = fn(self.eng[eng])
        val = self.cnt[eng] + 1
        if inc:
            inst.then_inc(self.sem[eng], 1)
            self.cnt[eng] = val
        ev = (("e", eng), self.sem[eng], val)
        for b in R:
            b.r[ev[0]] = ev
        for b in W:
            b.w = ev
            b.r = {}
        return inst

    def _dsem(self, key):
        if key not in self.dsems:
            d = DSem()
            d.handle = self.es.enter_context(self.nc.semaphore("dsem%d" % self.nds))
            d.count = 0
            d.id = self.nds
            self.nds += 1
            self.dsems[key] = d
        return self.dsems[key]

    def dma(self, out, in_, key, R=(), W=(), const=False):
        if self.dead:
            return
        self._need("sp", R, W)
        d = self._dsem(key)
        d.count += 16
        self.nc.sync.dma_start(out=out, in_=in_).then_inc(d.handle, 16)
        ev = (("d", d.id), d.handle, d.count)
        for b in R:
            b.r[ev[0]] = ev
        for b in W:
            b.w = ev
            b.r = {}
            if const:
                self.const_bufs.append(b)

    def finalize_consts(self, key):
        d = self._dsem(key)
        ev = (("d", d.id), d.handle, d.count)
        for b in self.const_bufs:
            b.w = ev
        self.const_bufs = []

    def barrier(self):
        for e, E in self.eng.items():
            for f in self.eng:
                if f == e:
                    continue
                k = ("e", f)
                if self.cnt[f] > self.seen[e].get(k, 0):
                    E.wait_ge(self.sem[f], self.cnt[f])
                    self.seen[e][k] = self.cnt[f]
            for d in self.dsems.values():
                k = ("d", d.id)
                if d.count > self.seen[e].get(k, 0):
                    E.wait_ge(d.handle, d.count)
                    self.seen[e][k] = d.count


class _StopBuild(Exception):
    pass


_DBG = {"stop": None, "dumps": [], "meta": []}


def build_program():
    nc = bass.Bass("TRN2", target_bir_lowering=False)
    dbg_on = _DBG["stop"] is not None
    if dbg_on:
        dbg_d = nc.dram_tensor("dbg", [128, 65536], F32, kind="ExternalOutput").ap()
        _DBG["meta"] = []

    def din(name, shape):
        return nc.dram_tensor(name, list(shape), F32, kind="ExternalInput").ap()

    xcat = din("xcat", [2 * TOK, D])
    maskv_d = din("maskv", [128, 1])
    c_d = din("c_l", [128, 8])
    wada_d = din("w_ada", [D, 6 * D])
    bada_d = din("b_ada", [1, 6 * D])
    win_d = din("w_in", [D, 2304])
    mu_d = din("mu_l", [128, 14])
    rwp_d = din("rwp", [128, 28])
    lo2w_d = din("lo2w", [128, 512])
    lo2a_d = din("lo2a", [128, 512])
    g2_d = din("g2", [128, 512])
    s5la_d = din("s5la", [128, 48])
    bpre_d = din("bpre", [128, 2048])
    bpim_d = din("bpim", [128, 2048])
    lbare_d = din("lbare", [128, 2048])
    lbaim_d = din("lbaim", [128, 2048])
    lbldt_d = din("lbldt", [128, 2048])
    cpre_d = din("cpre", [128, 2048])
    cpim_d = din("cpim", [128, 2048])
    s5v_d = din("s5v", [128, 12])
    wglu_d = din("w_glu", [512, 512])
    wout_d = din("w_out", [D, D])
    wg_d = din("wg", [D, DFF])
    wu_d = din("wu", [D, DFF])
    wd_d = din("wd", [DFF, D])
    fgain_d = din("fgain", [1, D])
    ident_d = din("ident", [128, 128])
    bones_d = din("bones", [128, 128])
    maskar_d = din("maskar", [128, 128])
    masknt_d = din("masknt", [128, 64])
    identp_d = din("identp", [128, 64])
    idx1_d = din("idx1", [128, 64])
    rst_d = din("rst01", [128, 256])
    out_d = nc.dram_tensor("out", [TOK, D], F32, kind="ExternalOutput").ap()

    es = ExitStack()
    with es:
        S = Sched(nc, es)
        uid = [0]

        def sbt(stack, shape, dt, nm="t"):
            uid[0] += 1
            name = "%s_%d" % (nm, uid[0])
            t = stack.enter_context(nc.sbuf_tensor(name, list(shape), dt))
            return t, Buf(name)

        def pst(nm, dt, n):
            t = es.enter_context(nc.psum_tensor(nm, [128, n], dt))
            return t

        PT = pst("PT", BF16, 1024)
        bPT = Buf("PT", True)
        PJ = [pst("PJ0", F32, 512), pst("PJ1", F32, 512)]
        bPJ = [Buf("PJ0", True), Buf("PJ1", True)]
        PM = pst("PM", F32, 512)
        _bpm = Buf("PM", True)
        bPM = [_bpm, _bpm]
        PA = [pst("PA0", F32, 512), pst("PA1", F32, 512)]
        bPA = [Buf("PA0", True), Buf("PA1", True)]
        PI = pst("PI", F32, 512)
        _bpi = Buf("PI", True)
        bPI = [_bpi, _bpi]
        PC = pst("PC", F32, 512)
        _bpc = Buf("PC", True)
        bPC = [_bpc, _bpc]

        def tt(eng, out, in0, in1, op, R, Wb):
            return S.op(eng, lambda e: e.tensor_tensor(out=out, in0=in0, in1=in1, op=op), R, Wb)

        def ts(eng, out, in0, s1, s2, op0, op1, R, Wb):
            if op1 is None and eng == "pool" and op0 == ALU.mult:
                return S.op(eng, lambda e: e.tensor_scalar(out=out, in0=in0, scalar1=s1, scalar2=0.0, op0=op0, op1=ALU.add), R, Wb)
            if op1 is None:
                return S.op(eng, lambda e: e.tensor_scalar(out=out, in0=in0, scalar1=s1, scalar2=None, op0=op0), R, Wb)
            return S.op(eng, lambda e: e.tensor_scalar(out=out, in0=in0, scalar1=s1, scalar2=s2, op0=op0, op1=op1), R, Wb)

        def stt(out, in0, scalar, in1, op0, op1, R, Wb):
            return S.op("dve", lambda e: e.scalar_tensor_tensor(out=out, in0=in0, scalar=scalar, in1=in1, op0=op0, op1=op1), R, Wb)

        def act(out, in_, func, R, Wb, bias=None, scale=None, accum=None):
            kw = {}
            if bias is not None:
                kw["bias"] = bias
            if scale is not None:
                kw["scale"] = scale
            if accum is not None:
                kw["accum_out"] = accum
            return S.op("act", lambda e: e.activation(out=out, in_=in_, func=func, **kw), R, Wb)

        def cp(eng, out, in_, R, Wb):
            if eng == "act":
                return act(out, in_, AF.Identity, R, Wb)
            return S.op(eng, lambda e: e.tensor_copy(out=out, in_=in_), R, Wb)

        def mm(out, lhsT, rhs, R, Wb, start=True, stop=True, inc=True):
            return S.op("pe", lambda e: e.matmul(out, lhsT=lhsT, rhs=rhs, start=start, stop=stop), R, Wb, inc=inc)

        def tr(out, in_, ident, R, Wb, inc=True):
            return S.op("pe", lambda e: e.transpose(out, in_, ident), R, Wb, inc=inc)

        def bc1(ap, n):
            return ap.unsqueeze(1).to_broadcast([ap.shape[0], n, ap.shape[1]])

        def bc2(ap, n):
            return ap.unsqueeze(2).to_broadcast([ap.shape[0], ap.shape[1], n])

        dbg_off = [0]
        if dbg_on:
            dstage = [sbt(es, [128, 2048], F32, "dstage%d" % i) for i in range(1)]
        dcount = [0]

        def dump(name, ap, buf):
            shape = list(ap.shape)
            n = 1
            for d_ in shape[1:]:
                n *= d_
            P_ = shape[0]
            st_t, st_b = dstage[0]
            dcount[0] += 1
            dst = st_t[0:P_, 0:n]
            if len(shape) == 3:
                dst = dst.rearrange("p (a b) -> p a b", b=shape[2])
            elif len(shape) == 4:
                dst = dst.rearrange("p (a b c) -> p a b c", b=shape[2], c=shape[3])
            cp("dve", dst, ap, [buf], [st_b])
            S.dma(dbg_d[0:P_, dbg_off[0]:dbg_off[0] + n], st_t[0:P_, 0:n], "dstage0", R=[st_b])
            _DBG["meta"].append((name, dbg_off[0], shape))
            dbg_off[0] += n

        def sub(name):
            if dbg_on and _DBG.get("sub") == name:
                S.dead = True

        def stage(name, fn=None):
            if dbg_on and _DBG["stop"] == name:
                S.dead = False
                if fn is not None:
                    fn()
                S.barrier()
                return True
            return False

        ident_f, b_ident_f = sbt(es, [128, 128], F32, "identf")
        ident_b, b_ident_b = sbt(es, [128, 128], BF16, "identb")
        maskv, b_maskv = sbt(es, [128, 1], F32, "maskv")
        modp, b_modp = sbt(es, [128, 4, 8], F32, "modp")
        gf_bc, b_gf = sbt(es, [128, D], F32, "gfbc")

        S.dma(ident_f[:], ident_d, "const", W=[b_ident_f], const=True)
        S.dma(maskv[:], maskv_d, "const", W=[b_maskv], const=True)

        esA = ExitStack()
        with esA:
            W1, b_W1 = sbt(esA, [128, 8, 2304], BF16, "W1")
            Wo, b_Wo = sbt(esA, [128, 8, D], BF16, "Wo")
            wglu, b_wglu = sbt(esA, [128, 4, 512], BF16, "wglu")
            lo2w, b_lo2w = sbt(esA, [128, 512], BF16, "lo2w")
            lo2a, b_lo2a = sbt(esA, [128, 512], BF16, "lo2a")
            g2, b_g2 = sbt(esA, [128, 512], BF16, "g2")
            bones_b, b_bones_b = sbt(esA, [128, 128], BF16, "bonesb")
            bones_f, b_bones_f = sbt(esA, [128, 128], F32, "bonesf")
            ones_b, b_ones_b = sbt(esA, [128, 128], BF16, "onesb")
            maskar, b_maskar = sbt(esA, [128, 128], F32, "maskar")
            masknt, b_masknt = sbt(esA, [128, 64], F32, "masknt")
            identp, b_identp = sbt(esA, [128, 64], F32, "identp")
            rst01, b_rst = sbt(esA, [128, 256], F32, "rst01")
            mu, b_mu = sbt(esA, [128, 14], F32, "mu")
            omu, b_omu = sbt(esA, [128, 14], F32, "omu")
            rwp, b_rwp = sbt(esA, [128, 8, 4], F32, "rwp")
            s5v, b_s5v = sbt(esA, [128, 3, 4], F32, "s5v")
            Bb_re, b_Bbre = sbt(esA, [128, 16, 128], BF16, "Bbre")
            Bb_im, b_Bbim = sbt(esA, [128, 16, 128], BF16, "Bbim")
            Cp_re, b_Cpre = sbt(esA, [128, 16, 128], BF16, "Cpre")
            Cp_imn, b_Cpim = sbt(esA, [128, 16, 128], BF16, "Cpimn")
            tcos, b_tcos = sbt(esA, [128, 16, 64], F32, "tcos")
            tsin, b_tsin = sbt(esA, [128, 16, 64], F32, "tsin")
            rho0, b_rho0 = sbt(esA, [128, 16, 64], F32, "rho0")
            rho1, b_rho1 = sbt(esA, [128, 16], F32, "rho1")
            T63, b_T63 = sbt(esA, [128, 2, 2, 16], F32, "T63")

            S.dma(maskar[:], maskar_d, "const", W=[b_maskar], const=True)
            S.dma(masknt[:], masknt_d, "const", W=[b_masknt], const=True)
            S.dma(identp[:], identp_d, "const", W=[b_identp], const=True)
            S.dma(rst01[:], rst_d, "const", W=[b_rst], const=True)
            S.dma(mu[:], mu_d, "const", W=[b_mu], const=True)
            S.dma(rwp[:, 0:7, :].rearrange("p a b -> p (a b)"), rwp_d, "const", W=[b_rwp], const=True)
            S.dma(s5v[:].rearrange("p a b -> p (a b)"), s5v_d, "const", W=[b_s5v], const=True)
            S.dma(bones_f[:], bones_d, "const", W=[b_bones_f], const=True)

            esS = ExitStack()
            with esS:
                c_l, b_c = sbt(esS, [128, 8], F32, "c")
                c_act, b_cact = sbt(esS, [128, 8], F32, "cact")
                c_rep, b_crep = sbt(esS, [128, 8, 128], F32, "crep")
                adaR, b_adaR = sbt(esS, [128, 6 * D], F32, "adaR")
                badab, b_badab = sbt(esS, [128, 6 * D], F32, "badab")
                ada_fm, b_adafm = sbt(esS, [128, 48], F32, "adafm")
                stg = [sbt(esS, [128, 8, 512], F32, "stg%d" % i) for i in range(2)]
                lst = [sbt(esS, [128, 512], F32, "lst%d" % i) for i in range(3)]
                S.dma(lst[0][0][:], lo2w_d, "const", W=[lst[0][1]], const=True)
                S.dma(lst[1][0][:], lo2a_d, "const", W=[lst[1][1]], const=True)
                S.dma(lst[2][0][:], g2_d, "const", W=[lst[2][1]], const=True)
                S.dma(c_l[:], c_d, "const", W=[b_c], const=True)
                S.dma(badab[:], bada_d.partition_broadcast(128), "const", W=[b_badab], const=True)
                S.finalize_consts("const")

                cp("dve", ident_b[:], ident_f[:], [b_ident_f], [b_ident_b])
                cp("dve", bones_b[:], bones_f[:], [b_bones_f], [b_bones_b])
                S.op("pool", lambda e: e.memset(ones_b[:], 1.0 / 512.0), [], [b_ones_b])
                ts("dve", bones_f[:], bones_f[:], 1.0 / 64.0, None, ALU.mult, None, [b_bones_f], [b_bones_f])
                ts("dve", omu[:], mu[:], -1.0, 1.0, ALU.mult, ALU.add, [b_mu], [b_omu])
                ts("dve", rwp[:, 7, :], rwp[:, 3, :], -1.0, 1.0, ALU.mult, ALU.add, [b_rwp], [b_rwp])
                cp("act", lo2w[:], lst[0][0][:], [lst[0][1]], [b_lo2w])
                cp("act", lo2a[:], lst[1][0][:], [lst[1][1]], [b_lo2a])
                cp("act", g2[:], lst[2][0][:], [lst[2][1]], [b_g2])

                act(c_act[:], c_l[:], AF.Silu, [b_c], [b_cact])
                cp("dve", c_rep[:], bc2(c_act[:], 128), [b_cact], [b_crep])
                wada_v = wada_d.rearrange("(k p) n -> p k n", p=128)
                for blk in range(12):
                    st_t, st_b = stg[blk % 2]
                    S.dma(st_t[:], wada_v[:, :, blk * 512:(blk + 1) * 512], "stg%d" % (blk % 2), W=[st_b])
                    pj, bpj = PJ[blk % 2], bPJ[blk % 2]
                    for k in range(8):
                        mm(pj[:, :], c_rep[:, k, :], st_t[:, k, :], [b_crep, st_b], [bpj],
                           start=(k == 0), stop=(k == 7), inc=(k == 7))
                    tt("dve", adaR[:, blk * 512:(blk + 1) * 512], pj[:, :], badab[:, blk * 512:(blk + 1) * 512],
                       ALU.add, [bpj, b_badab], [b_adaR])
                tt("dve", badab[:].rearrange("p (j q) -> p j q", q=128), adaR[:].rearrange("p (j q) -> p j q", q=128),
                   bc1(ident_f[:], 48), ALU.mult, [b_adaR, b_ident_f, b_badab], [b_badab])
                S.op("dve", lambda e: e.tensor_reduce(out=ada_fm[:], in_=badab[:].rearrange("p (j q) -> p j q", q=128),
                                                       axis=AX.X, op=ALU.add), [b_badab], [b_adafm])
                cp("dve", modp[:, 0, :], ada_fm[:, 0:8], [b_adafm], [b_modp])
                ts("dve", modp[:, 1, :], ada_fm[:, 8:16], 1.0, None, ALU.add, None, [b_adafm], [b_modp])
                cp("dve", modp[:, 2, :], ada_fm[:, 24:32], [b_adafm], [b_modp])
                ts("dve", modp[:, 3, :], ada_fm[:, 32:40], 1.0, None, ALU.add, None, [b_adafm], [b_modp])
                cp("dve", gf_bc[:], adaR[:, 5 * D:6 * D], [b_adaR], [b_gf])

                ci = 0
                for k in range(8):
                    st_t, st_b = stg[k % 2]
                    stv = st_t[:].rearrange("p a b -> p (a b)")
                    S.dma(stv[:, 0:2304], win_d[k * 128:(k + 1) * 128, :], "stg%d" % (k % 2), W=[st_b])
                    cp(("act", "dve", "pool")[ci % 3], W1[:, k, :], stv[:, 0:2304], [st_b], [b_W1])
                    ci += 1
                for k in range(8):
                    st_t, st_b = stg[k % 2]
                    stv = st_t[:].rearrange("p a b -> p (a b)")
                    S.dma(stv[:, 0:D], wout_d[k * 128:(k + 1) * 128, :], "stg%d" % (k % 2), W=[st_b])
                    tt("dve", Wo[:, k, :], stv[:, 0:D], adaR[:, 2 * D:3 * D], ALU.mult, [st_b, b_adaR], [b_Wo])
                for k in range(4):
                    st_t, st_b = stg[k % 2]
                    stv = st_t[:].rearrange("p a b -> p (a b)")
                    S.dma(stv[:, 0:512], wglu_d[k * 128:(k + 1) * 128, :], "stg%d" % (k % 2), W=[st_b])
                    cp("act", wglu[:, k, :], stv[:, 0:512], [st_b], [b_wglu])

                if stage('setup1', lambda: (dump('modp', modp[:], b_modp), dump('gf', gf_bc[:], b_gf), dump('W1', W1[:, 3, 0:2048], b_W1), dump('Wo', Wo[:, 2, :], b_Wo))):
                    return nc
                S.barrier()
            esS2 = ExitStack()
            with esS2:
                s5la, b_s5la = sbt(esS2, [128, 3, 16], F32, "s5la")
                idx1, b_idx1 = sbt(esS2, [128, 64], F32, "idx1")
                lb = {}
                for nm, dd in (("bpre", bpre_d), ("bpim", bpim_d), ("are", lbare_d), ("aim", lbaim_d), ("ldt", lbldt_d)):
                    lb[nm] = sbt(esS2, [128, 2048], F32, "lb" + nm)
                    S.dma(lb[nm][0][:], dd, "const2", W=[lb[nm][1]], const=True)
                cst = [sbt(esS2, [128, 2048], F32, "cst%d" % i) for i in range(1)]
                S.dma(s5la[:].rearrange("p a b -> p (a b)"), s5la_d, "const2", W=[b_s5la], const=True)
                S.dma(idx1[:], idx1_d, "const2", W=[b_idx1], const=True)
                S.finalize_consts("const2")
                tmp = [sbt(esS2, [128, 1024], F32, "s5t%d" % i) for i in range(8)]
                tmpi, b_tmpi = sbt(esS2, [128, 1024], I32, "s5ti")

                def sincos(ang, b_ang, n, o_sin, b_osin, o_cos, b_ocos, ta, tb):
                    for shift, o, bo in ((0.0, o_sin, b_osin), (0.5 * math.pi, o_cos, b_ocos)):
                        ts("dve", ta[0][:, 0:n], ang, shift, 1.0 / TWO_PI, ALU.add, ALU.mult, [b_ang], [ta[1]])
                        cp("dve", tmpi[:, 0:n], ta[0][:, 0:n], [ta[1]], [b_tmpi])
                        cp("dve", tb[0][:, 0:n], tmpi[:, 0:n], [b_tmpi], [tb[1]])
                        ts("dve", ta[0][:, 0:n], ang, shift, None, ALU.add, None, [b_ang], [ta[1]])
                        stt(ta[0][:, 0:n], tb[0][:, 0:n], -TWO_PI, ta[0][:, 0:n], ALU.mult, ALU.add, [tb[1], ta[1]], [ta[1]])
                        ts("dve", ta[0][:, 0:n], ta[0][:, 0:n], -PI_SAFE, PI_SAFE, ALU.max, ALU.min, [ta[1]], [ta[1]])
                        act(o, ta[0][:, 0:n], AF.Sin, [ta[1]], [bo])

                t_dt, t_xr, t_xi, t_mag, t_sin, t_cos, t_a, t_b = tmp
                Bre_flat = Bb_re[:].rearrange("p a b -> p (a b)")
                Bim_flat = Bb_im[:].rearrange("p a b -> p (a b)")
                for hh in range(2):
                    c2 = slice(hh * 1024, (hh + 1) * 1024)
                    are_t, are_b = lb["are"]
                    aim_t, aim_b = lb["aim"]
                    ldt_t, ldt_b = lb["ldt"]
                    bre_t, bre_b = lb["bpre"]
                    bim_t, bim_b = lb["bpim"]
                    A_re, A_im = are_t[:, c2], aim_t[:, c2]
                    act(t_dt[0][:], ldt_t[:, c2], AF.Exp, [ldt_b], [t_dt[1]])
                    tt("dve", t_xr[0][:], t_dt[0][:], A_re, ALU.mult, [t_dt[1], are_b], [t_xr[1]])
                    tt("dve", t_xi[0][:], t_dt[0][:], A_im, ALU.mult, [t_dt[1], aim_b], [t_xi[1]])
                    act(t_mag[0][:], t_xr[0][:], AF.Exp, [t_xr[1]], [t_mag[1]])
                    sincos(t_xi[0][:], t_xi[1], 1024, t_sin[0][:], t_sin[1], t_cos[0][:], t_cos[1], t_a, t_b)
                    tt("dve", t_cos[0][:], t_cos[0][:], t_mag[0][:], ALU.mult, [t_cos[1], t_mag[1]], [t_cos[1]])
                    ts("dve", t_cos[0][:], t_cos[0][:], -1.0, None, ALU.add, None, [t_cos[1]], [t_cos[1]])
                    tt("dve", t_sin[0][:], t_sin[0][:], t_mag[0][:], ALU.mult, [t_sin[1], t_mag[1]], [t_sin[1]])
                    tt("dve", t_dt[0][:], A_re, A_re, ALU.mult, [are_b], [t_dt[1]])
                    tt("dve", t_xr[0][:], A_im, A_im, ALU.mult, [aim_b], [t_xr[1]])
                    tt("dve", t_dt[0][:], t_dt[0][:], t_xr[0][:], ALU.add, [t_dt[1], t_xr[1]], [t_dt[1]])
                    S.op("dve", lambda e: e.reciprocal(out=t_dt[0][:], in_=t_dt[0][:]), [t_dt[1]], [t_dt[1]])
                    tt("dve", t_xr[0][:], t_cos[0][:], A_re, ALU.mult, [t_cos[1], are_b], [t_xr[1]])
                    tt("dve", t_a[0][:], t_sin[0][:], A_im, ALU.mult, [t_sin[1], aim_b], [t_a[1]])
                    tt("dve", t_xr[0][:], t_xr[0][:], t_a[0][:], ALU.add, [t_xr[1], t_a[1]], [t_xr[1]])
                    tt("dve", t_xr[0][:], t_xr[0][:], t_dt[0][:], ALU.mult, [t_xr[1], t_dt[1]], [t_xr[1]])
                    tt("dve", t_xi[0][:], t_sin[0][:], A_re, ALU.mult, [t_sin[1], are_b], [t_xi[1]])
                    tt("dve", t_a[0][:], t_cos[0][:], A_im, ALU.mult, [t_cos[1], aim_b], [t_a[1]])
                    tt("dve", t_xi[0][:], t_xi[0][:], t_a[0][:], ALU.subtract, [t_xi[1], t_a[1]], [t_xi[1]])
                    tt("dve", t_xi[0][:], t_xi[0][:], t_dt[0][:], ALU.mult, [t_xi[1], t_dt[1]], [t_xi[1]])
                    tt("dve", t_a[0][:], t_xr[0][:], bre_t[:, c2], ALU.mult, [t_xr[1], bre_b], [t_a[1]])
                    tt("dve", t_b[0][:], t_xi[0][:], bim_t[:, c2], ALU.mult, [t_xi[1], bim_b], [t_b[1]])
                    tt("dve", Bre_flat[:, c2], t_a[0][:], t_b[0][:], ALU.subtract, [t_a[1], t_b[1]], [b_Bbre])
                    tt("dve", t_a[0][:], t_xr[0][:], bim_t[:, c2], ALU.mult, [t_xr[1], bim_b], [t_a[1]])
                    tt("dve", t_b[0][:], t_xi[0][:], bre_t[:, c2], ALU.mult, [t_xi[1], bre_b], [t_b[1]])
                    tt("dve", Bim_flat[:, c2], t_a[0][:], t_b[0][:], ALU.add, [t_a[1], t_b[1]], [b_Bbim])
                S.dma(cst[0][0][:], cpre_d, "cst", W=[cst[0][1]])
                cp("act", Cp_re[:].rearrange("p a b -> p (a b)"), cst[0][0][:], [cst[0][1]], [b_Cpre])
                S.dma(cst[0][0][:], cpim_d, "cst", W=[cst[0][1]])
                S.op("act", lambda e: e.mul(Cp_imn[:].rearrange("p a b -> p (a b)"), cst[0][0][:], -1.0), [cst[0][1]], [b_Cpim])
                la_dt, la_th = t_mag, t_dt
                act(la_dt[0][:, 0:16], s5la[:, 2, :], AF.Exp, [b_s5la], [la_dt[1]])
                tt("dve", la_th[0][:, 0:16], la_dt[0][:, 0:16], s5la[:, 1, :], ALU.mult, [la_dt[1], b_s5la], [la_th[1]])
                tt("dve", la_dt[0][:, 0:16], la_dt[0][:, 0:16], s5la[:, 0, :], ALU.mult, [la_dt[1], b_s5la], [la_dt[1]])
                act(rho1[:], la_dt[0][:, 0:16], AF.Exp, [la_dt[1]], [b_rho1])
                tt("dve", t_xr[0][:, 0:1024].rearrange("p (a b) -> p a b", b=64), bc2(la_th[0][:, 0:16], 64), bc1(idx1[:], 16),
                   ALU.mult, [la_th[1], b_idx1], [t_xr[1]])
                sincos(t_xr[0][:, 0:1024], t_xr[1], 1024, tsin[:].rearrange("p a b -> p (a b)"), b_tsin,
                       tcos[:].rearrange("p a b -> p (a b)"), b_tcos, t_a, t_b)
                cp("dve", rho0[:], bc2(rho1[:], 64), [b_rho1], [b_rho0])
                S.op("dve", lambda e: e.memset(rho0[:, :, 0:1], 0.0), [], [b_rho0])
                tt("dve", T63[:, 0, 0, :], tcos[:, :, 63], rho1[:], ALU.mult, [b_tcos, b_rho1], [b_T63])
                tt("dve", T63[:, 0, 1, :], tsin[:, :, 63], rho1[:], ALU.mult, [b_tsin, b_rho1], [b_T63])
                tt("dve", T63[:, 1, 1, :], tcos[:, :, 63], rho1[:], ALU.mult, [b_tcos, b_rho1], [b_T63])
                stt(T63[:, 1, 0, :], tsin[:, :, 63], -1.0, rho1[:], ALU.mult, ALU.mult, [b_tsin, b_rho1], [b_T63])
                if stage('setup2', lambda: (dump('Bbre', Bb_re[:, 5, :], b_Bbre), dump('Bbim', Bb_im[:, 5, :], b_Bbim), dump('tcos', tcos[:], b_tcos), dump('tsin', tsin[:], b_tsin), dump('rho1', rho1[:], b_rho1), dump('rho0', rho0[:, 3, :], b_rho0), dump('Cpimn', Cp_imn[:, 9, :], b_Cpim))):
                    return nc
                S.barrier()

            xs = [sbt(esA, [128, D], F32, "xs%d" % i) for i in range(2)]
            xnb = [sbt(esA, [128, D], BF16, "xnb%d" % i) for i in range(1)]
            stat, b_stat = sbt(esA, [128, 8], F32, "stat")
            hT = [sbt(esA, [128, 8, W], BF16, "hT%d" % i) for i in range(1)]
            Pprev, b_Pprev = sbt(esA, [128, 14], F32, "Pprev")
            Pb = [sbt(esA, [128, W + 1], F32, "Pb%d" % i) for i in range(2)]
            zt = [sbt(esA, [128, W], F32, "zt%d" % i) for i in range(4)]
            LA, b_LA = sbt(esA, [128, W], BF16, "LA")
            gs, b_gs = sbt(esA, [128, W], BF16, "gs")
            ARs = [sbt(esA, [128, 4, 4, 128], BF16, "AR%d" % i) for i in range(2)]
            KHs = [sbt(esA, [128, 4, W], BF16, "KH%d" % i) for i in range(2)]
            BHs = [sbt(esA, [128, 4, W], BF16, "BH%d" % i) for i in range(2)]
            VBs = [sbt(esA, [128, 4, W], BF16, "VB%d" % i) for i in range(2)]
            kT, b_kT = sbt(esA, [128, 4, 4, 64], BF16, "kT")
            bT, b_bT = sbt(esA, [128, 4, 4, 64], BF16, "bT")
            vT, b_vT = sbt(esA, [128, 4, 4, 64], BF16, "vT")
            GLs = [sbt(esA, [128, 4, 4], F32, "GL%d" % i) for i in range(2)]
            gfms = [sbt(esA, [128, 4, W], BF16, "gfm%d" % i) for i in range(2)]
            bvs = [sbt(esA, [128, 4, W], BF16, "bv%d" % i) for i in range(2)]
            Yw, b_Yw = sbt(esA, [128, 4, W], F32, "Yw")
            ufms = [sbt(esA, [128, 4, W], BF16, "ufm%d" % i) for i in range(2)]
            Y5w, b_Y5w = sbt(esA, [128, 4, W], F32, "Y5w")
            ycat, b_ycat = sbt(esA, [128, 8, W], BF16, "ycat")
            et = [sbt(esA, [128, W], F32, "et%d" % i) for i in range(12)]
            etb = [sbt(esA, [128, W], BF16, "etb%d" % i) for i in range(2)]
            AbmS = [sbt(esA, [128, 4, 128], BF16, "Abm%d" % i) for i in range(2)]
            AkmS = [sbt(esA, [128, 4, 128], BF16, "Akm%d" % i) for i in range(2)]
            TfinS = [sbt(esA, [128, 4, 64], BF16, "Tfin%d" % i) for i in range(2)]
            ImT, b_ImT = sbt(esA, [128, 4, 64], F32, "ImT")
            Xb = [sbt(esA, [128, 4, 64], BF16, "Xb%d" % i) for i in range(2)]
            XTb = [sbt(esA, [128, 4, 64], BF16, "XTb%d" % i) for i in range(2)]
            Pm = [sbt(esA, [128, 4, 64], BF16, "Pm%d" % i) for i in range(2)]
            NTb, b_NTb = sbt(esA, [128, 4, 64], BF16, "NTb")
            Rb, b_Rb = Xb[0]
            TTb, b_TTb = Xb[1]
            WT, b_WT = sbt(esA, [128, 4, 64], BF16, "WT")
            UT, b_UT = sbt(esA, [128, 4, 64], BF16, "UT")
            Sf, b_Sf = sbt(esA, [128, 4, 64], F32, "Sf")
            Sb = [sbt(esA, [128, 4, 64], BF16, "Sb%d" % i) for i in range(2)]
            stmp, b_stmp = sbt(esA, [128, 4, 64], F32, "stmp")
            s5a = [sbt(esA, [128, 512], F32, "s5a%d" % i) for i in range(4)]
            s_tr, b_str = sbt(esA, [128, 512], F32, "str")
            s_ti, b_sti = sbt(esA, [128, 512], F32, "sti")
            srb, b_srb = sbt(esA, [128, 512], BF16, "srb")
            sib, b_sib = sbt(esA, [128, 512], BF16, "sib")
            s5s, b_s5s = sbt(esA, [128, 2, 16], F32, "s5s")
            s5q, b_s5q = sbt(esA, [128, 6, 8], F32, "s5q")

            S.op("dve", lambda e: e.memset(Pprev[:], 0.0), [], [b_Pprev])
            S.op("dve", lambda e: e.memset(Sf[:], 0.0), [], [b_Sf])
            S.op("dve", lambda e: e.memset(Sb[0][0][:], 0.0), [], [Sb[0][1]])
            S.op("dve", lambda e: e.memset(s5s[:], 0.0), [], [b_s5s])
            S.op("dve", lambda e: e.memset(ARs[0][0][:], 0.0), [], [ARs[0][1]])
            S.op("dve", lambda e: e.memset(ARs[1][0][:], 0.0), [], [ARs[1][1]])
            M = {}

            cur_s = [0]
            lastT = [None]
            xslot = [0]

            def load_x(row0):
                i = xslot[0] % 2
                xslot[0] += 1
                t, b = xs[i]
                S.dma(t[:], xcat[row0:row0 + 128, :], "xs%d" % i, W=[b])
                return t, b

            def norm_transpose(xt, bx, st, hT_t, hT_b, sh_i, sc_i, width):
                xn_t, xn_b = xnb[0]
                col = st % 4
                act(xn_t[:], xt[:], AF.Square, [bx], [xn_b, b_stat], accum=stat[:, col:col + 1])
                act(stat[:, 4 + col:5 + col], stat[:, col:col + 1], AF.Ln, [b_stat], [b_stat], bias=1e-6, scale=1.0 / D)
                act(stat[:, 4 + col:5 + col], stat[:, 4 + col:5 + col], AF.Exp, [b_stat], [b_stat], scale=-0.5)
                act(xn_t[:], xt[:], AF.Identity, [bx, b_stat], [xn_b], scale=stat[:, 4 + col:5 + col])
                for k in range(8):
                    tr(PT[:, k * 128:(k + 1) * 128], xn_t[:, k * 128:(k + 1) * 128], ident_b[:], [xn_b, b_ident_b], [bPT], inc=(k == 7))
                for k in range(8):
                    o = hT_t[:, k, st * 128:(st + 1) * 128]
                    i_ = PT[:, k * 128:(k + 1) * 128]
                    if True:
                        act(o, i_, AF.Identity, [bPT, b_modp], [hT_b], scale=modp[:, sc_i, k:k + 1], bias=modp[:, sh_i, k:k + 1])
                    else:
                        ts("dve", o, i_, modp[:, sc_i, k:k + 1], modp[:, sh_i, k:k + 1], ALU.mult, ALU.add, [bPT, b_modp], [hT_b])

            pbi = [0]
            pji = [0]

            def project(hT_t, hT_b, ct):
                pj, bpj = PA[1], bPA[1]
                for k in range(8):
                    mm(pj[:, 0:W], W1[:, k, ct * 128:(ct + 1) * 128], hT_t[:, k, :], [b_W1, hT_b], [bpj],
                       start=(k == 0), stop=(k == 7), inc=(k == 7))
                return pj, bpj

            def shifted(hT_t, hT_b, ct, zo, b_zo):
                pj, bpj = project(hT_t, hT_b, ct)
                p_t, p_b = Pb[pbi[0] % 2]
                pbi[0] += 1
                cp("pool", p_t[:, 0:1], Pprev[:, ct:ct + 1], [b_Pprev], [p_b])
                act(p_t[:, 1:W + 1], pj[:, 0:W], AF.Identity, [bpj], [p_b])
                cp("pool", Pprev[:, ct:ct + 1], p_t[:, W:W + 1], [p_b], [b_Pprev])
                act(zo, p_t[:, 0:W], AF.Identity, [p_b, b_mu], [b_zo], scale=mu[:, ct:ct + 1])
                stt(zo, p_t[:, 1:W + 1], omu[:, ct:ct + 1], zo, ALU.mult, ALU.add, [p_b, b_omu, b_zo], [b_zo])

            def rp(i, f):
                return rwp[:, i, f:f + 1]

            def drive(gens, weights=None, background=()):
                gens = list(gens)
                bg = list(background)
                wts = {id(g_): 1 for g_ in gens}
                if weights:
                    for g_, w_ in zip(gens, weights):
                        wts[id(g_)] = w_
                while gens:
                    for g_ in list(gens):
                        for _ in range(wts[id(g_)]):
                            try:
                                next(g_)
                            except StopIteration:
                                gens.remove(g_)
                                break
                    for g_ in list(bg):
                        try:
                            next(g_)
                        except StopIteration:
                            bg.remove(g_)

            def pre(idx):
                own = idx >= NWIN
                win = idx % NWIN
                par = idx % 2
                AR, b_AR = ARs[par]
                KH, b_KH = KHs[par]
                BH, b_BH = BHs[par]
                GL, b_GL = GLs[par]
                gfm, b_gfm = gfms[par]
                bv, b_bv = bvs[par]
                ufm, b_ufm = ufms[par]
                row_base = (TOK if own else 0) + win * W
                hT_t, hT_b = hT[0]
                for st in range(W // 128):
                    xt, bx = load_x(row_base + st * 128)
                    norm_transpose(xt, bx, st, hT_t, hT_b, 0, 1, W)
                    yield
                z12, b_z12 = zt[3]
                shifted(hT_t, hT_b, 12, z12[:], b_z12)
                act(LA[0:64, :], z12[0:64, :], AF.Tanh, [b_z12], [b_LA])
                cp("pool", LA[64:128, :], z12[64:128, :], [b_z12], [b_LA])
                yield
                need_r = own or (win == NWIN - 1)
                if need_r:
                    shifted(hT_t, hT_b, 13, z12[:], b_z12)
                if own:
                    act(gs[:], z12[:], AF.Sigmoid, [b_z12], [b_gs])
                yield
                for f in range(4):
                    (zr, b_zr), (zk, b_zk), (zv, b_zv) = zt[0], zt[1], zt[2]
                    if need_r:
                        shifted(hT_t, hT_b, f, zr[:], b_zr)
                        yield
                    shifted(hT_t, hT_b, 4 + f, zk[:], b_zk)
                    yield
                    shifted(hT_t, hT_b, 8 + f, zv[:], b_zv)
                    yield
                    (sw, b_sw), (asg, b_asg), (kkr, b_kkr), (nrm, b_nrm), (kk, b_kk), (t1, b_t1) = et[0:6]
                    (kmod, b_kmod), (bvec, b_bvec), (lw, b_lw), (cc, b_cc), (cm, b_cm), (ex, b_ex) = et[6:12]
                    (sq, b_sq), (rk, b_rk) = etb
                    mm(PA[1][:, 0:W], lo2w[:, f * 128:(f + 1) * 128], LA[:], [b_lo2w, b_LA], [bPA[1]])
                    mm(PA[1][:, W:2 * W], lo2a[:, f * 128:(f + 1) * 128], LA[:], [b_lo2a, b_LA], [bPA[1]])
                    act(sw[:], PA[1][:, 0:W], AF.Sigmoid, [bPA[1], b_rwp], [b_sw], bias=rp(0, f))
                    act(asg[:], PA[1][:, W:2 * W], AF.Sigmoid, [bPA[1], b_rwp], [b_asg], bias=rp(1, f))
                    if own:
                        mm(PA[1][:, 0:W], g2[:, f * 128:(f + 1) * 128], gs[:], [b_g2, b_gs], [bPA[1]])
                        act(gfm[:, f, :], PA[1][:, 0:W], AF.Identity, [bPA[1]], [b_gfm])
                    act(kkr[:], zk[:], AF.Identity, [b_zk, b_rwp], [b_kkr], scale=rp(2, f))
                    act(sq[:], kkr[:], AF.Square, [b_kkr], [b_sq])
                    yield
                    mm(PA[1][:, 0:W], bones_b[:], sq[:], [b_bones_b, b_sq], [bPA[1]])
                    ts("dve", nrm[:], PA[1][:, 0:W], 1e-24, None, ALU.max, None, [bPA[1]], [b_nrm])
                    act(nrm[:], nrm[:], AF.Ln, [b_nrm], [b_nrm], scale=float(2.0 ** 40))
                    act(nrm[:], nrm[:], AF.Exp, [b_nrm], [b_nrm], scale=-0.5, bias=20.0 * math.log(2.0))
                    tt("pool", kk[:], kkr[:], nrm[:], ALU.mult, [b_kkr, b_nrm], [b_kk])
                    act(t1[:], asg[:], AF.Identity, [b_asg, b_rwp], [b_t1], scale=rp(3, f), bias=rp(7, f))
                    yield
                    tt("pool", kmod[:], zk[:], t1[:], ALU.mult, [b_zk, b_t1], [b_kmod])
                    tt("pool", bvec[:], kk[:], asg[:], ALU.mult, [b_kk, b_asg], [b_bvec])
                    S.op("act", lambda e: e.mul(lw[:], sw[:], -math.exp(-0.5)), [b_sw], [b_lw])
                    S.op("dve", lambda e: e.tensor_tensor_scan(out=cc[:], data0=rst01[:], data1=lw[:], initial=0.0,
                                                                op0=ALU.mult, op1=ALU.add), [b_rst, b_lw], [b_cc])
                    yield
                    tt("pool", cm[:], cc[:], lw[:], ALU.subtract, [b_cc, b_lw], [b_cm])
                    act(sw[:], cc[:], AF.Exp, [b_cc], [b_sw])
                    act(ex[:], cc[:], AF.Exp, [b_cc], [b_ex], scale=-1.0)
                    act(cm[:], cm[:], AF.Exp, [b_cm], [b_cm])
                    yield
                    if own:
                        tt("dve", AR[:, f, :, 64:128], zr[:].rearrange("p (c t) -> p c t", t=64),
                           sw[:].rearrange("p (c t) -> p c t", t=64), ALU.mult, [b_zr, b_sw], [b_AR])
                    stt(AR[:, f, :, 0:64], kk[:].rearrange("p (c t) -> p c t", t=64), -1.0,
                        cm[:].rearrange("p (c t) -> p c t", t=64), ALU.mult, ALU.mult, [b_kk, b_cm], [b_AR])
                    tt("dve", KH[:, f, :], kmod[:], ex[:], ALU.mult, [b_kmod, b_ex], [b_KH])
                    tt("pool", BH[:, f, :], bvec[:], ex[:], ALU.mult, [b_bvec, b_ex], [b_BH])
                    cp("pool", GL[:, f, :], sw[:].rearrange("p (c t) -> p c t", t=64)[:, :, 63], [b_sw], [b_GL])
                    yield
                    if own:
                        stt(rk[:], zr[:], rp(4, f), kmod[:], ALU.mult, ALU.mult, [b_zr, b_rwp, b_kmod], [b_rk])
                        mm(PA[1][:, 0:W], bones_b[:], rk[:], [b_bones_b, b_rk], [bPA[1]])
                        tt("dve", bv[:, f, :], PA[1][:, 0:W], zv[:], ALU.mult, [bPA[1], b_zv], [b_bv])
                        yield
                    cp("act", VBs[par][0][:, f, :], zv[:], [b_zv], [VBs[par][1]])
                    yield
                for t in range(4):
                    pj, bpj = project(hT_t, hT_b, 14 + t)
                    act(ufm[:, t, :], pj[:, 0:W], AF.Identity, [bpj], [b_ufm])
                    yield

            def main(idx, nxt):
                own = idx >= NWIN
                win = idx % NWIN
                par = idx % 2
                M['AR'] = ARs[par]
                M['KH'] = KHs[par]
                M['BH'] = BHs[par]
                M['GL'] = GLs[par]
                M['gfm'] = gfms[par]
                M['bv'] = bvs[par]
                M['ufm'] = ufms[par]
                KH, b_KH = KHs[par]
                BH, b_BH = BHs[par]
                VB, b_VB = VBs[par]
                for (src, b_src, dst, b_dst) in ((KH, b_KH, kT, b_kT), (BH, b_BH, bT, b_bT), (VB, b_VB, vT, b_vT)):
                    n = 0
                    for c in range(4):
                        for f in range(4):
                            for hp in range(2):
                                rs = slice(hp * 64, hp * 64 + 64)
                                n += 1
                                o0 = (c * 4 + f) * 64
                                tr(PT[rs, o0:o0 + 64], src[rs, f, c * 64:(c + 1) * 64], ident_b[rs, rs],
                                   [b_src, b_ident_b], [bPT], inc=(n == 32))
                    cp("act", dst[:].rearrange("p a b c -> p (a b c)"), PT[:, 0:1024], [bPT], [b_dst])
                if M.get('inv0_done') != idx:
                    drive([chunk_inv(0, own, 0, par)])
                extra = [nxt] if nxt is not None else []
                for c in range(4):
                    grp = [chunk_chain(c, own, c % 2), s5_chunk(c, own)]
                    wts = [2, 1]
                    if c < 3:
                        grp = [chunk_inv(c + 1, own, (c + 1) % 2, par)] + grp
                        wts = [2, 2, 1]
                    elif nxt is not None:
                        drive([nxt])
                        grp = [chunk_inv(0, (idx + 1) >= NWIN, 0, (idx + 1) % 2)] + grp
                        wts = [2, 2, 1]
                        M['inv0_done'] = idx + 1
                    drive(grp, wts, background=extra)
                    extra = []
                    if nxt is not None:
                        extra = [nxt]
                if nxt is not None:
                    drive([nxt])
                if own:
                    outputs(win, None, None)

            HEADS = [(f, hp) for f in range(4) for hp in range(2)]

            def chunk_inv(c, own, slot, par):
                AR, b_AR = ARs[par]
                KH, b_KH = KHs[par]
                BH, b_BH = BHs[par]
                cs = slice(c * 64, (c + 1) * 64)
                na = 128 if own else 64
                heads = HEADS
                Abm, b_Abm = AbmS[slot]
                Akm, b_Akm = AkmS[slot]
                for i, (f, hp) in enumerate(heads):
                    rs = slice(hp * 64, hp * 64 + 64)
                    mm(PA[0][rs, f * 128:f * 128 + na], BH[rs, f, cs], AR[rs, f, c, 0:na], [b_BH, b_AR], [bPA[0]], inc=(i == 7))
                for i, (f, hp) in enumerate(heads):
                    rs = slice(hp * 64, hp * 64 + 64)
                    mm(PA[1][rs, f * 128:f * 128 + na], KH[rs, f, cs], AR[rs, f, c, 0:na], [b_KH, b_AR], [bPA[1]], inc=(i == 7))
                for i, (f, hp) in enumerate(heads):
                    rs = slice(hp * 64, hp * 64 + 64)
                    mm(PI[rs, f * 64:(f + 1) * 64], AR[rs, f, c, 0:64], BH[rs, f, cs], [b_AR, b_BH], [bPI[0]], inc=(i == 7))
                yield
                pa0 = PA[0][:, :].rearrange("p (f n) -> p f n", n=128)
                pa1 = PA[1][:, :].rearrange("p (f n) -> p f n", n=128)
                mk = maskar[:, 0:na]
                tt("dve", Abm[:, :, 0:na], pa0[:, :, 0:na], bc1(mk, 4), ALU.mult, [bPA[0], b_maskar], [b_Abm])
                xt_t, xt_b = NTb, b_NTb
                tt("dve", xt_t[:], PI[:, 0:256].rearrange("p (f n) -> p f n", n=64), bc1(masknt[:], 4), ALU.mult,
                   [bPI[0], b_masknt], [xt_b])
                tt("dve", Akm[:, :, 0:na], pa1[:, :, 0:na], bc1(mk, 4), ALU.mult, [bPA[1], b_maskar], [b_Akm])
                p_t, p_b = Pm[0]
                tt("pool", p_t[:], Abm[:, :, 0:64], bc1(identp[:], 4), ALU.add, [b_Abm, b_identp], [p_b])
                yield
                x_ap = lambda f, rs: Abm[rs, f, 0:64]
                x_b = b_Abm
                for lvl in range(1, 6):
                    xtn_t, xtn_b = XTb[lvl % 2]
                    for i, (f, hp) in enumerate(heads):
                        rs = slice(hp * 64, hp * 64 + 64)
                        mm(PI[rs, f * 64:(f + 1) * 64], x_ap(f, rs), xt_t[rs, f, :], [x_b, xt_b], [bPI[0]], inc=(i == 7))
                    if lvl <= 4:
                        xn_t, xn_b = Xb[lvl % 2]
                        for i, (f, hp) in enumerate(heads):
                            rs = slice(hp * 64, hp * 64 + 64)
                            mm(PA[0][rs, f * 64:(f + 1) * 64], xt_t[rs, f, :], x_ap(f, rs), [x_b, xt_b], [bPA[0]], inc=(i == 7))
                    yield
                    cp("act", xtn_t[:], PI[:, 0:256].rearrange("p (f n) -> p f n", n=64), [bPI[0]], [xtn_b])
                    if lvl <= 4:
                        cp("act", xn_t[:], PA[0][:, 0:256].rearrange("p (f n) -> p f n", n=64), [bPA[0]], [xn_b])
                    yield
                    pn_t, pn_b = Pm[lvl % 2]
                    for i, (f, hp) in enumerate(heads):
                        rs = slice(hp * 64, hp * 64 + 64)
                        mm(PC[rs, f * 64:(f + 1) * 64], xtn_t[rs, f, :], p_t[rs, f, :], [xtn_b, p_b], [bPC[0]], inc=(i == 7))
                    yield
                    tt("dve", pn_t[:], PC[:, 0:256].rearrange("p (f n) -> p f n", n=64), p_t[:], ALU.add, [bPC[0], p_b], [pn_b])
                    yield
                    p_t, p_b = pn_t, pn_b
                    xt_t, xt_b = xtn_t, xtn_b
                    if lvl <= 4:
                        x_ap = (lambda t_: (lambda f, rs: t_[rs, f, :]))(xn_t)
                        x_b = xn_b
                T0_t, T0_b = p_t, p_b
                Rb, b_Rb = Xb[0]
                TTb, b_TTb = Xb[1]
                for i, (f, hp) in enumerate(heads):
                    rs = slice(hp * 64, hp * 64 + 64)
                    mm(PI[rs, f * 64:(f + 1) * 64], NTb[rs, f, :], T0_t[rs, f, :], [b_NTb, T0_b], [bPI[0]], inc=(i == 7))
                for i, (f, hp) in enumerate(heads):
                    rs = slice(hp * 64, hp * 64 + 64)
                    tr(PT[rs, f * 64:(f + 1) * 64], T0_t[rs, f, :], ident_b[rs, rs], [T0_b, b_ident_b], [bPT], inc=(i == 7))
                tt("pool", ImT[:], bc1(identp[:], 4), T0_t[:], ALU.subtract, [b_identp, T0_b], [b_ImT])
                yield
                tt("dve", Rb[:], PI[:, 0:256].rearrange("p (f n) -> p f n", n=64), ImT[:], ALU.add, [bPI[0], b_ImT], [b_Rb])
                cp("act", TTb[:], PT[:, 0:256].rearrange("p (f n) -> p f n", n=64), [bPT], [b_TTb])
                yield
                for i, (f, hp) in enumerate(heads):
                    rs = slice(hp * 64, hp * 64 + 64)
                    mm(PC[rs, f * 64:(f + 1) * 64], TTb[rs, f, :], Rb[rs, f, :], [b_TTb, b_Rb], [bPC[0]], inc=(i == 7))
                yield
                T_t, T_b = TfinS[slot]
                tt("dve", T_t[:], PC[:, 0:256].rearrange("p (f n) -> p f n", n=64), T0_t[:], ALU.add, [bPC[0], T0_b], [T_b])
                lastT[0] = (T_t, T_b)
                yield

            def chunk_chain(c, own, slot):
                AR, b_AR = M['AR']
                KH, b_KH = M['KH']
                BH, b_BH = M['BH']
                GL, b_GL = M['GL']
                gfm, b_gfm = M['gfm']
                bv, b_bv = M['bv']
                ufm, b_ufm = M['ufm']
                cs = slice(c * 64, (c + 1) * 64)
                heads = HEADS
                Abm, b_Abm = AbmS[slot]
                Akm, b_Akm = AkmS[slot]
                T_t, T_b = TfinS[slot]
                s0_t, s0_b = Sb[cur_s[0] % 2]
                s1_t, s1_b = Sb[(cur_s[0] + 1) % 2]
                cur_s[0] += 1
                pm3 = PM[:, 0:256].rearrange("p (f n) -> p f n", n=64)
                for i, (f, hp) in enumerate(heads):
                    rs = slice(hp * 64, hp * 64 + 64)
                    mm(PM[rs, f * 64:(f + 1) * 64], AR[rs, f, c, 0:64], s0_t[rs, f, :], [b_AR, s0_b], [bPM[0]], start=True, stop=False, inc=False)
                    mm(PM[rs, f * 64:(f + 1) * 64], Akm[rs, f, 0:64], vT[rs, c, f, :], [b_Akm, b_vT], [bPM[0]], start=False, stop=True, inc=(i == 7))
                yield
                cp("act", WT[:], pm3, [bPM[0]], [b_WT])
                yield
                for i, (f, hp) in enumerate(heads):
                    rs = slice(hp * 64, hp * 64 + 64)
                    mm(PM[rs, f * 64:(f + 1) * 64], T_t[rs, f, :], WT[rs, f, :], [T_b, b_WT], [bPM[0]], inc=(i == 7))
                yield
                cp("act", UT[:], pm3, [bPM[0]], [b_UT])
                yield
                for i, (f, hp) in enumerate(heads):
                    rs = slice(hp * 64, hp * 64 + 64)
                    mm(PM[rs, f * 64:(f + 1) * 64], bT[rs, c, f, :], UT[rs, f, :], [b_bT, b_UT], [bPM[0]], start=True, stop=False, inc=False)
                    mm(PM[rs, f * 64:(f + 1) * 64], kT[rs, c, f, :], vT[rs, c, f, :], [b_kT, b_vT], [bPM[0]], start=False, stop=True, inc=(i == 7))
                yield
                tt("dve", stmp[:], pm3, Sf[:], ALU.add, [bPM[0], b_Sf], [b_stmp])
                yield
                if own:
                    for i, (f, hp) in enumerate(heads):
                        rs = slice(hp * 64, hp * 64 + 64)
                        o = PM[rs, f * 64:(f + 1) * 64]
                        mm(o, s0_t[rs, f, :], AR[rs, f, c, 64:128], [s0_b, b_AR], [bPM[0]], start=True, stop=False, inc=False)
                        mm(o, UT[rs, f, :], Abm[rs, f, 64:128], [b_UT, b_Abm], [bPM[0]], start=False, stop=False, inc=False)
                        mm(o, vT[rs, c, f, :], Akm[rs, f, 64:128], [b_vT, b_Akm], [bPM[0]], start=False, stop=True, inc=(i == 7))
                    yield
                    cp("act", Yw[:, :, cs], pm3, [bPM[0]], [b_Yw])
                tt("dve", Sf[:], stmp[:], bc2(GL[:, :, c], 64), ALU.mult, [b_stmp, b_GL], [b_Sf])
                cp("act", s1_t[:], Sf[:], [b_Sf], [s1_b])
                yield

            def s5_chunk(c, own):
                AR, b_AR = M['AR']
                KH, b_KH = M['KH']
                BH, b_BH = M['BH']
                GL, b_GL = M['GL']
                gfm, b_gfm = M['gfm']
                bv, b_bv = M['bv']
                ufm, b_ufm = M['ufm']
                cs = slice(c * 64, (c + 1) * 64)
                for hf in range(2):
                    ps_ = slice(hf * 8, hf * 8 + 8)
                    for pl in range(8):
                        pr = hf * 8 + pl
                        t = pr // 4
                        mm(PJ[0][:, pl * 64:(pl + 1) * 64], Bb_re[:, pr, :], ufm[:, t, cs], [b_Bbre, b_ufm], [bPJ[0]], inc=(pl == 7))
                    for pl in range(8):
                        pr = hf * 8 + pl
                        t = pr // 4
                        mm(PJ[1][:, pl * 64:(pl + 1) * 64], Bb_im[:, pr, :], ufm[:, t, cs], [b_Bbim, b_ufm], [bPJ[1]], inc=(pl == 7))
                    yield
                    cosv = tcos[:, ps_, :].rearrange("p a b -> p (a b)")
                    sinv = tsin[:, ps_, :].rearrange("p a b -> p (a b)")
                    (a0, ba0), (a1, ba1), (a2, ba2), (a3, ba3) = s5a
                    tt("dve", a0[:], PJ[0][:, :], cosv, ALU.mult, [bPJ[0], b_tcos], [ba0])
                    tt("dve", a1[:], PJ[1][:, :], sinv, ALU.mult, [bPJ[1], b_tsin], [ba1])
                    tt("dve", a2[:], PJ[1][:, :], cosv, ALU.mult, [bPJ[1], b_tcos], [ba2])
                    tt("dve", a3[:], PJ[0][:, :], sinv, ALU.mult, [bPJ[0], b_tsin], [ba3])
                    btr, b_btr, bti, b_bti = a0, ba0, a2, ba2
                    yield
                    tt("pool", btr[:], a0[:], a1[:], ALU.add, [ba0, ba1], [b_btr])
                    tt("pool", bti[:], a2[:], a3[:], ALU.subtract, [ba2, ba3], [b_bti])
                    b3r = btr[:].rearrange("p (a b) -> p a b", b=64)
                    b3i = bti[:].rearrange("p (a b) -> p a b", b=64)
                    tt("pool", b3r[:, :, 0], b3r[:, :, 0], s5s[:, 0, ps_], ALU.add, [b_btr, b_s5s], [b_btr])
                    tt("pool", b3i[:, :, 0], b3i[:, :, 0], s5s[:, 1, ps_], ALU.add, [b_bti, b_s5s], [b_bti])
                    yield
                    rv = rho0[:, ps_, :].rearrange("p a b -> p (a b)")
                    S.op("dve", lambda e: e.tensor_tensor_scan(out=s_tr[:], data0=rv, data1=btr[:], initial=0.0,
                                                                op0=ALU.mult, op1=ALU.add), [b_rho0, b_btr], [b_str])
                    S.op("dve", lambda e: e.tensor_tensor_scan(out=s_ti[:], data0=rv, data1=bti[:], initial=0.0,
                                                                op0=ALU.mult, op1=ALU.add), [b_rho0, b_bti], [b_sti])
                    yield
                    s3r = s_tr[:].rearrange("p (a b) -> p a b", b=64)
                    s3i = s_ti[:].rearrange("p (a b) -> p a b", b=64)
                    qa = s5q[:, 0:2, :]
                    qb = s5q[:, 2:4, :]
                    tt("pool", qa, s3r[:, :, 63].unsqueeze(1).to_broadcast([128, 2, 8]), T63[:, 0, :, ps_], ALU.mult, [b_str, b_T63], [b_s5q])
                    tt("pool", qb, s3i[:, :, 63].unsqueeze(1).to_broadcast([128, 2, 8]), T63[:, 1, :, ps_], ALU.mult, [b_sti, b_T63], [b_s5q])
                    tt("pool", s5s[:, :, ps_], qa, qb, ALU.add, [b_s5q], [b_s5s])
                    yield
                    if own:
                        tt("dve", a0[:], s_tr[:], cosv, ALU.mult, [b_str, b_tcos], [ba0])
                        tt("pool", a1[:], s_ti[:], sinv, ALU.mult, [b_sti, b_tsin], [ba1])
                        tt("dve", a2[:], s_tr[:], sinv, ALU.mult, [b_str, b_tsin], [ba2])
                        tt("pool", a3[:], s_ti[:], cosv, ALU.mult, [b_sti, b_tcos], [ba3])
                        tt("dve", srb[:], a0[:], a1[:], ALU.subtract, [ba0, ba1], [b_srb])
                        tt("pool", sib[:], a2[:], a3[:], ALU.add, [ba2, ba3], [b_sib])
                        yield
                        for pl in range(8):
                            pr = hf * 8 + pl
                            t = pr // 4
                            tl = t % 2
                            o = PJ[0][:, tl * 64:(tl + 1) * 64]
                            mm(o, Cp_re[:, pr, :], srb[:, pl * 64:(pl + 1) * 64], [b_Cpre, b_srb], [bPJ[0]],
                               start=(pr % 4 == 0), stop=False, inc=False)
                            mm(o, Cp_imn[:, pr, :], sib[:, pl * 64:(pl + 1) * 64], [b_Cpim, b_sib], [bPJ[0]],
                               start=False, stop=(pr % 4 == 3), inc=(pl == 7))
                        yield
                        for tl in range(2):
                            t = hf * 2 + tl
                            stt(Y5w[:, t, cs], ufm[:, t, cs], s5v[:, 0, t:t + 1], PJ[0][:, tl * 64:(tl + 1) * 64], ALU.mult, ALU.add,
                                [b_ufm, b_s5v, bPJ[0]], [b_Y5w])
                    yield

            def outputs(win, hT_t, hT_b):
                AR, b_AR = M['AR']
                KH, b_KH = M['KH']
                BH, b_BH = M['BH']
                GL, b_GL = M['GL']
                gfm, b_gfm = M['gfm']
                bv, b_bv = M['bv']
                ufm, b_ufm = M['ufm']
                xres = [load_x(TOK + win * W + st * 128) + (xslot[0] - 1,) for st in range(W // 128)]
                for f in range(4):
                    (ysq, b_ysq), (mu2, b_mu2), (var, b_var), (dd, b_dd) = et[(f % 2) * 4:(f % 2) * 4 + 4]
                    act(ysq[:], Yw[:, f, :], AF.Square, [b_Yw], [b_ysq])
                    mm(PM[:, 0:W], bones_f[:], Yw[:, f, :], [b_bones_f, b_Yw], [bPM[0]])
                    mm(PA[0][:, 0:W], bones_f[:], ysq[:], [b_bones_f, b_ysq], [bPA[0]])
                    act(mu2[:], PM[:, 0:W], AF.Square, [bPM[0]], [b_mu2])
                    tt("dve", var[:], PA[0][:, 0:W], mu2[:], ALU.subtract, [bPA[0], b_mu2], [b_var])
                    act(var[:], var[:], AF.Ln, [b_var], [b_var], bias=64e-5, scale=1.0)
                    act(var[:], var[:], AF.Exp, [b_var], [b_var], scale=-0.5)
                    tt("dve", dd[:], Yw[:, f, :], PM[:, 0:W], ALU.subtract, [b_Yw, bPM[0]], [b_dd])
                    tt("pool", dd[:], dd[:], var[:], ALU.mult, [b_dd, b_var], [b_dd])
                    ts("pool", dd[:], dd[:], rp(5, f), rp(6, f), ALU.mult, ALU.add, [b_dd, b_rwp], [b_dd])
                    tt("pool", dd[:], dd[:], bv[:, f, :], ALU.add, [b_dd, b_bv], [b_dd])
                    tt("dve", ycat[:, f, :], dd[:], gfm[:, f, :], ALU.mult, [b_dd, b_gfm], [b_ycat])
                zzbv = lambda t: (srb if t < 2 else sib)[:, (t % 2) * W:(t % 2 + 1) * W]
                zzbb = lambda t: (b_srb if t < 2 else b_sib)
                zzv = lambda t: (s_tr if t < 2 else s_ti)[:, (t % 2) * W:(t % 2 + 1) * W]
                zzb_ = lambda t: (b_str if t < 2 else b_sti)
                oo, b_oo = sbt_oo
                (x2, b_x2), (pq, b_pq), (sg, b_sg) = et[8:11]
                (osq, b_osq), _ = etb
                for t in range(4):
                    act(x2[:], Y5w[:, t, :], AF.Square, [b_Y5w], [b_x2])
                    ts("pool", pq[:], x2[:], 0.044715, 1.0, ALU.mult, ALU.add, [b_x2], [b_pq])
                    tt("pool", pq[:], pq[:], Y5w[:, t, :], ALU.mult, [b_pq, b_Y5w], [b_pq])
                    act(sg[:], pq[:], AF.Sigmoid, [b_pq], [b_sg], scale=2.0 * math.sqrt(2.0 / math.pi))
                    tt("dve", zzv(t), Y5w[:, t, :], sg[:], ALU.mult, [b_Y5w, b_sg], [zzb_(t)])
                    cp("act", zzbv(t), zzv(t), [zzb_(t)], [zzbb(t)])
                for t2 in range(4):
                    for t in range(4):
                        mm(PM[:, 0:W], wglu[:, t, t2 * 128:(t2 + 1) * 128], zzbv(t), [b_wglu, zzbb(t)], [bPM[0]],
                           start=(t == 0), stop=(t == 3), inc=(t == 3))
                    act(sg[:], PM[:, 0:W], AF.Sigmoid, [bPM[0], b_s5v], [b_sg], bias=s5v[:, 1, t2:t2 + 1])
                    tt("dve", oo[:, t2, :], zzv(t2), sg[:], ALU.mult, [zzb_(t2), b_sg], [b_oo])
                for t in range(4):
                    act(osq[:], oo[:, t, :], AF.Square, [b_oo], [b_osq])
                    mm(PA[0][:, 0:W], ones_b[:], osq[:], [b_ones_b, b_osq], [bPA[0]], start=(t == 0), stop=(t == 3))
                act(sg[:], PA[0][:, 0:W], AF.Ln, [bPA[0]], [b_sg], bias=1e-6, scale=1.0)
                act(sg[:], sg[:], AF.Exp, [b_sg], [b_sg], scale=-0.5)
                for t in range(4):
                    stt(ycat[:, 4 + t, :], oo[:, t, :], s5v[:, 2, t:t + 1], sg[:], ALU.mult, ALU.mult, [b_oo, b_s5v, b_sg], [b_ycat])
                for st in range(W // 128):
                    xt, bx, xsl_i = xres[st]
                    for hf in range(2):
                        pj, bpj = PJ[hf], bPJ[hf]
                        for k in range(8):
                            mm(pj[:, :], ycat[:, k, st * 128:(st + 1) * 128], Wo[:, k, hf * 512:(hf + 1) * 512], [b_ycat, b_Wo], [bpj],
                               start=(k == 0), stop=(k == 7), inc=(k == 7))
                        tt("dve", xt[:, hf * 512:(hf + 1) * 512], pj[:, :], xt[:, hf * 512:(hf + 1) * 512], ALU.add, [bpj, bx], [bx])
                    orow = win * W + st * 128
                    S.dma(out_d[orow:orow + 128, :], xt[:], "xst%d" % (xsl_i % 2), R=[bx], W=[b_outrows[orow // 128]])

            sbt_oo = (Yw, b_Yw)
            b_outrows = [Buf("orow%d" % i) for i in range(TOK // 128)]

            def dump_win():
                dump('hT', hT[0][0][:, 2, :], hT[0][1])
                dump('Pprev', Pprev[:], b_Pprev)
                dump('AR', AR[:, 1, :, :], b_AR)
                dump('KH', KH[:, 1, :], b_KH)
                dump('BH', BH[:, 1, :], b_BH)
                dump('kT', kT[:, 3, 1, :], b_kT)
                dump('vT', vT[:, 3, 1, :], b_vT)
                dump('GL', GL[:], b_GL)
                dump('Abm', AbmS[1][0][:], AbmS[1][1])
                dump('Akm', AkmS[1][0][:], AkmS[1][1])
                dump('T', lastT[0][0][:], lastT[0][1])
                dump('WT', WT[:], b_WT)
                dump('UT', UT[:], b_UT)
                dump('Sf', Sf[:], b_Sf)
                dump('s5s', s5s[:], b_s5s)
                dump('ufm', ufm[:, 2, :], b_ufm)
                dump('Yw', Yw[:], b_Yw)
                dump('Y5w', Y5w[:], b_Y5w)
                dump('ycat', ycat[:], b_ycat)

            gens = {0: pre(0)}
            drive([gens[0]])
            for idx in range(2 * NWIN):
                nxt = None
                if idx + 1 < 2 * NWIN:
                    if idx + 1 == NWIN:
                        ts("dve", Pprev[:], Pprev[:], maskv[:, 0:1], None, ALU.mult, None, [b_Pprev, b_maskv], [b_Pprev])
                    nxt = pre(idx + 1)
                main(idx, nxt)
                if idx == NWIN - 1:
                    ts("dve", Sf[:].rearrange("p a b -> p (a b)"), Sf[:].rearrange("p a b -> p (a b)"), maskv[:, 0:1], None, ALU.mult, None,
                       [b_Sf, b_maskv], [b_Sf])
                    sbc_t, sbc_b = Sb[cur_s[0] % 2]
                    cp("act", sbc_t[:], Sf[:], [b_Sf], [sbc_b])
                    ts("dve", s5s[:].rearrange("p a b -> p (a b)"), s5s[:].rearrange("p a b -> p (a b)"), maskv[:, 0:1], None, ALU.mult, None,
                       [b_s5s, b_maskv], [b_s5s])
                if stage(('own:%d' % (idx - NWIN)) if idx >= NWIN else ('pre:%d' % idx), dump_win):
                    return nc
            S.barrier()
        esB = ExitStack()
        with esB:
            HS = DFF // 2
            wgb, b_wgb = sbt(esB, [128, 8, DFF], BF16, "wgb")
            wub, b_wub = sbt(esB, [128, 8, DFF], BF16, "wub")
            wdb, b_wdb = sbt(esB, [128, NFF, D], BF16, "wdb")
            stB = [sbt(esB, [128, HS], F32, "stB%d" % i) for i in range(2)]
            fg_bc, b_fg = sbt(esB, [128, D], F32, "fgbc")
            S.dma(fg_bc[:], fgain_d.partition_broadcast(128), "fgbc", W=[b_fg])
            TB = 512
            NSB = TB // 128
            xsB = [sbt(esB, [128, D], F32, "xsB%d" % i) for i in range(NSB + 1)]
            xnB, b_xnB = sbt(esB, [128, D], BF16, "xnB")
            statB, b_statB = sbt(esB, [128, 4], F32, "statB")
            h2T = [sbt(esB, [128, 8, TB], BF16, "h2T%d" % i) for i in range(1)]
            actT, b_actT = sbt(esB, [128, NFF, TB], BF16, "actT")
            sil = [sbt(esB, [128, TB], BF16, "sil%d" % i) for i in range(2)]
            b_wgh = [Buf("wg_h0"), Buf("wg_h1")]
            b_wuh = [Buf("wu_h0"), Buf("wu_h1")]
            b_wdj = [Buf("wd_%d" % j) for j in range(NFF)]
            wprog = [0]

            def wload():
                ci = 0
                for hh in range(2):
                    for (src_d, dst, bl) in ((wg_d, wgb, b_wgh), (wu_d, wub, b_wuh)):
                        for k in range(8):
                            st_t, st_b = stB[ci % 2]
                            S.dma(st_t[:], src_d[k * 128:(k + 1) * 128, hh * HS:(hh + 1) * HS], "stB%d" % (ci % 2), W=[st_b])
                            cp(("act", "dve", "pool")[ci % 3], dst[:, k, hh * HS:(hh + 1) * HS], st_t[:], [st_b], [bl[hh]])
                            ci += 1
                            wprog[0] += 1
                            yield
                for j in range(NFF):
                    st_t, st_b = stB[ci % 2]
                    S.dma(st_t[:, 0:D], wd_d[j * 128:(j + 1) * 128, :], "stB%d" % (ci % 2), W=[st_b])
                    tt(("dve", "pool")[ci % 2], wdb[:, j, :], st_t[:, 0:D], gf_bc[:], ALU.mult, [st_b, b_gf], [b_wdj[j]])
                    ci += 1
                    wprog[0] += 1
                    yield

            wgen = wload()

            def need(n):
                while wprog[0] < n:
                    next(wgen)

            def bgstep():
                try:
                    next(wgen)
                except StopIteration:
                    pass

            xq = [0]
            need(4)
            for wi in range(TOK // TB):
                hT_t, hT_b = h2T[0]
                tiles = []
                for st in range(NSB):
                    ti = wi * NSB + st
                    i = xq[0] % (NSB + 1)
                    xq[0] += 1
                    xt, bx = xsB[i]
                    orow = ti * 128
                    S.dma(xt[:], out_d[orow:orow + 128, :], "xsB%d" % i, R=[b_outrows[ti]], W=[bx])
                    tiles.append((xt, bx, i, orow, ti))
                for st in range(NSB):
                    xt, bx, i, orow, ti = tiles[st]
                    bgstep()
                    bgstep()
                    act(xnB[:], xt[:], AF.Square, [bx], [b_xnB, b_statB], accum=statB[:, 0:1])
                    act(statB[:, 1:2], statB[:, 0:1], AF.Ln, [b_statB], [b_statB], bias=1e-6, scale=1.0 / D)
                    act(statB[:, 1:2], statB[:, 1:2], AF.Exp, [b_statB], [b_statB], scale=-0.5)
                    act(xnB[:], xt[:], AF.Identity, [bx, b_statB], [b_xnB], scale=statB[:, 1:2])
                    for k in range(8):
                        tr(PT[:, k * 128:(k + 1) * 128], xnB[:, k * 128:(k + 1) * 128], ident_b[:], [b_xnB, b_ident_b], [bPT], inc=(k == 7))
                    for k in range(8):
                        o = hT_t[:, k, st * 128:(st + 1) * 128]
                        i_ = PT[:, k * 128:(k + 1) * 128]
                        if st % 2 == 0:
                            act(o, i_, AF.Identity, [bPT, b_modp], [hT_b], scale=modp[:, 3, k:k + 1], bias=modp[:, 2, k:k + 1])
                        else:
                            ts("dve", o, i_, modp[:, 3, k:k + 1], modp[:, 2, k:k + 1], ALU.mult, ALU.add, [bPT, b_modp], [hT_b])
                for j in range(NFF):
                    wh = 0 if j < 11 else 1
                    need(16 if wh == 0 else 32)
                    bgstep()
                    pg, bpg = (PJ[0], bPJ[0]) if j % 2 == 0 else (PA[0], bPA[0])
                    pu, bpu = (PJ[1], bPJ[1]) if j % 2 == 0 else (PA[1], bPA[1])
                    for k in range(8):
                        mm(pg[:, 0:TB], wgb[:, k, j * 128:(j + 1) * 128], hT_t[:, k, :], [b_wgh[wh], hT_b], [bpg],
                           start=(k == 0), stop=(k == 7), inc=(k == 7))
                    for k in range(8):
                        mm(pu[:, 0:TB], wub[:, k, j * 128:(j + 1) * 128], hT_t[:, k, :], [b_wuh[wh], hT_b], [bpu],
                           start=(k == 0), stop=(k == 7), inc=(k == 7))
                    s_t, s_b = sil[j % 2]
                    act(s_t[:], pg[:, 0:TB], AF.Silu, [bpg], [s_b])
                    tt("dve", actT[:, j, :], s_t[:], pu[:, 0:TB], ALU.mult, [s_b, bpu], [b_actT])
                for st in range(NSB):
                    xt, bx, i, orow, ti = tiles[st]
                    for hf in range(2):
                        pd, bpd = (PI, bPI[0]) if hf == 0 else (PC, bPC[0])
                        need(32 + NFF)
                        for j in range(NFF):
                            mm(pd[:, :], actT[:, j, st * 128:(st + 1) * 128], wdb[:, j, hf * 512:(hf + 1) * 512], [b_actT, b_wdj[j]], [bpd],
                               start=(j == 0), stop=(j == NFF - 1), inc=(j == NFF - 1))
                        tt("dve", xt[:, hf * 512:(hf + 1) * 512], pd[:, :], xt[:, hf * 512:(hf + 1) * 512], ALU.add, [bpd, bx], [bx])
                    act(xnB[:], xt[:], AF.Square, [bx], [b_xnB, b_statB], accum=statB[:, 2:3])
                    act(statB[:, 3:4], statB[:, 2:3], AF.Ln, [b_statB], [b_statB], bias=1e-6, scale=1.0 / D)
                    act(statB[:, 3:4], statB[:, 3:4], AF.Exp, [b_statB], [b_statB], scale=-0.5)
                    stt(xt[:], xt[:], statB[:, 3:4], fg_bc[:], ALU.mult, ALU.mult, [bx, b_statB, b_fg], [bx])
                    S.dma(out_d[orow:orow + 128, :], xt[:], "xstB%d" % i, R=[bx], W=[b_outrows[ti]])
            S.barrier()
    return nc


_NC = None


def _layout_inputs(inp):
    f32 = np.float32
    g = lambda k: np.asarray(inp[k], dtype=f32)
    x = g("x")
    c = g("c")
    shared = {}
    shared["w_ada"] = np.ascontiguousarray(g("w_ada")[0])
    shared["b_ada"] = np.ascontiguousarray(g("b_ada")[0][None, :])
    shared["w_in"] = np.ascontiguousarray(g("w_in")[0])
    shared["mu_l"] = np.ascontiguousarray(g("mu_shift")[0].reshape(14, 128).T)
    v512 = lambda a: a.reshape(4, 128).T
    rw = [g("rw_w0")[0], g("rw_a0")[0], g("rw_k_k")[0], g("rw_k_a")[0], g("rw_r_k")[0].reshape(512),
          g("rw_lnx_w")[0], g("rw_lnx_b")[0]]
    shared["rwp"] = np.ascontiguousarray(np.stack([v512(a) for a in rw], axis=1).reshape(128, 28))
    lo2w = np.zeros((128, 512), f32)
    lo2w[0:64] = g("rw_w2")[0]
    lo2a = np.zeros((128, 512), f32)
    lo2a[64:128] = g("rw_a2")[0]
    shared["lo2w"] = lo2w
    shared["lo2a"] = lo2a
    shared["g2"] = np.ascontiguousarray(g("rw_g2")[0])
    a_re = g("s5_a_re")[0]
    a_im = g("s5_a_im")[0]
    ldt = g("s5_log_dt")[0]
    b_re = g("s5_b_re")[0]
    b_im = g("s5_b_im")[0]
    c_re = g("s5_c_re")[0]
    c_im = g("s5_c_im")[0]
    la = np.zeros((128, 3, 16), f32)
    for pr in range(16):
        for gp in range(2):
            gg = 2 * pr + gp
            la[gp * 64:(gp + 1) * 64, 0, pr] = a_re[gg]
            la[gp * 64:(gp + 1) * 64, 1, pr] = a_im[gg]
            la[gp * 64:(gp + 1) * 64, 2, pr] = ldt[gg]
    shared["s5la"] = la.reshape(128, 48)
    bpre = np.zeros((128, 16, 128), f32)
    bpim = np.zeros((128, 16, 128), f32)
    lbare = np.zeros((128, 16, 128), f32)
    lbaim = np.zeros((128, 16, 128), f32)
    lbldt = np.zeros((128, 16, 128), f32)
    cpre = np.zeros((128, 16, 128), f32)
    cpim = np.zeros((128, 16, 128), f32)
    for pr in range(16):
        tile = pr // 4
        for g8 in range(8):
            gg = tile * 8 + g8
            rows = slice(g8 * 16, g8 * 16 + 16)
            for gp in range(2):
                cols = slice(gp * 64, gp * 64 + 64)
                lbare[rows, pr, cols] = a_re[gg][None, :]
                lbaim[rows, pr, cols] = a_im[gg][None, :]
                lbldt[rows, pr, cols] = ldt[gg]
            if g8 // 2 == pr % 4:
                gp = g8 % 2
                cols = slice(gp * 64, gp * 64 + 64)
                bpre[rows, pr, cols] = b_re[gg].T
                bpim[rows, pr, cols] = b_im[gg].T
        for gp in range(2):
            gg = 2 * pr + gp
            g8 = gg % 8
            cpre[gp * 64:(gp + 1) * 64, pr, g8 * 16:(g8 + 1) * 16] = c_re[gg].T
            cpim[gp * 64:(gp + 1) * 64, pr, g8 * 16:(g8 + 1) * 16] = c_im[gg].T
    shared["bpre"] = bpre.reshape(128, 2048)
    shared["bpim"] = bpim.reshape(128, 2048)
    shared["lbare"] = lbare.reshape(128, 2048)
    shared["lbaim"] = lbaim.reshape(128, 2048)
    shared["lbldt"] = lbldt.reshape(128, 2048)
    shared["cpre"] = cpre.reshape(128, 2048)
    shared["cpim"] = cpim.reshape(128, 2048)
    s5v = [g("s5_d")[0].reshape(512), g("s5_b_glu")[0], g("s5_gain")[0]]
    shared["s5v"] = np.ascontiguousarray(np.stack([v512(a) for a in s5v], axis=1).reshape(128, 12))
    shared["w_glu"] = np.ascontiguousarray(g("s5_w_glu")[0])
    shared["w_out"] = np.ascontiguousarray(g("w_out")[0])
    shared["wg"] = np.ascontiguousarray(g("ffn_w_gate")[0])
    shared["wu"] = np.ascontiguousarray(g("ffn_w_up")[0])
    shared["wd"] = np.ascontiguousarray(g("ffn_w_down")[0])
    shared["fgain"] = np.ascontiguousarray(g("final_gain")[None, :])
    shared["ident"] = np.eye(128, dtype=f32)
    bo = np.zeros((128, 128), f32)
    bo[0:64, 0:64] = 1.0
    bo[64:128, 64:128] = 1.0
    shared["bones"] = bo
    j = np.arange(64)
    strict = (j[:, None] < j[None, :]).astype(f32)
    incl = (j[:, None] <= j[None, :]).astype(f32)
    mar = np.concatenate([strict, incl], axis=1)
    shared["maskar"] = np.concatenate([mar, mar], axis=0)
    low = (j[None, :] < j[:, None]).astype(f32)
    shared["masknt"] = np.concatenate([low, low], axis=0)
    shared["identp"] = np.concatenate([np.eye(64, dtype=f32)] * 2, axis=0)
    shared["idx1"] = np.tile((np.arange(64, dtype=f32) + 1.0)[None, :], (128, 1))
    rst = np.ones((128, 256), f32)
    rst[:, ::64] = 0.0
    shared["rst01"] = rst
    maps = []
    for core in range(8):
        b, s = core // 2, core % 2
        m = dict(shared)
        m["xcat"] = np.ascontiguousarray(np.concatenate([x[b, 0:TOK], x[b, s * TOK:(s + 1) * TOK]], axis=0))
        m["maskv"] = np.full((128, 1), float(s), f32)
        m["c_l"] = np.ascontiguousarray(c[b].reshape(8, 128).T)
        maps.append(m)
    return maps


def kernel(**inputs):
    global _NC
    if _NC is None:
        _NC = build_program()
    maps = _layout_inputs(inputs)
    res = run_bass_kernel_spmd(_NC, maps, core_ids=list(range(8)))
    out = np.zeros((4, 2 * TOK, D), np.float32)
    for core in range(8):
        b, s = core // 2, core % 2
        out[b, s * TOK:(s + 1) * TOK] = res.results[core]["out"]
    return out
```
